# Optimizing a Trainium2 kernel written in Bass

```python
import jax, jax.numpy as jnp
from jax import lax
import numpy as np

D_MODEL = 2048
BATCH = 4
SEQ = 2048
DEPTH = 2
DEC_BATCH = 128
DEC_SEQ = 1
PAST_LEN = 16384
PAGE_SIZE = 128

N_MIXERS = 2
N_A = (DEPTH + N_MIXERS - 1) // N_MIXERS
N_B = DEPTH // N_MIXERS
HEAD_SIZE = 64
N_HEADS = D_MODEL // HEAD_SIZE
D_DECAY_LORA = max(32, int(round(1.8 * D_MODEL ** 0.5 / 32) * 32))
D_AAA_LORA = max(32, int(round(2.5 * D_MODEL ** 0.5 / 32) * 32))
D_GATE_LORA = max(32, int(round(0.6 * D_MODEL ** 0.8 / 32) * 32))
LNX_EPS = 64e-5
D_RNN = D_MODEL
LRU_BLOCKS = 8
LRU_BW = D_RNN // LRU_BLOCKS
LRU_CONV = 4
LRU_C = 8.0
D_FF = 11 * D_MODEL // 4
FFN_CONV = 3
NORM_EPS = 1e-6
N_MOD = 6

kernel_name = "hybrid_rwkv7_rglru_convffn_step"


def rms_norm(x, g):
    xf = x.astype(jnp.float32)
    y = xf * lax.rsqrt(jnp.mean(xf * xf, axis=-1, keepdims=True) + NORM_EPS)
    return (y * g.astype(jnp.float32)).astype(x.dtype)


def causal_dwconv(x, buf, w, b):
    width = w.shape[0]
    T = x.shape[1]
    xp = jnp.concatenate([buf.astype(x.dtype), x], axis=1)
    y = b + xp[:, 0:T] * w[0]
    for j in range(1, width):
        y = y + xp[:, j:j + T] * w[j]
    return y, xp[:, T:]


def _wkv_step(S, inp):
    r_t, d_t, k_t, v_t, a_t, b_t = inp
    sa = jnp.einsum('bhvk,bhk->bhv', S, a_t)
    S = S * d_t[:, :, None, :] + sa[..., None] * b_t[:, :, None, :] + v_t[..., None] * k_t[:, :, None, :]
    y = jnp.einsum('bhvk,bhk->bhv', S, r_t)
    return S, y


def rwkv7_time_mix(x, shift, s0, mix, w_r, w_k, w_v, w_o, w0, w1, w2, a0, a1, a2,
                   g1, g2, k_k, k_a, r_k, lnx_g, lnx_b):
    B, T, D = x.shape
    f32 = jnp.float32
    x_prev = jnp.concatenate([shift[:, None, :].astype(x.dtype), x[:, :-1]], axis=1)
    xx = x_prev - x
    xr, xw, xk = x + xx * mix[0], x + xx * mix[1], x + xx * mix[2]
    xv, xa, xg = x + xx * mix[3], x + xx * mix[4], x + xx * mix[5]
    r = xr @ w_r
    k = xk @ w_k
    v = xv @ w_v
    w_log = -jax.nn.softplus(-(w0 + jnp.tanh(xw @ w1) @ w2)) - 0.5
    a = jax.nn.sigmoid(a0 + (xa @ a1) @ a2)
    g = jax.nn.sigmoid(xg @ g1) @ g2
    hd = lambda t: t.reshape(B, T, N_HEADS, HEAD_SIZE).astype(f32)
    kk = hd(k * k_k)
    kk = kk / jnp.maximum(jnp.sqrt(jnp.sum(kk * kk, axis=-1, keepdims=True)), 1e-12)
    k = k * (1 + (a - 1) * k_a)
    r_h, k_h, v_h, a_h = hd(r), hd(k), hd(v), hd(a)
    decay = jnp.exp(-jnp.exp(hd(w_log)))
    tm = lambda t: jnp.swapaxes(t, 0, 1)
    xs = (tm(r_h), tm(decay), tm(k_h), tm(v_h), tm(-kk), tm(kk * a_h))
    S_T, ys = lax.scan(_wkv_step, s0.astype(f32), xs)
    y = tm(ys)
    mu = jnp.mean(y, axis=-1, keepdims=True)
    var = jnp.mean(jnp.square(y - mu), axis=-1, keepdims=True)
    y = ((y - mu) * lax.rsqrt(var + LNX_EPS)).reshape(B, T, D) * lnx_g.astype(f32) + lnx_b.astype(f32)
    bonus = (jnp.sum(r_h * k_h * r_k.astype(f32), axis=-1, keepdims=True) * v_h).reshape(B, T, D)
    out = ((y + bonus).astype(x.dtype) * g) @ w_o
    return out, x[:, -1], S_T


def _lin_combine(e1, e2):
    a1, b1 = e1
    a2, b2 = e2
    return a1 * a2, a2 * b1 + b2


def rglru_block(x, h0, conv_buf, reset_first, w_in, b_in, conv_w, conv_b, w_a, b_a, w_x, b_x, lam, w_out):
    B, T, _ = x.shape
    f32 = jnp.float32
    proj = x @ w_in + b_in
    y_br = jax.nn.gelu(proj[..., :D_RNN], approximate=True)
    x_c, new_buf = causal_dwconv(proj[..., D_RNN:], conv_buf, conv_w, conv_b)
    xb = x_c.reshape(B, T, LRU_BLOCKS, LRU_BW)
    gate_a = jax.nn.sigmoid(jnp.einsum('btnk,nkj->btnj', xb, w_a).reshape(B, T, D_RNN) + b_a).astype(f32)
    gate_x = jax.nn.sigmoid(jnp.einsum('btnk,nkj->btnj', xb, w_x).reshape(B, T, D_RNN) + b_x).astype(f32)
    log_a = -LRU_C * gate_a * jax.nn.softplus(-lam.astype(f32))
    a = jnp.exp(log_a)
    mult = jnp.sqrt(-jnp.expm1(2.0 * log_a))
    if reset_first:
        mult = mult.at[:, 0].set(1.0)
    bt = mult * gate_x * x_c.astype(f32)
    bt = bt.at[:, 0].add(a[:, 0] * h0.astype(f32))
    _, h = lax.associative_scan(_lin_combine, (a, bt), axis=1)
    out = (h.astype(x.dtype) * y_br) @ w_out
    return out, h[:, -1], new_buf


def conv_ffn(x, buf, w_gate, w_up, w_down, conv_w, conv_b):
    gt = x @ w_gate
    u = x @ w_up
    gc, new_buf = causal_dwconv(gt, buf, conv_w, conv_b)
    return (jax.nn.gelu(gc, approximate=True) * u) @ w_down, new_buf


def trunk(x, c, st_wkv, st_shift, st_lru_h, st_lru_conv, st_ffn_conv, reset_first, P):
    B = x.shape[0]
    dt = x.dtype
    o_wkv, o_shift, o_h, o_lconv, o_fconv = [], [], [], [], []
    for i in range(DEPTH):
        mod = (jax.nn.silu(c) @ P['w_mod'][i] + P['b_mod'][i]).reshape(B, N_MOD, 1, D_MODEL)
        sh_m, sc_m, gt_m, sh_f, sc_f, gt_f = [mod[:, j] for j in range(N_MOD)]
        h = rms_norm(x, P['norm_g'][i, 0]) * (1 + sc_m) + sh_m
        if i % N_MIXERS == 0:
            j = i // N_MIXERS
            out, shift_new, S = rwkv7_time_mix(
                h, st_shift[j], st_wkv[j], P['rw_mix'][j], P['rw_wr'][j], P['rw_wk'][j], P['rw_wv'][j],
                P['rw_wo'][j], P['rw_w0'][j], P['rw_w1'][j], P['rw_w2'][j], P['rw_a0'][j], P['rw_a1'][j],
                P['rw_a2'][j], P['rw_g1'][j], P['rw_g2'][j], P['rw_kk'][j], P['rw_ka'][j], P['rw_rk'][j],
                P['rw_lnx_g'][j], P['rw_lnx_b'][j])
            o_wkv.append(S.astype(dt))
            o_shift.append(shift_new.astype(dt))
        else:
            j = i // N_MIXERS
            out, h_new, lbuf = rglru_block(
                h, st_lru_h[j], st_lru_conv[j], reset_first, P['lru_w_in'][j], P['lru_b_in'][j],
                P['lru_conv_w'][j], P['lru_conv_b'][j], P['lru_wa'][j], P['lru_ba'][j], P['lru_wx'][j],
                P['lru_bx'][j], P['lru_lambda'][j], P['lru_w_out'][j])
            o_h.append(h_new.astype(dt))
            o_lconv.append(lbuf.astype(dt))
        x = x + gt_m * rms_norm(out, P['norm_g'][i, 1])
        h = rms_norm(x, P['norm_g'][i, 2]) * (1 + sc_f) + sh_f
        f, fbuf = conv_ffn(h, st_ffn_conv[i], P['ffn_w_gate'][i], P['ffn_w_up'][i], P['ffn_w_down'][i],
                           P['ffn_conv_w'][i], P['ffn_conv_b'][i])
        o_fconv.append(fbuf.astype(dt))
        x = x + gt_f * rms_norm(f, P['norm_g'][i, 3])
    return x, jnp.stack(o_wkv), jnp.stack(o_shift), jnp.stack(o_h), jnp.stack(o_lconv), jnp.stack(o_fconv)


def setup_inputs(seed: int = 0) -> dict:
    key = jax.random.key(seed)
    ks = iter(jax.random.split(key, 64))
    f32 = jnp.float32
    nrm = lambda shape, s: jax.random.normal(next(ks), shape, f32) * s
    uni = lambda shape, lo, hi: jax.random.uniform(next(ks), shape, f32, lo, hi)
    D = D_MODEL
    n = jnp.arange(D, dtype=f32) / (D - 1)
    decay_speed = -7.0 + 5.0 * n ** 0.85
    u = uni((N_B, D_RNN), 0.9, 0.999)
    a_base = u ** (1.0 / LRU_C)
    return {
        'x_prompt': nrm((BATCH, SEQ, D), 1.0),
        'x_sample': nrm((DEC_BATCH, DEC_SEQ, D), 1.0),
        'c_prompt': nrm((BATCH, D), 1.0),
        'c_sample': nrm((DEC_BATCH, D), 1.0),
        'state_rwkv_wkv': nrm((N_A, DEC_BATCH, N_HEADS, HEAD_SIZE, HEAD_SIZE), 0.5),
        'state_rwkv_shift': nrm((N_A, DEC_BATCH, D), 1.0),
        'state_lru_h': nrm((N_B, DEC_BATCH, D_RNN), 1.0),
        'state_lru_conv': nrm((N_B, DEC_BATCH, LRU_CONV - 1, D_RNN), 1.0),
        'state_ffn_conv': nrm((DEPTH, DEC_BATCH, FFN_CONV - 1, D_FF), 1.0),
        'w_mod': nrm((DEPTH, D, N_MOD * D), 0.5 * D ** -0.5),
        'b_mod': nrm((DEPTH, N_MOD * D), 0.02),
        'norm_g': 1.0 + nrm((DEPTH, 4, D), 0.05),
        'rw_mix': uni((N_A, 6, D), 0.0, 1.0),
        'rw_wr': nrm((N_A, D, D), D ** -0.5),
        'rw_wk': nrm((N_A, D, D), D ** -0.5),
        'rw_wv': nrm((N_A, D, D), D ** -0.5),
        'rw_wo': nrm((N_A, D, D), D ** -0.5),
        'rw_w0': decay_speed + 0.5 + nrm((N_A, D), 0.1),
        'rw_w1': nrm((N_A, D, D_DECAY_LORA), D ** -0.5),
        'rw_w2': nrm((N_A, D_DECAY_LORA, D), 0.1 * D_DECAY_LORA ** -0.5),
        'rw_a0': nrm((N_A, D), 0.1),
        'rw_a1': nrm((N_A, D, D_AAA_LORA), D ** -0.5),
        'rw_a2': nrm((N_A, D_AAA_LORA, D), 0.5 * D_AAA_LORA ** -0.5),
        'rw_g1': nrm((N_A, D, D_GATE_LORA), D ** -0.5),
        'rw_g2': nrm((N_A, D_GATE_LORA, D), D_GATE_LORA ** -0.5),
        'rw_kk': 0.85 + nrm((N_A, D), 0.05),
        'rw_ka': 1.0 + nrm((N_A, D), 0.05),
        'rw_rk': nrm((N_A, N_HEADS, HEAD_SIZE), 0.1),
        'rw_lnx_g': 1.0 + nrm((N_A, D), 0.05),
        'rw_lnx_b': nrm((N_A, D), 0.02),
        'lru_w_in': nrm((N_B, D, 2 * D_RNN), D ** -0.5),
        'lru_b_in': nrm((N_B, 2 * D_RNN), 0.02),
        'lru_conv_w': nrm((N_B, LRU_CONV, D_RNN), LRU_CONV ** -0.5),
        'lru_conv_b': nrm((N_B, D_RNN), 0.02),
        'lru_wa': nrm((N_B, LRU_BLOCKS, LRU_BW, LRU_BW), LRU_BW ** -0.5),
        'lru_ba': nrm((N_B, D_RNN), 0.02),
        'lru_wx': nrm((N_B, LRU_BLOCKS, LRU_BW, LRU_BW), LRU_BW ** -0.5),
        'lru_bx': nrm((N_B, D_RNN), 0.02),
        'lru_lambda': jnp.log(a_base) - jnp.log1p(-a_base),
        'lru_w_out': nrm((N_B, D_RNN, D), D_RNN ** -0.5),
        'ffn_w_gate': nrm((DEPTH, D, D_FF), D ** -0.5),
        'ffn_w_up': nrm((DEPTH, D, D_FF), D ** -0.5),
        'ffn_w_down': nrm((DEPTH, D_FF, D), D_FF ** -0.5),
        'ffn_conv_w': nrm((DEPTH, FFN_CONV, D_FF), FFN_CONV ** -0.5),
        'ffn_conv_b': nrm((DEPTH, D_FF), 0.02),
    }


def reference(x_prompt, x_sample, c_prompt, c_sample, state_rwkv_wkv, state_rwkv_shift, state_lru_h,
              state_lru_conv, state_ffn_conv, w_mod, b_mod, norm_g, rw_mix, rw_wr, rw_wk, rw_wv, rw_wo,
              rw_w0, rw_w1, rw_w2, rw_a0, rw_a1, rw_a2, rw_g1, rw_g2, rw_kk, rw_ka, rw_rk, rw_lnx_g,
              rw_lnx_b, lru_w_in, lru_b_in, lru_conv_w, lru_conv_b, lru_wa, lru_ba, lru_wx, lru_bx,
              lru_lambda, lru_w_out, ffn_w_gate, ffn_w_up, ffn_w_down, ffn_conv_w, ffn_conv_b):
    P = dict(w_mod=w_mod, b_mod=b_mod, norm_g=norm_g, rw_mix=rw_mix, rw_wr=rw_wr, rw_wk=rw_wk,
             rw_wv=rw_wv, rw_wo=rw_wo, rw_w0=rw_w0, rw_w1=rw_w1, rw_w2=rw_w2, rw_a0=rw_a0, rw_a1=rw_a1,
             rw_a2=rw_a2, rw_g1=rw_g1, rw_g2=rw_g2, rw_kk=rw_kk, rw_ka=rw_ka, rw_rk=rw_rk,
             rw_lnx_g=rw_lnx_g, rw_lnx_b=rw_lnx_b, lru_w_in=lru_w_in, lru_b_in=lru_b_in,
             lru_conv_w=lru_conv_w, lru_conv_b=lru_conv_b, lru_wa=lru_wa, lru_ba=lru_ba, lru_wx=lru_wx,
             lru_bx=lru_bx, lru_lambda=lru_lambda, lru_w_out=lru_w_out, ffn_w_gate=ffn_w_gate,
             ffn_w_up=ffn_w_up, ffn_w_down=ffn_w_down, ffn_conv_w=ffn_conv_w, ffn_conv_b=ffn_conv_b)
    dt = x_prompt.dtype
    z_wkv = jnp.zeros((N_A, BATCH, N_HEADS, HEAD_SIZE, HEAD_SIZE), dt)
    z_shift = jnp.zeros((N_A, BATCH, D_MODEL), dt)
    z_h = jnp.zeros((N_B, BATCH, D_RNN), dt)
    z_lconv = jnp.zeros((N_B, BATCH, LRU_CONV - 1, D_RNN), dt)
    z_fconv = jnp.zeros((DEPTH, BATCH, FFN_CONV - 1, D_FF), dt)
    y_prompt, p_wkv, p_shift, p_h, p_lconv, p_fconv = trunk(
        x_prompt, c_prompt, z_wkv, z_shift, z_h, z_lconv, z_fconv, True, P)
    y_sample, s_wkv, s_shift, s_h, s_lconv, s_fconv = trunk(
        x_sample, c_sample, state_rwkv_wkv, state_rwkv_shift, state_lru_h, state_lru_conv,
        state_ffn_conv, False, P)
    return (y_prompt, y_sample, p_wkv, p_shift, p_h, p_lconv, p_fconv,
            s_wkv, s_shift, s_h, s_lconv, s_fconv)
```

```python
import numpy as np
import concourse.bass as bass
import concourse.mybir as mybir
from concourse.bass_utils import run_bass_kernel_spmd

F32 = mybir.dt.float32
BF16 = mybir.dt.bfloat16
AF = mybir.ActivationFunctionType
ALU = mybir.AluOpType
AX = mybir.AxisListType

D = 2048
NCH = 16
DFF = 5632
NF = 44
TS = 512
NS = 16
NH = 32
EPS = 1e-6
LNX_EPS = 64e-5
DEC = -0.6065306597126334
WB_ELEMS = 6144
NWB = 3


class Buf:
    __slots__ = ("name", "lw", "wd", "rd", "rdd", "alias", "psum", "notrack")

    def __init__(self, name):
        self.name = name
        self.lw = None
        self.wd = []
        self.rd = {}
        self.rdd = []
        self.alias = []
        self.psum = False
        self.notrack = False


class V:
    __slots__ = ("ap", "b")

    def __init__(self, ap, b):
        self.ap = ap
        self.b = b

    def __getitem__(self, k):
        return V(self.ap[k], self.b)

    def re(self, s, **kw):
        return V(self.ap.rearrange(s, **kw), self.b)

    def bc(self, shape):
        return V(self.ap.to_broadcast(list(shape)), self.b)

    def us(self, ax):
        return V(self.ap.unsqueeze(ax), self.b)


class Op:
    __slots__ = ("eng", "fn", "reads", "writes", "dma", "waits", "sig", "cnt", "slot", "slotcnt", "prewait")

    def __init__(self, eng, fn, reads, writes, dma):
        self.eng = eng
        self.fn = fn
        self.reads = reads
        self.writes = writes
        self.dma = dma
        self.waits = []
        self.sig = False
        self.cnt = 0


NSLOT = 16
ARENA_WORDS = 53150


class KB:
    ENGS = ("pe", "dve", "act", "pool", "sp")

    def __init__(self):
        self.nc = bass.Bass("TRN2", target_bir_lowering=False)
        self.ops = []
        self._ctx = []
        g = self.nc.sbuf_tensor("arena", [128, ARENA_WORDS], F32)
        self.arena = g.__enter__()[:]
        self._ctx.append(g)
        self.top = 0
        self.live = []
        self.dead = []
        self.banks = []
        for i in range(8):
            g = self.nc.psum_tensor(f"bank{i}", [128, 512], F32)
            self.banks.append(V(g.__enter__()[:], Buf(f"bank{i}")))
            self.banks[-1].b.psum = True
            self._ctx.append(g)
        self.reserved = set()
        self.bi = 0
        self.maxtop = 0

    def alloc(self, shape, dt=F32, name="t"):
        n = 1
        for s in shape[1:]:
            n *= s
        words = n if dt == F32 else (n + 1) // 2
        words = (words + 1) // 2 * 2
        st = self.top
        en = st + words
        assert en <= ARENA_WORDS, f"arena overflow {en}"
        self.top = en
        self.maxtop = max(self.maxtop, en)
        ap = self.arena[:, st:en]
        if dt != F32:
            ap = ap.bitcast(dt)
        ap = ap[0:shape[0], 0:n]
        if len(shape) == 3:
            ap = ap.rearrange("p (a b) -> p a b", b=shape[2])
        elif len(shape) == 4:
            ap = ap.rearrange("p (a b c) -> p a b c", b=shape[2], c=shape[3])
        b = Buf(name)
        keep = []
        for (s0, e0, ob) in self.dead:
            if s0 < en and st < e0:
                b.alias.append(ob)
                if s0 < st:
                    keep.append((s0, st, ob))
                if e0 > en:
                    keep.append((en, e0, ob))
            else:
                keep.append((s0, e0, ob))
        self.dead = keep
        self.live.append((st, en, b))
        return V(ap, b)

    def mark(self):
        return (self.top, len(self.live))

    def release(self, m):
        top, nl = m
        for ent in self.live[nl:]:
            self.dead.append(ent)
        del self.live[nl:]
        self.top = top

    def bank(self):
        for _ in range(8):
            i = self.bi % 8
            self.bi += 1
            if i not in self.reserved:
                return self.banks[i]
        raise RuntimeError("no bank")

    def reserve(self):
        for _ in range(8):
            i = self.bi % 8
            self.bi += 1
            if i not in self.reserved:
                self.reserved.add(i)
                return self.banks[i]
        raise RuntimeError("no bank")

    def unreserve(self, b):
        for i, x in enumerate(self.banks):
            if x.b is b.b:
                self.reserved.discard(i)

    def dram(self, name, shape, dt=F32, kind="Internal"):
        t = self.nc.dram_tensor(name, list(shape), dt, kind=kind)
        return V(t.ap(), Buf(name))

    def op(self, eng, fn, reads=(), writes=(), dma=False):
        o = Op(eng, fn, [r.b for r in reads], [w.b for w in writes], dma)
        self.ops.append(o)
        return o

    def dma(self, out, in_, eng="sp"):
        return self.op(eng, lambda e: e.dma_start(out=out.ap, in_=in_.ap), [in_], [out], dma=True)

    def mm(self, out, lhsT, rhs, start=True, stop=True):
        return self.op("pe", lambda e: e.matmul(out=out.ap, lhsT=lhsT.ap, rhs=rhs.ap, start=start, stop=stop),
                       [lhsT, rhs], [out])

    def tr(self, out, in_, ident):
        return self.op("pe", lambda e: e.transpose(out=out.ap, in_=in_.ap, identity=ident.ap), [in_, ident], [out])

    def act(self, out, in_, func, bias=None, scale=None):
        reads = [in_] + [x for x in (bias, scale) if isinstance(x, V)]
        kw = {}
        if bias is not None:
            kw["bias"] = bias.ap if isinstance(bias, V) else float(bias)
        if scale is not None:
            kw["scale"] = scale.ap if isinstance(scale, V) else float(scale)
        return self.op("act", lambda e: e.activation(out=out.ap, in_=in_.ap, func=func, **kw), reads, [out])

    def copy(self, eng, out, in_):
        if eng == "act":
            return self.op("act", lambda e: e.copy(out=out.ap, in_=in_.ap), [in_], [out])
        return self.op(eng, lambda e: e.tensor_copy(out=out.ap, in_=in_.ap), [in_], [out])

    def tt(self, eng, out, a, b, op):
        return self.op(eng, lambda e: e.tensor_tensor(out=out.ap, in0=a.ap, in1=b.ap, op=op), [a, b], [out])

    def ts(self, eng, out, a, s1, s2, op0, op1=None):
        reads = [a] + [x for x in (s1, s2) if isinstance(x, V)]
        v1 = s1.ap if isinstance(s1, V) else float(s1)
        v2 = None if s2 is None else (s2.ap if isinstance(s2, V) else float(s2))
        if op1 is None:
            return self.op(eng, lambda e: e.tensor_scalar(out=out.ap, in0=a.ap, scalar1=v1, scalar2=None, op0=op0),
                           reads, [out])
        return self.op(eng, lambda e: e.tensor_scalar(out=out.ap, in0=a.ap, scalar1=v1, scalar2=v2, op0=op0, op1=op1),
                       reads, [out])

    def stt(self, out, a, s, b, op0, op1):
        reads = [a, b] + ([s] if isinstance(s, V) else [])
        sv = s.ap if isinstance(s, V) else float(s)
        return self.op("dve", lambda e: e.scalar_tensor_tensor(out=out.ap, in0=a.ap, scalar=sv, in1=b.ap, op0=op0, op1=op1),
                       reads, [out])

    def memset(self, eng, out, val):
        return self.op(eng, lambda e: e.memset(out.ap, val), [], [out])

    def finalize(self):
        ops = self.ops
        for i, o in enumerate(ops):
            deps = set()
            for b in o.reads:
                if b.lw is not None:
                    deps.add(b.lw)
                deps.update(b.wd)
                if b.psum:
                    for e, j in b.rd.items():
                        if e != o.eng:
                            deps.add(j)
            for b in o.writes:
                if b.notrack:
                    continue
                for ob in b.alias:
                    if ob.lw is not None:
                        deps.add(ob.lw)
                    deps.update(ob.wd)
                    deps.update(ob.rd.values())
                    deps.update(ob.rdd)
                b.alias = []
                if b.lw is not None:
                    deps.add(b.lw)
                if not o.dma:
                    deps.update(b.wd)
                deps.update(b.rd.values())
                deps.update(b.rdd)
            for b in o.reads:
                if o.dma:
                    b.rdd.append(i)
                else:
                    b.rd[o.eng] = i
            for b in o.writes:
                if b.notrack:
                    continue
                had_readers = bool(b.rd or b.rdd)
                if o.dma:
                    if had_readers:
                        b.wd = [i]
                        b.lw = None
                    else:
                        b.wd.append(i)
                else:
                    b.lw = i
                    b.wd = []
                b.rd = {}
                b.rdd = []
            best = {}
            dl = []
            for j in deps:
                if j == i:
                    continue
                p = ops[j]
                e = p.eng
                if p.dma:
                    dl.append(j)
                else:
                    if e == "pe" and o.eng == "pe" and not o.dma:
                        continue
                    if e not in best or best[e] < j:
                        best[e] = j
            o.waits = sorted(set(list(best.values()) + dl))
            for j in o.waits:
                ops[j].sig = True
        cnt = {e: 0 for e in self.ENGS}
        dcount = {e: 0 for e in self.ENGS}
        slotcnt = {e: [0] * NSLOT for e in self.ENGS}
        for o in ops:
            if o.dma:
                kk = dcount[o.eng]
                dcount[o.eng] += 1
                s = kk % NSLOT
                o.slot = s
                o.prewait = slotcnt[o.eng][s]
                slotcnt[o.eng][s] += 16
                o.slotcnt = slotcnt[o.eng][s]
            elif o.sig:
                cnt[o.eng] += 1
                o.cnt = cnt[o.eng]
        self.final_dma = {e: list(slotcnt[e]) for e in self.ENGS}

    def emit(self):
        nc = self.nc
        ops = self.ops
        self.finalize()
        sems = {}
        for e in self.ENGS:
            g = nc.semaphore(f"s_{e}")
            sems[e] = g.__enter__()
            self._ctx.append(g)
        dsem = {}
        for e in self.ENGS:
            if any(o.dma and o.eng == e for o in ops):
                dsem[e] = []
                for s in range(NSLOT):
                    g = nc.semaphore(f"d_{e}_{s}")
                    dsem[e].append(g.__enter__())
                    self._ctx.append(g)
        per = {e: [o for o in ops if o.eng == e] for e in self.ENGS}
        final_dma = self.final_dma

        def run(eng_name, eng):
            waited = {}
            for o in per[eng_name]:
                if o.dma and o.prewait > 0:
                    key = ("d", eng_name, o.slot)
                    if waited.get(key, 0) < o.prewait:
                        eng.wait_ge(dsem[eng_name][o.slot], o.prewait)
                        waited[key] = o.prewait
                for j in o.waits:
                    p = ops[j]
                    if p.dma:
                        key = ("d", p.eng, p.slot)
                        val = p.slotcnt
                        sem = dsem[p.eng][p.slot]
                    else:
                        key = ("c", p.eng)
                        val = p.cnt
                        sem = sems[p.eng]
                    if waited.get(key, 0) < val:
                        eng.wait_ge(sem, val)
                        waited[key] = val
                ins = o.fn(eng)
                if o.dma:
                    ins.then_inc(dsem[eng_name][o.slot], 16)
                elif o.sig:
                    ins.then_inc(sems[eng_name], 1)
            if eng_name in dsem:
                for s in range(NSLOT):
                    v = final_dma[eng_name][s]
                    if v > 0 and waited.get(("d", eng_name, s), 0) < v:
                        eng.wait_ge(dsem[eng_name][s], v)

        with nc.Block() as block:
            @block.sync
            def _(e):
                run("sp", e)

            @block.tensor
            def _(e):
                run("pe", e)

            @block.vector
            def _(e):
                run("dve", e)

            @block.scalar
            def _(e):
                run("act", e)

            @block.gpsimd
            def _(e):
                run("pool", e)
        return nc


def pvec_layout():
    names = []
    for i in range(2):
        for j in range(4):
            names.append((f"ng{i}{j}", 16))
    for i in range(2):
        names.append((f"bmod{i}", 96))
    for j in range(6):
        names.append((f"mix{j}", 16))
    for n in ("w0", "a0", "kk", "ka", "rk", "lcb", "ba", "bx", "lam"):
        names.append((n, 16))
    names.append(("bin", 32))
    for j in range(4):
        names.append((f"lcw{j}", 16))
    for i in range(2):
        for j in range(3):
            names.append((f"fcw{i}{j}", 44))
        names.append((f"fcb{i}", 44))
    names.append(("omka", 16))
    names.append(("cL", 16))
    names.append(("hba", 16))
    names.append(("hbx", 16))
    names.append(("hcL", 16))
    off = {}
    o = 0
    for n, w in names:
        off[n] = (o, w)
        o += w
    return off, o


PV_OFF, PV_N = pvec_layout()


def build(NSEG):
    k = KB()
    T = NSEG * TS
    IN = lambda n, s: k.dram(n, s, F32, kind="ExternalInput")
    def OUT(n, s):
        v = k.dram(n, s, F32, kind="ExternalOutput")
        v.b.notrack = True
        return v
    xp_d = IN("xp", [T, D])
    xs_d = IN("xs", [NS, D])
    call_d = IN("call", [17, D])
    stw_d = IN("st_wkv", [NS * NH, 4096])
    stsh_d = IN("st_shift", [NS, D])
    sth_d = IN("st_h", [NS, D])
    stlc_d = IN("st_lconv", [NS, 3, D])
    stfc_d = IN("st_fconv", [2, NS, 2, DFF])
    pvec_d = IN("pvec", [128, PV_N])
    rows_d = IN("rows", [3, D])
    cmask_d = IN("cmask", [128, 7, 512])
    W = {}
    for n, s in (("w_mod", [2, D, 6 * D]), ("rw_wr", [D, D]), ("rw_wk", [D, D]), ("rw_wv", [D, D]), ("rw_wo", [D, D]),
                 ("rw_w1", [D, 96]), ("rw_w2", [96, D]), ("rw_a1", [D, 128]), ("rw_a2", [128, D]),
                 ("rw_g1", [D, 256]), ("rw_g2", [256, D]), ("lru_w_in", [D, 2 * D]), ("lru_wa", [8, 256, 256]),
                 ("lru_wx", [8, 256, 256]), ("lru_w_out", [D, D]), ("ffn_w_gate", [2, D, DFF]),
                 ("ffn_w_up", [2, D, DFF]), ("ffn_w_down", [2, DFF, D])):
        W[n] = IN(n, s)
    yp_d = OUT("y_p", [T, D])
    ys_d = OUT("y_s", [NS, D])
    pwkv_d = OUT("p_wkv", [NH, 64, 64])
    pshift_d = OUT("p_shift", [16, 128])
    ph_d = OUT("p_h", [16, 128])
    plc_d = OUT("p_lconv", [3, D])
    pfc_d = OUT("p_fconv", [2, 2, DFF])
    swkv_d = OUT("s_wkv", [NS * NH, 4096])
    sshift_d = OUT("s_shift", [NS, D])
    sh_d = OUT("s_h", [NS, D])
    slc_d = OUT("s_lconv", [NS, 3, D])
    sfc_d = OUT("s_fconv", [2, NS, 2, DFF])
    import os
    DBG = os.environ.get("KDBG") == "1"
    dbg_n = {"n": 0}

    def dump(name, v, shape, dt_is_bf16=False):
        if not DBG:
            return
        d = k.dram("dbg_" + name, list(shape), F32, kind="ExternalOutput")
        if dt_is_bf16:
            k.dma(d, v, eng="pool")
        else:
            k.dma(d, v)
    STOP = int(os.environ.get("KSTOP", "0"))

    class StopBuild(Exception):
        pass

    def ckpt(n):
        if STOP == n:
            raise StopBuild()
    xres_d = [k.dram(f"xres{c}", [128, TS]) for c in range(16)]
    scr6_d = k.dram("scr6", [6, NS, D])
    scro_d = k.dram("scro", [NS, D])

    ones_f = k.alloc([128, 128], F32, "ones_f")
    k.memset("pool", ones_f, 1.0)
    identf = k.alloc([128, 128], F32, "identf")
    k.op("pool", lambda e: e.affine_select(out=identf.ap, in_=ones_f.ap, pattern=[[-1, 128]], compare_op=ALU.is_equal,
                                           fill=0.0, base=0, channel_multiplier=1), [ones_f], [identf])
    identb = k.alloc([128, 128], BF16, "identb")
    k.copy("pool", identb, identf)
    onesb = k.alloc([128, 128], BF16, "onesb")
    k.copy("pool", onesb, ones_f)
    blkdiag = k.alloc([128, 128], BF16, "blkdiag")
    k.memset("pool", blkdiag, 0.0)
    k.memset("pool", blkdiag[0:64, 0:64], 1.0)
    k.memset("pool", blkdiag[64:128, 64:128], 1.0)
    blk2 = k.alloc([128, 2], BF16, "blk2")
    k.memset("pool", blk2, 0.0)
    k.memset("pool", blk2[0:64, 0:1], 1.0)
    k.memset("pool", blk2[64:128, 1:2], 1.0)
    mA = k.alloc([128, 512], F32, "mA")
    mB = k.alloc([128, 512], F32, "mB")
    mC = k.alloc([128, 256], F32, "mC")

    def sel(dst, kind):
        if kind == "Ls":
            pat, base, cm = [[-1, 128]], -1, 1
        elif kind == "Us":
            pat, base, cm = [[1, 128]], -1, -1
        else:
            pat, base, cm = [[1, 128]], 0, -1
        k.op("pool", lambda e: e.affine_select(out=dst.ap, in_=ones_f.ap, pattern=pat, compare_op=ALU.is_ge,
                                               fill=0.0, base=base, channel_multiplier=cm), [ones_f], [dst])
    for i, kd in enumerate(("Ls", "Ls", "Us", "Us")):
        sel(mA[:, i * 128:(i + 1) * 128], kd)
    for i, kd in enumerate(("Us", "Us", "Ui", "Ui")):
        sel(mB[:, i * 128:(i + 1) * 128], kd)
    for i, kd in enumerate(("Ui", "Ui")):
        sel(mC[:, i * 128:(i + 1) * 128], kd)

    ML = k.alloc([128, 7, 512], BF16, "ML")
    for l_ in range(7):
        k.dma(ML[:, l_, :], cmask_d[:, l_, :], eng="pool")
    I4 = k.alloc([128, 512], BF16, "I4")
    for i_ in range(4):
        k.copy("pool", I4[:, i_ * 128:(i_ + 1) * 128], identf)
    pv = k.alloc([128, PV_N], F32, "pv")
    k.dma(pv, pvec_d)

    def P(name, c):
        o, w = PV_OFF[name]
        return pv[:, o + c:o + c + 1]

    def PR(name):
        o, w = PV_OFF[name]
        return pv[:, o:o + w]
    k.ts("dve", PR("omka"), PR("ka"), -1.0, 1.0, ALU.mult, ALU.add)
    k.act(PR("cL"), PR("lam"), AF.Exp, scale=-1.0)
    k.act(PR("cL"), PR("cL"), AF.Ln, bias=1.0)
    k.ts("dve", PR("cL"), PR("cL"), -8.0, None, ALU.mult)
    k.ts("dve", PR("hcL"), PR("cL"), 0.5, None, ALU.mult)
    k.ts("dve", PR("hba"), PR("ba"), 0.5, None, ALU.mult)
    k.ts("dve", PR("hbx"), PR("bx"), 0.5, None, ALU.mult)

    modT = [k.alloc([128, 96, 17], F32, f"modT{i}") for i in range(2)]
    shift_car = k.alloc([128, 16], F32, "shift_car")
    k.memset("pool", shift_car, 0.0)
    fcar = [k.alloc([128, 2, NF], F32, f"fcar{i}") for i in range(2)]
    for i in range(2):
        k.memset("pool", fcar[i], 0.0)
    lcar = k.alloc([128, 3, 16], F32, "lcar")
    k.memset("pool", lcar, 0.0)
    hcar = k.alloc([128, 16], F32, "hcar")
    k.memset("pool", hcar, 0.0)
    T32 = k.alloc([128, 16, 64], F32, "T32")
    k.memset("pool", T32, 0.0)
    xsT = [k.alloc([128, NS], F32, f"xsT{c}") for c in range(16)]
    wbufs = [k.alloc([128, WB_ELEMS], BF16, f"wb{i}") for i in range(NWB)]
    wstate = {"n": 0}

    class WS:
        def __init__(self, reqs):
            self.reqs = reqs
            self.emitted = 0
            self.views = {}

        def _emit(self, i):
            buf = wbufs[wstate["n"] % NWB]
            wstate["n"] += 1
            off = 0
            vs = []
            for (src, nk) in self.reqs[i]:
                rows, cols = src.ap.shape
                pr = rows // nk
                dst = buf[0:pr, off:off + nk * cols].re("p (a b) -> p a b", b=cols)
                k.dma(dst, src.re("(a p) c -> p a c", p=pr), eng="pool")
                vs.append(dst)
                off += nk * cols
            assert off <= WB_ELEMS
            self.views[i] = vs

        def get(self, i):
            lim = min(len(self.reqs), i + NWB - 1)
            while self.emitted < max(lim, i + 1):
                self._emit(self.emitted)
                self.emitted += 1
            return self.views.pop(i)

    evt = {"n": 0}

    def ev(out, in_):
        evt["n"] += 1
        k.copy("act" if evt["n"] % 2 else "dve", out, in_)

    def tok2fm(dst, src_d, R):
        m = k.mark()
        t = k.alloc([R, 128], F32, "t2f")
        k.dma(t, src_d)
        b = k.bank()
        k.tr(b[:, 0:R], t, identf[0:R, 0:R])
        ev(dst, b[:, 0:R])
        k.release(m)

    def fm2tok(dst_d, src, R):
        m = k.mark()
        t = k.alloc([R, 128], F32, "f2t")
        b = k.bank()
        k.tr(b[0:R, 0:128], src, identf)
        ev(t, b[0:R, 0:128])
        k.dma(dst_d, t)
        k.release(m)

    def compute_mod():
        m = k.mark()
        ct = k.alloc([17, D], F32, "ct")
        k.dma(ct, call_d)
        k.act(ct, ct, AF.Silu)
        siluT = k.alloc([128, 16, 17], BF16, "siluT")
        for c in range(16):
            b = k.bank()
            k.tr(b[:, 0:17], ct[:, c * 128:(c + 1) * 128], identf[0:17, 0:17])
            ev(siluT[:, c, :], b[:, 0:17])
        for i in range(2):
            ws = WS([[(W["w_mod"][i][:, g * 256:(g + 1) * 256], 16)] for g in range(48)])
            for g in range(48):
                (wv,) = ws.get(g)
                b = k.bank()
                for j in range(2):
                    for kc in range(16):
                        k.mm(b[:, j * 17:(j + 1) * 17], wv[:, kc, j * 128:(j + 1) * 128], siluT[:, kc, :],
                             start=(kc == 0), stop=(kc == 15))
                for j in range(2):
                    jc = g * 2 + j
                    k.act(modT[i][:, jc, :], b[:, j * 17:(j + 1) * 17], AF.Identity, bias=P(f"bmod{i}", jc))
            for c in range(16):
                k.ts("dve", modT[i][:, 16 + c, :], modT[i][:, 16 + c, :], 1.0, P(f"ng{i}0", c), ALU.add, ALU.mult)
                k.ts("dve", modT[i][:, 64 + c, :], modT[i][:, 64 + c, :], 1.0, P(f"ng{i}2", c), ALU.add, ALU.mult)
                k.ts("dve", modT[i][:, 32 + c, :], modT[i][:, 32 + c, :], P(f"ng{i}1", c), None, ALU.mult)
                k.ts("dve", modT[i][:, 80 + c, :], modT[i][:, 80 + c, :], P(f"ng{i}3", c), None, ALU.mult)
        k.release(m)

    def modv(i, j, c, sample):
        return modT[i][:, j * 16 + c, 1:17] if sample else modT[i][:, j * 16 + c, 0:1]

    def rstd_of(chunks, N):
        rstd = k.alloc([128, N], F32, "rstd")
        m = k.mark()
        sq = [k.alloc([128, N], BF16, "sq") for _ in range(2)]
        bank = k.reserve()
        for c in range(len(chunks)):
            k.act(sq[c % 2], chunks[c], AF.Square)
            k.mm(bank[:, 0:N], onesb, sq[c % 2], start=(c == 0), stop=(c == len(chunks) - 1))
        k.act(rstd, bank[:, 0:N], AF.Ln, bias=EPS, scale=1.0 / D)
        k.act(rstd, rstd, AF.Exp, scale=-0.5)
        k.unreserve(bank)
        k.release(m)
        return rstd

    def adaln_chunk(out, tmp, x_c, rstd, gsc, sh, sample):
        k.tt("dve", tmp, x_c, rstd, ALU.mult)
        if not sample:
            k.act(out, tmp, AF.Identity, bias=sh, scale=gsc)
        else:
            k.tt("dve", tmp, tmp, gsc, ALU.mult)
            k.tt("dve", out, tmp, sh, ALU.add)

    def load_x():
        xs_ = [k.alloc([128, TS], F32, f"x{c}") for c in range(16)]
        for c in range(16):
            k.dma(xs_[c], xres_d[c])
        return xs_

    def proj_post(layer, wsrc, KC, in_p, in_s, jgate, ncol):
        m = k.mark()
        pout = [k.alloc([128, TS], F32, f"po{c}") for c in range(16)]
        pouts = [k.alloc([128, NS], F32, f"pos{c}") for c in range(16)] if in_s else None
        sq = [k.alloc([128, TS], BF16, "psq") for _ in range(2)]
        sqs = k.alloc([128, NS], BF16, "psqs")
        ngrp = D // ncol
        ws = WS([[(wsrc[:, g * ncol:(g + 1) * ncol], KC)] for g in range(ngrp)])
        bssq = k.reserve()
        bssqs = k.reserve() if in_s else None
        for g in range(ngrp):
            (wv,) = ws.get(g)
            for j in range(ncol // 128):
                c = g * (ncol // 128) + j
                b = k.bank()
                for kc in range(KC):
                    k.mm(b, wv[:, kc, j * 128:(j + 1) * 128], in_p[kc], start=(kc == 0), stop=(kc == KC - 1))
                k.copy("act", pout[c], b)
                k.act(sq[c % 2], pout[c], AF.Square)
                k.mm(bssq, onesb, sq[c % 2], start=(c == 0), stop=(c == 15))
                if in_s:
                    b2 = k.bank()
                    for kc in range(KC):
                        k.mm(b2[:, 0:NS], wv[:, kc, j * 128:(j + 1) * 128], in_s[kc], start=(kc == 0), stop=(kc == KC - 1))
                    k.copy("dve", pouts[c], b2[:, 0:NS])
                    k.act(sqs, pouts[c], AF.Square)
                    k.mm(bssqs[:, 0:NS], onesb, sqs, start=(c == 0), stop=(c == 15))
        rstd = k.alloc([128, TS], F32, "prstd")
        k.act(rstd, bssq, AF.Ln, bias=EPS, scale=1.0 / D)
        k.act(rstd, rstd, AF.Exp, scale=-0.5)
        k.unreserve(bssq)
        if in_s:
            rstds = k.alloc([128, NS], F32, "prstds")
            k.act(rstds, bssqs[:, 0:NS], AF.Ln, bias=EPS, scale=1.0 / D)
            k.act(rstds, rstds, AF.Exp, scale=-0.5)
            k.unreserve(bssqs)
        xt = [k.alloc([128, TS], F32, "pxt") for _ in range(2)]
        for c in range(16):
            x_ = xt[c % 2]
            k.dma(x_, xres_d[c])
            k.tt("dve", pout[c], pout[c], rstd, ALU.mult)
            k.stt(x_, pout[c], modv(layer, jgate, c, False), x_, ALU.mult, ALU.add)
            k.dma(xres_d[c], x_)
            if in_s:
                k.tt("dve", pouts[c], pouts[c], rstds, ALU.mult)
                k.tt("dve", pouts[c], pouts[c], modv(layer, jgate, c, True), ALU.mult)
                k.tt("dve", xsT[c], xsT[c], pouts[c], ALU.add)
        k.release(m)

    def rwkv_layer(seg, do_s, last):
        L = 0
        mtop = k.mark()
        mixes = {j: [k.alloc([128, TS], BF16, f"mx{j}_{c}") for c in range(16)] for j in (0, 2, 3)}
        yg = [k.alloc([128, TS], BF16, f"yg{c}") for c in range(16)]
        tw = k.alloc([128, TS], BF16, "tw")
        la = k.alloc([128, TS], BF16, "la")
        sg = [k.alloc([128, TS], BF16, f"sg{i}") for i in range(2)]
        if do_s:
            smix = {j: [k.alloc([128, NS], BF16, f"smx{j}_{c}") for c in range(16)] for j in range(6)}
            ygs = [k.alloc([128, NS], BF16, f"ygs{c}") for c in range(16)]
            tws = k.alloc([128, NS], BF16, "tws")
            las = k.alloc([128, NS], BF16, "las")
            sgs = [k.alloc([128, NS], BF16, f"sgs{i}") for i in range(2)]
            gs = k.alloc([128, 16, NS], F32, "gs")
        m1 = k.mark()
        x = [k.alloc([128, TS], F32, f"x{c}") for c in range(16)]
        mxin = k.mark()
        xin = [k.alloc([128, D], F32, "xin") for _ in range(2)]
        for tt_ in range(4):
            xi = xin[tt_ % 2]
            k.dma(xi, xp_d[seg * TS + tt_ * 128: seg * TS + (tt_ + 1) * 128, :])
            for c4 in range(4):
                if os.environ.get("KVAR") == "notr":
                    continue
                b = k.bank()
                for q in range(4):
                    c = c4 * 4 + q
                    k.tr(b[:, q * 128:(q + 1) * 128], xi[:, c * 128:(c + 1) * 128], identf)
                for q in range(4):
                    c = c4 * 4 + q
                    ev(x[c][:, tt_ * 128:(tt_ + 1) * 128], b[:, q * 128:(q + 1) * 128])
        if os.environ.get("KVAR") != "nostore":
            for c in range(16):
                k.dma(xres_d[c], x[c])
        k.release(mxin)
        ckpt(11)
        rstd = rstd_of(x, TS)
        ckpt(12)
        wsl = WS([[(W["rw_w1"], 16), (W["rw_a1"], 16)], [(W["rw_g1"], 16)]])
        w1v, a1v = wsl.get(0)
        (g1v,) = wsl.get(1)
        bw, ba, bg0, bg1 = k.reserve(), k.reserve(), k.reserve(), k.reserve()
        hc = [k.alloc([128, 1 + TS], F32, "hc") for _ in range(2)]
        xx = k.alloc([128, TS], F32, "xx")
        tmpn = k.alloc([128, TS], F32, "tmpn")
        tb = [k.alloc([128, TS], BF16, "tb") for _ in range(3)]
        for c in range(16):
            h_ = hc[c % 2]
            k.copy("dve", h_[:, 0:1], shift_car[:, c:c + 1])
            adaln_chunk(h_[:, 1:1 + TS], tmpn, x[c], rstd, modv(L, 1, c, False), modv(L, 0, c, False), False)
            k.copy("dve", shift_car[:, c:c + 1], h_[:, TS:TS + 1])
            k.tt("dve", xx, h_[:, 0:TS], h_[:, 1:1 + TS], ALU.subtract)
            ti = 0
            for j in range(6):
                if j in (0, 2, 3):
                    dst = mixes[j][c]
                else:
                    dst = tb[ti]
                    ti += 1
                k.stt(dst, xx, P(f"mix{j}", c), h_[:, 1:1 + TS], ALU.mult, ALU.add)
                if j == 1:
                    k.mm(bw[0:96, :], w1v[:, c, :], dst, start=(c == 0), stop=(c == 15))
                elif j == 4:
                    k.mm(ba, a1v[:, c, :], dst, start=(c == 0), stop=(c == 15))
                elif j == 5:
                    k.mm(bg0, g1v[:, c, 0:128], dst, start=(c == 0), stop=(c == 15))
                    k.mm(bg1, g1v[:, c, 128:256], dst, start=(c == 0), stop=(c == 15))
        k.act(tw[0:96, :], bw[0:96, :], AF.Tanh)
        k.copy("dve", la, ba)
        k.act(sg[0], bg0, AF.Sigmoid)
        k.act(sg[1], bg1, AF.Sigmoid)
        for b_ in (bw, ba, bg0, bg1):
            k.unreserve(b_)
        ckpt(13)
        if do_s:
            rstds = rstd_of(xsT, NS)
            hs = k.alloc([128, NS], F32, "hs")
            hp = k.alloc([128, NS], F32, "hp")
            xxs = k.alloc([128, NS], F32, "xxs")
            tmps = k.alloc([128, NS], F32, "tmps")
            for c in range(16):
                adaln_chunk(hs, tmps, xsT[c], rstds, modv(L, 1, c, True), modv(L, 0, c, True), True)
                tok2fm(hp, stsh_d[:, c * 128:(c + 1) * 128], NS)
                fm2tok(sshift_d[:, c * 128:(c + 1) * 128], hs, NS)
                k.tt("dve", xxs, hp, hs, ALU.subtract)
                for j in range(6):
                    k.stt(smix[j][c], xxs, P(f"mix{j}", c), hs, ALU.mult, ALU.add)
            b = k.bank()
            for c in range(16):
                k.mm(b[0:96, 0:16], w1v[:, c, :], smix[1][c], start=(c == 0), stop=(c == 15))
            for c in range(16):
                k.mm(b[:, 16:32], a1v[:, c, :], smix[4][c], start=(c == 0), stop=(c == 15))
            for c in range(16):
                k.mm(b[:, 32:48], g1v[:, c, 0:128], smix[5][c], start=(c == 0), stop=(c == 15))
            for c in range(16):
                k.mm(b[:, 48:64], g1v[:, c, 128:256], smix[5][c], start=(c == 0), stop=(c == 15))
            k.act(tws[0:96, :], b[0:96, 0:16], AF.Tanh)
            k.copy("dve", las, b[:, 16:32])
            k.act(sgs[0], b[:, 32:48], AF.Sigmoid)
            k.act(sgs[1], b[:, 48:64], AF.Sigmoid)
        k.release(m1)
        ckpt(2)

        m2 = k.mark()
        l2t = [k.alloc([128, 4, 128], BF16, f"lora2_{i}") for i in range(2)]
        A = lambda n, dt=F32, w=TS: k.alloc([128, w], dt, n)
        at_f, bt_f, kt_f, rt_f = [A(n, BF16) for n in ("at_f", "bt_f", "kt_f", "rt_f")]
        gTb = A("gTb", BF16)
        at_b, bt_b, rt_b = [k.alloc([128, 4, 2, 128], BF16, n) for n in ("at_b", "bt_b", "rt_b")]
        for t_ in (at_b, bt_b, rt_b):
            k.memset("pool", t_, 0.0)
        v32 = k.alloc([128, 4, 128], F32, "v32")
        vb = k.alloc([128, 4, 128], BF16, "vb")
        s_tok = A("s_tok", F32, 8)
        DL = A("DL", F32, 4)
        Tblk = k.alloc([128, 2, 64], BF16, "Tblk")
        k.memset("pool", Tblk, 0.0)
        Gbc = A("Gbc", F32, 128)
        Bbc = A("Bbc", F32, 128)
        if do_s:
            Q5 = k.alloc([128, 5, NS], F32, "Q5")
            sA = [A(f"sA{i}", F32, NS) for i in range(6)]
            sqs_ = A("sqs_", BF16, NS)
            q4t = k.alloc([NS, 4, 128], F32, "q4t")
            q2t = k.alloc([NS, 2, 128], F32, "q2t")
        ws = WS([[(W["rw_wr"][:, m * 128:(m + 1) * 128], 16), (W["rw_wk"][:, m * 128:(m + 1) * 128], 16),
                  (W["rw_wv"][:, m * 128:(m + 1) * 128], 16)] for m in range(16)])
        for m in range(16):
            mc = slice(m * 128, (m + 1) * 128)
            wr, wk, wv = ws.get(m)
            l2 = l2t[m % 2]
            k.dma(l2[0:96, 0, :], W["rw_w2"][:, mc], eng="pool")
            k.dma(l2[:, 1, :], W["rw_a2"][:, mc], eng="pool")
            k.dma(l2[:, 2, :], W["rw_g2"][0:128, mc], eng="pool")
            k.dma(l2[:, 3, :], W["rw_g2"][128:256, mc], eng="pool")
            k.dma(Gbc, V(rows_d.ap[0:1, mc].partition_broadcast(128), rows_d.b))
            k.dma(Bbc, V(rows_d.ap[1:2, mc].partition_broadcast(128), rows_d.b))
            mE = k.mark()
            r32, k32, ag, sgw, kk, cum, e_pos, e_neg, e_prev, tA, tB = [A(n) for n in
                ("r32", "k32", "ag", "sgw", "kk", "cum", "e_pos", "e_neg", "e_prev", "tA", "tB")]
            sqb, rkr = A("sqb", BF16), A("rkr", BF16)
            b = k.bank()
            for kc in range(16):
                k.mm(b, wr[:, kc, :], mixes[0][kc], start=(kc == 0), stop=(kc == 15))
            k.copy("act", r32, b)
            b = k.bank()
            for kc in range(16):
                k.mm(b, wk[:, kc, :], mixes[2][kc], start=(kc == 0), stop=(kc == 15))
            k.copy("act", k32, b)
            b = k.bank()
            for t4 in range(4):
                for kc in range(16):
                    k.mm(b[:, t4 * 128:(t4 + 1) * 128], mixes[3][kc][:, t4 * 128:(t4 + 1) * 128], wv[:, kc, :],
                         start=(kc == 0), stop=(kc == 15))
            k.copy("dve", v32.re("p a b -> p (a b)"), b)
            k.copy("dve", vb.re("p a b -> p (a b)"), b)
            b = k.bank()
            k.mm(b, l2[0:96, 0, :], tw[0:96, :])
            k.act(sgw, b, AF.Sigmoid, bias=P("w0", m))
            b = k.bank()
            k.mm(b, l2[:, 1, :], la)
            k.act(ag, b, AF.Sigmoid, bias=P("a0", m))
            b = k.bank()
            k.mm(b, l2[:, 2, :], sg[0], start=True, stop=False)
            k.mm(b, l2[:, 3, :], sg[1], start=False, stop=True)
            k.copy("act", gTb, b)
            k.ts("dve", kk, k32, P("kk", m), None, ALU.mult)
            k.act(sqb, kk, AF.Square)
            b = k.bank()
            k.mm(b, blkdiag, sqb)
            k.act(tA, b, AF.Ln)
            k.act(tA, tA, AF.Exp, scale=-0.5)
            k.tt("dve", kk, kk, tA, ALU.mult)
            k.ts("pool", tB, ag, P("ka", m), P("omka", m), ALU.mult, ALU.add)
            k.tt("pool", k32, k32, tB, ALU.mult)
            k.stt(rkr, r32, P("rk", m), k32, ALU.mult, ALU.mult)
            b = k.bank()
            for t4 in range(4):
                k.mm(b[:, t4 * 2:(t4 + 1) * 2], rkr[:, t4 * 128:(t4 + 1) * 128], blk2)
            k.copy("act", s_tok, b[:, 0:8])
            for cc in range(4):
                sl = slice(cc * 128, (cc + 1) * 128)
                k.op("dve", lambda e, sl=sl: e.tensor_tensor_scan(out=cum.ap[:, sl], data0=ones_f.ap, data1=sgw.ap[:, sl],
                                                                  initial=0.0, op0=ALU.mult, op1=ALU.add),
                     [ones_f, sgw], [cum])
            k.act(e_pos, cum, AF.Exp, scale=DEC)
            k.act(e_neg, cum, AF.Exp, scale=-DEC)
            k.tt("pool", tA, cum, sgw, ALU.subtract)
            k.act(e_prev, tA, AF.Exp, scale=DEC)
            k.copy("act", DL, e_pos.re("p (c t) -> p c t", t=128)[:, :, 127])
            k.tt("dve", rt_f, r32, e_pos, ALU.mult)
            k.tt("pool", kt_f, k32, e_neg, ALU.mult)
            k.tt("pool", tB, kk, ag, ALU.mult)
            k.tt("dve", bt_f, tB, e_neg, ALU.mult)
            k.stt(at_f, kk, -1.0, e_prev, ALU.mult, ALU.mult)
            for (Xb_, Xf_) in ((at_b, at_f), (bt_b, bt_f), (rt_b, rt_f)):
                for h in range(2):
                    k.copy("act" if h else "pool", Xb_[64 * h:64 * h + 64, :, h, :],
                           Xf_[64 * h:64 * h + 64, :].re("p (c t) -> p c t", t=128))
            if do_s:
                r_s, k_s, sgw_s, ag_s, kk_s, t_s = sA
                b = k.bank()
                for kc in range(16):
                    k.mm(b[:, 0:16], wr[:, kc, :], smix[0][kc], start=(kc == 0), stop=(kc == 15))
                for kc in range(16):
                    k.mm(b[:, 16:32], wk[:, kc, :], smix[2][kc], start=(kc == 0), stop=(kc == 15))
                k.mm(b[:, 32:48], l2[0:96, 0, :], tws[0:96, :])
                k.mm(b[:, 48:64], l2[:, 1, :], las)
                k.mm(b[:, 64:80], l2[:, 2, :], sgs[0], start=True, stop=False)
                k.mm(b[:, 64:80], l2[:, 3, :], sgs[1], start=False, stop=True)
                bv_ = k.bank()
                for kc in range(16):
                    k.mm(bv_[0:NS, 0:128], smix[3][kc], wv[:, kc, :], start=(kc == 0), stop=(kc == 15))
                k.copy("act", Q5[:, 0, :], b[:, 0:16])
                k.copy("act", k_s, b[:, 16:32])
                k.act(sgw_s, b[:, 32:48], AF.Sigmoid, bias=P("w0", m))
                k.act(ag_s, b[:, 48:64], AF.Sigmoid, bias=P("a0", m))
                k.copy("act", gs[:, m, :], b[:, 64:80])
                k.ts("dve", kk_s, k_s, P("kk", m), None, ALU.mult)
                k.act(sqs_, kk_s, AF.Square)
                b2 = k.bank()
                k.mm(b2[:, 0:NS], blkdiag, sqs_)
                k.act(t_s, b2[:, 0:NS], AF.Ln)
                k.act(t_s, t_s, AF.Exp, scale=-0.5)
                k.tt("dve", kk_s, kk_s, t_s, ALU.mult)
                k.act(Q5[:, 1, :], sgw_s, AF.Exp, scale=DEC)
                k.ts("dve", t_s, ag_s, P("ka", m), P("omka", m), ALU.mult, ALU.add)
                k.tt("dve", Q5[:, 2, :], k_s, t_s, ALU.mult)
                k.ts("dve", Q5[:, 3, :], kk_s, -1.0, None, ALU.mult)
                k.tt("dve", Q5[:, 4, :], kk_s, ag_s, ALU.mult)
                bq = k.bank()
                for q in range(4):
                    k.tr(bq[0:NS, q * 128:(q + 1) * 128], Q5[:, q, :], identf)
                k.tr(bv_[0:NS, 128:256], Q5[:, 4, :], identf)
                k.copy("act", q4t.re("p a b -> p (a b)"), bq[0:NS, 0:512])
                k.copy("act", q2t.re("p a b -> p (a b)"), bv_[0:NS, 0:256])
                k.dma(scr6_d[0:4, :, mc].re("q r c -> r q c"), q4t)
                k.dma(scr6_d[4:6, :, mc].re("q r c -> r q c"), q2t)
            k.release(mE)
            mCh = k.mark()
            C4 = range(4)
            tok3 = [A("tok3", BF16, 384) for _ in C4]
            XA = [[A("XA", BF16, 512) for _ in range(2)] for _ in C4]
            AA = [A("AA", BF16, 512) for _ in C4]
            RQ = [A("RQ", BF16, 512) for _ in C4]
            XB = [A("XB", BF16, 512) for _ in C4]
            XC = [A("XC", BF16, 256) for _ in C4]
            Z32 = [A("Z32", F32, 256) for _ in C4]
            Zb = [A("Zb", BF16, 256) for _ in C4]
            AwT = [A("AwT", BF16, 128) for _ in C4]
            Ub = A("Ub", BF16, 128)
            tmpT = A("tmpT", F32, 64)
            y32 = [A("y32", F32, 128) for _ in range(2)]
            o1 = [A("o1", F32, 128) for _ in range(2)]
            st6 = A("st6", F32, 12)
            mv = A("mv", F32, 4)
            rs2 = A("rs2", F32, 2)
            pwk = A("pwk", F32, 128)
            SL = [slice(cc * 128, (cc + 1) * 128) for cc in C4]
            atb = [at_b[:, cc].re("p a b -> p (a b)") for cc in C4]
            btb = [bt_b[:, cc].re("p a b -> p (a b)") for cc in C4]
            rtb = [rt_b[:, cc].re("p a b -> p (a b)") for cc in C4]
            for cc in C4:
                PBk = k.bank()
                PB = V(PBk.ap.bitcast(BF16), PBk.b)
                for i_, X_ in enumerate((at_f, bt_f, kt_f)):
                    k.tr(PB[:, i_ * 128:(i_ + 1) * 128], X_[:, SL[cc]], identb)
                k.copy("act", tok3[cc], PB[:, 0:384])
            for cc in C4:
                bA = k.bank()
                k.mm(bA[:, 0:256], at_f[:, SL[cc]], btb[cc])
                k.mm(bA[:, 256:512], bt_f[:, SL[cc]], atb[cc])
                k.tt("dve", AA[cc], bA, mA, ALU.mult)
            for cc in C4:
                bB = k.bank()
                k.mm(bB[:, 0:256], kt_f[:, SL[cc]], atb[cc])
                k.mm(bB[:, 256:512], bt_f[:, SL[cc]], rtb[cc])
                k.tt("dve", XB[cc], bB, mB, ALU.mult)
            for cc in C4:
                bC = k.bank()
                k.mm(bC[:, 0:256], kt_f[:, SL[cc]], rtb[cc])
                k.tt("dve", XC[cc], bC[:, 0:256], mC, ALU.mult)
            for cc in C4:
                bZ = k.bank()
                for h in range(2):
                    k.mm(bZ[:, h * 64:(h + 1) * 64], XB[cc][:, h * 128:(h + 1) * 128], vb[:, cc, h * 64:(h + 1) * 64])
                Z32v = Z32[cc].re("p (h a v) -> p h a v", h=2, a=2)
                k.copy("act", Z32v[:, :, 0, :], tok3[cc][:, 0:128].re("p (h v) -> p h v", v=64))
                k.copy("act", Z32v[:, :, 1, :], bZ[:, 0:128].re("p (h v) -> p h v", v=64))
                k.copy("pool", Zb[cc], Z32[cc])
            cur = 0
            for cc in C4:
                k.tt("pool", RQ[cc], AA[cc], ML[:, 0, :], ALU.mult)
                k.tt("pool", XA[cc][0], RQ[cc], I4, ALU.add)
            for l_ in range(1, 7):
                b1 = []
                for cc in C4:
                    Wc = XA[cc][cur]
                    b1_ = k.bank()
                    b1.append(b1_)
                    for h in range(2):
                        k.mm(b1_[:, h * 128:(h + 1) * 128], AA[cc][:, 256 + h * 128:256 + (h + 1) * 128], Wc[:, h * 128:(h + 1) * 128])
                    for h in range(2):
                        k.mm(b1_[:, 256 + h * 128:256 + (h + 1) * 128], AA[cc][:, h * 128:(h + 1) * 128], Wc[:, 256 + h * 128:256 + (h + 1) * 128])
                for cc in C4:
                    k.tt("dve", RQ[cc], b1[cc], ML[:, l_, :], ALU.mult)
                b2 = []
                for cc in C4:
                    Wc = XA[cc][cur]
                    b2_ = k.bank()
                    b2.append(b2_)
                    for h in range(2):
                        o_ = b2_[:, h * 128:(h + 1) * 128]
                        k.mm(o_, Wc[:, 256 + h * 128:256 + (h + 1) * 128], RQ[cc][:, h * 128:(h + 1) * 128], start=True, stop=False)
                        k.mm(o_, identb, Wc[:, h * 128:(h + 1) * 128], start=False, stop=True)
                    for h in range(2):
                        o_ = b2_[:, 256 + h * 128:256 + (h + 1) * 128]
                        k.mm(o_, Wc[:, h * 128:(h + 1) * 128], RQ[cc][:, 256 + h * 128:256 + (h + 1) * 128], start=True, stop=False)
                        k.mm(o_, identb, Wc[:, 256 + h * 128:256 + (h + 1) * 128], start=False, stop=True)
                for cc in C4:
                    k.copy("act", XA[cc][1 - cur], b2[cc])
                cur = 1 - cur
            for cc in C4:
                Wc = XA[cc][cur]
                bz = k.bank()
                for h in range(2):
                    k.mm(bz[:, h * 128:(h + 1) * 128], Wc[:, 256 + h * 128:256 + (h + 1) * 128], Zb[cc][:, h * 128:(h + 1) * 128])
                k.copy("dve", Z32[cc], bz[:, 0:256])
                k.copy("act", Zb[cc], bz[:, 0:256])
            for cc in C4:
                PBk = k.bank()
                PB = V(PBk.ap.bitcast(BF16), PBk.b)
                for h in range(2):
                    k.tr(PB[64 * h:64 * h + 64, 0:128], Zb[cc][:, h * 128:h * 128 + 64], identb)
                k.copy("act", AwT[cc], PB[:, 0:128])
            for h in range(2):
                k.copy("dve", Tblk[64 * h:64 * h + 64, h, :], T32[64 * h:64 * h + 64, m, :])
            Tflat = Tblk.re("p a b -> p (a b)")
            for cc in C4:
                sl = SL[cc]
                b_tok, k_tok = tok3[cc][:, 128:256], tok3[cc][:, 256:384]
                Z32v = Z32[cc].re("p (h a v) -> p h a v", h=2, a=2)
                bU = k.bank()
                k.mm(bU[:, 0:128], AwT[cc], Tflat)
                k.tt("dve", Ub.re("p (h v) -> p h v", v=64), bU[:, 0:128].re("p (h v) -> p h v", v=64), Z32v[:, :, 1, :], ALU.add)
                bT = k.bank()
                for h in range(2):
                    hs_ = slice(h * 64, (h + 1) * 64)
                    k.mm(bT[64 * h:64 * h + 64, 0:64], b_tok[:, hs_], Ub[:, hs_], start=True, stop=False)
                    k.mm(bT[64 * h:64 * h + 64, 0:64], k_tok[:, hs_], vb[:, cc, hs_], start=False, stop=True)
                bY = k.bank()
                k.mm(bY[:, 0:128], rt_f[:, sl], Tflat, start=True, stop=False)
                for h in range(2):
                    hs_ = slice(h * 64, (h + 1) * 64)
                    k.mm(bY[:, hs_], XB[cc][:, 256 + h * 128:256 + (h + 1) * 128], Ub[:, hs_], start=False, stop=False)
                    k.mm(bY[:, hs_], XC[cc][:, h * 128:(h + 1) * 128], vb[:, cc, hs_], start=False, stop=(h == 1))
                k.tt("dve", tmpT, bT[:, 0:64], T32[:, m, :], ALU.add)
                for h in range(2):
                    k.ts("dve", Tblk[64 * h:64 * h + 64, h, :], tmpT[64 * h:64 * h + 64, :], DL[64 * h:64 * h + 64, cc:cc + 1], None, ALU.mult)
                k.act(T32[:, m, :], tmpT, AF.Identity, scale=DL[:, cc:cc + 1])
                y_ = y32[cc % 2]
                o_ = o1[cc % 2]
                k.copy("act", y_, bY[:, 0:128])
                for h in range(2):
                    hs_ = slice(h * 64, (h + 1) * 64)
                    k.op("dve", lambda e, h=h, hs_=hs_, y_=y_: e.bn_stats(out=st6.ap[:, h * 6:(h + 1) * 6], in_=y_.ap[:, hs_]),
                         [y_], [st6])
                    k.op("dve", lambda e, h=h: e.bn_aggr(out=mv.ap[:, 2 * h:2 * h + 2], in_=st6.ap[:, h * 6:(h + 1) * 6]),
                         [st6], [mv])
                mvv = mv.re("p (h t) -> p h t", t=2)
                k.act(rs2, mvv[:, :, 1], AF.Ln, bias=LNX_EPS)
                k.act(rs2, rs2, AF.Exp, scale=-0.5)
                for h in range(2):
                    hs_ = slice(h * 64, (h + 1) * 64)
                    k.ts("pool", o_[:, hs_], y_[:, hs_], mv[:, 2 * h:2 * h + 1], rs2[:, h:h + 1], ALU.subtract, ALU.mult)
                k.tt("pool", o_, o_, Gbc, ALU.mult)
                k.tt("pool", o_, o_, Bbc, ALU.add)
                for h in range(2):
                    hs_ = slice(h * 64, (h + 1) * 64)
                    k.stt(o_[:, hs_], v32[:, cc, hs_], s_tok[:, cc * 2 + h:cc * 2 + h + 1], o_[:, hs_], ALU.mult, ALU.add)
                bO = k.bank()
                k.tr(bO[:, 0:128], o_, identf)
                k.tt("dve", yg[m][:, sl], bO[:, 0:128], gTb[:, sl], ALU.mult)
            if last:
                b = k.bank()
                k.tr(b[0:64, 0:128], T32[:, m, :], identf)
                k.copy("act", pwk[0:64, :], b[0:64, 0:128])
                k.dma(pwkv_d[2 * m:2 * m + 2].re("h v k -> v h k"), pwk[0:64, :].re("p (h k) -> p h k", k=64))
            k.release(mCh)
        k.release(m2)
        ckpt(3)

        if do_s:
            m3 = k.mark()
            Grh = k.alloc([128, 64], F32, "Grh")
            Brh = k.alloc([128, 64], F32, "Brh")
            RKrh = k.alloc([128, 64], F32, "RKrh")
            for r4 in range(4):
                k.dma(Grh[32 * r4:32 * r4 + 32, :], rows_d[0].re("(h v) -> h v", v=64))
                k.dma(Brh[32 * r4:32 * r4 + 32, :], rows_d[1].re("(h v) -> h v", v=64))
                k.dma(RKrh[32 * r4:32 * r4 + 32, :], rows_d[2].re("(h v) -> h v", v=64))
            q6 = k.alloc([128, 6, 64], F32, "q6")
            S = k.alloc([128, 32, 64], F32, "S")
            tmp = k.alloc([128, 32, 64], F32, "Stmp")
            sa = k.alloc([128, 32], F32, "sa")
            yrh = k.alloc([128, 64], F32, "yrh")
            orh = k.alloc([128, 64], F32, "orh")
            t64 = k.alloc([128, 64], F32, "t64")
            s1 = k.alloc([128, 1], F32, "s1")
            st6b = k.alloc([128, 6], F32, "st6b")
            mvb = k.alloc([128, 2], F32, "mvb")
            rsb = k.alloc([128, 1], F32, "rsb")
            B3 = [128, 32, 64]
            for i in range(4):
                k.dma(q6, scr6_d[:, 4 * i:4 * i + 4, :].re("q r (h k) -> (r h) q k", k=64))
                r_, d_, k_, a_, v_, b_ = [q6[:, q, :] for q in range(6)]
                for vh in range(2):
                    vs = slice(vh * 32, (vh + 1) * 32)
                    k.dma(S, stw_d[i * 128:(i + 1) * 128, vh * 2048:(vh + 1) * 2048].re("p (v k) -> p v k", k=64))
                    k.tt("dve", tmp, S, a_.us(1).bc(B3), ALU.mult)
                    k.op("dve", lambda e: e.tensor_reduce(out=sa.ap, in_=tmp.ap, axis=AX.X, op=ALU.add), [tmp], [sa])
                    k.tt("dve", S, S, d_.us(1).bc(B3), ALU.mult)
                    k.tt("dve", tmp, sa.us(2).bc(B3), b_.us(1).bc(B3), ALU.mult)
                    k.tt("dve", S, S, tmp, ALU.add)
                    k.tt("dve", tmp, v_[:, vs].us(2).bc(B3), k_.us(1).bc(B3), ALU.mult)
                    k.tt("dve", S, S, tmp, ALU.add)
                    k.dma(swkv_d[i * 128:(i + 1) * 128, vh * 2048:(vh + 1) * 2048].re("p (v k) -> p v k", k=64), S)
                    k.tt("dve", tmp, S, r_.us(1).bc(B3), ALU.mult)
                    k.op("dve", lambda e, vs=vs: e.tensor_reduce(out=yrh.ap[:, vs], in_=tmp.ap, axis=AX.X, op=ALU.add),
                         [tmp], [yrh])
                k.op("dve", lambda e: e.bn_stats(out=st6b.ap, in_=yrh.ap), [yrh], [st6b])
                k.op("dve", lambda e: e.bn_aggr(out=mvb.ap, in_=st6b.ap), [st6b], [mvb])
                k.act(rsb, mvb[:, 1:2], AF.Ln, bias=LNX_EPS)
                k.act(rsb, rsb, AF.Exp, scale=-0.5)
                k.ts("dve", orh, yrh, mvb[:, 0:1], rsb, ALU.subtract, ALU.mult)
                k.tt("dve", orh, orh, Grh, ALU.mult)
                k.tt("dve", orh, orh, Brh, ALU.add)
                k.tt("dve", t64, r_, k_, ALU.mult)
                k.tt("dve", t64, t64, RKrh, ALU.mult)
                k.op("dve", lambda e: e.tensor_reduce(out=s1.ap, in_=t64.ap, axis=AX.X, op=ALU.add), [t64], [s1])
                k.stt(orh, v_, s1, orh, ALU.mult, ALU.add)
                k.dma(scro_d[4 * i:4 * i + 4, :].re("r (h v) -> (r h) v", v=64), orh)
            of = k.alloc([128, NS], F32, "of")
            for m in range(16):
                tok2fm(of, scro_d[:, m * 128:(m + 1) * 128], NS)
                k.tt("dve", ygs[m], of, gs[:, m, :], ALU.mult)
            k.release(m3)

        ckpt(4)
        proj_post(L, W["rw_wo"], 16, yg, ygs if do_s else None, 2, 256)
        ckpt(5)
        k.release(mtop)

    def ffn_layer(L, seg, do_s, last):
        mtop = k.mark()
        z = [k.alloc([128, TS], BF16, f"z{f}") for f in range(NF)]
        zs = [k.alloc([128, NS], BF16, f"zs{f}") for f in range(NF)] if do_s else None
        m1 = k.mark()
        h2 = [k.alloc([128, TS], BF16, f"h2_{c}") for c in range(16)]
        h2s = [k.alloc([128, NS], BF16, f"h2s_{c}") for c in range(16)] if do_s else None
        m2 = k.mark()
        x = load_x()
        rstd = rstd_of(x, TS)
        tmpn = k.alloc([128, TS], F32, "tmpn")
        for c in range(16):
            adaln_chunk(h2[c], tmpn, x[c], rstd, modv(L, 4, c, False), modv(L, 3, c, False), False)
        if do_s:
            rstds = rstd_of(xsT, NS)
            tmps = k.alloc([128, NS], F32, "tmps")
            for c in range(16):
                adaln_chunk(h2s[c], tmps, xsT[c], rstds, modv(L, 4, c, True), modv(L, 3, c, True), True)
        k.release(m2)
        gt = [k.alloc([128, 2 + TS], F32, "gt") for _ in range(2)]
        t1 = [k.alloc([128, TS], F32, "t1") for _ in range(2)]
        if do_s:
            cs = k.alloc([128, 32], F32, "cs")
            gts = k.alloc([128, NS], F32, "gts")
            t1s = k.alloc([128, NS], F32, "t1s")
            o2 = k.alloc([128, NS, 2], F32, "o2")
            tk = k.alloc([32, 128], F32, "tk")
            tk2 = k.alloc([32, 128], F32, "tk2")
        reqs = []
        for g in range(22):
            reqs.append([(W["ffn_w_gate"][L][:, g * 256:(g + 1) * 256], 16)])
            reqs.append([(W["ffn_w_up"][L][:, g * 256:(g + 1) * 256], 16)])
        ws = WS(reqs)
        for g in range(22):
            (wg,) = ws.get(2 * g)
            (wu,) = ws.get(2 * g + 1)
            for j in range(2):
                f = 2 * g + j
                fc = slice(f * 128, (f + 1) * 128)
                js = slice(j * 128, (j + 1) * 128)
                bg = k.bank()
                for kc in range(16):
                    k.mm(bg, wg[:, kc, js], h2[kc], start=(kc == 0), stop=(kc == 15))
                bu = k.bank()
                for kc in range(16):
                    k.mm(bu, wu[:, kc, js], h2[kc], start=(kc == 0), stop=(kc == 15))
                g_ = gt[f % 2]
                t_ = t1[f % 2]
                k.copy("dve", g_[:, 0:2], fcar[L][:, :, f])
                k.copy("act", g_[:, 2:2 + TS], bg)
                k.copy("dve", fcar[L][:, :, f], g_[:, TS:TS + 2])
                k.ts("dve", t_, g_[:, 0:TS], P(f"fcw{L}0", f), P(f"fcb{L}", f), ALU.mult, ALU.add)
                k.stt(t_, g_[:, 1:1 + TS], P(f"fcw{L}1", f), t_, ALU.mult, ALU.add)
                k.stt(t_, g_[:, 2:2 + TS], P(f"fcw{L}2", f), t_, ALU.mult, ALU.add)
                k.act(t_, t_, AF.Gelu_apprx_tanh)
                k.tt("dve", z[f], bu, t_, ALU.mult)
                if do_s:
                    bs_ = k.bank()
                    for kc in range(16):
                        k.mm(bs_[:, 0:16], wg[:, kc, js], h2s[kc], start=(kc == 0), stop=(kc == 15))
                    for kc in range(16):
                        k.mm(bs_[:, 16:32], wu[:, kc, js], h2s[kc], start=(kc == 0), stop=(kc == 15))
                    k.dma(tk, stfc_d[L][:, :, fc].re("r j c -> (r j) c"))
                    b3 = k.bank()
                    k.tr(b3[:, 0:32], tk, identf[0:32, 0:32])
                    k.copy("act", cs, b3[:, 0:32])
                    csv = cs.re("p (r j) -> p r j", j=2)
                    k.copy("dve", gts, bs_[:, 0:16])
                    k.ts("dve", t1s, csv[:, :, 0], P(f"fcw{L}0", f), P(f"fcb{L}", f), ALU.mult, ALU.add)
                    k.stt(t1s, csv[:, :, 1], P(f"fcw{L}1", f), t1s, ALU.mult, ALU.add)
                    k.stt(t1s, gts, P(f"fcw{L}2", f), t1s, ALU.mult, ALU.add)
                    k.act(t1s, t1s, AF.Gelu_apprx_tanh)
                    k.tt("dve", zs[f], bs_[:, 16:32], t1s, ALU.mult)
                    k.copy("dve", o2[:, :, 0], csv[:, :, 1])
                    k.copy("dve", o2[:, :, 1], gts)
                    b4 = k.bank()
                    k.tr(b4[0:32, 0:128], o2.re("p r j -> p (r j)"), identf)
                    k.copy("act", tk2, b4[0:32, 0:128])
                    k.dma(sfc_d[L][:, :, fc].re("r j c -> (r j) c"), tk2)
        if last:
            t88 = k.alloc([88, 128], F32, "t88")
            b = k.bank()
            k.tr(b[0:88, 0:128], fcar[L].re("p j f -> p (j f)"), identf)
            k.copy("act", t88, b[0:88, 0:128])
            for j in range(2):
                k.dma(pfc_d[L][j].re("(f p) -> f p", p=128), t88[j * NF:(j + 1) * NF, :])
        k.release(m1)
        proj_post(L, W["ffn_w_down"][L], NF, z, zs, 5, 128)
        k.release(mtop)

    def lru_layer(seg, do_s, last):
        L = 1
        mtop = k.mark()
        og = [k.alloc([128, TS], BF16, f"og{c}") for c in range(16)]
        ogs = [k.alloc([128, NS], BF16, f"ogs{c}") for c in range(16)] if do_s else None
        h = [k.alloc([128, TS], BF16, f"h_{c}") for c in range(16)]
        hs_l = [k.alloc([128, NS], BF16, f"hs_{c}") for c in range(16)] if do_s else None
        m2 = k.mark()
        x = load_x()
        rstd = rstd_of(x, TS)
        tmpn = k.alloc([128, TS], F32, "tmpn")
        for c in range(16):
            adaln_chunk(h[c], tmpn, x[c], rstd, modv(L, 1, c, False), modv(L, 0, c, False), False)
        if do_s:
            rstds = rstd_of(xsT, NS)
            tmps = k.alloc([128, NS], F32, "tmps")
            for c in range(16):
                adaln_chunk(hs_l[c], tmps, xsT[c], rstds, modv(L, 1, c, True), modv(L, 0, c, True), True)
        k.release(m2)
        wab = k.alloc([128, 8, 2, 256], BF16, "wab")
        wxb = k.alloc([128, 8, 2, 256], BF16, "wxb")
        k.dma(wab, W["lru_wa"].re("n (a p) c -> p n a c", p=128), eng="pool")
        k.dma(wxb, W["lru_wx"].re("n (a p) c -> p n a c", p=128), eng="pool")
        ybr = [k.alloc([128, TS], BF16, f"ybr{j}") for j in range(2)]
        xb32 = [k.alloc([128, 3 + TS], F32, f"xb32{j}") for j in range(2)]
        xc32 = [k.alloc([128, TS], F32, f"xc32{j}") for j in range(2)]
        xcb = [k.alloc([128, TS], BF16, f"xcb{j}") for j in range(2)]
        aa, mu, bt, hsn = [k.alloc([128, TS], F32, n) for n in ("aa", "mu", "bt", "hsn")]
        gA = [k.alloc([128, TS], F32, f"gA{j}") for j in range(2)]
        gX = [k.alloc([128, TS], F32, f"gX{j}") for j in range(2)]
        if do_s:
            ybrs = [k.alloc([128, NS], BF16, f"ybrs{j}") for j in range(2)]
            xbs = [k.alloc([128, NS], F32, f"xbs{j}") for j in range(2)]
            xcs = [k.alloc([128, NS], F32, f"xcs{j}") for j in range(2)]
            xcbs = [k.alloc([128, NS], BF16, f"xcbs{j}") for j in range(2)]
            csl = k.alloc([128, 48], F32, "csl")
            h0s = k.alloc([128, NS], F32, "h0s")
            o3 = k.alloc([128, NS, 3], F32, "o3")
            tk3 = k.alloc([48, 128], F32, "tk3")
            tk4 = k.alloc([48, 128], F32, "tk4")
            aas, mus, bts, hns = [k.alloc([128, NS], F32, n) for n in ("aas", "mus", "bts", "hns")]
            gAs = [k.alloc([128, NS], F32, f"gAs{j}") for j in range(2)]
            gXs = [k.alloc([128, NS], F32, f"gXs{j}") for j in range(2)]
        reqs = []
        for n in range(8):
            reqs.append([(W["lru_w_in"][:, n * 256:(n + 1) * 256], 16)])
            reqs.append([(W["lru_w_in"][:, D + n * 256:D + (n + 1) * 256], 16)])
        ws = WS(reqs)
        for n in range(8):
            (wy,) = ws.get(2 * n)
            (wx_,) = ws.get(2 * n + 1)
            for j in range(2):
                c = 2 * n + j
                js = slice(j * 128, (j + 1) * 128)
                cc_ = slice(c * 128, (c + 1) * 128)
                b = k.bank()
                for kc in range(16):
                    k.mm(b, wy[:, kc, js], h[kc], start=(kc == 0), stop=(kc == 15))
                k.act(ybr[j], b, AF.Gelu_apprx_tanh, bias=P("bin", c))
                b = k.bank()
                for kc in range(16):
                    k.mm(b, wx_[:, kc, js], h[kc], start=(kc == 0), stop=(kc == 15))
                xb_ = xb32[j]
                k.copy("dve", xb_[:, 0:3], lcar[:, :, c])
                k.act(xb_[:, 3:3 + TS], b, AF.Identity, bias=P("bin", 16 + c))
                k.copy("dve", lcar[:, :, c], xb_[:, TS:TS + 3])
                k.ts("dve", xc32[j], xb_[:, 0:TS], P("lcw0", c), P("lcb", c), ALU.mult, ALU.add)
                for q in range(1, 4):
                    k.stt(xc32[j], xb_[:, q:q + TS], P(f"lcw{q}", c), xc32[j], ALU.mult, ALU.add)
                k.copy("act", xcb[j], xc32[j])
                if do_s:
                    b = k.bank()
                    for kc in range(16):
                        k.mm(b[:, 0:16], wy[:, kc, js], hs_l[kc], start=(kc == 0), stop=(kc == 15))
                    for kc in range(16):
                        k.mm(b[:, 16:32], wx_[:, kc, js], hs_l[kc], start=(kc == 0), stop=(kc == 15))
                    k.act(ybrs[j], b[:, 0:16], AF.Gelu_apprx_tanh, bias=P("bin", c))
                    k.act(xbs[j], b[:, 16:32], AF.Identity, bias=P("bin", 16 + c))
                    k.dma(tk3, stlc_d[:, :, cc_].re("r j c -> (r j) c"))
                    b3 = k.bank()
                    k.tr(b3[:, 0:48], tk3, identf[0:48, 0:48])
                    k.copy("act", csl, b3[:, 0:48])
                    cv = csl.re("p (r j) -> p r j", j=3)
                    k.ts("dve", xcs[j], cv[:, :, 0], P("lcw0", c), P("lcb", c), ALU.mult, ALU.add)
                    k.stt(xcs[j], cv[:, :, 1], P("lcw1", c), xcs[j], ALU.mult, ALU.add)
                    k.stt(xcs[j], cv[:, :, 2], P("lcw2", c), xcs[j], ALU.mult, ALU.add)
                    k.stt(xcs[j], xbs[j], P("lcw3", c), xcs[j], ALU.mult, ALU.add)
                    k.copy("act", xcbs[j], xcs[j])
                    k.copy("dve", o3[:, :, 0], cv[:, :, 1])
                    k.copy("dve", o3[:, :, 1], cv[:, :, 2])
                    k.copy("dve", o3[:, :, 2], xbs[j])
                    b4 = k.bank()
                    k.tr(b4[0:48, 0:128], o3.re("p r j -> p (r j)"), identf)
                    k.copy("act", tk4, b4[0:48, 0:128])
                    k.dma(slc_d[:, :, cc_].re("r j c -> (r j) c"), tk4)
            LN_HALF = -0.6931471805599453
            for jo in range(2):
                c = 2 * n + jo
                jos = slice(jo * 128, (jo + 1) * 128)
                ba_ = k.bank()
                for ji in range(2):
                    k.mm(ba_, wab[:, n, ji, jos], xcb[ji], start=(ji == 0), stop=(ji == 1))
                bx_ = k.bank()
                for ji in range(2):
                    k.mm(bx_, wxb[:, n, ji, jos], xcb[ji], start=(ji == 0), stop=(ji == 1))
                k.act(gA[jo], ba_, AF.Tanh, bias=P("hba", c), scale=0.5)
                k.act(gX[jo], bx_, AF.Tanh, bias=P("hbx", c), scale=0.5)
                if do_s:
                    bs_ = k.bank()
                    for ji in range(2):
                        k.mm(bs_[:, 0:16], wab[:, n, ji, jos], xcbs[ji], start=(ji == 0), stop=(ji == 1))
                    for ji in range(2):
                        k.mm(bs_[:, 16:32], wxb[:, n, ji, jos], xcbs[ji], start=(ji == 0), stop=(ji == 1))
                    k.act(gAs[jo], bs_[:, 0:16], AF.Tanh, bias=P("hba", c), scale=0.5)
                    k.act(gXs[jo], bs_[:, 16:32], AF.Tanh, bias=P("hbx", c), scale=0.5)
            for jo in range(2):
                c = 2 * n + jo
                cc_ = slice(c * 128, (c + 1) * 128)
                k.act(aa, gA[jo], AF.Exp, bias=P("hcL", c), scale=P("hcL", c))
                k.tt("dve", mu, aa, aa, ALU.mult)
                k.act(mu, mu, AF.Ln, bias=1.0, scale=-1.0)
                k.act(mu, mu, AF.Exp, bias=LN_HALF, scale=0.5)
                k.stt(bt, gX[jo], 1.0, xc32[jo], ALU.add, ALU.mult)
                k.tt("dve", mu, mu, bt, ALU.mult)
                if seg == 0:
                    k.ts("dve", mu[:, 0:1], bt[:, 0:1], 0.5, None, ALU.mult)
                k.op("dve", lambda e, c=c: e.tensor_tensor_scan(out=hsn.ap, data0=aa.ap, data1=mu.ap,
                                                                initial=hcar.ap[:, c:c + 1], op0=ALU.mult, op1=ALU.add),
                     [aa, mu, hcar], [hsn])
                k.copy("dve", hcar[:, c:c + 1], hsn[:, TS - 1:TS])
                k.tt("dve", og[c], hsn, ybr[jo], ALU.mult)
                if do_s:
                    k.act(aas, gAs[jo], AF.Exp, bias=P("hcL", c), scale=P("hcL", c))
                    k.tt("dve", mus, aas, aas, ALU.mult)
                    k.act(mus, mus, AF.Ln, bias=1.0, scale=-1.0)
                    k.act(mus, mus, AF.Exp, bias=LN_HALF, scale=0.5)
                    k.stt(bts, gXs[jo], 1.0, xcs[jo], ALU.add, ALU.mult)
                    k.tt("dve", mus, mus, bts, ALU.mult)
                    tok2fm(h0s, sth_d[:, cc_], NS)
                    k.tt("dve", hns, aas, h0s, ALU.mult)
                    k.tt("dve", hns, hns, mus, ALU.add)
                    fm2tok(sh_d[:, cc_], hns, NS)
                    k.tt("dve", ogs[c], hns, ybrs[jo], ALU.mult)
        if last:
            fm2tok(ph_d, hcar, 16)
            t48 = k.alloc([48, 128], F32, "t48")
            b = k.bank()
            k.tr(b[0:48, 0:128], lcar.re("p j c -> p (j c)"), identf)
            k.copy("act", t48, b[0:48, 0:128])
            for j in range(3):
                k.dma(plc_d[j].re("(c p) -> c p", p=128), t48[j * 16:(j + 1) * 16, :])
        k.release(k.mark())
        proj_post(L, W["lru_w_out"], 16, og, ogs, 2, 256)
        k.release(mtop)

    try:
        compute_mod()
        ckpt(1)
        for c in range(16):
            tok2fm(xsT[c], xs_d[:, c * 128:(c + 1) * 128], NS)
        ckpt(10)
        for seg in range(NSEG):
            do_s = (seg == 0)
            last = (seg == NSEG - 1)
            rwkv_layer(seg, do_s, last)
            ffn_layer(0, seg, do_s, last)
            ckpt(6)
            lru_layer(seg, do_s, last)
            ckpt(7)
            ffn_layer(1, seg, do_s, last)
            m = k.mark()
            x = load_x()
            yt = [k.alloc([128, D], F32, "yt") for _ in range(2)]
            for t4 in range(4):
                y_ = yt[t4 % 2]
                for c4 in range(4):
                    b = k.bank()
                    for q in range(4):
                        c = c4 * 4 + q
                        k.tr(b[:, q * 128:(q + 1) * 128], x[c][:, t4 * 128:(t4 + 1) * 128], identf)
                    ev(y_[:, c4 * 512:(c4 + 1) * 512], b)
                k.dma(yp_d[seg * TS + t4 * 128:seg * TS + (t4 + 1) * 128, :], y_)
            k.release(m)
            if last:
                fm2tok(pshift_d, shift_car, 16)
        for c in range(16):
            fm2tok(ys_d[:, c * 128:(c + 1) * 128], xsT[c], NS)

    except StopBuild:
        pass
    nc = k.emit()
    return nc, k


def _fm(v):
    v = np.asarray(v, np.float32).reshape(-1, 128)
    return np.ascontiguousarray(v.T)


def make_pvec(I):
    pv = np.zeros((128, PV_N), np.float32)

    def put(name, v):
        o, w = PV_OFF[name]
        pv[:, o:o + w] = _fm(v)
    for i in range(2):
        for j in range(4):
            put(f"ng{i}{j}", I["norm_g"][i, j])
        put(f"bmod{i}", I["b_mod"][i])
    for j in range(6):
        put(f"mix{j}", I["rw_mix"][0, j])
    put("w0", I["rw_w0"][0]); put("a0", I["rw_a0"][0]); put("kk", I["rw_kk"][0]); put("ka", I["rw_ka"][0])
    put("rk", I["rw_rk"][0].reshape(-1)); put("lcb", I["lru_conv_b"][0]); put("ba", I["lru_ba"][0])
    put("bx", I["lru_bx"][0]); put("lam", I["lru_lambda"][0]); put("bin", I["lru_b_in"][0])
    for j in range(4):
        put(f"lcw{j}", I["lru_conv_w"][0, j])
    for i in range(2):
        for j in range(3):
            put(f"fcw{i}{j}", I["ffn_conv_w"][i, j])
        put(f"fcb{i}", I["ffn_conv_b"][i])
    return pv


def make_cmask():
    t = np.arange(128)[:, None]
    c = np.arange(128)[None, :]
    out = np.zeros((128, 7, 512), np.float32)
    for l in range(7):
        b = 1 << l
        M = ((t // (2 * b) == c // (2 * b)) & (t % (2 * b) >= b) & (c % (2 * b) < b)).astype(np.float32)
        out[:, l, 0:128] = M
        out[:, l, 128:256] = M
        out[:, l, 256:384] = M.T
        out[:, l, 384:512] = M.T
    return out


_CACHE = {}


def kernel(**I):
    I = {k_: np.asarray(v) for k_, v in I.items()}
    B, S, _ = I["x_prompt"].shape
    NSEG = S // TS
    if NSEG not in _CACHE:
        _CACHE[NSEG] = build(NSEG)[0]
    nc = _CACHE[NSEG]
    pv = make_pvec(I)
    rows = np.ascontiguousarray(np.stack([I["rw_lnx_g"][0], I["rw_lnx_b"][0], I["rw_rk"][0].reshape(-1)]).astype(np.float32))
    shared = {
        "pvec": pv, "rows": rows, "cmask": make_cmask(),
        "w_mod": I["w_mod"], "rw_wr": I["rw_wr"][0], "rw_wk": I["rw_wk"][0], "rw_wv": I["rw_wv"][0], "rw_wo": I["rw_wo"][0],
        "rw_w1": I["rw_w1"][0], "rw_w2": I["rw_w2"][0], "rw_a1": I["rw_a1"][0], "rw_a2": I["rw_a2"][0],
        "rw_g1": I["rw_g1"][0], "rw_g2": I["rw_g2"][0], "lru_w_in": I["lru_w_in"][0], "lru_wa": I["lru_wa"][0],
        "lru_wx": I["lru_wx"][0], "lru_w_out": I["lru_w_out"][0], "ffn_w_gate": I["ffn_w_gate"],
        "ffn_w_up": I["ffn_w_up"], "ffn_w_down": I["ffn_w_down"],
    }
    shared = {k_: np.ascontiguousarray(v, dtype=np.float32) for k_, v in shared.items()}
    in_maps = []
    for c in range(8):
        b = c % B
        r = slice(NS * c, NS * (c + 1))
        d = dict(shared)
        d["xp"] = np.ascontiguousarray(I["x_prompt"][b])
        d["xs"] = np.ascontiguousarray(I["x_sample"][r, 0])
        d["call"] = np.ascontiguousarray(np.concatenate([I["c_prompt"][b:b + 1], I["c_sample"][r]], 0))
        d["st_wkv"] = np.ascontiguousarray(I["state_rwkv_wkv"][0, r].reshape(NS * NH, 4096))
        d["st_shift"] = np.ascontiguousarray(I["state_rwkv_shift"][0, r])
        d["st_h"] = np.ascontiguousarray(I["state_lru_h"][0, r])
        d["st_lconv"] = np.ascontiguousarray(I["state_lru_conv"][0, r])
        d["st_fconv"] = np.ascontiguousarray(I["state_ffn_conv"][:, r])
        in_maps.append(d)
    res = run_bass_kernel_spmd(nc, in_maps, core_ids=list(range(8)))
    R = res.results
    f32 = np.float32
    y_prompt = np.stack([R[b]["y_p"] for b in range(B)]).astype(f32)
    y_sample = np.concatenate([R[c]["y_s"] for c in range(8)], 0).reshape(8 * NS, 1, D).astype(f32)
    p_wkv = np.stack([R[b]["p_wkv"] for b in range(B)])[None].astype(f32)
    p_shift = np.stack([R[b]["p_shift"].reshape(D) for b in range(B)])[None].astype(f32)
    p_h = np.stack([R[b]["p_h"].reshape(D) for b in range(B)])[None].astype(f32)
    p_lconv = np.stack([R[b]["p_lconv"] for b in range(B)])[None].astype(f32)
    p_fconv = np.stack([R[b]["p_fconv"] for b in range(B)], 1).astype(f32)
    s_wkv = np.concatenate([R[c]["s_wkv"].reshape(NS, NH, 64, 64) for c in range(8)], 0)[None].astype(f32)
    s_shift = np.concatenate([R[c]["s_shift"] for c in range(8)], 0)[None].astype(f32)
    s_h = np.concatenate([R[c]["s_h"] for c in range(8)], 0)[None].astype(f32)
    s_lconv = np.concatenate([R[c]["s_lconv"] for c in range(8)], 0)[None].astype(f32)
    s_fconv = np.concatenate([R[c]["s_fconv"] for c in range(8)], 1).astype(f32)
    return (y_prompt, y_sample, p_wkv, p_shift, p_h, p_lconv, p_fconv, s_wkv, s_shift, s_h, s_lconv, s_fconv)
```

```python
import numpy as np
import concourse.bass as bass
import concourse.mybir as mybir
from concourse.bass_utils import run_bass_kernel_spmd

F32 = mybir.dt.float32
BF16 = mybir.dt.bfloat16
AF = mybir.ActivationFunctionType
ALU = mybir.AluOpType
AX = mybir.AxisListType

D = 2048
NCH = 16
DFF = 5632
NF = 44
TS = 512
NS = 16
NH = 32
EPS = 1e-6
LNX_EPS = 64e-5
DEC = -0.6065306597126334
WB_ELEMS = 6144
NWB = 3


class Buf:
    __slots__ = ("name", "lw", "wd", "rd", "rdd", "alias", "psum", "notrack")

    def __init__(self, name):
        self.name = name
        self.lw = None
        self.wd = []
        self.rd = {}
        self.rdd = []
        self.alias = []
        self.psum = False
        self.notrack = False


class V:
    __slots__ = ("ap", "b")

    def __init__(self, ap, b):
        self.ap = ap
        self.b = b

    def __getitem__(self, k):
        return V(self.ap[k], self.b)

    def re(self, s, **kw):
        return V(self.ap.rearrange(s, **kw), self.b)

    def bc(self, shape):
        return V(self.ap.to_broadcast(list(shape)), self.b)

    def us(self, ax):
        return V(self.ap.unsqueeze(ax), self.b)


class Op:
    __slots__ = ("eng", "fn", "reads", "writes", "dma", "waits", "sig", "cnt", "slot", "slotcnt", "prewait")

    def __init__(self, eng, fn, reads, writes, dma):
        self.eng = eng
        self.fn = fn
        self.reads = reads
        self.writes = writes
        self.dma = dma
        self.waits = []
        self.sig = False
        self.cnt = 0


NSLOT = 16
ARENA_WORDS = 53150


class KB:
    ENGS = ("pe", "dve", "act", "pool", "sp")

    def __init__(self):
        self.nc = bass.Bass("TRN2", target_bir_lowering=False)
        self.ops = []
        self._ctx = []
        g = self.nc.sbuf_tensor("arena", [128, ARENA_WORDS], F32)
        self.arena = g.__enter__()[:]
        self._ctx.append(g)
        self.top = 0
        self.live = []
        self.dead = []
        self.banks = []
        for i in range(8):
            g = self.nc.psum_tensor(f"bank{i}", [128, 512], F32)
            self.banks.append(V(g.__enter__()[:], Buf(f"bank{i}")))
            self.banks[-1].b.psum = True
            self._ctx.append(g)
        self.reserved = set()
        self.bi = 0
        self.maxtop = 0

    def alloc(self, shape, dt=F32, name="t"):
        n = 1
        for s in shape[1:]:
            n *= s
        words = n if dt == F32 else (n + 1) // 2
        words = (words + 1) // 2 * 2
        st = self.top
        en = st + words
        assert en <= ARENA_WORDS, f"arena overflow {en}"
        self.top = en
        self.maxtop = max(self.maxtop, en)
        ap = self.arena[:, st:en]
        if dt != F32:
            ap = ap.bitcast(dt)
        ap = ap[0:shape[0], 0:n]
        if len(shape) == 3:
            ap = ap.rearrange("p (a b) -> p a b", b=shape[2])
        elif len(shape) == 4:
            ap = ap.rearrange("p (a b c) -> p a b c", b=shape[2], c=shape[3])
        b = Buf(name)
        keep = []
        for (s0, e0, ob) in self.dead:
            if s0 < en and st < e0:
                b.alias.append(ob)
                if s0 < st:
                    keep.append((s0, st, ob))
                if e0 > en:
                    keep.append((en, e0, ob))
            else:
                keep.append((s0, e0, ob))
        self.dead = keep
        self.live.append((st, en, b))
        return V(ap, b)

    def mark(self):
        return (self.top, len(self.live))

    def release(self, m):
        top, nl = m
        for ent in self.live[nl:]:
            self.dead.append(ent)
        del self.live[nl:]
        self.top = top

    def bank(self):
        for _ in range(8):
            i = self.bi % 8
            self.bi += 1
            if i not in self.reserved:
                return self.banks[i]
        raise RuntimeError("no bank")

    def reserve(self):
        for _ in range(8):
            i = self.bi % 8
            self.bi += 1
            if i not in self.reserved:
                self.reserved.add(i)
                return self.banks[i]
        raise RuntimeError("no bank")

    def unreserve(self, b):
        for i, x in enumerate(self.banks):
            if x.b is b.b:
                self.reserved.discard(i)

    def dram(self, name, shape, dt=F32, kind="Internal"):
        t = self.nc.dram_tensor(name, list(shape), dt, kind=kind)
        return V(t.ap(), Buf(name))

    def op(self, eng, fn, reads=(), writes=(), dma=False):
        o = Op(eng, fn, [r.b for r in reads], [w.b for w in writes], dma)
        self.ops.append(o)
        return o

    def dma(self, out, in_, eng="sp"):
        return self.op(eng, lambda e: e.dma_start(out=out.ap, in_=in_.ap), [in_], [out], dma=True)

    def mm(self, out, lhsT, rhs, start=True, stop=True):
        return self.op("pe", lambda e: e.matmul(out=out.ap, lhsT=lhsT.ap, rhs=rhs.ap, start=start, stop=stop),
                       [lhsT, rhs], [out])

    def tr(self, out, in_, ident):
        return self.op("pe", lambda e: e.transpose(out=out.ap, in_=in_.ap, identity=ident.ap), [in_, ident], [out])

    def act(self, out, in_, func, bias=None, scale=None):
        reads = [in_] + [x for x in (bias, scale) if isinstance(x, V)]
        kw = {}
        if bias is not None:
            kw["bias"] = bias.ap if isinstance(bias, V) else float(bias)
        if scale is not None:
            kw["scale"] = scale.ap if isinstance(scale, V) else float(scale)
        return self.op("act", lambda e: e.activation(out=out.ap, in_=in_.ap, func=func, **kw), reads, [out])

    def copy(self, eng, out, in_):
        if eng == "act":
            return self.op("act", lambda e: e.copy(out=out.ap, in_=in_.ap), [in_], [out])
        return self.op(eng, lambda e: e.tensor_copy(out=out.ap, in_=in_.ap), [in_], [out])

    def tt(self, eng, out, a, b, op):
        return self.op(eng, lambda e: e.tensor_tensor(out=out.ap, in0=a.ap, in1=b.ap, op=op), [a, b], [out])

    def ts(self, eng, out, a, s1, s2, op0, op1=None):
        reads = [a] + [x for x in (s1, s2) if isinstance(x, V)]
        v1 = s1.ap if isinstance(s1, V) else float(s1)
        v2 = None if s2 is None else (s2.ap if isinstance(s2, V) else float(s2))
        if op1 is None:
            return self.op(eng, lambda e: e.tensor_scalar(out=out.ap, in0=a.ap, scalar1=v1, scalar2=None, op0=op0),
                           reads, [out])
        return self.op(eng, lambda e: e.tensor_scalar(out=out.ap, in0=a.ap, scalar1=v1, scalar2=v2, op0=op0, op1=op1),
                       reads, [out])

    def stt(self, out, a, s, b, op0, op1):
        reads = [a, b] + ([s] if isinstance(s, V) else [])
        sv = s.ap if isinstance(s, V) else float(s)
        return self.op("dve", lambda e: e.scalar_tensor_tensor(out=out.ap, in0=a.ap, scalar=sv, in1=b.ap, op0=op0, op1=op1),
                       reads, [out])

    def memset(self, eng, out, val):
        return self.op(eng, lambda e: e.memset(out.ap, val), [], [out])

    def finalize(self):
        ops = self.ops
        for i, o in enumerate(ops):
            deps = set()
            for b in o.reads:
                if b.lw is not None:
                    deps.add(b.lw)
                deps.update(b.wd)
                if b.psum:
                    for e, j in b.rd.items():
                        if e != o.eng:
                            deps.add(j)
            for b in o.writes:
                if b.notrack:
                    continue
                for ob in b.alias:
                    if ob.lw is not None:
                        deps.add(ob.lw)
                    deps.update(ob.wd)
                    deps.update(ob.rd.values())
                    deps.update(ob.rdd)
                b.alias = []
                if b.lw is not None:
                    deps.add(b.lw)
                if not o.dma:
                    deps.update(b.wd)
                deps.update(b.rd.values())
                deps.update(b.rdd)
            for b in o.reads:
                if o.dma:
                    b.rdd.append(i)
                else:
                    b.rd[o.eng] = i
            for b in o.writes:
                if b.notrack:
                    continue
                had_readers = bool(b.rd or b.rdd)
                if o.dma:
                    if had_readers:
                        b.wd = [i]
                        b.lw = None
                    else:
                        b.wd.append(i)
                else:
                    b.lw = i
                    b.wd = []
                b.rd = {}
                b.rdd = []
            best = {}
            dl = []
            for j in deps:
                if j == i:
                    continue
                p = ops[j]
                e = p.eng
                if p.dma:
                    dl.append(j)
                else:
                    if e == "pe" and o.eng == "pe" and not o.dma:
                        continue
                    if e not in best or best[e] < j:
                        best[e] = j
            o.waits = sorted(set(list(best.values()) + dl))
            for j in o.waits:
                ops[j].sig = True
        cnt = {e: 0 for e in self.ENGS}
        dcount = {e: 0 for e in self.ENGS}
        slotcnt = {e: [0] * NSLOT for e in self.ENGS}
        for o in ops:
            if o.dma:
                kk = dcount[o.eng]
                dcount[o.eng] += 1
                s = kk % NSLOT
                o.slot = s
                o.prewait = slotcnt[o.eng][s]
                slotcnt[o.eng][s] += 16
                o.slotcnt = slotcnt[o.eng][s]
            elif o.sig:
                cnt[o.eng] += 1
                o.cnt = cnt[o.eng]
        self.final_dma = {e: list(slotcnt[e]) for e in self.ENGS}

    def emit(self):
        nc = self.nc
        ops = self.ops
        self.finalize()
        sems = {}
        for e in self.ENGS:
            g = nc.semaphore(f"s_{e}")
            sems[e] = g.__enter__()
            self._ctx.append(g)
        dsem = {}
        for e in self.ENGS:
            if any(o.dma and o.eng == e for o in ops):
                dsem[e] = []
                for s in range(NSLOT):
                    g = nc.semaphore(f"d_{e}_{s}")
                    dsem[e].append(g.__enter__())
                    self._ctx.append(g)
        per = {e: [o for o in ops if o.eng == e] for e in self.ENGS}
        final_dma = self.final_dma

        def run(eng_name, eng):
            waited = {}
            for o in per[eng_name]:
                if o.dma and o.prewait > 0:
                    key = ("d", eng_name, o.slot)
                    if waited.get(key, 0) < o.prewait:
                        eng.wait_ge(dsem[eng_name][o.slot], o.prewait)
                        waited[key] = o.prewait
                for j in o.waits:
                    p = ops[j]
                    if p.dma:
                        key = ("d", p.eng, p.slot)
                        val = p.slotcnt
                        sem = dsem[p.eng][p.slot]
                    else:
                        key = ("c", p.eng)
                        val = p.cnt
                        sem = sems[p.eng]
                    if waited.get(key, 0) < val:
                        eng.wait_ge(sem, val)
                        waited[key] = val
                ins = o.fn(eng)
                if o.dma:
                    ins.then_inc(dsem[eng_name][o.slot], 16)
                elif o.sig:
                    ins.then_inc(sems[eng_name], 1)
            if eng_name in dsem:
                for s in range(NSLOT):
                    v = final_dma[eng_name][s]
                    if v > 0 and waited.get(("d", eng_name, s), 0) < v:
                        eng.wait_ge(dsem[eng_name][s], v)

        with nc.Block() as block:
            @block.sync
            def _(e):
                run("sp", e)

            @block.tensor
            def _(e):
                run("pe", e)

            @block.vector
            def _(e):
                run("dve", e)

            @block.scalar
            def _(e):
                run("act", e)

            @block.gpsimd
            def _(e):
                run("pool", e)
        return nc


def pvec_layout():
    names = []
    for i in range(2):
        for j in range(4):
            names.append((f"ng{i}{j}", 16))
    for i in range(2):
        names.append((f"bmod{i}", 96))
    for j in range(6):
        names.append((f"mix{j}", 16))
    for n in ("w0", "a0", "kk", "ka", "rk", "lcb", "ba", "bx", "lam"):
        names.append((n, 16))
    names.append(("bin", 32))
    for j in range(4):
        names.append((f"lcw{j}", 16))
    for i in range(2):
        for j in range(3):
            names.append((f"fcw{i}{j}", 44))
        names.append((f"fcb{i}", 44))
    names.append(("omka", 16))
    names.append(("cL", 16))
    names.append(("hba", 16))
    names.append(("hbx", 16))
    names.append(("hcL", 16))
    off = {}
    o = 0
    for n, w in names:
        off[n] = (o, w)
        o += w
    return off, o


PV_OFF, PV_N = pvec_layout()


def build(NSEG):
    k = KB()
    T = NSEG * TS
    IN = lambda n, s: k.dram(n, s, F32, kind="ExternalInput")
    def OUT(n, s):
        v = k.dram(n, s, F32, kind="ExternalOutput")
        v.b.notrack = True
        return v
    xp_d = IN("xp", [T, D])
    xs_d = IN("xs", [NS, D])
    call_d = IN("call", [17, D])
    stw_d = IN("st_wkv", [NS * NH, 4096])
    stsh_d = IN("st_shift", [NS, D])
    sth_d = IN("st_h", [NS, D])
    stlc_d = IN("st_lconv", [NS, 3, D])
    stfc_d = IN("st_fconv", [2, NS, 2, DFF])
    pvec_d = IN("pvec", [128, PV_N])
    rows_d = IN("rows", [3, D])
    cmask_d = IN("cmask", [128, 7, 512])
    W = {}
    for n, s in (("w_mod", [2, D, 6 * D]), ("rw_wr", [D, D]), ("rw_wk", [D, D]), ("rw_wv", [D, D]), ("rw_wo", [D, D]),
                 ("rw_w1", [D, 96]), ("rw_w2", [96, D]), ("rw_a1", [D, 128]), ("rw_a2", [128, D]),
                 ("rw_g1", [D, 256]), ("rw_g2", [256, D]), ("lru_w_in", [D, 2 * D]), ("lru_wa", [8, 256, 256]),
                 ("lru_wx", [8, 256, 256]), ("lru_w_out", [D, D]), ("ffn_w_gate", [2, D, DFF]),
                 ("ffn_w_up", [2, D, DFF]), ("ffn_w_down", [2, DFF, D])):
        W[n] = IN(n, s)
    yp_d = OUT("y_p", [T, D])
    ys_d = OUT("y_s", [NS, D])
    pwkv_d = OUT("p_wkv", [NH, 64, 64])
    pshift_d = OUT("p_shift", [16, 128])
    ph_d = OUT("p_h", [16, 128])
    plc_d = OUT("p_lconv", [3, D])
    pfc_d = OUT("p_fconv", [2, 2, DFF])
    swkv_d = OUT("s_wkv", [NS * NH, 4096])
    sshift_d = OUT("s_shift", [NS, D])
    sh_d = OUT("s_h", [NS, D])
    slc_d = OUT("s_lconv", [NS, 3, D])
    sfc_d = OUT("s_fconv", [2, NS, 2, DFF])
    import os
    DBG = os.environ.get("KDBG") == "1"
    dbg_n = {"n": 0}

    def dump(name, v, shape, dt_is_bf16=False):
        if not DBG:
            return
        d = k.dram("dbg_" + name, list(shape), F32, kind="ExternalOutput")
        if dt_is_bf16:
            k.dma(d, v, eng="pool")
        else:
            k.dma(d, v)
    STOP = int(os.environ.get("KSTOP", "0"))

    class StopBuild(Exception):
        pass

    def ckpt(n):
        if STOP == n:
            raise StopBuild()
    xres_d = [k.dram(f"xres{c}", [128, TS]) for c in range(16)]
    scr6_d = k.dram("scr6", [6, NS, D])
    scro_d = k.dram("scro", [NS, D])

    ones_f = k.alloc([128, 128], F32, "ones_f")
    k.memset("pool", ones_f, 1.0)
    identf = k.alloc([128, 128], F32, "identf")
    k.op("pool", lambda e: e.affine_select(out=identf.ap, in_=ones_f.ap, pattern=[[-1, 128]], compare_op=ALU.is_equal,
                                           fill=0.0, base=0, channel_multiplier=1), [ones_f], [identf])
    identb = k.alloc([128, 128], BF16, "identb")
    k.copy("pool", identb, identf)
    onesb = k.alloc([128, 128], BF16, "onesb")
    k.copy("pool", onesb, ones_f)
    blkdiag = k.alloc([128, 128], BF16, "blkdiag")
    k.memset("pool", blkdiag, 0.0)
    k.memset("pool", blkdiag[0:64, 0:64], 1.0)
    k.memset("pool", blkdiag[64:128, 64:128], 1.0)
    blk2 = k.alloc([128, 2], BF16, "blk2")
    k.memset("pool", blk2, 0.0)
    k.memset("pool", blk2[0:64, 0:1], 1.0)
    k.memset("pool", blk2[64:128, 1:2], 1.0)
    mA = k.alloc([128, 512], F32, "mA")
    mB = k.alloc([128, 512], F32, "mB")
    mC = k.alloc([128, 256], F32, "mC")

    def sel(dst, kind):
        if kind == "Ls":
            pat, base, cm = [[-1, 128]], -1, 1
        elif kind == "Us":
            pat, base, cm = [[1, 128]], -1, -1
        else:
            pat, base, cm = [[1, 128]], 0, -1
        k.op("pool", lambda e: e.affine_select(out=dst.ap, in_=ones_f.ap, pattern=pat, compare_op=ALU.is_ge,
                                               fill=0.0, base=base, channel_multiplier=cm), [ones_f], [dst])
    for i, kd in enumerate(("Ls", "Ls", "Us", "Us")):
        sel(mA[:, i * 128:(i + 1) * 128], kd)
    for i, kd in enumerate(("Us", "Us", "Ui", "Ui")):
        sel(mB[:, i * 128:(i + 1) * 128], kd)
    for i, kd in enumerate(("Ui", "Ui")):
        sel(mC[:, i * 128:(i + 1) * 128], kd)

    ML = k.alloc([128, 7, 512], BF16, "ML")
    for l_ in range(7):
        k.dma(ML[:, l_, :], cmask_d[:, l_, :], eng="pool")
    I4 = k.alloc([128, 512], BF16, "I4")
    for i_ in range(4):
        k.copy("pool", I4[:, i_ * 128:(i_ + 1) * 128], identf)
    pv = k.alloc([128, PV_N], F32, "pv")
    k.dma(pv, pvec_d)

    def P(name, c):
        o, w = PV_OFF[name]
        return pv[:, o + c:o + c + 1]

    def PR(name):
        o, w = PV_OFF[name]
        return pv[:, o:o + w]
    k.ts("dve", PR("omka"), PR("ka"), -1.0, 1.0, ALU.mult, ALU.add)
    k.act(PR("cL"), PR("lam"), AF.Exp, scale=-1.0)
    k.act(PR("cL"), PR("cL"), AF.Ln, bias=1.0)
    k.ts("dve", PR("cL"), PR("cL"), -8.0, None, ALU.mult)
    k.ts("dve", PR("hcL"), PR("cL"), 0.5, None, ALU.mult)
    k.ts("dve", PR("hba"), PR("ba"), 0.5, None, ALU.mult)
    k.ts("dve", PR("hbx"), PR("bx"), 0.5, None, ALU.mult)

    modT = [k.alloc([128, 96, 17], F32, f"modT{i}") for i in range(2)]
    shift_car = k.alloc([128, 16], F32, "shift_car")
    k.memset("pool", shift_car, 0.0)
    fcar = [k.alloc([128, 2, NF], F32, f"fcar{i}") for i in range(2)]
    for i in range(2):
        k.memset("pool", fcar[i], 0.0)
    lcar = k.alloc([128, 3, 16], F32, "lcar")
    k.memset("pool", lcar, 0.0)
    hcar = k.alloc([128, 16], F32, "hcar")
    k.memset("pool", hcar, 0.0)
    T32 = k.alloc([128, 16, 64], F32, "T32")
    k.memset("pool", T32, 0.0)
    xsT = [k.alloc([128, NS], F32, f"xsT{c}") for c in range(16)]
    wbufs = [k.alloc([128, WB_ELEMS], BF16, f"wb{i}") for i in range(NWB)]
    wstate = {"n": 0}

    class WS:
        def __init__(self, reqs, extra=None):
            self.reqs = reqs
            self.emitted = 0
            self.views = {}
            self.bufs = wbufs + (extra or [])
            self.n = wstate["n"] % NWB

        def _emit(self, i):
            buf = self.bufs[self.n % len(self.bufs)]
            self.n += 1
            wstate["n"] += 1
            off = 0
            vs = []
            for (src, nk) in self.reqs[i]:
                rows, cols = src.ap.shape
                pr = rows // nk
                dst = buf[0:pr, off:off + nk * cols].re("p (a b) -> p a b", b=cols)
                k.dma(dst, src.re("(a p) c -> p a c", p=pr), eng="pool")
                vs.append(dst)
                off += nk * cols
            assert off <= WB_ELEMS
            self.views[i] = vs

        def get(self, i):
            lim = min(len(self.reqs), i + len(self.bufs) - 1)
            while self.emitted < max(lim, i + 1):
                self._emit(self.emitted)
                self.emitted += 1
            return self.views.pop(i)

    evt = {"n": 0}

    def ev(out, in_):
        evt["n"] += 1
        k.copy("act" if evt["n"] % 2 else "dve", out, in_)

    def tok2fm(dst, src_d, R):
        m = k.mark()
        t = k.alloc([R, 128], F32, "t2f")
        k.dma(t, src_d)
        b = k.bank()
        k.tr(b[:, 0:R], t, identf[0:R, 0:R])
        ev(dst, b[:, 0:R])
        k.release(m)

    def fm2tok(dst_d, src, R):
        m = k.mark()
        t = k.alloc([R, 128], F32, "f2t")
        b = k.bank()
        k.tr(b[0:R, 0:128], src, identf)
        ev(t, b[0:R, 0:128])
        k.dma(dst_d, t)
        k.release(m)

    def compute_mod():
        m = k.mark()
        ct = k.alloc([17, D], F32, "ct")
        k.dma(ct, call_d)
        k.act(ct, ct, AF.Silu)
        siluT = k.alloc([128, 16, 17], BF16, "siluT")
        for c in range(16):
            b = k.bank()
            k.tr(b[:, 0:17], ct[:, c * 128:(c + 1) * 128], identf[0:17, 0:17])
            ev(siluT[:, c, :], b[:, 0:17])
        for i in range(2):
            ws = WS([[(W["w_mod"][i][:, g * 256:(g + 1) * 256], 16)] for g in range(48)])
            for g in range(48):
                (wv,) = ws.get(g)
                b = k.bank()
                for j in range(2):
                    for kc in range(16):
                        k.mm(b[:, j * 17:(j + 1) * 17], wv[:, kc, j * 128:(j + 1) * 128], siluT[:, kc, :],
                             start=(kc == 0), stop=(kc == 15))
                for j in range(2):
                    jc = g * 2 + j
                    k.act(modT[i][:, jc, :], b[:, j * 17:(j + 1) * 17], AF.Identity, bias=P(f"bmod{i}", jc))
            for c in range(16):
                k.ts("dve", modT[i][:, 16 + c, :], modT[i][:, 16 + c, :], 1.0, P(f"ng{i}0", c), ALU.add, ALU.mult)
                k.ts("dve", modT[i][:, 64 + c, :], modT[i][:, 64 + c, :], 1.0, P(f"ng{i}2", c), ALU.add, ALU.mult)
                k.ts("dve", modT[i][:, 32 + c, :], modT[i][:, 32 + c, :], P(f"ng{i}1", c), None, ALU.mult)
                k.ts("dve", modT[i][:, 80 + c, :], modT[i][:, 80 + c, :], P(f"ng{i}3", c), None, ALU.mult)
        k.release(m)

    def modv(i, j, c, sample):
        return modT[i][:, j * 16 + c, 1:17] if sample else modT[i][:, j * 16 + c, 0:1]

    def rstd_of(chunks, N):
        rstd = k.alloc([128, N], F32, "rstd")
        m = k.mark()
        sq = [k.alloc([128, N], BF16, "sq") for _ in range(2)]
        bank = k.reserve()
        for c in range(len(chunks)):
            k.act(sq[c % 2], chunks[c], AF.Square)
            k.mm(bank[:, 0:N], onesb, sq[c % 2], start=(c == 0), stop=(c == len(chunks) - 1))
        k.act(rstd, bank[:, 0:N], AF.Ln, bias=EPS, scale=1.0 / D)
        k.act(rstd, rstd, AF.Exp, scale=-0.5)
        k.unreserve(bank)
        k.release(m)
        return rstd

    def adaln_chunk(out, tmp, x_c, rstd, gsc, sh, sample):
        k.tt("dve", tmp, x_c, rstd, ALU.mult)
        if not sample:
            k.act(out, tmp, AF.Identity, bias=sh, scale=gsc)
        else:
            k.tt("dve", tmp, tmp, gsc, ALU.mult)
            k.tt("dve", out, tmp, sh, ALU.add)

    def load_x():
        xs_ = [k.alloc([128, TS], F32, f"x{c}") for c in range(16)]
        for c in range(16):
            k.dma(xs_[c], xres_d[c])
        return xs_

    def proj_post(layer, wsrc, KC, in_p, in_s, jgate, ncol, extra=None):
        m = k.mark()
        pout = [k.alloc([128, TS], F32, f"po{c}") for c in range(16)]
        pouts = [k.alloc([128, NS], F32, f"pos{c}") for c in range(16)] if in_s else None
        sq = [k.alloc([128, TS], BF16, "psq") for _ in range(2)]
        sqs = k.alloc([128, NS], BF16, "psqs")
        ngrp = D // ncol
        ws = WS([[(wsrc[:, g * ncol:(g + 1) * ncol], KC)] for g in range(ngrp)], extra)
        bssq = k.reserve()
        bssqs = k.reserve() if in_s else None
        for g in range(ngrp):
            (wv,) = ws.get(g)
            for j in range(ncol // 128):
                c = g * (ncol // 128) + j
                b = k.bank()
                for kc in range(KC):
                    k.mm(b, wv[:, kc, j * 128:(j + 1) * 128], in_p[kc], start=(kc == 0), stop=(kc == KC - 1))
                k.copy("act", pout[c], b)
                k.act(sq[c % 2], pout[c], AF.Square)
                k.mm(bssq, onesb, sq[c % 2], start=(c == 0), stop=(c == 15))
                if in_s:
                    b2 = k.bank()
                    for kc in range(KC):
                        k.mm(b2[:, 0:NS], wv[:, kc, j * 128:(j + 1) * 128], in_s[kc], start=(kc == 0), stop=(kc == KC - 1))
                    k.copy("dve", pouts[c], b2[:, 0:NS])
                    k.act(sqs, pouts[c], AF.Square)
                    k.mm(bssqs[:, 0:NS], onesb, sqs, start=(c == 0), stop=(c == 15))
        rstd = k.alloc([128, TS], F32, "prstd")
        k.act(rstd, bssq, AF.Ln, bias=EPS, scale=1.0 / D)
        k.act(rstd, rstd, AF.Exp, scale=-0.5)
        k.unreserve(bssq)
        if in_s:
            rstds = k.alloc([128, NS], F32, "prstds")
            k.act(rstds, bssqs[:, 0:NS], AF.Ln, bias=EPS, scale=1.0 / D)
            k.act(rstds, rstds, AF.Exp, scale=-0.5)
            k.unreserve(bssqs)
        xt = [k.alloc([128, TS], F32, "pxt") for _ in range(2)]
        for c in range(16):
            x_ = xt[c % 2]
            k.dma(x_, xres_d[c])
            k.tt("dve", pout[c], pout[c], rstd, ALU.mult)
            k.stt(x_, pout[c], modv(layer, jgate, c, False), x_, ALU.mult, ALU.add)
            k.dma(xres_d[c], x_)
            if in_s:
                k.tt("dve", pouts[c], pouts[c], rstds, ALU.mult)
                k.tt("dve", pouts[c], pouts[c], modv(layer, jgate, c, True), ALU.mult)
                k.tt("dve", xsT[c], xsT[c], pouts[c], ALU.add)
        k.release(m)

    def rwkv_layer(seg, do_s, last):
        L = 0
        mtop = k.mark()
        mixes = {j: [k.alloc([128, TS], BF16, f"mx{j}_{c}") for c in range(16)] for j in (0, 2, 3)}
        yg = [k.alloc([128, TS], BF16, f"yg{c}") for c in range(16)]
        tw = k.alloc([128, TS], BF16, "tw")
        la = k.alloc([128, TS], BF16, "la")
        sg = [k.alloc([128, TS], BF16, f"sg{i}") for i in range(2)]
        if do_s:
            smix = {j: [k.alloc([128, NS], BF16, f"smx{j}_{c}") for c in range(16)] for j in range(6)}
            ygs = [k.alloc([128, NS], BF16, f"ygs{c}") for c in range(16)]
            tws = k.alloc([128, NS], BF16, "tws")
            las = k.alloc([128, NS], BF16, "las")
            sgs = [k.alloc([128, NS], BF16, f"sgs{i}") for i in range(2)]
            gs = k.alloc([128, 16, NS], F32, "gs")
        m1 = k.mark()
        x = [k.alloc([128, TS], F32, f"x{c}") for c in range(16)]
        mxin = k.mark()
        xin = [k.alloc([128, D], F32, "xin") for _ in range(2)]
        for tt_ in range(4):
            xi = xin[tt_ % 2]
            k.dma(xi, xp_d[seg * TS + tt_ * 128: seg * TS + (tt_ + 1) * 128, :])
            for c4 in range(4):
                if os.environ.get("KVAR") == "notr":
                    continue
                b = k.bank()
                for q in range(4):
                    c = c4 * 4 + q
                    k.tr(b[:, q * 128:(q + 1) * 128], xi[:, c * 128:(c + 1) * 128], identf)
                for q in range(4):
                    c = c4 * 4 + q
                    ev(x[c][:, tt_ * 128:(tt_ + 1) * 128], b[:, q * 128:(q + 1) * 128])
        if os.environ.get("KVAR") != "nostore":
            for c in range(16):
                k.dma(xres_d[c], x[c])
        k.release(mxin)
        ckpt(11)
        rstd = rstd_of(x, TS)
        ckpt(12)
        wsl = WS([[(W["rw_w1"], 16), (W["rw_a1"], 16)], [(W["rw_g1"], 16)]])
        w1v, a1v = wsl.get(0)
        (g1v,) = wsl.get(1)
        bw, ba, bg0, bg1 = k.reserve(), k.reserve(), k.reserve(), k.reserve()
        hc = [k.alloc([128, 1 + TS], F32, "hc") for _ in range(2)]
        xx = k.alloc([128, TS], F32, "xx")
        tmpn = k.alloc([128, TS], F32, "tmpn")
        tb = [k.alloc([128, TS], BF16, "tb") for _ in range(3)]
        for c in range(16):
            h_ = hc[c % 2]
            k.copy("dve", h_[:, 0:1], shift_car[:, c:c + 1])
            adaln_chunk(h_[:, 1:1 + TS], tmpn, x[c], rstd, modv(L, 1, c, False), modv(L, 0, c, False), False)
            k.copy("dve", shift_car[:, c:c + 1], h_[:, TS:TS + 1])
            k.tt("dve", xx, h_[:, 0:TS], h_[:, 1:1 + TS], ALU.subtract)
            ti = 0
            for j in range(6):
                if j in (0, 2, 3):
                    dst = mixes[j][c]
                else:
                    dst = tb[ti]
                    ti += 1
                k.stt(dst, xx, P(f"mix{j}", c), h_[:, 1:1 + TS], ALU.mult, ALU.add)
                if j == 1:
                    k.mm(bw[0:96, :], w1v[:, c, :], dst, start=(c == 0), stop=(c == 15))
                elif j == 4:
                    k.mm(ba, a1v[:, c, :], dst, start=(c == 0), stop=(c == 15))
                elif j == 5:
                    k.mm(bg0, g1v[:, c, 0:128], dst, start=(c == 0), stop=(c == 15))
                    k.mm(bg1, g1v[:, c, 128:256], dst, start=(c == 0), stop=(c == 15))
        k.act(tw[0:96, :], bw[0:96, :], AF.Tanh)
        k.copy("dve", la, ba)
        k.act(sg[0], bg0, AF.Sigmoid)
        k.act(sg[1], bg1, AF.Sigmoid)
        for b_ in (bw, ba, bg0, bg1):
            k.unreserve(b_)
        ckpt(13)
        if do_s:
            rstds = rstd_of(xsT, NS)
            hs = k.alloc([128, NS], F32, "hs")
            hp = k.alloc([128, NS], F32, "hp")
            xxs = k.alloc([128, NS], F32, "xxs")
            tmps = k.alloc([128, NS], F32, "tmps")
            for c in range(16):
                adaln_chunk(hs, tmps, xsT[c], rstds, modv(L, 1, c, True), modv(L, 0, c, True), True)
                tok2fm(hp, stsh_d[:, c * 128:(c + 1) * 128], NS)
                fm2tok(sshift_d[:, c * 128:(c + 1) * 128], hs, NS)
                k.tt("dve", xxs, hp, hs, ALU.subtract)
                for j in range(6):
                    k.stt(smix[j][c], xxs, P(f"mix{j}", c), hs, ALU.mult, ALU.add)
            b = k.bank()
            for c in range(16):
                k.mm(b[0:96, 0:16], w1v[:, c, :], smix[1][c], start=(c == 0), stop=(c == 15))
            for c in range(16):
                k.mm(b[:, 16:32], a1v[:, c, :], smix[4][c], start=(c == 0), stop=(c == 15))
            for c in range(16):
                k.mm(b[:, 32:48], g1v[:, c, 0:128], smix[5][c], start=(c == 0), stop=(c == 15))
            for c in range(16):
                k.mm(b[:, 48:64], g1v[:, c, 128:256], smix[5][c], start=(c == 0), stop=(c == 15))
            k.act(tws[0:96, :], b[0:96, 0:16], AF.Tanh)
            k.copy("dve", las, b[:, 16:32])
            k.act(sgs[0], b[:, 32:48], AF.Sigmoid)
            k.act(sgs[1], b[:, 48:64], AF.Sigmoid)
        k.release(m1)
        ckpt(2)

        m2 = k.mark()
        l2t = [k.alloc([128, 4, 128], BF16, f"lora2_{i}") for i in range(2)]
        A = lambda n, dt=F32, w=TS: k.alloc([128, w], dt, n)
        at_f, bt_f, kt_f, rt_f = [A(n, BF16) for n in ("at_f", "bt_f", "kt_f", "rt_f")]
        gTb = A("gTb", BF16)
        at_b, bt_b, rt_b = [k.alloc([128, 4, 2, 128], BF16, n) for n in ("at_b", "bt_b", "rt_b")]
        for t_ in (at_b, bt_b, rt_b):
            k.memset("pool", t_, 0.0)
        v32 = k.alloc([128, 4, 128], F32, "v32")
        vb = k.alloc([128, 4, 128], BF16, "vb")
        s_tok = A("s_tok", F32, 8)
        DL = A("DL", F32, 4)
        Tblk = k.alloc([128, 2, 64], BF16, "Tblk")
        k.memset("pool", Tblk, 0.0)
        Gbc = A("Gbc", F32, 128)
        Bbc = A("Bbc", F32, 128)
        if do_s:
            Q5 = k.alloc([128, 5, NS], F32, "Q5")
            sA = [A(f"sA{i}", F32, NS) for i in range(6)]
            sqs_ = A("sqs_", BF16, NS)
            q4t = k.alloc([NS, 4, 128], F32, "q4t")
            q2t = k.alloc([NS, 2, 128], F32, "q2t")
        ws = WS([[(W["rw_wr"][:, m * 128:(m + 1) * 128], 16), (W["rw_wk"][:, m * 128:(m + 1) * 128], 16),
                  (W["rw_wv"][:, m * 128:(m + 1) * 128], 16)] for m in range(16)])
        for m in range(16):
            mc = slice(m * 128, (m + 1) * 128)
            wr, wk, wv = ws.get(m)
            l2 = l2t[m % 2]
            k.dma(l2[0:96, 0, :], W["rw_w2"][:, mc], eng="pool")
            k.dma(l2[:, 1, :], W["rw_a2"][:, mc], eng="pool")
            k.dma(l2[:, 2, :], W["rw_g2"][0:128, mc], eng="pool")
            k.dma(l2[:, 3, :], W["rw_g2"][128:256, mc], eng="pool")
            k.dma(Gbc, V(rows_d.ap[0:1, mc].partition_broadcast(128), rows_d.b))
            k.dma(Bbc, V(rows_d.ap[1:2, mc].partition_broadcast(128), rows_d.b))
            mE = k.mark()
            r32, k32, ag, sgw, kk, cum, e_pos, e_neg, e_prev, tA, tB = [A(n) for n in
                ("r32", "k32", "ag", "sgw", "kk", "cum", "e_pos", "e_neg", "e_prev", "tA", "tB")]
            sqb, rkr = A("sqb", BF16), A("rkr", BF16)
            b = k.bank()
            for kc in range(16):
                k.mm(b, wr[:, kc, :], mixes[0][kc], start=(kc == 0), stop=(kc == 15))
            k.copy("act", r32, b)
            b = k.bank()
            for kc in range(16):
                k.mm(b, wk[:, kc, :], mixes[2][kc], start=(kc == 0), stop=(kc == 15))
            k.copy("act", k32, b)
            b = k.bank()
            for t4 in range(4):
                for kc in range(16):
                    k.mm(b[:, t4 * 128:(t4 + 1) * 128], mixes[3][kc][:, t4 * 128:(t4 + 1) * 128], wv[:, kc, :],
                         start=(kc == 0), stop=(kc == 15))
            k.copy("dve", v32.re("p a b -> p (a b)"), b)
            k.copy("dve", vb.re("p a b -> p (a b)"), b)
            b = k.bank()
            k.mm(b, l2[0:96, 0, :], tw[0:96, :])
            k.act(sgw, b, AF.Sigmoid, bias=P("w0", m))
            b = k.bank()
            k.mm(b, l2[:, 1, :], la)
            k.act(ag, b, AF.Sigmoid, bias=P("a0", m))
            b = k.bank()
            k.mm(b, l2[:, 2, :], sg[0], start=True, stop=False)
            k.mm(b, l2[:, 3, :], sg[1], start=False, stop=True)
            k.copy("act", gTb, b)
            k.ts("dve", kk, k32, P("kk", m), None, ALU.mult)
            k.act(sqb, kk, AF.Square)
            b = k.bank()
            k.mm(b, blkdiag, sqb)
            k.act(tA, b, AF.Ln)
            k.act(tA, tA, AF.Exp, scale=-0.5)
            k.tt("dve", kk, kk, tA, ALU.mult)
            k.ts("pool", tB, ag, P("ka", m), P("omka", m), ALU.mult, ALU.add)
            k.tt("pool", k32, k32, tB, ALU.mult)
            k.stt(rkr, r32, P("rk", m), k32, ALU.mult, ALU.mult)
            b = k.bank()
            for t4 in range(4):
                k.mm(b[:, t4 * 2:(t4 + 1) * 2], rkr[:, t4 * 128:(t4 + 1) * 128], blk2)
            k.copy("act", s_tok, b[:, 0:8])
            for cc in range(4):
                sl = slice(cc * 128, (cc + 1) * 128)
                k.op("dve", lambda e, sl=sl: e.tensor_tensor_scan(out=cum.ap[:, sl], data0=ones_f.ap, data1=sgw.ap[:, sl],
                                                                  initial=0.0, op0=ALU.mult, op1=ALU.add),
                     [ones_f, sgw], [cum])
            k.act(e_pos, cum, AF.Exp, scale=DEC)
            k.act(e_neg, cum, AF.Exp, scale=-DEC)
            k.tt("pool", tA, cum, sgw, ALU.subtract)
            k.act(e_prev, tA, AF.Exp, scale=DEC)
            k.copy("act", DL, e_pos.re("p (c t) -> p c t", t=128)[:, :, 127])
            k.tt("dve", rt_f, r32, e_pos, ALU.mult)
            k.tt("pool", kt_f, k32, e_neg, ALU.mult)
            k.tt("pool", tB, kk, ag, ALU.mult)
            k.tt("dve", bt_f, tB, e_neg, ALU.mult)
            k.stt(at_f, kk, -1.0, e_prev, ALU.mult, ALU.mult)
            for (Xb_, Xf_) in ((at_b, at_f), (bt_b, bt_f), (rt_b, rt_f)):
                for h in range(2):
                    k.copy("act" if h else "pool", Xb_[64 * h:64 * h + 64, :, h, :],
                           Xf_[64 * h:64 * h + 64, :].re("p (c t) -> p c t", t=128))
            if do_s:
                r_s, k_s, sgw_s, ag_s, kk_s, t_s = sA
                b = k.bank()
                for kc in range(16):
                    k.mm(b[:, 0:16], wr[:, kc, :], smix[0][kc], start=(kc == 0), stop=(kc == 15))
                for kc in range(16):
                    k.mm(b[:, 16:32], wk[:, kc, :], smix[2][kc], start=(kc == 0), stop=(kc == 15))
                k.mm(b[:, 32:48], l2[0:96, 0, :], tws[0:96, :])
                k.mm(b[:, 48:64], l2[:, 1, :], las)
                k.mm(b[:, 64:80], l2[:, 2, :], sgs[0], start=True, stop=False)
                k.mm(b[:, 64:80], l2[:, 3, :], sgs[1], start=False, stop=True)
                bv_ = k.bank()
                for kc in range(16):
                    k.mm(bv_[0:NS, 0:128], smix[3][kc], wv[:, kc, :], start=(kc == 0), stop=(kc == 15))
                k.copy("act", Q5[:, 0, :], b[:, 0:16])
                k.copy("act", k_s, b[:, 16:32])
                k.act(sgw_s, b[:, 32:48], AF.Sigmoid, bias=P("w0", m))
                k.act(ag_s, b[:, 48:64], AF.Sigmoid, bias=P("a0", m))
                k.copy("act", gs[:, m, :], b[:, 64:80])
                k.ts("dve", kk_s, k_s, P("kk", m), None, ALU.mult)
                k.act(sqs_, kk_s, AF.Square)
                b2 = k.bank()
                k.mm(b2[:, 0:NS], blkdiag, sqs_)
                k.act(t_s, b2[:, 0:NS], AF.Ln)
                k.act(t_s, t_s, AF.Exp, scale=-0.5)
                k.tt("dve", kk_s, kk_s, t_s, ALU.mult)
                k.act(Q5[:, 1, :], sgw_s, AF.Exp, scale=DEC)
                k.ts("dve", t_s, ag_s, P("ka", m), P("omka", m), ALU.mult, ALU.add)
                k.tt("dve", Q5[:, 2, :], k_s, t_s, ALU.mult)
                k.ts("dve", Q5[:, 3, :], kk_s, -1.0, None, ALU.mult)
                k.tt("dve", Q5[:, 4, :], kk_s, ag_s, ALU.mult)
                bq = k.bank()
                for q in range(4):
                    k.tr(bq[0:NS, q * 128:(q + 1) * 128], Q5[:, q, :], identf)
                k.tr(bv_[0:NS, 128:256], Q5[:, 4, :], identf)
                k.copy("act", q4t.re("p a b -> p (a b)"), bq[0:NS, 0:512])
                k.copy("act", q2t.re("p a b -> p (a b)"), bv_[0:NS, 0:256])
                k.dma(scr6_d[0:4, :, mc].re("q r c -> r q c"), q4t)
                k.dma(scr6_d[4:6, :, mc].re("q r c -> r q c"), q2t)
            k.release(mE)
            mCh = k.mark()
            C4 = range(4)
            tok3 = [A("tok3", BF16, 384) for _ in C4]
            XA = [[A("XA", BF16, 512) for _ in range(2)] for _ in C4]
            AA = [A("AA", BF16, 512) for _ in C4]
            RQ = [A("RQ", BF16, 512) for _ in C4]
            XB = [A("XB", BF16, 512) for _ in C4]
            XC = [A("XC", BF16, 256) for _ in C4]
            Z32 = [A("Z32", F32, 256) for _ in C4]
            Zb = [A("Zb", BF16, 256) for _ in C4]
            AwT = [A("AwT", BF16, 128) for _ in C4]
            Ub = A("Ub", BF16, 128)
            tmpT = A("tmpT", F32, 64)
            y32 = [A("y32", F32, 128) for _ in range(2)]
            o1 = [A("o1", F32, 128) for _ in range(2)]
            st6 = A("st6", F32, 12)
            mv = A("mv", F32, 4)
            rs2 = A("rs2", F32, 2)
            pwk = A("pwk", F32, 128)
            SL = [slice(cc * 128, (cc + 1) * 128) for cc in C4]
            atb = [at_b[:, cc].re("p a b -> p (a b)") for cc in C4]
            btb = [bt_b[:, cc].re("p a b -> p (a b)") for cc in C4]
            rtb = [rt_b[:, cc].re("p a b -> p (a b)") for cc in C4]
            for cc in C4:
                PBk = k.bank()
                PB = V(PBk.ap.bitcast(BF16), PBk.b)
                for i_, X_ in enumerate((at_f, bt_f, kt_f)):
                    k.tr(PB[:, i_ * 128:(i_ + 1) * 128], X_[:, SL[cc]], identb)
                k.copy("act", tok3[cc], PB[:, 0:384])
            for cc in C4:
                bA = k.bank()
                k.mm(bA[:, 0:256], at_f[:, SL[cc]], btb[cc])
                k.mm(bA[:, 256:512], bt_f[:, SL[cc]], atb[cc])
                k.tt("dve", AA[cc], bA, mA, ALU.mult)
            for cc in C4:
                bB = k.bank()
                k.mm(bB[:, 0:256], kt_f[:, SL[cc]], atb[cc])
                k.mm(bB[:, 256:512], bt_f[:, SL[cc]], rtb[cc])
                k.tt("dve", XB[cc], bB, mB, ALU.mult)
            for cc in C4:
                bC = k.bank()
                k.mm(bC[:, 0:256], kt_f[:, SL[cc]], rtb[cc])
                k.tt("dve", XC[cc], bC[:, 0:256], mC, ALU.mult)
            for cc in C4:
                bZ = k.bank()
                for h in range(2):
                    k.mm(bZ[:, h * 64:(h + 1) * 64], XB[cc][:, h * 128:(h + 1) * 128], vb[:, cc, h * 64:(h + 1) * 64])
                Z32v = Z32[cc].re("p (h a v) -> p h a v", h=2, a=2)
                k.copy("act", Z32v[:, :, 0, :], tok3[cc][:, 0:128].re("p (h v) -> p h v", v=64))
                k.copy("act", Z32v[:, :, 1, :], bZ[:, 0:128].re("p (h v) -> p h v", v=64))
                k.copy("pool", Zb[cc], Z32[cc])
            cur = 0
            for cc in C4:
                k.tt("pool", RQ[cc], AA[cc], ML[:, 0, :], ALU.mult)
                k.tt("pool", XA[cc][0], RQ[cc], I4, ALU.add)
            for l_ in range(1, 7):
                b1 = []
                for cc in C4:
                    Wc = XA[cc][cur]
                    b1_ = k.bank()
                    b1.append(b1_)
                    for h in range(2):
                        k.mm(b1_[:, h * 128:(h + 1) * 128], AA[cc][:, 256 + h * 128:256 + (h + 1) * 128], Wc[:, h * 128:(h + 1) * 128])
                    for h in range(2):
                        k.mm(b1_[:, 256 + h * 128:256 + (h + 1) * 128], AA[cc][:, h * 128:(h + 1) * 128], Wc[:, 256 + h * 128:256 + (h + 1) * 128])
                for cc in C4:
                    k.tt("dve", RQ[cc], b1[cc], ML[:, l_, :], ALU.mult)
                b2 = []
                for cc in C4:
                    Wc = XA[cc][cur]
                    b2_ = k.bank()
                    b2.append(b2_)
                    for h in range(2):
                        o_ = b2_[:, h * 128:(h + 1) * 128]
                        k.mm(o_, Wc[:, 256 + h * 128:256 + (h + 1) * 128], RQ[cc][:, h * 128:(h + 1) * 128], start=True, stop=False)
                        k.mm(o_, identb, Wc[:, h * 128:(h + 1) * 128], start=False, stop=True)
                    for h in range(2):
                        o_ = b2_[:, 256 + h * 128:256 + (h + 1) * 128]
                        k.mm(o_, Wc[:, h * 128:(h + 1) * 128], RQ[cc][:, 256 + h * 128:256 + (h + 1) * 128], start=True, stop=False)
                        k.mm(o_, identb, Wc[:, 256 + h * 128:256 + (h + 1) * 128], start=False, stop=True)
                for cc in C4:
                    k.copy("act", XA[cc][1 - cur], b2[cc])
                cur = 1 - cur
            for cc in C4:
                Wc = XA[cc][cur]
                bz = k.bank()
                for h in range(2):
                    k.mm(bz[:, h * 128:(h + 1) * 128], Wc[:, 256 + h * 128:256 + (h + 1) * 128], Zb[cc][:, h * 128:(h + 1) * 128])
                k.copy("dve", Z32[cc], bz[:, 0:256])
                k.copy("act", Zb[cc], bz[:, 0:256])
            for cc in C4:
                PBk = k.bank()
                PB = V(PBk.ap.bitcast(BF16), PBk.b)
                for h in range(2):
                    k.tr(PB[64 * h:64 * h + 64, 0:128], Zb[cc][:, h * 128:h * 128 + 64], identb)
                k.copy("act", AwT[cc], PB[:, 0:128])
            for h in range(2):
                k.copy("dve", Tblk[64 * h:64 * h + 64, h, :], T32[64 * h:64 * h + 64, m, :])
            Tflat = Tblk.re("p a b -> p (a b)")
            for cc in C4:
                sl = SL[cc]
                b_tok, k_tok = tok3[cc][:, 128:256], tok3[cc][:, 256:384]
                Z32v = Z32[cc].re("p (h a v) -> p h a v", h=2, a=2)
                bU = k.bank()
                k.mm(bU[:, 0:128], AwT[cc], Tflat)
                k.tt("dve", Ub.re("p (h v) -> p h v", v=64), bU[:, 0:128].re("p (h v) -> p h v", v=64), Z32v[:, :, 1, :], ALU.add)
                bT = k.bank()
                for h in range(2):
                    hs_ = slice(h * 64, (h + 1) * 64)
                    k.mm(bT[64 * h:64 * h + 64, 0:64], b_tok[:, hs_], Ub[:, hs_], start=True, stop=False)
                    k.mm(bT[64 * h:64 * h + 64, 0:64], k_tok[:, hs_], vb[:, cc, hs_], start=False, stop=True)
                bY = k.bank()
                k.mm(bY[:, 0:128], rt_f[:, sl], Tflat, start=True, stop=False)
                for h in range(2):
                    hs_ = slice(h * 64, (h + 1) * 64)
                    k.mm(bY[:, hs_], XB[cc][:, 256 + h * 128:256 + (h + 1) * 128], Ub[:, hs_], start=False, stop=False)
                    k.mm(bY[:, hs_], XC[cc][:, h * 128:(h + 1) * 128], vb[:, cc, hs_], start=False, stop=(h == 1))
                k.tt("dve", tmpT, bT[:, 0:64], T32[:, m, :], ALU.add)
                for h in range(2):
                    k.ts("dve", Tblk[64 * h:64 * h + 64, h, :], tmpT[64 * h:64 * h + 64, :], DL[64 * h:64 * h + 64, cc:cc + 1], None, ALU.mult)
                k.act(T32[:, m, :], tmpT, AF.Identity, scale=DL[:, cc:cc + 1])
                y_ = y32[cc % 2]
                o_ = o1[cc % 2]
                k.copy("act", y_, bY[:, 0:128])
                for h in range(2):
                    hs_ = slice(h * 64, (h + 1) * 64)
                    k.op("dve", lambda e, h=h, hs_=hs_, y_=y_: e.bn_stats(out=st6.ap[:, h * 6:(h + 1) * 6], in_=y_.ap[:, hs_]),
                         [y_], [st6])
                    k.op("dve", lambda e, h=h: e.bn_aggr(out=mv.ap[:, 2 * h:2 * h + 2], in_=st6.ap[:, h * 6:(h + 1) * 6]),
                         [st6], [mv])
                mvv = mv.re("p (h t) -> p h t", t=2)
                k.act(rs2, mvv[:, :, 1], AF.Ln, bias=LNX_EPS)
                k.act(rs2, rs2, AF.Exp, scale=-0.5)
                for h in range(2):
                    hs_ = slice(h * 64, (h + 1) * 64)
                    k.ts("pool", o_[:, hs_], y_[:, hs_], mv[:, 2 * h:2 * h + 1], rs2[:, h:h + 1], ALU.subtract, ALU.mult)
                k.tt("pool", o_, o_, Gbc, ALU.mult)
                k.tt("pool", o_, o_, Bbc, ALU.add)
                for h in range(2):
                    hs_ = slice(h * 64, (h + 1) * 64)
                    k.stt(o_[:, hs_], v32[:, cc, hs_], s_tok[:, cc * 2 + h:cc * 2 + h + 1], o_[:, hs_], ALU.mult, ALU.add)
                bO = k.bank()
                k.tr(bO[:, 0:128], o_, identf)
                k.tt("dve", yg[m][:, sl], bO[:, 0:128], gTb[:, sl], ALU.mult)
            if last:
                b = k.bank()
                k.tr(b[0:64, 0:128], T32[:, m, :], identf)
                k.copy("act", pwk[0:64, :], b[0:64, 0:128])
                k.dma(pwkv_d[2 * m:2 * m + 2].re("h v k -> v h k"), pwk[0:64, :].re("p (h k) -> p h k", k=64))
            k.release(mCh)
        k.release(m2)
        ckpt(3)

        if do_s:
            m3 = k.mark()
            Grh = k.alloc([128, 64], F32, "Grh")
            Brh = k.alloc([128, 64], F32, "Brh")
            RKrh = k.alloc([128, 64], F32, "RKrh")
            for r4 in range(4):
                k.dma(Grh[32 * r4:32 * r4 + 32, :], rows_d[0].re("(h v) -> h v", v=64))
                k.dma(Brh[32 * r4:32 * r4 + 32, :], rows_d[1].re("(h v) -> h v", v=64))
                k.dma(RKrh[32 * r4:32 * r4 + 32, :], rows_d[2].re("(h v) -> h v", v=64))
            q6 = k.alloc([128, 6, 64], F32, "q6")
            S = k.alloc([128, 32, 64], F32, "S")
            tmp = k.alloc([128, 32, 64], F32, "Stmp")
            sa = k.alloc([128, 32], F32, "sa")
            yrh = k.alloc([128, 64], F32, "yrh")
            orh = k.alloc([128, 64], F32, "orh")
            t64 = k.alloc([128, 64], F32, "t64")
            s1 = k.alloc([128, 1], F32, "s1")
            st6b = k.alloc([128, 6], F32, "st6b")
            mvb = k.alloc([128, 2], F32, "mvb")
            rsb = k.alloc([128, 1], F32, "rsb")
            B3 = [128, 32, 64]
            for i in range(4):
                k.dma(q6, scr6_d[:, 4 * i:4 * i + 4, :].re("q r (h k) -> (r h) q k", k=64))
                r_, d_, k_, a_, v_, b_ = [q6[:, q, :] for q in range(6)]
                for vh in range(2):
                    vs = slice(vh * 32, (vh + 1) * 32)
                    k.dma(S, stw_d[i * 128:(i + 1) * 128, vh * 2048:(vh + 1) * 2048].re("p (v k) -> p v k", k=64))
                    k.tt("dve", tmp, S, a_.us(1).bc(B3), ALU.mult)
                    k.op("dve", lambda e: e.tensor_reduce(out=sa.ap, in_=tmp.ap, axis=AX.X, op=ALU.add), [tmp], [sa])
                    k.tt("dve", S, S, d_.us(1).bc(B3), ALU.mult)
                    k.tt("dve", tmp, sa.us(2).bc(B3), b_.us(1).bc(B3), ALU.mult)
                    k.tt("dve", S, S, tmp, ALU.add)
                    k.tt("dve", tmp, v_[:, vs].us(2).bc(B3), k_.us(1).bc(B3), ALU.mult)
                    k.tt("dve", S, S, tmp, ALU.add)
                    k.dma(swkv_d[i * 128:(i + 1) * 128, vh * 2048:(vh + 1) * 2048].re("p (v k) -> p v k", k=64), S)
                    k.tt("dve", tmp, S, r_.us(1).bc(B3), ALU.mult)
                    k.op("dve", lambda e, vs=vs: e.tensor_reduce(out=yrh.ap[:, vs], in_=tmp.ap, axis=AX.X, op=ALU.add),
                         [tmp], [yrh])
                k.op("dve", lambda e: e.bn_stats(out=st6b.ap, in_=yrh.ap), [yrh], [st6b])
                k.op("dve", lambda e: e.bn_aggr(out=mvb.ap, in_=st6b.ap), [st6b], [mvb])
                k.act(rsb, mvb[:, 1:2], AF.Ln, bias=LNX_EPS)
                k.act(rsb, rsb, AF.Exp, scale=-0.5)
                k.ts("dve", orh, yrh, mvb[:, 0:1], rsb, ALU.subtract, ALU.mult)
                k.tt("dve", orh, orh, Grh, ALU.mult)
                k.tt("dve", orh, orh, Brh, ALU.add)
                k.tt("dve", t64, r_, k_, ALU.mult)
                k.tt("dve", t64, t64, RKrh, ALU.mult)
                k.op("dve", lambda e: e.tensor_reduce(out=s1.ap, in_=t64.ap, axis=AX.X, op=ALU.add), [t64], [s1])
                k.stt(orh, v_, s1, orh, ALU.mult, ALU.add)
                k.dma(scro_d[4 * i:4 * i + 4, :].re("r (h v) -> (r h) v", v=64), orh)
            of = k.alloc([128, NS], F32, "of")
            for m in range(16):
                tok2fm(of, scro_d[:, m * 128:(m + 1) * 128], NS)
                k.tt("dve", ygs[m], of, gs[:, m, :], ALU.mult)
            k.release(m3)

        ckpt(4)
        proj_post(L, W["rw_wo"], 16, yg, ygs if do_s else None, 2, 256)
        ckpt(5)
        k.release(mtop)

    def ffn_layer(L, seg, do_s, last):
        mtop = k.mark()
        xw = [k.alloc([128, WB_ELEMS], BF16, f"xwb{i}") for i in range(2)]
        z = [k.alloc([128, TS], BF16, f"z{f}") for f in range(NF)]
        zs = [k.alloc([128, NS], BF16, f"zs{f}") for f in range(NF)] if do_s else None
        m1 = k.mark()
        h2 = [k.alloc([128, TS], BF16, f"h2_{c}") for c in range(16)]
        h2s = [k.alloc([128, NS], BF16, f"h2s_{c}") for c in range(16)] if do_s else None
        m2 = k.mark()
        x = load_x()
        rstd = rstd_of(x, TS)
        tmpn = k.alloc([128, TS], F32, "tmpn")
        for c in range(16):
            adaln_chunk(h2[c], tmpn, x[c], rstd, modv(L, 4, c, False), modv(L, 3, c, False), False)
        if do_s:
            rstds = rstd_of(xsT, NS)
            tmps = k.alloc([128, NS], F32, "tmps")
            for c in range(16):
                adaln_chunk(h2s[c], tmps, xsT[c], rstds, modv(L, 4, c, True), modv(L, 3, c, True), True)
        k.release(m2)
        gt = [k.alloc([128, 2 + TS], F32, "gt") for _ in range(2)]
        t1 = [k.alloc([128, TS], F32, "t1") for _ in range(2)]
        if do_s:
            cs = k.alloc([128, 32], F32, "cs")
            gts = k.alloc([128, NS], F32, "gts")
            t1s = k.alloc([128, NS], F32, "t1s")
            o2 = k.alloc([128, NS, 2], F32, "o2")
            tk = k.alloc([32, 128], F32, "tk")
            tk2 = k.alloc([32, 128], F32, "tk2")
        reqs = []
        for g in range(22):
            reqs.append([(W["ffn_w_gate"][L][:, g * 256:(g + 1) * 256], 16)])
            reqs.append([(W["ffn_w_up"][L][:, g * 256:(g + 1) * 256], 16)])
        ws = WS(reqs, xw)
        for g in range(22):
            (wg,) = ws.get(2 * g)
            (wu,) = ws.get(2 * g + 1)
            for j in range(2):
                f = 2 * g + j
                fc = slice(f * 128, (f + 1) * 128)
                js = slice(j * 128, (j + 1) * 128)
                bg = k.bank()
                for kc in range(16):
                    k.mm(bg, wg[:, kc, js], h2[kc], start=(kc == 0), stop=(kc == 15))
                bu = k.bank()
                for kc in range(16):
                    k.mm(bu, wu[:, kc, js], h2[kc], start=(kc == 0), stop=(kc == 15))
                g_ = gt[f % 2]
                t_ = t1[f % 2]
                k.copy("dve", g_[:, 0:2], fcar[L][:, :, f])
                k.copy("act", g_[:, 2:2 + TS], bg)
                k.copy("dve", fcar[L][:, :, f], g_[:, TS:TS + 2])
                k.ts("dve", t_, g_[:, 0:TS], P(f"fcw{L}0", f), P(f"fcb{L}", f), ALU.mult, ALU.add)
                k.stt(t_, g_[:, 1:1 + TS], P(f"fcw{L}1", f), t_, ALU.mult, ALU.add)
                k.stt(t_, g_[:, 2:2 + TS], P(f"fcw{L}2", f), t_, ALU.mult, ALU.add)
                k.act(t_, t_, AF.Gelu_apprx_tanh)
                k.tt("dve", z[f], bu, t_, ALU.mult)
                if do_s:
                    bs_ = k.bank()
                    for kc in range(16):
                        k.mm(bs_[:, 0:16], wg[:, kc, js], h2s[kc], start=(kc == 0), stop=(kc == 15))
                    for kc in range(16):
                        k.mm(bs_[:, 16:32], wu[:, kc, js], h2s[kc], start=(kc == 0), stop=(kc == 15))
                    k.dma(tk, stfc_d[L][:, :, fc].re("r j c -> (r j) c"))
                    b3 = k.bank()
                    k.tr(b3[:, 0:32], tk, identf[0:32, 0:32])
                    k.copy("act", cs, b3[:, 0:32])
                    csv = cs.re("p (r j) -> p r j", j=2)
                    k.copy("dve", gts, bs_[:, 0:16])
                    k.ts("dve", t1s, csv[:, :, 0], P(f"fcw{L}0", f), P(f"fcb{L}", f), ALU.mult, ALU.add)
                    k.stt(t1s, csv[:, :, 1], P(f"fcw{L}1", f), t1s, ALU.mult, ALU.add)
                    k.stt(t1s, gts, P(f"fcw{L}2", f), t1s, ALU.mult, ALU.add)
                    k.act(t1s, t1s, AF.Gelu_apprx_tanh)
                    k.tt("dve", zs[f], bs_[:, 16:32], t1s, ALU.mult)
                    k.copy("dve", o2[:, :, 0], csv[:, :, 1])
                    k.copy("dve", o2[:, :, 1], gts)
                    b4 = k.bank()
                    k.tr(b4[0:32, 0:128], o2.re("p r j -> p (r j)"), identf)
                    k.copy("act", tk2, b4[0:32, 0:128])
                    k.dma(sfc_d[L][:, :, fc].re("r j c -> (r j) c"), tk2)
        if last:
            t88 = k.alloc([88, 128], F32, "t88")
            b = k.bank()
            k.tr(b[0:88, 0:128], fcar[L].re("p j f -> p (j f)"), identf)
            k.copy("act", t88, b[0:88, 0:128])
            for j in range(2):
                k.dma(pfc_d[L][j].re("(f p) -> f p", p=128), t88[j * NF:(j + 1) * NF, :])
        k.release(m1)
        proj_post(L, W["ffn_w_down"][L], NF, z, zs, 5, 128, xw)
        k.release(mtop)

    def lru_layer(seg, do_s, last):
        L = 1
        mtop = k.mark()
        xw = [k.alloc([128, WB_ELEMS], BF16, f"xwb{i}") for i in range(3)]
        og = [k.alloc([128, TS], BF16, f"og{c}") for c in range(16)]
        ogs = [k.alloc([128, NS], BF16, f"ogs{c}") for c in range(16)] if do_s else None
        mL = k.mark()
        h = [k.alloc([128, TS], BF16, f"h_{c}") for c in range(16)]
        hs_l = [k.alloc([128, NS], BF16, f"hs_{c}") for c in range(16)] if do_s else None
        m2 = k.mark()
        x = load_x()
        rstd = rstd_of(x, TS)
        tmpn = k.alloc([128, TS], F32, "tmpn")
        for c in range(16):
            adaln_chunk(h[c], tmpn, x[c], rstd, modv(L, 1, c, False), modv(L, 0, c, False), False)
        if do_s:
            rstds = rstd_of(xsT, NS)
            tmps = k.alloc([128, NS], F32, "tmps")
            for c in range(16):
                adaln_chunk(hs_l[c], tmps, xsT[c], rstds, modv(L, 1, c, True), modv(L, 0, c, True), True)
        k.release(m2)
        wab = k.alloc([128, 8, 2, 256], BF16, "wab")
        wxb = k.alloc([128, 8, 2, 256], BF16, "wxb")
        k.dma(wab, W["lru_wa"].re("n (a p) c -> p n a c", p=128), eng="pool")
        k.dma(wxb, W["lru_wx"].re("n (a p) c -> p n a c", p=128), eng="pool")
        ybr = [k.alloc([128, TS], BF16, f"ybr{j}") for j in range(2)]
        xb32 = [k.alloc([128, 3 + TS], F32, f"xb32{j}") for j in range(2)]
        xc32 = [k.alloc([128, TS], F32, f"xc32{j}") for j in range(2)]
        xcb = [k.alloc([128, TS], BF16, f"xcb{j}") for j in range(2)]
        aa, mu, bt, hsn = [k.alloc([128, TS], F32, n) for n in ("aa", "mu", "bt", "hsn")]
        gA = [k.alloc([128, TS], F32, f"gA{j}") for j in range(2)]
        gX = [k.alloc([128, TS], F32, f"gX{j}") for j in range(2)]
        if do_s:
            ybrs = [k.alloc([128, NS], BF16, f"ybrs{j}") for j in range(2)]
            xbs = [k.alloc([128, NS], F32, f"xbs{j}") for j in range(2)]
            xcs = [k.alloc([128, NS], F32, f"xcs{j}") for j in range(2)]
            xcbs = [k.alloc([128, NS], BF16, f"xcbs{j}") for j in range(2)]
            csl = k.alloc([128, 48], F32, "csl")
            h0s = k.alloc([128, NS], F32, "h0s")
            o3 = k.alloc([128, NS, 3], F32, "o3")
            tk3 = k.alloc([48, 128], F32, "tk3")
            tk4 = k.alloc([48, 128], F32, "tk4")
            aas, mus, bts, hns = [k.alloc([128, NS], F32, n) for n in ("aas", "mus", "bts", "hns")]
            gAs = [k.alloc([128, NS], F32, f"gAs{j}") for j in range(2)]
            gXs = [k.alloc([128, NS], F32, f"gXs{j}") for j in range(2)]
        reqs = []
        for n in range(8):
            reqs.append([(W["lru_w_in"][:, n * 256:(n + 1) * 256], 16)])
            reqs.append([(W["lru_w_in"][:, D + n * 256:D + (n + 1) * 256], 16)])
        ws = WS(reqs, xw)
        for n in range(8):
            (wy,) = ws.get(2 * n)
            (wx_,) = ws.get(2 * n + 1)
            for j in range(2):
                c = 2 * n + j
                js = slice(j * 128, (j + 1) * 128)
                cc_ = slice(c * 128, (c + 1) * 128)
                b = k.bank()
                for kc in range(16):
                    k.mm(b, wy[:, kc, js], h[kc], start=(kc == 0), stop=(kc == 15))
                k.act(ybr[j], b, AF.Gelu_apprx_tanh, bias=P("bin", c))
                b = k.bank()
                for kc in range(16):
                    k.mm(b, wx_[:, kc, js], h[kc], start=(kc == 0), stop=(kc == 15))
                xb_ = xb32[j]
                k.copy("dve", xb_[:, 0:3], lcar[:, :, c])
                k.act(xb_[:, 3:3 + TS], b, AF.Identity, bias=P("bin", 16 + c))
                k.copy("dve", lcar[:, :, c], xb_[:, TS:TS + 3])
                k.ts("dve", xc32[j], xb_[:, 0:TS], P("lcw0", c), P("lcb", c), ALU.mult, ALU.add)
                for q in range(1, 4):
                    k.stt(xc32[j], xb_[:, q:q + TS], P(f"lcw{q}", c), xc32[j], ALU.mult, ALU.add)
                k.copy("act", xcb[j], xc32[j])
                if do_s:
                    b = k.bank()
                    for kc in range(16):
                        k.mm(b[:, 0:16], wy[:, kc, js], hs_l[kc], start=(kc == 0), stop=(kc == 15))
                    for kc in range(16):
                        k.mm(b[:, 16:32], wx_[:, kc, js], hs_l[kc], start=(kc == 0), stop=(kc == 15))
                    k.act(ybrs[j], b[:, 0:16], AF.Gelu_apprx_tanh, bias=P("bin", c))
                    k.act(xbs[j], b[:, 16:32], AF.Identity, bias=P("bin", 16 + c))
                    k.dma(tk3, stlc_d[:, :, cc_].re("r j c -> (r j) c"))
                    b3 = k.bank()
                    k.tr(b3[:, 0:48], tk3, identf[0:48, 0:48])
                    k.copy("act", csl, b3[:, 0:48])
                    cv = csl.re("p (r j) -> p r j", j=3)
                    k.ts("dve", xcs[j], cv[:, :, 0], P("lcw0", c), P("lcb", c), ALU.mult, ALU.add)
                    k.stt(xcs[j], cv[:, :, 1], P("lcw1", c), xcs[j], ALU.mult, ALU.add)
                    k.stt(xcs[j], cv[:, :, 2], P("lcw2", c), xcs[j], ALU.mult, ALU.add)
                    k.stt(xcs[j], xbs[j], P("lcw3", c), xcs[j], ALU.mult, ALU.add)
                    k.copy("act", xcbs[j], xcs[j])
                    k.copy("dve", o3[:, :, 0], cv[:, :, 1])
                    k.copy("dve", o3[:, :, 1], cv[:, :, 2])
                    k.copy("dve", o3[:, :, 2], xbs[j])
                    b4 = k.bank()
                    k.tr(b4[0:48, 0:128], o3.re("p r j -> p (r j)"), identf)
                    k.copy("act", tk4, b4[0:48, 0:128])
                    k.dma(slc_d[:, :, cc_].re("r j c -> (r j) c"), tk4)
            LN_HALF = -0.6931471805599453
            for jo in range(2):
                c = 2 * n + jo
                jos = slice(jo * 128, (jo + 1) * 128)
                ba_ = k.bank()
                for ji in range(2):
                    k.mm(ba_, wab[:, n, ji, jos], xcb[ji], start=(ji == 0), stop=(ji == 1))
                bx_ = k.bank()
                for ji in range(2):
                    k.mm(bx_, wxb[:, n, ji, jos], xcb[ji], start=(ji == 0), stop=(ji == 1))
                k.act(gA[jo], ba_, AF.Tanh, bias=P("hba", c), scale=0.5)
                k.act(gX[jo], bx_, AF.Tanh, bias=P("hbx", c), scale=0.5)
                if do_s:
                    bs_ = k.bank()
                    for ji in range(2):
                        k.mm(bs_[:, 0:16], wab[:, n, ji, jos], xcbs[ji], start=(ji == 0), stop=(ji == 1))
                    for ji in range(2):
                        k.mm(bs_[:, 16:32], wxb[:, n, ji, jos], xcbs[ji], start=(ji == 0), stop=(ji == 1))
                    k.act(gAs[jo], bs_[:, 0:16], AF.Tanh, bias=P("hba", c), scale=0.5)
                    k.act(gXs[jo], bs_[:, 16:32], AF.Tanh, bias=P("hbx", c), scale=0.5)
            for jo in range(2):
                c = 2 * n + jo
                cc_ = slice(c * 128, (c + 1) * 128)
                k.act(aa, gA[jo], AF.Exp, bias=P("hcL", c), scale=P("hcL", c))
                k.tt("dve", mu, aa, aa, ALU.mult)
                k.act(mu, mu, AF.Ln, bias=1.0, scale=-1.0)
                k.act(mu, mu, AF.Exp, bias=LN_HALF, scale=0.5)
                k.stt(bt, gX[jo], 1.0, xc32[jo], ALU.add, ALU.mult)
                k.tt("dve", mu, mu, bt, ALU.mult)
                if seg == 0:
                    k.ts("dve", mu[:, 0:1], bt[:, 0:1], 0.5, None, ALU.mult)
                k.op("dve", lambda e, c=c: e.tensor_tensor_scan(out=hsn.ap, data0=aa.ap, data1=mu.ap,
                                                                initial=hcar.ap[:, c:c + 1], op0=ALU.mult, op1=ALU.add),
                     [aa, mu, hcar], [hsn])
                k.copy("dve", hcar[:, c:c + 1], hsn[:, TS - 1:TS])
                k.tt("dve", og[c], hsn, ybr[jo], ALU.mult)
                if do_s:
                    k.act(aas, gAs[jo], AF.Exp, bias=P("hcL", c), scale=P("hcL", c))
                    k.tt("dve", mus, aas, aas, ALU.mult)
                    k.act(mus, mus, AF.Ln, bias=1.0, scale=-1.0)
                    k.act(mus, mus, AF.Exp, bias=LN_HALF, scale=0.5)
                    k.stt(bts, gXs[jo], 1.0, xcs[jo], ALU.add, ALU.mult)
                    k.tt("dve", mus, mus, bts, ALU.mult)
                    tok2fm(h0s, sth_d[:, cc_], NS)
                    k.tt("dve", hns, aas, h0s, ALU.mult)
                    k.tt("dve", hns, hns, mus, ALU.add)
                    fm2tok(sh_d[:, cc_], hns, NS)
                    k.tt("dve", ogs[c], hns, ybrs[jo], ALU.mult)
        if last:
            fm2tok(ph_d, hcar, 16)
            t48 = k.alloc([48, 128], F32, "t48")
            b = k.bank()
            k.tr(b[0:48, 0:128], lcar.re("p j c -> p (j c)"), identf)
            k.copy("act", t48, b[0:48, 0:128])
            for j in range(3):
                k.dma(plc_d[j].re("(c p) -> c p", p=128), t48[j * 16:(j + 1) * 16, :])
        k.release(mL)
        proj_post(L, W["lru_w_out"], 16, og, ogs, 2, 256, xw)
        k.release(mtop)

    try:
        compute_mod()
        ckpt(1)
        for c in range(16):
            tok2fm(xsT[c], xs_d[:, c * 128:(c + 1) * 128], NS)
        ckpt(10)
        for seg in range(NSEG):
            do_s = (seg == 0)
            last = (seg == NSEG - 1)
            rwkv_layer(seg, do_s, last)
            ffn_layer(0, seg, do_s, last)
            ckpt(6)
            lru_layer(seg, do_s, last)
            ckpt(7)
            ffn_layer(1, seg, do_s, last)
            m = k.mark()
            x = load_x()
            yt = [k.alloc([128, D], F32, "yt") for _ in range(2)]
            for t4 in range(4):
                y_ = yt[t4 % 2]
                for c4 in range(4):
                    b = k.bank()
                    for q in range(4):
                        c = c4 * 4 + q
                        k.tr(b[:, q * 128:(q + 1) * 128], x[c][:, t4 * 128:(t4 + 1) * 128], identf)
                    ev(y_[:, c4 * 512:(c4 + 1) * 512], b)
                k.dma(yp_d[seg * TS + t4 * 128:seg * TS + (t4 + 1) * 128, :], y_)
            k.release(m)
            if last:
                fm2tok(pshift_d, shift_car, 16)
        for c in range(16):
            fm2tok(ys_d[:, c * 128:(c + 1) * 128], xsT[c], NS)

    except StopBuild:
        pass
    nc = k.emit()
    return nc, k


def _fm(v):
    v = np.asarray(v, np.float32).reshape(-1, 128)
    return np.ascontiguousarray(v.T)


def make_pvec(I):
    pv = np.zeros((128, PV_N), np.float32)

    def put(name, v):
        o, w = PV_OFF[name]
        pv[:, o:o + w] = _fm(v)
    for i in range(2):
        for j in range(4):
            put(f"ng{i}{j}", I["norm_g"][i, j])
        put(f"bmod{i}", I["b_mod"][i])
    for j in range(6):
        put(f"mix{j}", I["rw_mix"][0, j])
    put("w0", I["rw_w0"][0]); put("a0", I["rw_a0"][0]); put("kk", I["rw_kk"][0]); put("ka", I["rw_ka"][0])
    put("rk", I["rw_rk"][0].reshape(-1)); put("lcb", I["lru_conv_b"][0]); put("ba", I["lru_ba"][0])
    put("bx", I["lru_bx"][0]); put("lam", I["lru_lambda"][0]); put("bin", I["lru_b_in"][0])
    for j in range(4):
        put(f"lcw{j}", I["lru_conv_w"][0, j])
    for i in range(2):
        for j in range(3):
            put(f"fcw{i}{j}", I["ffn_conv_w"][i, j])
        put(f"fcb{i}", I["ffn_conv_b"][i])
    return pv


def make_cmask():
    t = np.arange(128)[:, None]
    c = np.arange(128)[None, :]
    out = np.zeros((128, 7, 512), np.float32)
    for l in range(7):
        b = 1 << l
        M = ((t // (2 * b) == c // (2 * b)) & (t % (2 * b) >= b) & (c % (2 * b) < b)).astype(np.float32)
        out[:, l, 0:128] = M
        out[:, l, 128:256] = M
        out[:, l, 256:384] = M.T
        out[:, l, 384:512] = M.T
    return out


_CACHE = {}


def kernel(**I):
    I = {k_: np.asarray(v) for k_, v in I.items()}
    B, S, _ = I["x_prompt"].shape
    NSEG = S // TS
    if NSEG not in _CACHE:
        _CACHE[NSEG] = build(NSEG)[0]
    nc = _CACHE[NSEG]
    pv = make_pvec(I)
    rows = np.ascontiguousarray(np.stack([I["rw_lnx_g"][0], I["rw_lnx_b"][0], I["rw_rk"][0].reshape(-1)]).astype(np.float32))
    shared = {
        "pvec": pv, "rows": rows, "cmask": make_cmask(),
        "w_mod": I["w_mod"], "rw_wr": I["rw_wr"][0], "rw_wk": I["rw_wk"][0], "rw_wv": I["rw_wv"][0], "rw_wo": I["rw_wo"][0],
        "rw_w1": I["rw_w1"][0], "rw_w2": I["rw_w2"][0], "rw_a1": I["rw_a1"][0], "rw_a2": I["rw_a2"][0],
        "rw_g1": I["rw_g1"][0], "rw_g2": I["rw_g2"][0], "lru_w_in": I["lru_w_in"][0], "lru_wa": I["lru_wa"][0],
        "lru_wx": I["lru_wx"][0], "lru_w_out": I["lru_w_out"][0], "ffn_w_gate": I["ffn_w_gate"],
        "ffn_w_up": I["ffn_w_up"], "ffn_w_down": I["ffn_w_down"],
    }
    shared = {k_: np.ascontiguousarray(v, dtype=np.float32) for k_, v in shared.items()}
    in_maps = []
    for c in range(8):
        b = c % B
        r = slice(NS * c, NS * (c + 1))
        d = dict(shared)
        d["xp"] = np.ascontiguousarray(I["x_prompt"][b])
        d["xs"] = np.ascontiguousarray(I["x_sample"][r, 0])
        d["call"] = np.ascontiguousarray(np.concatenate([I["c_prompt"][b:b + 1], I["c_sample"][r]], 0))
        d["st_wkv"] = np.ascontiguousarray(I["state_rwkv_wkv"][0, r].reshape(NS * NH, 4096))
        d["st_shift"] = np.ascontiguousarray(I["state_rwkv_shift"][0, r])
        d["st_h"] = np.ascontiguousarray(I["state_lru_h"][0, r])
        d["st_lconv"] = np.ascontiguousarray(I["state_lru_conv"][0, r])
        d["st_fconv"] = np.ascontiguousarray(I["state_ffn_conv"][:, r])
        in_maps.append(d)
    res = run_bass_kernel_spmd(nc, in_maps, core_ids=list(range(8)))
    R = res.results
    f32 = np.float32
    y_prompt = np.stack([R[b]["y_p"] for b in range(B)]).astype(f32)
    y_sample = np.concatenate([R[c]["y_s"] for c in range(8)], 0).reshape(8 * NS, 1, D).astype(f32)
    p_wkv = np.stack([R[b]["p_wkv"] for b in range(B)])[None].astype(f32)
    p_shift = np.stack([R[b]["p_shift"].reshape(D) for b in range(B)])[None].astype(f32)
    p_h = np.stack([R[b]["p_h"].reshape(D) for b in range(B)])[None].astype(f32)
    p_lconv = np.stack([R[b]["p_lconv"] for b in range(B)])[None].astype(f32)
    p_fconv = np.stack([R[b]["p_fconv"] for b in range(B)], 1).astype(f32)
    s_wkv = np.concatenate([R[c]["s_wkv"].reshape(NS, NH, 64, 64) for c in range(8)], 0)[None].astype(f32)
    s_shift = np.concatenate([R[c]["s_shift"] for c in range(8)], 0)[None].astype(f32)
    s_h = np.concatenate([R[c]["s_h"] for c in range(8)], 0)[None].astype(f32)
    s_lconv = np.concatenate([R[c]["s_lconv"] for c in range(8)], 0)[None].astype(f32)
    s_fconv = np.concatenate([R[c]["s_fconv"] for c in range(8)], 1).astype(f32)
    return (y_prompt, y_sample, p_wkv, p_shift, p_h, p_lconv, p_fconv, s_wkv, s_shift, s_h, s_lconv, s_fconv)
```

```python
import numpy as np
import concourse.bass as bass
import concourse.mybir as mybir
from concourse.bass_utils import run_bass_kernel_spmd

F32 = mybir.dt.float32
BF16 = mybir.dt.bfloat16
AF = mybir.ActivationFunctionType
ALU = mybir.AluOpType
AX = mybir.AxisListType

D = 2048
NCH = 16
DFF = 5632
NF = 44
TS = 512
NS = 16
NH = 32
EPS = 1e-6
LNX_EPS = 64e-5
DEC = -0.6065306597126334
WB_ELEMS = 6144
NWB = 3


class Buf:
    __slots__ = ("name", "lw", "wd", "rd", "rdd", "alias", "psum", "notrack")

    def __init__(self, name):
        self.name = name
        self.lw = None
        self.wd = []
        self.rd = {}
        self.rdd = []
        self.alias = []
        self.psum = False
        self.notrack = False


class V:
    __slots__ = ("ap", "b")

    def __init__(self, ap, b):
        self.ap = ap
        self.b = b

    def __getitem__(self, k):
        return V(self.ap[k], self.b)

    def re(self, s, **kw):
        return V(self.ap.rearrange(s, **kw), self.b)

    def bc(self, shape):
        return V(self.ap.to_broadcast(list(shape)), self.b)

    def us(self, ax):
        return V(self.ap.unsqueeze(ax), self.b)


class Op:
    __slots__ = ("eng", "fn", "reads", "writes", "dma", "waits", "sig", "cnt", "slot", "slotcnt", "prewait")

    def __init__(self, eng, fn, reads, writes, dma):
        self.eng = eng
        self.fn = fn
        self.reads = reads
        self.writes = writes
        self.dma = dma
        self.waits = []
        self.sig = False
        self.cnt = 0


NSLOT = 16
ARENA_WORDS = 53150


class KB:
    ENGS = ("pe", "dve", "act", "pool", "sp")

    def __init__(self):
        self.nc = bass.Bass("TRN2", target_bir_lowering=False)
        self.ops = []
        self._ctx = []
        g = self.nc.sbuf_tensor("arena", [128, ARENA_WORDS], F32)
        self.arena = g.__enter__()[:]
        self._ctx.append(g)
        self.top = 0
        self.live = []
        self.dead = []
        self.banks = []
        for i in range(8):
            g = self.nc.psum_tensor(f"bank{i}", [128, 512], F32)
            self.banks.append(V(g.__enter__()[:], Buf(f"bank{i}")))
            self.banks[-1].b.psum = True
            self._ctx.append(g)
        self.reserved = set()
        self.bi = 0
        self.maxtop = 0

    def alloc(self, shape, dt=F32, name="t"):
        n = 1
        for s in shape[1:]:
            n *= s
        words = n if dt == F32 else (n + 1) // 2
        words = (words + 1) // 2 * 2
        st = self.top
        en = st + words
        assert en <= ARENA_WORDS, f"arena overflow {en}"
        self.top = en
        self.maxtop = max(self.maxtop, en)
        ap = self.arena[:, st:en]
        if dt != F32:
            ap = ap.bitcast(dt)
        ap = ap[0:shape[0], 0:n]
        if len(shape) == 3:
            ap = ap.rearrange("p (a b) -> p a b", b=shape[2])
        elif len(shape) == 4:
            ap = ap.rearrange("p (a b c) -> p a b c", b=shape[2], c=shape[3])
        b = Buf(name)
        keep = []
        for (s0, e0, ob) in self.dead:
            if s0 < en and st < e0:
                b.alias.append(ob)
                if s0 < st:
                    keep.append((s0, st, ob))
                if e0 > en:
                    keep.append((en, e0, ob))
            else:
                keep.append((s0, e0, ob))
        self.dead = keep
        self.live.append((st, en, b))
        return V(ap, b)

    def mark(self):
        return (self.top, len(self.live))

    def release(self, m):
        top, nl = m
        for ent in self.live[nl:]:
            self.dead.append(ent)
        del self.live[nl:]
        self.top = top

    def bank(self):
        for _ in range(8):
            i = self.bi % 8
            self.bi += 1
            if i not in self.reserved:
                return self.banks[i]
        raise RuntimeError("no bank")

    def reserve(self):
        for _ in range(8):
            i = self.bi % 8
            self.bi += 1
            if i not in self.reserved:
                self.reserved.add(i)
                return self.banks[i]
        raise RuntimeError("no bank")

    def unreserve(self, b):
        for i, x in enumerate(self.banks):
            if x.b is b.b:
                self.reserved.discard(i)

    def dram(self, name, shape, dt=F32, kind="Internal"):
        t = self.nc.dram_tensor(name, list(shape), dt, kind=kind)
        return V(t.ap(), Buf(name))

    def op(self, eng, fn, reads=(), writes=(), dma=False):
        o = Op(eng, fn, [r.b for r in reads], [w.b for w in writes], dma)
        self.ops.append(o)
        return o

    def dma(self, out, in_, eng="sp"):
        return self.op(eng, lambda e: e.dma_start(out=out.ap, in_=in_.ap), [in_], [out], dma=True)

    def mm(self, out, lhsT, rhs, start=True, stop=True):
        return self.op("pe", lambda e: e.matmul(out=out.ap, lhsT=lhsT.ap, rhs=rhs.ap, start=start, stop=stop),
                       [lhsT, rhs], [out])

    def tr(self, out, in_, ident):
        return self.op("pe", lambda e: e.transpose(out=out.ap, in_=in_.ap, identity=ident.ap), [in_, ident], [out])

    def act(self, out, in_, func, bias=None, scale=None):
        reads = [in_] + [x for x in (bias, scale) if isinstance(x, V)]
        kw = {}
        if bias is not None:
            kw["bias"] = bias.ap if isinstance(bias, V) else float(bias)
        if scale is not None:
            kw["scale"] = scale.ap if isinstance(scale, V) else float(scale)
        return self.op("act", lambda e: e.activation(out=out.ap, in_=in_.ap, func=func, **kw), reads, [out])

    def copy(self, eng, out, in_):
        if eng == "act":
            return self.op("act", lambda e: e.copy(out=out.ap, in_=in_.ap), [in_], [out])
        return self.op(eng, lambda e: e.tensor_copy(out=out.ap, in_=in_.ap), [in_], [out])

    def tt(self, eng, out, a, b, op):
        return self.op(eng, lambda e: e.tensor_tensor(out=out.ap, in0=a.ap, in1=b.ap, op=op), [a, b], [out])

    def ts(self, eng, out, a, s1, s2, op0, op1=None):
        reads = [a] + [x for x in (s1, s2) if isinstance(x, V)]
        v1 = s1.ap if isinstance(s1, V) else float(s1)
        v2 = None if s2 is None else (s2.ap if isinstance(s2, V) else float(s2))
        if op1 is None:
            return self.op(eng, lambda e: e.tensor_scalar(out=out.ap, in0=a.ap, scalar1=v1, scalar2=None, op0=op0),
                           reads, [out])
        return self.op(eng, lambda e: e.tensor_scalar(out=out.ap, in0=a.ap, scalar1=v1, scalar2=v2, op0=op0, op1=op1),
                       reads, [out])

    def stt(self, out, a, s, b, op0, op1):
        reads = [a, b] + ([s] if isinstance(s, V) else [])
        sv = s.ap if isinstance(s, V) else float(s)
        return self.op("dve", lambda e: e.scalar_tensor_tensor(out=out.ap, in0=a.ap, scalar=sv, in1=b.ap, op0=op0, op1=op1),
                       reads, [out])

    def memset(self, eng, out, val):
        return self.op(eng, lambda e: e.memset(out.ap, val), [], [out])

    def finalize(self):
        ops = self.ops
        for i, o in enumerate(ops):
            deps = set()
            for b in o.reads:
                if b.lw is not None:
                    deps.add(b.lw)
                deps.update(b.wd)
                if b.psum:
                    for e, j in b.rd.items():
                        if e != o.eng:
                            deps.add(j)
            for b in o.writes:
                if b.notrack:
                    continue
                for ob in b.alias:
                    if ob.lw is not None:
                        deps.add(ob.lw)
                    deps.update(ob.wd)
                    deps.update(ob.rd.values())
                    deps.update(ob.rdd)
                b.alias = []
                if b.lw is not None:
                    deps.add(b.lw)
                if not o.dma:
                    deps.update(b.wd)
                deps.update(b.rd.values())
                deps.update(b.rdd)
            for b in o.reads:
                if o.dma:
                    b.rdd.append(i)
                else:
                    b.rd[o.eng] = i
            for b in o.writes:
                if b.notrack:
                    continue
                had_readers = bool(b.rd or b.rdd)
                if o.dma:
                    if had_readers:
                        b.wd = [i]
                        b.lw = None
                    else:
                        b.wd.append(i)
                else:
                    b.lw = i
                    b.wd = []
                b.rd = {}
                b.rdd = []
            best = {}
            dl = []
            for j in deps:
                if j == i:
                    continue
                p = ops[j]
                e = p.eng
                if p.dma:
                    dl.append(j)
                else:
                    if e == "pe" and o.eng == "pe" and not o.dma:
                        continue
                    if e not in best or best[e] < j:
                        best[e] = j
            o.waits = sorted(set(list(best.values()) + dl))
            for j in o.waits:
                ops[j].sig = True
        cnt = {e: 0 for e in self.ENGS}
        dcount = {e: 0 for e in self.ENGS}
        slotcnt = {e: [0] * NSLOT for e in self.ENGS}
        for o in ops:
            if o.dma:
                kk = dcount[o.eng]
                dcount[o.eng] += 1
                s = kk % NSLOT
                o.slot = s
                o.prewait = slotcnt[o.eng][s]
                slotcnt[o.eng][s] += 16
                o.slotcnt = slotcnt[o.eng][s]
            elif o.sig:
                cnt[o.eng] += 1
                o.cnt = cnt[o.eng]
        self.final_dma = {e: list(slotcnt[e]) for e in self.ENGS}

    def emit(self):
        nc = self.nc
        ops = self.ops
        self.finalize()
        sems = {}
        for e in self.ENGS:
            g = nc.semaphore(f"s_{e}")
            sems[e] = g.__enter__()
            self._ctx.append(g)
        dsem = {}
        for e in self.ENGS:
            if any(o.dma and o.eng == e for o in ops):
                dsem[e] = []
                for s in range(NSLOT):
                    g = nc.semaphore(f"d_{e}_{s}")
                    dsem[e].append(g.__enter__())
                    self._ctx.append(g)
        per = {e: [o for o in ops if o.eng == e] for e in self.ENGS}
        final_dma = self.final_dma

        def run(eng_name, eng):
            waited = {}
            for o in per[eng_name]:
                if o.dma and o.prewait > 0:
                    key = ("d", eng_name, o.slot)
                    if waited.get(key, 0) < o.prewait:
                        eng.wait_ge(dsem[eng_name][o.slot], o.prewait)
                        waited[key] = o.prewait
                for j in o.waits:
                    p = ops[j]
                    if p.dma:
                        key = ("d", p.eng, p.slot)
                        val = p.slotcnt
                        sem = dsem[p.eng][p.slot]
                    else:
                        key = ("c", p.eng)
                        val = p.cnt
                        sem = sems[p.eng]
                    if waited.get(key, 0) < val:
                        eng.wait_ge(sem, val)
                        waited[key] = val
                ins = o.fn(eng)
                if o.dma:
                    ins.then_inc(dsem[eng_name][o.slot], 16)
                elif o.sig:
                    ins.then_inc(sems[eng_name], 1)
            if eng_name in dsem:
                for s in range(NSLOT):
                    v = final_dma[eng_name][s]
                    if v > 0 and waited.get(("d", eng_name, s), 0) < v:
                        eng.wait_ge(dsem[eng_name][s], v)

        with nc.Block() as block:
            @block.sync
            def _(e):
                run("sp", e)

            @block.tensor
            def _(e):
                run("pe", e)

            @block.vector
            def _(e):
                run("dve", e)

            @block.scalar
            def _(e):
                run("act", e)

            @block.gpsimd
            def _(e):
                run("pool", e)
        return nc


def pvec_layout():
    names = []
    for i in range(2):
        for j in range(4):
            names.append((f"ng{i}{j}", 16))
    for i in range(2):
        names.append((f"bmod{i}", 96))
    for j in range(6):
        names.append((f"mix{j}", 16))
    for n in ("w0", "a0", "kk", "ka", "rk", "lcb", "ba", "bx", "lam"):
        names.append((n, 16))
    names.append(("bin", 32))
    for j in range(4):
        names.append((f"lcw{j}", 16))
    for i in range(2):
        for j in range(3):
            names.append((f"fcw{i}{j}", 44))
        names.append((f"fcb{i}", 44))
    names.append(("omka", 16))
    names.append(("cL", 16))
    names.append(("hba", 16))
    names.append(("hbx", 16))
    names.append(("hcL", 16))
    off = {}
    o = 0
    for n, w in names:
        off[n] = (o, w)
        o += w
    return off, o


PV_OFF, PV_N = pvec_layout()


def build(NSEG):
    k = KB()
    T = NSEG * TS
    IN = lambda n, s: k.dram(n, s, F32, kind="ExternalInput")
    def OUT(n, s):
        v = k.dram(n, s, F32, kind="ExternalOutput")
        v.b.notrack = True
        return v
    xp_d = IN("xp", [T, D])
    xs_d = IN("xs", [NS, D])
    call_d = IN("call", [17, D])
    stw_d = IN("st_wkv", [NS * NH, 4096])
    stsh_d = IN("st_shift", [NS, D])
    sth_d = IN("st_h", [NS, D])
    stlc_d = IN("st_lconv", [NS, 3, D])
    stfc_d = IN("st_fconv", [2, NS, 2, DFF])
    pvec_d = IN("pvec", [128, PV_N])
    rows_d = IN("rows", [3, D])
    cmask_d = IN("cmask", [128, 7, 512])
    W = {}
    for n, s in (("w_mod", [2, D, 6 * D]), ("rw_wr", [D, D]), ("rw_wk", [D, D]), ("rw_wv", [D, D]), ("rw_wo", [D, D]),
                 ("rw_w1", [D, 96]), ("rw_w2", [96, D]), ("rw_a1", [D, 128]), ("rw_a2", [128, D]),
                 ("rw_g1", [D, 256]), ("rw_g2", [256, D]), ("lru_w_in", [D, 2 * D]), ("lru_wa", [8, 256, 256]),
                 ("lru_wx", [8, 256, 256]), ("lru_w_out", [D, D]), ("ffn_w_gate", [2, D, DFF]),
                 ("ffn_w_up", [2, D, DFF]), ("ffn_w_down", [2, DFF, D])):
        W[n] = IN(n, s)
    yp_d = OUT("y_p", [T, D])
    ys_d = OUT("y_s", [NS, D])
    pwkv_d = OUT("p_wkv", [NH, 64, 64])
    pshift_d = OUT("p_shift", [16, 128])
    ph_d = OUT("p_h", [16, 128])
    plc_d = OUT("p_lconv", [3, D])
    pfc_d = OUT("p_fconv", [2, 2, DFF])
    swkv_d = OUT("s_wkv", [NS * NH, 4096])
    sshift_d = OUT("s_shift", [NS, D])
    sh_d = OUT("s_h", [NS, D])
    slc_d = OUT("s_lconv", [NS, 3, D])
    sfc_d = OUT("s_fconv", [2, NS, 2, DFF])
    import os
    DBG = os.environ.get("KDBG") == "1"
    dbg_n = {"n": 0}

    def dump(name, v, shape, dt_is_bf16=False):
        if not DBG:
            return
        d = k.dram("dbg_" + name, list(shape), F32, kind="ExternalOutput")
        if dt_is_bf16:
            k.dma(d, v, eng="pool")
        else:
            k.dma(d, v)
    STOP = int(os.environ.get("KSTOP", "0"))

    class StopBuild(Exception):
        pass

    def ckpt(n):
        if STOP == n:
            raise StopBuild()
    xres_d = [k.dram(f"xres{c}", [128, TS]) for c in range(16)]
    scr6_d = k.dram("scr6", [6, NS, D])
    scro_d = k.dram("scro", [NS, D])

    ones_f = k.alloc([128, 128], F32, "ones_f")
    k.memset("pool", ones_f, 1.0)
    identf = k.alloc([128, 128], F32, "identf")
    k.op("pool", lambda e: e.affine_select(out=identf.ap, in_=ones_f.ap, pattern=[[-1, 128]], compare_op=ALU.is_equal,
                                           fill=0.0, base=0, channel_multiplier=1), [ones_f], [identf])
    identb = k.alloc([128, 128], BF16, "identb")
    k.copy("pool", identb, identf)
    onesb = k.alloc([128, 128], BF16, "onesb")
    k.copy("pool", onesb, ones_f)
    blkdiag = k.alloc([128, 128], BF16, "blkdiag")
    k.memset("pool", blkdiag, 0.0)
    k.memset("pool", blkdiag[0:64, 0:64], 1.0)
    k.memset("pool", blkdiag[64:128, 64:128], 1.0)
    blk2 = k.alloc([128, 2], BF16, "blk2")
    k.memset("pool", blk2, 0.0)
    k.memset("pool", blk2[0:64, 0:1], 1.0)
    k.memset("pool", blk2[64:128, 1:2], 1.0)
    mA = k.alloc([128, 512], F32, "mA")
    mB = k.alloc([128, 512], F32, "mB")
    mC = k.alloc([128, 256], F32, "mC")

    def sel(dst, kind):
        if kind == "Ls":
            pat, base, cm = [[-1, 128]], -1, 1
        elif kind == "Us":
            pat, base, cm = [[1, 128]], -1, -1
        else:
            pat, base, cm = [[1, 128]], 0, -1
        k.op("pool", lambda e: e.affine_select(out=dst.ap, in_=ones_f.ap, pattern=pat, compare_op=ALU.is_ge,
                                               fill=0.0, base=base, channel_multiplier=cm), [ones_f], [dst])
    for i, kd in enumerate(("Ls", "Ls", "Us", "Us")):
        sel(mA[:, i * 128:(i + 1) * 128], kd)
    for i, kd in enumerate(("Us", "Us", "Ui", "Ui")):
        sel(mB[:, i * 128:(i + 1) * 128], kd)
    for i, kd in enumerate(("Ui", "Ui")):
        sel(mC[:, i * 128:(i + 1) * 128], kd)

    ML = k.alloc([128, 7, 512], BF16, "ML")
    for l_ in range(7):
        k.dma(ML[:, l_, :], cmask_d[:, l_, :], eng="pool")
    I4 = k.alloc([128, 512], BF16, "I4")
    for i_ in range(4):
        k.copy("pool", I4[:, i_ * 128:(i_ + 1) * 128], identf)
    pv = k.alloc([128, PV_N], F32, "pv")
    k.dma(pv, pvec_d)

    def P(name, c):
        o, w = PV_OFF[name]
        return pv[:, o + c:o + c + 1]

    def PR(name):
        o, w = PV_OFF[name]
        return pv[:, o:o + w]
    k.ts("dve", PR("omka"), PR("ka"), -1.0, 1.0, ALU.mult, ALU.add)
    k.act(PR("cL"), PR("lam"), AF.Exp, scale=-1.0)
    k.act(PR("cL"), PR("cL"), AF.Ln, bias=1.0)
    k.ts("dve", PR("cL"), PR("cL"), -8.0, None, ALU.mult)
    k.ts("dve", PR("hcL"), PR("cL"), 0.5, None, ALU.mult)
    k.ts("dve", PR("hba"), PR("ba"), 0.5, None, ALU.mult)
    k.ts("dve", PR("hbx"), PR("bx"), 0.5, None, ALU.mult)

    modT = [k.alloc([128, 96, 17], F32, f"modT{i}") for i in range(2)]
    shift_car = k.alloc([128, 16], F32, "shift_car")
    k.memset("pool", shift_car, 0.0)
    fcar = [k.alloc([128, 2, NF], F32, f"fcar{i}") for i in range(2)]
    for i in range(2):
        k.memset("pool", fcar[i], 0.0)
    lcar = k.alloc([128, 3, 16], F32, "lcar")
    k.memset("pool", lcar, 0.0)
    hcar = k.alloc([128, 16], F32, "hcar")
    k.memset("pool", hcar, 0.0)
    T32 = k.alloc([128, 16, 64], F32, "T32")
    k.memset("pool", T32, 0.0)
    xsT = [k.alloc([128, NS], F32, f"xsT{c}") for c in range(16)]
    wbufs = [k.alloc([128, WB_ELEMS], BF16, f"wb{i}") for i in range(NWB)]
    wstate = {"n": 0}

    class WS:
        def __init__(self, reqs, extra=None):
            self.reqs = reqs
            self.emitted = 0
            self.views = {}
            self.bufs = wbufs + (extra or [])
            self.n = wstate["n"] % NWB

        def _emit(self, i):
            buf = self.bufs[self.n % len(self.bufs)]
            self.n += 1
            wstate["n"] += 1
            off = 0
            vs = []
            for (src, nk) in self.reqs[i]:
                rows, cols = src.ap.shape
                pr = rows // nk
                dst = buf[0:pr, off:off + nk * cols].re("p (a b) -> p a b", b=cols)
                k.dma(dst, src.re("(a p) c -> p a c", p=pr), eng="pool")
                vs.append(dst)
                off += nk * cols
            assert off <= WB_ELEMS
            self.views[i] = vs

        def get(self, i):
            lim = min(len(self.reqs), i + len(self.bufs) - 1)
            while self.emitted < max(lim, i + 1):
                self._emit(self.emitted)
                self.emitted += 1
            return self.views.pop(i)

    evt = {"n": 0}

    def ev(out, in_):
        evt["n"] += 1
        k.copy("act" if evt["n"] % 2 else "dve", out, in_)

    def tok2fm(dst, src_d, R):
        m = k.mark()
        t = k.alloc([R, 128], F32, "t2f")
        k.dma(t, src_d)
        b = k.bank()
        k.tr(b[:, 0:R], t, identf[0:R, 0:R])
        ev(dst, b[:, 0:R])
        k.release(m)

    def fm2tok(dst_d, src, R):
        m = k.mark()
        t = k.alloc([R, 128], F32, "f2t")
        b = k.bank()
        k.tr(b[0:R, 0:128], src, identf)
        ev(t, b[0:R, 0:128])
        k.dma(dst_d, t)
        k.release(m)

    def compute_mod():
        m = k.mark()
        ct = k.alloc([17, D], F32, "ct")
        k.dma(ct, call_d)
        k.act(ct, ct, AF.Silu)
        siluT = k.alloc([128, 16, 17], BF16, "siluT")
        for c in range(16):
            b = k.bank()
            k.tr(b[:, 0:17], ct[:, c * 128:(c + 1) * 128], identf[0:17, 0:17])
            ev(siluT[:, c, :], b[:, 0:17])
        for i in range(2):
            ws = WS([[(W["w_mod"][i][:, g * 256:(g + 1) * 256], 16)] for g in range(48)])
            for g in range(48):
                (wv,) = ws.get(g)
                b = k.bank()
                for j in range(2):
                    for kc in range(16):
                        k.mm(b[:, j * 17:(j + 1) * 17], wv[:, kc, j * 128:(j + 1) * 128], siluT[:, kc, :],
                             start=(kc == 0), stop=(kc == 15))
                for j in range(2):
                    jc = g * 2 + j
                    k.act(modT[i][:, jc, :], b[:, j * 17:(j + 1) * 17], AF.Identity, bias=P(f"bmod{i}", jc))
            for c in range(16):
                k.ts("dve", modT[i][:, 16 + c, :], modT[i][:, 16 + c, :], 1.0, P(f"ng{i}0", c), ALU.add, ALU.mult)
                k.ts("dve", modT[i][:, 64 + c, :], modT[i][:, 64 + c, :], 1.0, P(f"ng{i}2", c), ALU.add, ALU.mult)
                k.ts("dve", modT[i][:, 32 + c, :], modT[i][:, 32 + c, :], P(f"ng{i}1", c), None, ALU.mult)
                k.ts("dve", modT[i][:, 80 + c, :], modT[i][:, 80 + c, :], P(f"ng{i}3", c), None, ALU.mult)
        k.release(m)

    def modv(i, j, c, sample):
        return modT[i][:, j * 16 + c, 1:17] if sample else modT[i][:, j * 16 + c, 0:1]

    def rstd_of(chunks, N):
        rstd = k.alloc([128, N], F32, "rstd")
        m = k.mark()
        sq = [k.alloc([128, N], BF16, "sq") for _ in range(2)]
        bank = k.reserve()
        for c in range(len(chunks)):
            k.act(sq[c % 2], chunks[c], AF.Square)
            k.mm(bank[:, 0:N], onesb, sq[c % 2], start=(c == 0), stop=(c == len(chunks) - 1))
        k.act(rstd, bank[:, 0:N], AF.Ln, bias=EPS, scale=1.0 / D)
        k.act(rstd, rstd, AF.Exp, scale=-0.5)
        k.unreserve(bank)
        k.release(m)
        return rstd

    def adaln_chunk(out, tmp, x_c, rstd, gsc, sh, sample):
        k.tt("dve", tmp, x_c, rstd, ALU.mult)
        if not sample:
            k.act(out, tmp, AF.Identity, bias=sh, scale=gsc)
        else:
            k.tt("dve", tmp, tmp, gsc, ALU.mult)
            k.tt("dve", out, tmp, sh, ALU.add)

    def load_x():
        xs_ = [k.alloc([128, TS], F32, f"x{c}") for c in range(16)]
        for c in range(16):
            k.dma(xs_[c], xres_d[c])
        return xs_

    def proj_post(layer, wsrc, KC, in_p, in_s, jgate, ncol, extra=None):
        m = k.mark()
        pout = [k.alloc([128, TS], F32, f"po{c}") for c in range(16)]
        pouts = [k.alloc([128, NS], F32, f"pos{c}") for c in range(16)] if in_s else None
        sq = [k.alloc([128, TS], BF16, "psq") for _ in range(2)]
        sqs = k.alloc([128, NS], BF16, "psqs")
        ngrp = D // ncol
        ws = WS([[(wsrc[:, g * ncol:(g + 1) * ncol], KC)] for g in range(ngrp)], extra)
        bssq = k.reserve()
        bssqs = k.reserve() if in_s else None
        for g in range(ngrp):
            (wv,) = ws.get(g)
            for j in range(ncol // 128):
                c = g * (ncol // 128) + j
                b = k.bank()
                for kc in range(KC):
                    k.mm(b, wv[:, kc, j * 128:(j + 1) * 128], in_p[kc], start=(kc == 0), stop=(kc == KC - 1))
                k.copy("act", pout[c], b)
                k.act(sq[c % 2], pout[c], AF.Square)
                k.mm(bssq, onesb, sq[c % 2], start=(c == 0), stop=(c == 15))
                if in_s:
                    b2 = k.bank()
                    for kc in range(KC):
                        k.mm(b2[:, 0:NS], wv[:, kc, j * 128:(j + 1) * 128], in_s[kc], start=(kc == 0), stop=(kc == KC - 1))
                    k.copy("dve", pouts[c], b2[:, 0:NS])
                    k.act(sqs, pouts[c], AF.Square)
                    k.mm(bssqs[:, 0:NS], onesb, sqs, start=(c == 0), stop=(c == 15))
        rstd = k.alloc([128, TS], F32, "prstd")
        k.act(rstd, bssq, AF.Ln, bias=EPS, scale=1.0 / D)
        k.act(rstd, rstd, AF.Exp, scale=-0.5)
        k.unreserve(bssq)
        if in_s:
            rstds = k.alloc([128, NS], F32, "prstds")
            k.act(rstds, bssqs[:, 0:NS], AF.Ln, bias=EPS, scale=1.0 / D)
            k.act(rstds, rstds, AF.Exp, scale=-0.5)
            k.unreserve(bssqs)
        xt = [k.alloc([128, TS], F32, "pxt") for _ in range(2)]
        for c in range(16):
            x_ = xt[c % 2]
            k.dma(x_, xres_d[c])
            k.tt("dve", pout[c], pout[c], rstd, ALU.mult)
            k.stt(x_, pout[c], modv(layer, jgate, c, False), x_, ALU.mult, ALU.add)
            k.dma(xres_d[c], x_)
            if in_s:
                k.tt("dve", pouts[c], pouts[c], rstds, ALU.mult)
                k.tt("dve", pouts[c], pouts[c], modv(layer, jgate, c, True), ALU.mult)
                k.tt("dve", xsT[c], xsT[c], pouts[c], ALU.add)
        k.release(m)

    def rwkv_layer(seg, do_s, last):
        L = 0
        mtop = k.mark()
        mixes = {j: [k.alloc([128, TS], BF16, f"mx{j}_{c}") for c in range(16)] for j in (0, 2, 3)}
        yg = [k.alloc([128, TS], BF16, f"yg{c}") for c in range(16)]
        tw = k.alloc([128, TS], BF16, "tw")
        la = k.alloc([128, TS], BF16, "la")
        sg = [k.alloc([128, TS], BF16, f"sg{i}") for i in range(2)]
        if do_s:
            smix = {j: [k.alloc([128, NS], BF16, f"smx{j}_{c}") for c in range(16)] for j in range(6)}
            ygs = [k.alloc([128, NS], BF16, f"ygs{c}") for c in range(16)]
            tws = k.alloc([128, NS], BF16, "tws")
            las = k.alloc([128, NS], BF16, "las")
            sgs = [k.alloc([128, NS], BF16, f"sgs{i}") for i in range(2)]
            gs = k.alloc([128, 16, NS], F32, "gs")
        m1 = k.mark()
        x = [k.alloc([128, TS], F32, f"x{c}") for c in range(16)]
        mxin = k.mark()
        xin = [k.alloc([128, D], F32, "xin") for _ in range(2)]
        for tt_ in range(4):
            xi = xin[tt_ % 2]
            k.dma(xi, xp_d[seg * TS + tt_ * 128: seg * TS + (tt_ + 1) * 128, :])
            for c4 in range(4):
                if os.environ.get("KVAR") == "notr":
                    continue
                b = k.bank()
                for q in range(4):
                    c = c4 * 4 + q
                    k.tr(b[:, q * 128:(q + 1) * 128], xi[:, c * 128:(c + 1) * 128], identf)
                for q in range(4):
                    c = c4 * 4 + q
                    ev(x[c][:, tt_ * 128:(tt_ + 1) * 128], b[:, q * 128:(q + 1) * 128])
        if os.environ.get("KVAR") != "nostore":
            for c in range(16):
                k.dma(xres_d[c], x[c])
        k.release(mxin)
        ckpt(11)
        rstd = rstd_of(x, TS)
        ckpt(12)
        wsl = WS([[(W["rw_w1"], 16), (W["rw_a1"], 16)], [(W["rw_g1"], 16)]])
        w1v, a1v = wsl.get(0)
        (g1v,) = wsl.get(1)
        bw, ba, bg0, bg1 = k.reserve(), k.reserve(), k.reserve(), k.reserve()
        hc = [k.alloc([128, 1 + TS], F32, "hc") for _ in range(2)]
        xx = k.alloc([128, TS], F32, "xx")
        tmpn = k.alloc([128, TS], F32, "tmpn")
        tb = [k.alloc([128, TS], BF16, "tb") for _ in range(3)]
        for c in range(16):
            h_ = hc[c % 2]
            k.copy("dve", h_[:, 0:1], shift_car[:, c:c + 1])
            adaln_chunk(h_[:, 1:1 + TS], tmpn, x[c], rstd, modv(L, 1, c, False), modv(L, 0, c, False), False)
            k.copy("dve", shift_car[:, c:c + 1], h_[:, TS:TS + 1])
            k.tt("dve", xx, h_[:, 0:TS], h_[:, 1:1 + TS], ALU.subtract)
            ti = 0
            for j in range(6):
                if j in (0, 2, 3):
                    dst = mixes[j][c]
                else:
                    dst = tb[ti]
                    ti += 1
                k.stt(dst, xx, P(f"mix{j}", c), h_[:, 1:1 + TS], ALU.mult, ALU.add)
                if j == 1:
                    k.mm(bw[0:96, :], w1v[:, c, :], dst, start=(c == 0), stop=(c == 15))
                elif j == 4:
                    k.mm(ba, a1v[:, c, :], dst, start=(c == 0), stop=(c == 15))
                elif j == 5:
                    k.mm(bg0, g1v[:, c, 0:128], dst, start=(c == 0), stop=(c == 15))
                    k.mm(bg1, g1v[:, c, 128:256], dst, start=(c == 0), stop=(c == 15))
        k.act(tw[0:96, :], bw[0:96, :], AF.Tanh)
        k.copy("dve", la, ba)
        k.act(sg[0], bg0, AF.Sigmoid)
        k.act(sg[1], bg1, AF.Sigmoid)
        for b_ in (bw, ba, bg0, bg1):
            k.unreserve(b_)
        ckpt(13)
        if do_s:
            rstds = rstd_of(xsT, NS)
            hs = k.alloc([128, NS], F32, "hs")
            hp = k.alloc([128, NS], F32, "hp")
            xxs = k.alloc([128, NS], F32, "xxs")
            tmps = k.alloc([128, NS], F32, "tmps")
            for c in range(16):
                adaln_chunk(hs, tmps, xsT[c], rstds, modv(L, 1, c, True), modv(L, 0, c, True), True)
                tok2fm(hp, stsh_d[:, c * 128:(c + 1) * 128], NS)
                fm2tok(sshift_d[:, c * 128:(c + 1) * 128], hs, NS)
                k.tt("dve", xxs, hp, hs, ALU.subtract)
                for j in range(6):
                    k.stt(smix[j][c], xxs, P(f"mix{j}", c), hs, ALU.mult, ALU.add)
            b = k.bank()
            for c in range(16):
                k.mm(b[0:96, 0:16], w1v[:, c, :], smix[1][c], start=(c == 0), stop=(c == 15))
            for c in range(16):
                k.mm(b[:, 16:32], a1v[:, c, :], smix[4][c], start=(c == 0), stop=(c == 15))
            for c in range(16):
                k.mm(b[:, 32:48], g1v[:, c, 0:128], smix[5][c], start=(c == 0), stop=(c == 15))
            for c in range(16):
                k.mm(b[:, 48:64], g1v[:, c, 128:256], smix[5][c], start=(c == 0), stop=(c == 15))
            k.act(tws[0:96, :], b[0:96, 0:16], AF.Tanh)
            k.copy("dve", las, b[:, 16:32])
            k.act(sgs[0], b[:, 32:48], AF.Sigmoid)
            k.act(sgs[1], b[:, 48:64], AF.Sigmoid)
        k.release(m1)
        ckpt(2)

        m2 = k.mark()
        l2t = [k.alloc([128, 4, 128], BF16, f"lora2_{i}") for i in range(2)]
        A = lambda n, dt=F32, w=TS: k.alloc([128, w], dt, n)
        at_f, bt_f, kt_f, rt_f = [A(n, BF16) for n in ("at_f", "bt_f", "kt_f", "rt_f")]
        gTb = A("gTb", BF16)
        at_b, bt_b, rt_b = [k.alloc([128, 4, 2, 128], BF16, n) for n in ("at_b", "bt_b", "rt_b")]
        for t_ in (at_b, bt_b, rt_b):
            k.memset("pool", t_, 0.0)
        v32 = k.alloc([128, 4, 128], F32, "v32")
        vb = k.alloc([128, 4, 128], BF16, "vb")
        s_tok = A("s_tok", F32, 8)
        DL = A("DL", F32, 4)
        Tblk = k.alloc([128, 2, 64], BF16, "Tblk")
        k.memset("pool", Tblk, 0.0)
        Gbc = A("Gbc", F32, 128)
        Bbc = A("Bbc", F32, 128)
        if do_s:
            Q5 = k.alloc([128, 5, NS], F32, "Q5")
            sA = [A(f"sA{i}", F32, NS) for i in range(6)]
            sqs_ = A("sqs_", BF16, NS)
            q4t = k.alloc([NS, 4, 128], F32, "q4t")
            q2t = k.alloc([NS, 2, 128], F32, "q2t")
        ws = WS([[(W["rw_wr"][:, m * 128:(m + 1) * 128], 16), (W["rw_wk"][:, m * 128:(m + 1) * 128], 16),
                  (W["rw_wv"][:, m * 128:(m + 1) * 128], 16)] for m in range(16)])
        pre = {}

        def prefetch(m):
            mc = slice(m * 128, (m + 1) * 128)
            wr, wk, wv = ws.get(m)
            l2 = l2t[m % 2]
            k.dma(l2[0:96, 0, :], W["rw_w2"][:, mc], eng="pool")
            k.dma(l2[:, 1, :], W["rw_a2"][:, mc], eng="pool")
            k.dma(l2[:, 2, :], W["rw_g2"][0:128, mc], eng="pool")
            k.dma(l2[:, 3, :], W["rw_g2"][128:256, mc], eng="pool")
            br, bk, bv = k.reserve(), k.reserve(), k.reserve()
            for kc in range(16):
                k.mm(br, wr[:, kc, :], mixes[0][kc], start=(kc == 0), stop=(kc == 15))
            for kc in range(16):
                k.mm(bk, wk[:, kc, :], mixes[2][kc], start=(kc == 0), stop=(kc == 15))
            for t4 in range(4):
                for kc in range(16):
                    k.mm(bv[:, t4 * 128:(t4 + 1) * 128], mixes[3][kc][:, t4 * 128:(t4 + 1) * 128], wv[:, kc, :],
                         start=(kc == 0), stop=(kc == 15))
            pre[m] = (wr, wk, wv, l2, br, bk, bv)
        prefetch(0)
        for m in range(16):
            mc = slice(m * 128, (m + 1) * 128)
            wr, wk, wv, l2, br, bk, bv = pre.pop(m)
            k.dma(Gbc, V(rows_d.ap[0:1, mc].partition_broadcast(128), rows_d.b))
            k.dma(Bbc, V(rows_d.ap[1:2, mc].partition_broadcast(128), rows_d.b))
            mE = k.mark()
            r32, k32, ag, sgw, kk, cum, e_pos, e_neg, e_prev, tA, tB = [A(n) for n in
                ("r32", "k32", "ag", "sgw", "kk", "cum", "e_pos", "e_neg", "e_prev", "tA", "tB")]
            sqb, rkr = A("sqb", BF16), A("rkr", BF16)
            k.copy("act", r32, br)
            k.copy("act", k32, bk)
            k.copy("dve", v32.re("p a b -> p (a b)"), bv)
            k.copy("dve", vb.re("p a b -> p (a b)"), bv)
            for b_ in (br, bk, bv):
                k.unreserve(b_)
            b = k.bank()
            k.mm(b, l2[0:96, 0, :], tw[0:96, :])
            k.act(sgw, b, AF.Sigmoid, bias=P("w0", m))
            b = k.bank()
            k.mm(b, l2[:, 1, :], la)
            k.act(ag, b, AF.Sigmoid, bias=P("a0", m))
            b = k.bank()
            k.mm(b, l2[:, 2, :], sg[0], start=True, stop=False)
            k.mm(b, l2[:, 3, :], sg[1], start=False, stop=True)
            k.copy("act", gTb, b)
            k.ts("dve", kk, k32, P("kk", m), None, ALU.mult)
            k.act(sqb, kk, AF.Square)
            b = k.bank()
            k.mm(b, blkdiag, sqb)
            k.act(tA, b, AF.Ln)
            k.act(tA, tA, AF.Exp, scale=-0.5)
            k.tt("dve", kk, kk, tA, ALU.mult)
            k.ts("pool", tB, ag, P("ka", m), P("omka", m), ALU.mult, ALU.add)
            k.tt("pool", k32, k32, tB, ALU.mult)
            k.stt(rkr, r32, P("rk", m), k32, ALU.mult, ALU.mult)
            b = k.bank()
            for t4 in range(4):
                k.mm(b[:, t4 * 2:(t4 + 1) * 2], rkr[:, t4 * 128:(t4 + 1) * 128], blk2)
            k.copy("act", s_tok, b[:, 0:8])
            for cc in range(4):
                sl = slice(cc * 128, (cc + 1) * 128)
                k.op("dve", lambda e, sl=sl: e.tensor_tensor_scan(out=cum.ap[:, sl], data0=ones_f.ap, data1=sgw.ap[:, sl],
                                                                  initial=0.0, op0=ALU.mult, op1=ALU.add),
                     [ones_f, sgw], [cum])
            k.act(e_pos, cum, AF.Exp, scale=DEC)
            k.act(e_neg, cum, AF.Exp, scale=-DEC)
            k.tt("pool", tA, cum, sgw, ALU.subtract)
            k.act(e_prev, tA, AF.Exp, scale=DEC)
            k.copy("act", DL, e_pos.re("p (c t) -> p c t", t=128)[:, :, 127])
            k.tt("dve", rt_f, r32, e_pos, ALU.mult)
            k.tt("pool", kt_f, k32, e_neg, ALU.mult)
            k.tt("pool", tB, kk, ag, ALU.mult)
            k.tt("dve", bt_f, tB, e_neg, ALU.mult)
            k.stt(at_f, kk, -1.0, e_prev, ALU.mult, ALU.mult)
            for (Xb_, Xf_) in ((at_b, at_f), (bt_b, bt_f), (rt_b, rt_f)):
                for h in range(2):
                    k.copy("act" if h else "pool", Xb_[64 * h:64 * h + 64, :, h, :],
                           Xf_[64 * h:64 * h + 64, :].re("p (c t) -> p c t", t=128))
            if do_s:
                r_s, k_s, sgw_s, ag_s, kk_s, t_s = sA
                b = k.bank()
                for kc in range(16):
                    k.mm(b[:, 0:16], wr[:, kc, :], smix[0][kc], start=(kc == 0), stop=(kc == 15))
                for kc in range(16):
                    k.mm(b[:, 16:32], wk[:, kc, :], smix[2][kc], start=(kc == 0), stop=(kc == 15))
                k.mm(b[:, 32:48], l2[0:96, 0, :], tws[0:96, :])
                k.mm(b[:, 48:64], l2[:, 1, :], las)
                k.mm(b[:, 64:80], l2[:, 2, :], sgs[0], start=True, stop=False)
                k.mm(b[:, 64:80], l2[:, 3, :], sgs[1], start=False, stop=True)
                bv_ = k.bank()
                for kc in range(16):
                    k.mm(bv_[0:NS, 0:128], smix[3][kc], wv[:, kc, :], start=(kc == 0), stop=(kc == 15))
                k.copy("act", Q5[:, 0, :], b[:, 0:16])
                k.copy("act", k_s, b[:, 16:32])
                k.act(sgw_s, b[:, 32:48], AF.Sigmoid, bias=P("w0", m))
                k.act(ag_s, b[:, 48:64], AF.Sigmoid, bias=P("a0", m))
                k.copy("act", gs[:, m, :], b[:, 64:80])
                k.ts("dve", kk_s, k_s, P("kk", m), None, ALU.mult)
                k.act(sqs_, kk_s, AF.Square)
                b2 = k.bank()
                k.mm(b2[:, 0:NS], blkdiag, sqs_)
                k.act(t_s, b2[:, 0:NS], AF.Ln)
                k.act(t_s, t_s, AF.Exp, scale=-0.5)
                k.tt("dve", kk_s, kk_s, t_s, ALU.mult)
                k.act(Q5[:, 1, :], sgw_s, AF.Exp, scale=DEC)
                k.ts("dve", t_s, ag_s, P("ka", m), P("omka", m), ALU.mult, ALU.add)
                k.tt("dve", Q5[:, 2, :], k_s, t_s, ALU.mult)
                k.ts("dve", Q5[:, 3, :], kk_s, -1.0, None, ALU.mult)
                k.tt("dve", Q5[:, 4, :], kk_s, ag_s, ALU.mult)
                bq = k.bank()
                for q in range(4):
                    k.tr(bq[0:NS, q * 128:(q + 1) * 128], Q5[:, q, :], identf)
                k.tr(bv_[0:NS, 128:256], Q5[:, 4, :], identf)
                k.copy("act", q4t.re("p a b -> p (a b)"), bq[0:NS, 0:512])
                k.copy("act", q2t.re("p a b -> p (a b)"), bv_[0:NS, 0:256])
                k.dma(scr6_d[0:4, :, mc].re("q r c -> r q c"), q4t)
                k.dma(scr6_d[4:6, :, mc].re("q r c -> r q c"), q2t)
            k.release(mE)
            mCh = k.mark()
            C4 = range(4)
            tok3 = [A("tok3", BF16, 384) for _ in C4]
            XA = [[A("XA", BF16, 512) for _ in range(2)] for _ in C4]
            AA = [A("AA", BF16, 512) for _ in C4]
            RQ = [A("RQ", BF16, 512) for _ in C4]
            XB = [A("XB", BF16, 512) for _ in C4]
            XC = [A("XC", BF16, 256) for _ in C4]
            Z32 = [A("Z32", F32, 256) for _ in C4]
            Zb = [A("Zb", BF16, 256) for _ in C4]
            AwT = [A("AwT", BF16, 128) for _ in C4]
            Ub = A("Ub", BF16, 128)
            tmpT = A("tmpT", F32, 64)
            y32 = [A("y32", F32, 128) for _ in C4]
            o1 = [A("o1", F32, 128) for _ in C4]
            st6L = [A("st6", F32, 12) for _ in C4]
            mvL = [A("mv", F32, 4) for _ in C4]
            rs2L = [A("rs2", F32, 2) for _ in C4]
            pwk = A("pwk", F32, 128)
            SL = [slice(cc * 128, (cc + 1) * 128) for cc in C4]
            atb = [at_b[:, cc].re("p a b -> p (a b)") for cc in C4]
            btb = [bt_b[:, cc].re("p a b -> p (a b)") for cc in C4]
            rtb = [rt_b[:, cc].re("p a b -> p (a b)") for cc in C4]
            for cc in C4:
                PBk = k.bank()
                PB = V(PBk.ap.bitcast(BF16), PBk.b)
                for i_, X_ in enumerate((at_f, bt_f, kt_f)):
                    k.tr(PB[:, i_ * 128:(i_ + 1) * 128], X_[:, SL[cc]], identb)
                k.copy("act", tok3[cc], PB[:, 0:384])
            for cc in C4:
                bA = k.bank()
                k.mm(bA[:, 0:256], at_f[:, SL[cc]], btb[cc])
                k.mm(bA[:, 256:512], bt_f[:, SL[cc]], atb[cc])
                k.tt("dve", AA[cc], bA, mA, ALU.mult)
            for cc in C4:
                bB = k.bank()
                k.mm(bB[:, 0:256], kt_f[:, SL[cc]], atb[cc])
                k.mm(bB[:, 256:512], bt_f[:, SL[cc]], rtb[cc])
                k.tt("dve", XB[cc], bB, mB, ALU.mult)
            for cc in C4:
                bC = k.bank()
                k.mm(bC[:, 0:256], kt_f[:, SL[cc]], rtb[cc])
                k.tt("dve", XC[cc], bC[:, 0:256], mC, ALU.mult)
            for cc in C4:
                bZ = k.bank()
                for h in range(2):
                    k.mm(bZ[:, h * 64:(h + 1) * 64], XB[cc][:, h * 128:(h + 1) * 128], vb[:, cc, h * 64:(h + 1) * 64])
                Z32v = Z32[cc].re("p (h a v) -> p h a v", h=2, a=2)
                k.copy("act", Z32v[:, :, 0, :], tok3[cc][:, 0:128].re("p (h v) -> p h v", v=64))
                k.copy("act", Z32v[:, :, 1, :], bZ[:, 0:128].re("p (h v) -> p h v", v=64))
                k.copy("pool", Zb[cc], Z32[cc])
            cur = 0
            for cc in C4:
                k.tt("pool", RQ[cc], AA[cc], ML[:, 0, :], ALU.mult)
                k.tt("pool", XA[cc][0], RQ[cc], I4, ALU.add)
            for l_ in range(1, 7):
                b1 = []
                for cc in C4:
                    Wc = XA[cc][cur]
                    b1_ = k.bank()
                    b1.append(b1_)
                    for h in range(2):
                        k.mm(b1_[:, h * 128:(h + 1) * 128], AA[cc][:, 256 + h * 128:256 + (h + 1) * 128], Wc[:, h * 128:(h + 1) * 128])
                    for h in range(2):
                        k.mm(b1_[:, 256 + h * 128:256 + (h + 1) * 128], AA[cc][:, h * 128:(h + 1) * 128], Wc[:, 256 + h * 128:256 + (h + 1) * 128])
                for cc in C4:
                    k.tt("dve", RQ[cc], b1[cc], ML[:, l_, :], ALU.mult)
                b2 = []
                for cc in C4:
                    Wc = XA[cc][cur]
                    b2_ = k.bank()
                    b2.append(b2_)
                    for h in range(2):
                        o_ = b2_[:, h * 128:(h + 1) * 128]
                        k.mm(o_, Wc[:, 256 + h * 128:256 + (h + 1) * 128], RQ[cc][:, h * 128:(h + 1) * 128], start=True, stop=False)
                        k.mm(o_, identb, Wc[:, h * 128:(h + 1) * 128], start=False, stop=True)
                    for h in range(2):
                        o_ = b2_[:, 256 + h * 128:256 + (h + 1) * 128]
                        k.mm(o_, Wc[:, h * 128:(h + 1) * 128], RQ[cc][:, 256 + h * 128:256 + (h + 1) * 128], start=True, stop=False)
                        k.mm(o_, identb, Wc[:, 256 + h * 128:256 + (h + 1) * 128], start=False, stop=True)
                for cc in C4:
                    k.copy("act", XA[cc][1 - cur], b2[cc])
                cur = 1 - cur
            for cc in C4:
                Wc = XA[cc][cur]
                bz = k.bank()
                for h in range(2):
                    k.mm(bz[:, h * 128:(h + 1) * 128], Wc[:, 256 + h * 128:256 + (h + 1) * 128], Zb[cc][:, h * 128:(h + 1) * 128])
                k.copy("dve", Z32[cc], bz[:, 0:256])
                k.copy("act", Zb[cc], bz[:, 0:256])
            for cc in C4:
                PBk = k.bank()
                PB = V(PBk.ap.bitcast(BF16), PBk.b)
                for h in range(2):
                    k.tr(PB[64 * h:64 * h + 64, 0:128], Zb[cc][:, h * 128:h * 128 + 64], identb)
                k.copy("act", AwT[cc], PB[:, 0:128])
            if m + 1 < 16:
                prefetch(m + 1)
            for h in range(2):
                k.copy("dve", Tblk[64 * h:64 * h + 64, h, :], T32[64 * h:64 * h + 64, m, :])
            Tflat = Tblk.re("p a b -> p (a b)")
            for cc in C4:
                sl = SL[cc]
                b_tok, k_tok = tok3[cc][:, 128:256], tok3[cc][:, 256:384]
                Z32v = Z32[cc].re("p (h a v) -> p h a v", h=2, a=2)
                bU = k.bank()
                k.mm(bU[:, 0:128], AwT[cc], Tflat)
                k.tt("dve", Ub.re("p (h v) -> p h v", v=64), bU[:, 0:128].re("p (h v) -> p h v", v=64), Z32v[:, :, 1, :], ALU.add)
                bT = k.bank()
                for h in range(2):
                    hs_ = slice(h * 64, (h + 1) * 64)
                    k.mm(bT[64 * h:64 * h + 64, 0:64], b_tok[:, hs_], Ub[:, hs_], start=True, stop=False)
                    k.mm(bT[64 * h:64 * h + 64, 0:64], k_tok[:, hs_], vb[:, cc, hs_], start=False, stop=True)
                bY = k.bank()
                k.mm(bY[:, 0:128], rt_f[:, sl], Tflat, start=True, stop=False)
                for h in range(2):
                    hs_ = slice(h * 64, (h + 1) * 64)
                    k.mm(bY[:, hs_], XB[cc][:, 256 + h * 128:256 + (h + 1) * 128], Ub[:, hs_], start=False, stop=False)
                    k.mm(bY[:, hs_], XC[cc][:, h * 128:(h + 1) * 128], vb[:, cc, hs_], start=False, stop=(h == 1))
                k.tt("dve", tmpT, bT[:, 0:64], T32[:, m, :], ALU.add)
                for h in range(2):
                    k.ts("dve", Tblk[64 * h:64 * h + 64, h, :], tmpT[64 * h:64 * h + 64, :], DL[64 * h:64 * h + 64, cc:cc + 1], None, ALU.mult)
                k.act(T32[:, m, :], tmpT, AF.Identity, scale=DL[:, cc:cc + 1])
                k.copy("act", y32[cc], bY[:, 0:128])
            for cc in C4:
                y_, st6, mv = y32[cc], st6L[cc], mvL[cc]
                for h in range(2):
                    hs_ = slice(h * 64, (h + 1) * 64)
                    k.op("dve", lambda e, h=h, hs_=hs_, y_=y_, st6=st6: e.bn_stats(out=st6.ap[:, h * 6:(h + 1) * 6], in_=y_.ap[:, hs_]),
                         [y_], [st6])
                    k.op("dve", lambda e, h=h, st6=st6, mv=mv: e.bn_aggr(out=mv.ap[:, 2 * h:2 * h + 2], in_=st6.ap[:, h * 6:(h + 1) * 6]),
                         [st6], [mv])
            for cc in C4:
                mvv = mvL[cc].re("p (h t) -> p h t", t=2)
                k.act(rs2L[cc], mvv[:, :, 1], AF.Ln, bias=LNX_EPS)
            for cc in C4:
                k.act(rs2L[cc], rs2L[cc], AF.Exp, scale=-0.5)
            for cc in C4:
                for h in range(2):
                    hs_ = slice(h * 64, (h + 1) * 64)
                    k.ts("pool", o1[cc][:, hs_], y32[cc][:, hs_], mvL[cc][:, 2 * h:2 * h + 1], rs2L[cc][:, h:h + 1], ALU.subtract, ALU.mult)
                k.tt("pool", o1[cc], o1[cc], Gbc, ALU.mult)
                k.tt("pool", o1[cc], o1[cc], Bbc, ALU.add)
            for cc in C4:
                for h in range(2):
                    hs_ = slice(h * 64, (h + 1) * 64)
                    k.stt(o1[cc][:, hs_], v32[:, cc, hs_], s_tok[:, cc * 2 + h:cc * 2 + h + 1], o1[cc][:, hs_], ALU.mult, ALU.add)
            bOs = []
            for cc in C4:
                bO = k.bank()
                bOs.append(bO)
                k.tr(bO[:, 0:128], o1[cc], identf)
            for cc in C4:
                k.tt("dve", yg[m][:, SL[cc]], bOs[cc][:, 0:128], gTb[:, SL[cc]], ALU.mult)
            if last:
                b = k.bank()
                k.tr(b[0:64, 0:128], T32[:, m, :], identf)
                k.copy("act", pwk[0:64, :], b[0:64, 0:128])
                k.dma(pwkv_d[2 * m:2 * m + 2].re("h v k -> v h k"), pwk[0:64, :].re("p (h k) -> p h k", k=64))
            k.release(mCh)
        k.release(m2)
        ckpt(3)

        if do_s:
            m3 = k.mark()
            Grh = k.alloc([128, 64], F32, "Grh")
            Brh = k.alloc([128, 64], F32, "Brh")
            RKrh = k.alloc([128, 64], F32, "RKrh")
            for r4 in range(4):
                k.dma(Grh[32 * r4:32 * r4 + 32, :], rows_d[0].re("(h v) -> h v", v=64))
                k.dma(Brh[32 * r4:32 * r4 + 32, :], rows_d[1].re("(h v) -> h v", v=64))
                k.dma(RKrh[32 * r4:32 * r4 + 32, :], rows_d[2].re("(h v) -> h v", v=64))
            q6 = k.alloc([128, 6, 64], F32, "q6")
            S = k.alloc([128, 32, 64], F32, "S")
            tmp = k.alloc([128, 32, 64], F32, "Stmp")
            sa = k.alloc([128, 32], F32, "sa")
            yrh = k.alloc([128, 64], F32, "yrh")
            orh = k.alloc([128, 64], F32, "orh")
            t64 = k.alloc([128, 64], F32, "t64")
            s1 = k.alloc([128, 1], F32, "s1")
            st6b = k.alloc([128, 6], F32, "st6b")
            mvb = k.alloc([128, 2], F32, "mvb")
            rsb = k.alloc([128, 1], F32, "rsb")
            B3 = [128, 32, 64]
            for i in range(4):
                k.dma(q6, scr6_d[:, 4 * i:4 * i + 4, :].re("q r (h k) -> (r h) q k", k=64))
                r_, d_, k_, a_, v_, b_ = [q6[:, q, :] for q in range(6)]
                for vh in range(2):
                    vs = slice(vh * 32, (vh + 1) * 32)
                    k.dma(S, stw_d[i * 128:(i + 1) * 128, vh * 2048:(vh + 1) * 2048].re("p (v k) -> p v k", k=64))
                    k.tt("dve", tmp, S, a_.us(1).bc(B3), ALU.mult)
                    k.op("dve", lambda e: e.tensor_reduce(out=sa.ap, in_=tmp.ap, axis=AX.X, op=ALU.add), [tmp], [sa])
                    k.tt("dve", S, S, d_.us(1).bc(B3), ALU.mult)
                    k.tt("dve", tmp, sa.us(2).bc(B3), b_.us(1).bc(B3), ALU.mult)
                    k.tt("dve", S, S, tmp, ALU.add)
                    k.tt("dve", tmp, v_[:, vs].us(2).bc(B3), k_.us(1).bc(B3), ALU.mult)
                    k.tt("dve", S, S, tmp, ALU.add)
                    k.dma(swkv_d[i * 128:(i + 1) * 128, vh * 2048:(vh + 1) * 2048].re("p (v k) -> p v k", k=64), S)
                    k.tt("dve", tmp, S, r_.us(1).bc(B3), ALU.mult)
                    k.op("dve", lambda e, vs=vs: e.tensor_reduce(out=yrh.ap[:, vs], in_=tmp.ap, axis=AX.X, op=ALU.add),
                         [tmp], [yrh])
                k.op("dve", lambda e: e.bn_stats(out=st6b.ap, in_=yrh.ap), [yrh], [st6b])
                k.op("dve", lambda e: e.bn_aggr(out=mvb.ap, in_=st6b.ap), [st6b], [mvb])
                k.act(rsb, mvb[:, 1:2], AF.Ln, bias=LNX_EPS)
                k.act(rsb, rsb, AF.Exp, scale=-0.5)
                k.ts("dve", orh, yrh, mvb[:, 0:1], rsb, ALU.subtract, ALU.mult)
                k.tt("dve", orh, orh, Grh, ALU.mult)
                k.tt("dve", orh, orh, Brh, ALU.add)
                k.tt("dve", t64, r_, k_, ALU.mult)
                k.tt("dve", t64, t64, RKrh, ALU.mult)
                k.op("dve", lambda e: e.tensor_reduce(out=s1.ap, in_=t64.ap, axis=AX.X, op=ALU.add), [t64], [s1])
                k.stt(orh, v_, s1, orh, ALU.mult, ALU.add)
                k.dma(scro_d[4 * i:4 * i + 4, :].re("r (h v) -> (r h) v", v=64), orh)
            of = k.alloc([128, NS], F32, "of")
            for m in range(16):
                tok2fm(of, scro_d[:, m * 128:(m + 1) * 128], NS)
                k.tt("dve", ygs[m], of, gs[:, m, :], ALU.mult)
            k.release(m3)

        ckpt(4)
        proj_post(L, W["rw_wo"], 16, yg, ygs if do_s else None, 2, 256)
        ckpt(5)
        k.release(mtop)

    def ffn_layer(L, seg, do_s, last):
        mtop = k.mark()
        xw = [k.alloc([128, WB_ELEMS], BF16, f"xwb{i}") for i in range(2)]
        z = [k.alloc([128, TS], BF16, f"z{f}") for f in range(NF)]
        zs = [k.alloc([128, NS], BF16, f"zs{f}") for f in range(NF)] if do_s else None
        m1 = k.mark()
        h2 = [k.alloc([128, TS], BF16, f"h2_{c}") for c in range(16)]
        h2s = [k.alloc([128, NS], BF16, f"h2s_{c}") for c in range(16)] if do_s else None
        m2 = k.mark()
        x = load_x()
        rstd = rstd_of(x, TS)
        tmpn = k.alloc([128, TS], F32, "tmpn")
        for c in range(16):
            adaln_chunk(h2[c], tmpn, x[c], rstd, modv(L, 4, c, False), modv(L, 3, c, False), False)
        if do_s:
            rstds = rstd_of(xsT, NS)
            tmps = k.alloc([128, NS], F32, "tmps")
            for c in range(16):
                adaln_chunk(h2s[c], tmps, xsT[c], rstds, modv(L, 4, c, True), modv(L, 3, c, True), True)
        k.release(m2)
        gt = [k.alloc([128, 2 + TS], F32, "gt") for _ in range(2)]
        t1 = [k.alloc([128, TS], F32, "t1") for _ in range(2)]
        if do_s:
            cs = k.alloc([128, 32], F32, "cs")
            gts = k.alloc([128, NS], F32, "gts")
            t1s = k.alloc([128, NS], F32, "t1s")
            o2 = k.alloc([128, NS, 2], F32, "o2")
            tk = k.alloc([32, 128], F32, "tk")
            tk2 = k.alloc([32, 128], F32, "tk2")
        reqs = []
        for g in range(22):
            reqs.append([(W["ffn_w_gate"][L][:, g * 256:(g + 1) * 256], 16)])
            reqs.append([(W["ffn_w_up"][L][:, g * 256:(g + 1) * 256], 16)])
        ws = WS(reqs, xw)
        for g in range(22):
            (wg,) = ws.get(2 * g)
            (wu,) = ws.get(2 * g + 1)
            for j in range(2):
                f = 2 * g + j
                fc = slice(f * 128, (f + 1) * 128)
                js = slice(j * 128, (j + 1) * 128)
                bg = k.bank()
                for kc in range(16):
                    k.mm(bg, wg[:, kc, js], h2[kc], start=(kc == 0), stop=(kc == 15))
                bu = k.bank()
                for kc in range(16):
                    k.mm(bu, wu[:, kc, js], h2[kc], start=(kc == 0), stop=(kc == 15))
                g_ = gt[f % 2]
                t_ = t1[f % 2]
                k.copy("dve", g_[:, 0:2], fcar[L][:, :, f])
                k.copy("act", g_[:, 2:2 + TS], bg)
                k.copy("dve", fcar[L][:, :, f], g_[:, TS:TS + 2])
                k.ts("dve", t_, g_[:, 0:TS], P(f"fcw{L}0", f), P(f"fcb{L}", f), ALU.mult, ALU.add)
                k.stt(t_, g_[:, 1:1 + TS], P(f"fcw{L}1", f), t_, ALU.mult, ALU.add)
                k.stt(t_, g_[:, 2:2 + TS], P(f"fcw{L}2", f), t_, ALU.mult, ALU.add)
                k.act(t_, t_, AF.Gelu_apprx_tanh)
                k.tt("dve", z[f], bu, t_, ALU.mult)
                if do_s:
                    bs_ = k.bank()
                    for kc in range(16):
                        k.mm(bs_[:, 0:16], wg[:, kc, js], h2s[kc], start=(kc == 0), stop=(kc == 15))
                    for kc in range(16):
                        k.mm(bs_[:, 16:32], wu[:, kc, js], h2s[kc], start=(kc == 0), stop=(kc == 15))
                    k.dma(tk, stfc_d[L][:, :, fc].re("r j c -> (r j) c"))
                    b3 = k.bank()
                    k.tr(b3[:, 0:32], tk, identf[0:32, 0:32])
                    k.copy("act", cs, b3[:, 0:32])
                    csv = cs.re("p (r j) -> p r j", j=2)
                    k.copy("dve", gts, bs_[:, 0:16])
                    k.ts("dve", t1s, csv[:, :, 0], P(f"fcw{L}0", f), P(f"fcb{L}", f), ALU.mult, ALU.add)
                    k.stt(t1s, csv[:, :, 1], P(f"fcw{L}1", f), t1s, ALU.mult, ALU.add)
                    k.stt(t1s, gts, P(f"fcw{L}2", f), t1s, ALU.mult, ALU.add)
                    k.act(t1s, t1s, AF.Gelu_apprx_tanh)
                    k.tt("dve", zs[f], bs_[:, 16:32], t1s, ALU.mult)
                    k.copy("dve", o2[:, :, 0], csv[:, :, 1])
                    k.copy("dve", o2[:, :, 1], gts)
                    b4 = k.bank()
                    k.tr(b4[0:32, 0:128], o2.re("p r j -> p (r j)"), identf)
                    k.copy("act", tk2, b4[0:32, 0:128])
                    k.dma(sfc_d[L][:, :, fc].re("r j c -> (r j) c"), tk2)
        if last:
            t88 = k.alloc([88, 128], F32, "t88")
            b = k.bank()
            k.tr(b[0:88, 0:128], fcar[L].re("p j f -> p (j f)"), identf)
            k.copy("act", t88, b[0:88, 0:128])
            for j in range(2):
                k.dma(pfc_d[L][j].re("(f p) -> f p", p=128), t88[j * NF:(j + 1) * NF, :])
        k.release(m1)
        proj_post(L, W["ffn_w_down"][L], NF, z, zs, 5, 128, xw)
        k.release(mtop)

    def lru_layer(seg, do_s, last):
        L = 1
        mtop = k.mark()
        xw = [k.alloc([128, WB_ELEMS], BF16, f"xwb{i}") for i in range(3)]
        og = [k.alloc([128, TS], BF16, f"og{c}") for c in range(16)]
        ogs = [k.alloc([128, NS], BF16, f"ogs{c}") for c in range(16)] if do_s else None
        mL = k.mark()
        h = [k.alloc([128, TS], BF16, f"h_{c}") for c in range(16)]
        hs_l = [k.alloc([128, NS], BF16, f"hs_{c}") for c in range(16)] if do_s else None
        m2 = k.mark()
        x = load_x()
        rstd = rstd_of(x, TS)
        tmpn = k.alloc([128, TS], F32, "tmpn")
        for c in range(16):
            adaln_chunk(h[c], tmpn, x[c], rstd, modv(L, 1, c, False), modv(L, 0, c, False), False)
        if do_s:
            rstds = rstd_of(xsT, NS)
            tmps = k.alloc([128, NS], F32, "tmps")
            for c in range(16):
                adaln_chunk(hs_l[c], tmps, xsT[c], rstds, modv(L, 1, c, True), modv(L, 0, c, True), True)
        k.release(m2)
        wab = k.alloc([128, 8, 2, 256], BF16, "wab")
        wxb = k.alloc([128, 8, 2, 256], BF16, "wxb")
        k.dma(wab, W["lru_wa"].re("n (a p) c -> p n a c", p=128), eng="pool")
        k.dma(wxb, W["lru_wx"].re("n (a p) c -> p n a c", p=128), eng="pool")
        ybr = [k.alloc([128, TS], BF16, f"ybr{j}") for j in range(2)]
        xb32 = [k.alloc([128, 3 + TS], F32, f"xb32{j}") for j in range(2)]
        xc32 = [k.alloc([128, TS], F32, f"xc32{j}") for j in range(2)]
        xcb = [k.alloc([128, TS], BF16, f"xcb{j}") for j in range(2)]
        aa, mu, bt, hsn = [k.alloc([128, TS], F32, n) for n in ("aa", "mu", "bt", "hsn")]
        gA = [k.alloc([128, TS], F32, f"gA{j}") for j in range(2)]
        gX = [k.alloc([128, TS], F32, f"gX{j}") for j in range(2)]
        if do_s:
            ybrs = [k.alloc([128, NS], BF16, f"ybrs{j}") for j in range(2)]
            xbs = [k.alloc([128, NS], F32, f"xbs{j}") for j in range(2)]
            xcs = [k.alloc([128, NS], F32, f"xcs{j}") for j in range(2)]
            xcbs = [k.alloc([128, NS], BF16, f"xcbs{j}") for j in range(2)]
            csl = k.alloc([128, 48], F32, "csl")
            h0s = k.alloc([128, NS], F32, "h0s")
            o3 = k.alloc([128, NS, 3], F32, "o3")
            tk3 = k.alloc([48, 128], F32, "tk3")
            tk4 = k.alloc([48, 128], F32, "tk4")
            aas, mus, bts, hns = [k.alloc([128, NS], F32, n) for n in ("aas", "mus", "bts", "hns")]
            gAs = [k.alloc([128, NS], F32, f"gAs{j}") for j in range(2)]
            gXs = [k.alloc([128, NS], F32, f"gXs{j}") for j in range(2)]
        reqs = []
        for n in range(8):
            reqs.append([(W["lru_w_in"][:, n * 256:(n + 1) * 256], 16)])
            reqs.append([(W["lru_w_in"][:, D + n * 256:D + (n + 1) * 256], 16)])
        ws = WS(reqs, xw)
        for n in range(8):
            (wy,) = ws.get(2 * n)
            (wx_,) = ws.get(2 * n + 1)
            for j in range(2):
                c = 2 * n + j
                js = slice(j * 128, (j + 1) * 128)
                cc_ = slice(c * 128, (c + 1) * 128)
                b = k.bank()
                for kc in range(16):
                    k.mm(b, wy[:, kc, js], h[kc], start=(kc == 0), stop=(kc == 15))
                k.act(ybr[j], b, AF.Gelu_apprx_tanh, bias=P("bin", c))
                b = k.bank()
                for kc in range(16):
                    k.mm(b, wx_[:, kc, js], h[kc], start=(kc == 0), stop=(kc == 15))
                xb_ = xb32[j]
                k.copy("dve", xb_[:, 0:3], lcar[:, :, c])
                k.act(xb_[:, 3:3 + TS], b, AF.Identity, bias=P("bin", 16 + c))
                k.copy("dve", lcar[:, :, c], xb_[:, TS:TS + 3])
                k.ts("dve", xc32[j], xb_[:, 0:TS], P("lcw0", c), P("lcb", c), ALU.mult, ALU.add)
                for q in range(1, 4):
                    k.stt(xc32[j], xb_[:, q:q + TS], P(f"lcw{q}", c), xc32[j], ALU.mult, ALU.add)
                k.copy("act", xcb[j], xc32[j])
                if do_s:
                    b = k.bank()
                    for kc in range(16):
                        k.mm(b[:, 0:16], wy[:, kc, js], hs_l[kc], start=(kc == 0), stop=(kc == 15))
                    for kc in range(16):
                        k.mm(b[:, 16:32], wx_[:, kc, js], hs_l[kc], start=(kc == 0), stop=(kc == 15))
                    k.act(ybrs[j], b[:, 0:16], AF.Gelu_apprx_tanh, bias=P("bin", c))
                    k.act(xbs[j], b[:, 16:32], AF.Identity, bias=P("bin", 16 + c))
                    k.dma(tk3, stlc_d[:, :, cc_].re("r j c -> (r j) c"))
                    b3 = k.bank()
                    k.tr(b3[:, 0:48], tk3, identf[0:48, 0:48])
                    k.copy("act", csl, b3[:, 0:48])
                    cv = csl.re("p (r j) -> p r j", j=3)
                    k.ts("dve", xcs[j], cv[:, :, 0], P("lcw0", c), P("lcb", c), ALU.mult, ALU.add)
                    k.stt(xcs[j], cv[:, :, 1], P("lcw1", c), xcs[j], ALU.mult, ALU.add)
                    k.stt(xcs[j], cv[:, :, 2], P("lcw2", c), xcs[j], ALU.mult, ALU.add)
                    k.stt(xcs[j], xbs[j], P("lcw3", c), xcs[j], ALU.mult, ALU.add)
                    k.copy("act", xcbs[j], xcs[j])
                    k.copy("dve", o3[:, :, 0], cv[:, :, 1])
                    k.copy("dve", o3[:, :, 1], cv[:, :, 2])
                    k.copy("dve", o3[:, :, 2], xbs[j])
                    b4 = k.bank()
                    k.tr(b4[0:48, 0:128], o3.re("p r j -> p (r j)"), identf)
                    k.copy("act", tk4, b4[0:48, 0:128])
                    k.dma(slc_d[:, :, cc_].re("r j c -> (r j) c"), tk4)
            LN_HALF = -0.6931471805599453
            for jo in range(2):
                c = 2 * n + jo
                jos = slice(jo * 128, (jo + 1) * 128)
                ba_ = k.bank()
                for ji in range(2):
                    k.mm(ba_, wab[:, n, ji, jos], xcb[ji], start=(ji == 0), stop=(ji == 1))
                bx_ = k.bank()
                for ji in range(2):
                    k.mm(bx_, wxb[:, n, ji, jos], xcb[ji], start=(ji == 0), stop=(ji == 1))
                k.act(gA[jo], ba_, AF.Tanh, bias=P("hba", c), scale=0.5)
                k.act(gX[jo], bx_, AF.Tanh, bias=P("hbx", c), scale=0.5)
                if do_s:
                    bs_ = k.bank()
                    for ji in range(2):
                        k.mm(bs_[:, 0:16], wab[:, n, ji, jos], xcbs[ji], start=(ji == 0), stop=(ji == 1))
                    for ji in range(2):
                        k.mm(bs_[:, 16:32], wxb[:, n, ji, jos], xcbs[ji], start=(ji == 0), stop=(ji == 1))
                    k.act(gAs[jo], bs_[:, 0:16], AF.Tanh, bias=P("hba", c), scale=0.5)
                    k.act(gXs[jo], bs_[:, 16:32], AF.Tanh, bias=P("hbx", c), scale=0.5)
            for jo in range(2):
                c = 2 * n + jo
                cc_ = slice(c * 128, (c + 1) * 128)
                k.act(aa, gA[jo], AF.Exp, bias=P("hcL", c), scale=P("hcL", c))
                k.tt("dve", mu, aa, aa, ALU.mult)
                k.act(mu, mu, AF.Ln, bias=1.0, scale=-1.0)
                k.act(mu, mu, AF.Exp, bias=LN_HALF, scale=0.5)
                k.stt(bt, gX[jo], 1.0, xc32[jo], ALU.add, ALU.mult)
                k.tt("dve", mu, mu, bt, ALU.mult)
                if seg == 0:
                    k.ts("dve", mu[:, 0:1], bt[:, 0:1], 0.5, None, ALU.mult)
                k.op("dve", lambda e, c=c: e.tensor_tensor_scan(out=hsn.ap, data0=aa.ap, data1=mu.ap,
                                                                initial=hcar.ap[:, c:c + 1], op0=ALU.mult, op1=ALU.add),
                     [aa, mu, hcar], [hsn])
                k.copy("dve", hcar[:, c:c + 1], hsn[:, TS - 1:TS])
                k.tt("dve", og[c], hsn, ybr[jo], ALU.mult)
                if do_s:
                    k.act(aas, gAs[jo], AF.Exp, bias=P("hcL", c), scale=P("hcL", c))
                    k.tt("dve", mus, aas, aas, ALU.mult)
                    k.act(mus, mus, AF.Ln, bias=1.0, scale=-1.0)
                    k.act(mus, mus, AF.Exp, bias=LN_HALF, scale=0.5)
                    k.stt(bts, gXs[jo], 1.0, xcs[jo], ALU.add, ALU.mult)
                    k.tt("dve", mus, mus, bts, ALU.mult)
                    tok2fm(h0s, sth_d[:, cc_], NS)
                    k.tt("dve", hns, aas, h0s, ALU.mult)
                    k.tt("dve", hns, hns, mus, ALU.add)
                    fm2tok(sh_d[:, cc_], hns, NS)
                    k.tt("dve", ogs[c], hns, ybrs[jo], ALU.mult)
        if last:
            fm2tok(ph_d, hcar, 16)
            t48 = k.alloc([48, 128], F32, "t48")
            b = k.bank()
            k.tr(b[0:48, 0:128], lcar.re("p j c -> p (j c)"), identf)
            k.copy("act", t48, b[0:48, 0:128])
            for j in range(3):
                k.dma(plc_d[j].re("(c p) -> c p", p=128), t48[j * 16:(j + 1) * 16, :])
        k.release(mL)
        proj_post(L, W["lru_w_out"], 16, og, ogs, 2, 256, xw)
        k.release(mtop)

    try:
        compute_mod()
        ckpt(1)
        for c in range(16):
            tok2fm(xsT[c], xs_d[:, c * 128:(c + 1) * 128], NS)
        ckpt(10)
        for seg in range(NSEG):
            do_s = (seg == 0)
            last = (seg == NSEG - 1)
            rwkv_layer(seg, do_s, last)
            ffn_layer(0, seg, do_s, last)
            ckpt(6)
            lru_layer(seg, do_s, last)
            ckpt(7)
            ffn_layer(1, seg, do_s, last)
            m = k.mark()
            x = load_x()
            yt = [k.alloc([128, D], F32, "yt") for _ in range(2)]
            for t4 in range(4):
                y_ = yt[t4 % 2]
                for c4 in range(4):
                    b = k.bank()
                    for q in range(4):
                        c = c4 * 4 + q
                        k.tr(b[:, q * 128:(q + 1) * 128], x[c][:, t4 * 128:(t4 + 1) * 128], identf)
                    ev(y_[:, c4 * 512:(c4 + 1) * 512], b)
                k.dma(yp_d[seg * TS + t4 * 128:seg * TS + (t4 + 1) * 128, :], y_)
            k.release(m)
            if last:
                fm2tok(pshift_d, shift_car, 16)
        for c in range(16):
            fm2tok(ys_d[:, c * 128:(c + 1) * 128], xsT[c], NS)

    except StopBuild:
        pass
    nc = k.emit()
    return nc, k


def _fm(v):
    v = np.asarray(v, np.float32).reshape(-1, 128)
    return np.ascontiguousarray(v.T)


def make_pvec(I):
    pv = np.zeros((128, PV_N), np.float32)

    def put(name, v):
        o, w = PV_OFF[name]
        pv[:, o:o + w] = _fm(v)
    for i in range(2):
        for j in range(4):
            put(f"ng{i}{j}", I["norm_g"][i, j])
        put(f"bmod{i}", I["b_mod"][i])
    for j in range(6):
        put(f"mix{j}", I["rw_mix"][0, j])
    put("w0", I["rw_w0"][0]); put("a0", I["rw_a0"][0]); put("kk", I["rw_kk"][0]); put("ka", I["rw_ka"][0])
    put("rk", I["rw_rk"][0].reshape(-1)); put("lcb", I["lru_conv_b"][0]); put("ba", I["lru_ba"][0])
    put("bx", I["lru_bx"][0]); put("lam", I["lru_lambda"][0]); put("bin", I["lru_b_in"][0])
    for j in range(4):
        put(f"lcw{j}", I["lru_conv_w"][0, j])
    for i in range(2):
        for j in range(3):
            put(f"fcw{i}{j}", I["ffn_conv_w"][i, j])
        put(f"fcb{i}", I["ffn_conv_b"][i])
    return pv


def make_cmask():
    t = np.arange(128)[:, None]
    c = np.arange(128)[None, :]
    out = np.zeros((128, 7, 512), np.float32)
    for l in range(7):
        b = 1 << l
        M = ((t // (2 * b) == c // (2 * b)) & (t % (2 * b) >= b) & (c % (2 * b) < b)).astype(np.float32)
        out[:, l, 0:128] = M
        out[:, l, 128:256] = M
        out[:, l, 256:384] = M.T
        out[:, l, 384:512] = M.T
    return out


_CACHE = {}


def kernel(**I):
    I = {k_: np.asarray(v) for k_, v in I.items()}
    B, S, _ = I["x_prompt"].shape
    NSEG = S // TS
    if NSEG not in _CACHE:
        _CACHE[NSEG] = build(NSEG)[0]
    nc = _CACHE[NSEG]
    pv = make_pvec(I)
    rows = np.ascontiguousarray(np.stack([I["rw_lnx_g"][0], I["rw_lnx_b"][0], I["rw_rk"][0].reshape(-1)]).astype(np.float32))
    shared = {
        "pvec": pv, "rows": rows, "cmask": make_cmask(),
        "w_mod": I["w_mod"], "rw_wr": I["rw_wr"][0], "rw_wk": I["rw_wk"][0], "rw_wv": I["rw_wv"][0], "rw_wo": I["rw_wo"][0],
        "rw_w1": I["rw_w1"][0], "rw_w2": I["rw_w2"][0], "rw_a1": I["rw_a1"][0], "rw_a2": I["rw_a2"][0],
        "rw_g1": I["rw_g1"][0], "rw_g2": I["rw_g2"][0], "lru_w_in": I["lru_w_in"][0], "lru_wa": I["lru_wa"][0],
        "lru_wx": I["lru_wx"][0], "lru_w_out": I["lru_w_out"][0], "ffn_w_gate": I["ffn_w_gate"],
        "ffn_w_up": I["ffn_w_up"], "ffn_w_down": I["ffn_w_down"],
    }
    shared = {k_: np.ascontiguousarray(v, dtype=np.float32) for k_, v in shared.items()}
    in_maps = []
    for c in range(8):
        b = c % B
        r = slice(NS * c, NS * (c + 1))
        d = dict(shared)
        d["xp"] = np.ascontiguousarray(I["x_prompt"][b])
        d["xs"] = np.ascontiguousarray(I["x_sample"][r, 0])
        d["call"] = np.ascontiguousarray(np.concatenate([I["c_prompt"][b:b + 1], I["c_sample"][r]], 0))
        d["st_wkv"] = np.ascontiguousarray(I["state_rwkv_wkv"][0, r].reshape(NS * NH, 4096))
        d["st_shift"] = np.ascontiguousarray(I["state_rwkv_shift"][0, r])
        d["st_h"] = np.ascontiguousarray(I["state_lru_h"][0, r])
        d["st_lconv"] = np.ascontiguousarray(I["state_lru_conv"][0, r])
        d["st_fconv"] = np.ascontiguousarray(I["state_ffn_conv"][:, r])
        in_maps.append(d)
    res = run_bass_kernel_spmd(nc, in_maps, core_ids=list(range(8)))
    R = res.results
    f32 = np.float32
    y_prompt = np.stack([R[b]["y_p"] for b in range(B)]).astype(f32)
    y_sample = np.concatenate([R[c]["y_s"] for c in range(8)], 0).reshape(8 * NS, 1, D).astype(f32)
    p_wkv = np.stack([R[b]["p_wkv"] for b in range(B)])[None].astype(f32)
    p_shift = np.stack([R[b]["p_shift"].reshape(D) for b in range(B)])[None].astype(f32)
    p_h = np.stack([R[b]["p_h"].reshape(D) for b in range(B)])[None].astype(f32)
    p_lconv = np.stack([R[b]["p_lconv"] for b in range(B)])[None].astype(f32)
    p_fconv = np.stack([R[b]["p_fconv"] for b in range(B)], 1).astype(f32)
    s_wkv = np.concatenate([R[c]["s_wkv"].reshape(NS, NH, 64, 64) for c in range(8)], 0)[None].astype(f32)
    s_shift = np.concatenate([R[c]["s_shift"] for c in range(8)], 0)[None].astype(f32)
    s_h = np.concatenate([R[c]["s_h"] for c in range(8)], 0)[None].astype(f32)
    s_lconv = np.concatenate([R[c]["s_lconv"] for c in range(8)], 0)[None].astype(f32)
    s_fconv = np.concatenate([R[c]["s_fconv"] for c in range(8)], 1).astype(f32)
    return (y_prompt, y_sample, p_wkv, p_shift, p_h, p_lconv, p_fconv, s_wkv, s_shift, s_h, s_lconv, s_fconv)
```

```python
import numpy as np
import concourse.bass as bass
import concourse.mybir as mybir
from concourse.bass_utils import run_bass_kernel_spmd

F32 = mybir.dt.float32
BF16 = mybir.dt.bfloat16
AF = mybir.ActivationFunctionType
ALU = mybir.AluOpType
AX = mybir.AxisListType

D = 2048
NCH = 16
DFF = 5632
NF = 44
TS = 512
NS = 16
NH = 32
EPS = 1e-6
LNX_EPS = 64e-5
DEC = -0.6065306597126334
WB_ELEMS = 6144
NWB = 3


class Buf:
    __slots__ = ("name", "lw", "wd", "rd", "rdd", "alias", "psum", "notrack")

    def __init__(self, name):
        self.name = name
        self.lw = None
        self.wd = []
        self.rd = {}
        self.rdd = []
        self.alias = []
        self.psum = False
        self.notrack = False


class V:
    __slots__ = ("ap", "b")

    def __init__(self, ap, b):
        self.ap = ap
        self.b = b

    def __getitem__(self, k):
        return V(self.ap[k], self.b)

    def re(self, s, **kw):
        return V(self.ap.rearrange(s, **kw), self.b)

    def bc(self, shape):
        return V(self.ap.to_broadcast(list(shape)), self.b)

    def us(self, ax):
        return V(self.ap.unsqueeze(ax), self.b)


class Op:
    __slots__ = ("eng", "fn", "reads", "writes", "dma", "waits", "sig", "cnt", "slot", "slotcnt", "prewait")

    def __init__(self, eng, fn, reads, writes, dma):
        self.eng = eng
        self.fn = fn
        self.reads = reads
        self.writes = writes
        self.dma = dma
        self.waits = []
        self.sig = False
        self.cnt = 0


NSLOT = 16
ARENA_WORDS = 53150


class KB:
    ENGS = ("pe", "dve", "act", "pool", "sp")

    def __init__(self):
        self.nc = bass.Bass("TRN2", target_bir_lowering=False)
        self.ops = []
        self._ctx = []
        g = self.nc.sbuf_tensor("arena", [128, ARENA_WORDS], F32)
        self.arena = g.__enter__()[:]
        self._ctx.append(g)
        self.top = 0
        self.live = []
        self.dead = []
        self.banks = []
        for i in range(8):
            g = self.nc.psum_tensor(f"bank{i}", [128, 512], F32)
            self.banks.append(V(g.__enter__()[:], Buf(f"bank{i}")))
            self.banks[-1].b.psum = True
            self._ctx.append(g)
        self.reserved = set()
        self.bi = 0
        self.maxtop = 0

    def alloc(self, shape, dt=F32, name="t"):
        n = 1
        for s in shape[1:]:
            n *= s
        words = n if dt == F32 else (n + 1) // 2
        words = (words + 1) // 2 * 2
        st = self.top
        en = st + words
        assert en <= ARENA_WORDS, f"arena overflow {en}"
        self.top = en
        self.maxtop = max(self.maxtop, en)
        ap = self.arena[:, st:en]
        if dt != F32:
            ap = ap.bitcast(dt)
        ap = ap[0:shape[0], 0:n]
        if len(shape) == 3:
            ap = ap.rearrange("p (a b) -> p a b", b=shape[2])
        elif len(shape) == 4:
            ap = ap.rearrange("p (a b c) -> p a b c", b=shape[2], c=shape[3])
        b = Buf(name)
        keep = []
        for (s0, e0, ob) in self.dead:
            if s0 < en and st < e0:
                b.alias.append(ob)
                if s0 < st:
                    keep.append((s0, st, ob))
                if e0 > en:
                    keep.append((en, e0, ob))
            else:
                keep.append((s0, e0, ob))
        self.dead = keep
        self.live.append((st, en, b))
        return V(ap, b)

    def mark(self):
        return (self.top, len(self.live))

    def release(self, m):
        top, nl = m
        for ent in self.live[nl:]:
            self.dead.append(ent)
        del self.live[nl:]
        self.top = top

    def bank(self):
        for _ in range(8):
            i = self.bi % 8
            self.bi += 1
            if i not in self.reserved:
                return self.banks[i]
        raise RuntimeError("no bank")

    def reserve(self):
        for _ in range(8):
            i = self.bi % 8
            self.bi += 1
            if i not in self.reserved:
                self.reserved.add(i)
                return self.banks[i]
        raise RuntimeError("no bank")

    def unreserve(self, b):
        for i, x in enumerate(self.banks):
            if x.b is b.b:
                self.reserved.discard(i)

    def dram(self, name, shape, dt=F32, kind="Internal"):
        t = self.nc.dram_tensor(name, list(shape), dt, kind=kind)
        return V(t.ap(), Buf(name))

    def op(self, eng, fn, reads=(), writes=(), dma=False):
        o = Op(eng, fn, [r.b for r in reads], [w.b for w in writes], dma)
        self.ops.append(o)
        return o

    def dma(self, out, in_, eng="sp"):
        return self.op(eng, lambda e: e.dma_start(out=out.ap, in_=in_.ap), [in_], [out], dma=True)

    def mm(self, out, lhsT, rhs, start=True, stop=True):
        return self.op("pe", lambda e: e.matmul(out=out.ap, lhsT=lhsT.ap, rhs=rhs.ap, start=start, stop=stop),
                       [lhsT, rhs], [out])

    def tr(self, out, in_, ident):
        return self.op("pe", lambda e: e.transpose(out=out.ap, in_=in_.ap, identity=ident.ap), [in_, ident], [out])

    def act(self, out, in_, func, bias=None, scale=None):
        reads = [in_] + [x for x in (bias, scale) if isinstance(x, V)]
        kw = {}
        if bias is not None:
            kw["bias"] = bias.ap if isinstance(bias, V) else float(bias)
        if scale is not None:
            kw["scale"] = scale.ap if isinstance(scale, V) else float(scale)
        return self.op("act", lambda e: e.activation(out=out.ap, in_=in_.ap, func=func, **kw), reads, [out])

    def copy(self, eng, out, in_):
        if eng == "act":
            return self.op("act", lambda e: e.copy(out=out.ap, in_=in_.ap), [in_], [out])
        return self.op(eng, lambda e: e.tensor_copy(out=out.ap, in_=in_.ap), [in_], [out])

    def tt(self, eng, out, a, b, op):
        return self.op(eng, lambda e: e.tensor_tensor(out=out.ap, in0=a.ap, in1=b.ap, op=op), [a, b], [out])

    def ts(self, eng, out, a, s1, s2, op0, op1=None):
        reads = [a] + [x for x in (s1, s2) if isinstance(x, V)]
        v1 = s1.ap if isinstance(s1, V) else float(s1)
        v2 = None if s2 is None else (s2.ap if isinstance(s2, V) else float(s2))
        if op1 is None:
            return self.op(eng, lambda e: e.tensor_scalar(out=out.ap, in0=a.ap, scalar1=v1, scalar2=None, op0=op0),
                           reads, [out])
        return self.op(eng, lambda e: e.tensor_scalar(out=out.ap, in0=a.ap, scalar1=v1, scalar2=v2, op0=op0, op1=op1),
                       reads, [out])

    def stt(self, out, a, s, b, op0, op1):
        reads = [a, b] + ([s] if isinstance(s, V) else [])
        sv = s.ap if isinstance(s, V) else float(s)
        return self.op("dve", lambda e: e.scalar_tensor_tensor(out=out.ap, in0=a.ap, scalar=sv, in1=b.ap, op0=op0, op1=op1),
                       reads, [out])

    def memset(self, eng, out, val):
        return self.op(eng, lambda e: e.memset(out.ap, val), [], [out])

    def finalize(self):
        ops = self.ops
        for i, o in enumerate(ops):
            deps = set()
            for b in o.reads:
                if b.lw is not None:
                    deps.add(b.lw)
                deps.update(b.wd)
                if b.psum:
                    for e, j in b.rd.items():
                        if e != o.eng:
                            deps.add(j)
            for b in o.writes:
                if b.notrack:
                    continue
                for ob in b.alias:
                    if ob.lw is not None:
                        deps.add(ob.lw)
                    deps.update(ob.wd)
                    deps.update(ob.rd.values())
                    deps.update(ob.rdd)
                b.alias = []
                if b.lw is not None:
                    deps.add(b.lw)
                if not o.dma:
                    deps.update(b.wd)
                deps.update(b.rd.values())
                deps.update(b.rdd)
            for b in o.reads:
                if o.dma:
                    b.rdd.append(i)
                else:
                    b.rd[o.eng] = i
            for b in o.writes:
                if b.notrack:
                    continue
                had_readers = bool(b.rd or b.rdd)
                if o.dma:
                    if had_readers:
                        b.wd = [i]
                        b.lw = None
                    else:
                        b.wd.append(i)
                else:
                    b.lw = i
                    b.wd = []
                b.rd = {}
                b.rdd = []
            best = {}
            dl = []
            for j in deps:
                if j == i:
                    continue
                p = ops[j]
                e = p.eng
                if p.dma:
                    dl.append(j)
                else:
                    if e == "pe" and o.eng == "pe" and not o.dma:
                        continue
                    if e not in best or best[e] < j:
                        best[e] = j
            o.waits = sorted(set(list(best.values()) + dl))
            for j in o.waits:
                ops[j].sig = True
        cnt = {e: 0 for e in self.ENGS}
        dcount = {e: 0 for e in self.ENGS}
        slotcnt = {e: [0] * NSLOT for e in self.ENGS}
        for o in ops:
            if o.dma:
                kk = dcount[o.eng]
                dcount[o.eng] += 1
                s = kk % NSLOT
                o.slot = s
                o.prewait = slotcnt[o.eng][s]
                slotcnt[o.eng][s] += 16
                o.slotcnt = slotcnt[o.eng][s]
            elif o.sig:
                cnt[o.eng] += 1
                o.cnt = cnt[o.eng]
        self.final_dma = {e: list(slotcnt[e]) for e in self.ENGS}

    def emit(self):
        nc = self.nc
        ops = self.ops
        self.finalize()
        sems = {}
        for e in self.ENGS:
            g = nc.semaphore(f"s_{e}")
            sems[e] = g.__enter__()
            self._ctx.append(g)
        dsem = {}
        for e in self.ENGS:
            if any(o.dma and o.eng == e for o in ops):
                dsem[e] = []
                for s in range(NSLOT):
                    g = nc.semaphore(f"d_{e}_{s}")
                    dsem[e].append(g.__enter__())
                    self._ctx.append(g)
        per = {e: [o for o in ops if o.eng == e] for e in self.ENGS}
        final_dma = self.final_dma

        def run(eng_name, eng):
            waited = {}
            for o in per[eng_name]:
                if o.dma and o.prewait > 0:
                    key = ("d", eng_name, o.slot)
                    if waited.get(key, 0) < o.prewait:
                        eng.wait_ge(dsem[eng_name][o.slot], o.prewait)
                        waited[key] = o.prewait
                for j in o.waits:
                    p = ops[j]
                    if p.dma:
                        key = ("d", p.eng, p.slot)
                        val = p.slotcnt
                        sem = dsem[p.eng][p.slot]
                    else:
                        key = ("c", p.eng)
                        val = p.cnt
                        sem = sems[p.eng]
                    if waited.get(key, 0) < val:
                        eng.wait_ge(sem, val)
                        waited[key] = val
                ins = o.fn(eng)
                if o.dma:
                    ins.then_inc(dsem[eng_name][o.slot], 16)
                elif o.sig:
                    ins.then_inc(sems[eng_name], 1)
            if eng_name in dsem:
                for s in range(NSLOT):
                    v = final_dma[eng_name][s]
                    if v > 0 and waited.get(("d", eng_name, s), 0) < v:
                        eng.wait_ge(dsem[eng_name][s], v)

        with nc.Block() as block:
            @block.sync
            def _(e):
                run("sp", e)

            @block.tensor
            def _(e):
                run("pe", e)

            @block.vector
            def _(e):
                run("dve", e)

            @block.scalar
            def _(e):
                run("act", e)

            @block.gpsimd
            def _(e):
                run("pool", e)
        return nc


def pvec_layout():
    names = []
    for i in range(2):
        for j in range(4):
            names.append((f"ng{i}{j}", 16))
    for i in range(2):
        names.append((f"bmod{i}", 96))
    for j in range(6):
        names.append((f"mix{j}", 16))
    for n in ("w0", "a0", "kk", "ka", "rk", "lcb", "ba", "bx", "lam"):
        names.append((n, 16))
    names.append(("bin", 32))
    for j in range(4):
        names.append((f"lcw{j}", 16))
    for i in range(2):
        for j in range(3):
            names.append((f"fcw{i}{j}", 44))
        names.append((f"fcb{i}", 44))
    names.append(("omka", 16))
    names.append(("cL", 16))
    names.append(("hba", 16))
    names.append(("hbx", 16))
    names.append(("hcL", 16))
    off = {}
    o = 0
    for n, w in names:
        off[n] = (o, w)
        o += w
    return off, o


PV_OFF, PV_N = pvec_layout()


def build(NSEG):
    k = KB()
    T = NSEG * TS
    IN = lambda n, s: k.dram(n, s, F32, kind="ExternalInput")
    def OUT(n, s):
        v = k.dram(n, s, F32, kind="ExternalOutput")
        v.b.notrack = True
        return v
    xp_d = IN("xp", [T, D])
    xs_d = IN("xs", [NS, D])
    call_d = IN("call", [17, D])
    stw_d = IN("st_wkv", [NS * NH, 4096])
    stsh_d = IN("st_shift", [NS, D])
    sth_d = IN("st_h", [NS, D])
    stlc_d = IN("st_lconv", [NS, 3, D])
    stfc_d = IN("st_fconv", [2, NS, 2, DFF])
    pvec_d = IN("pvec", [128, PV_N])
    rows_d = IN("rows", [3, D])
    cmask_d = IN("cmask", [128, 7, 512])
    W = {}
    for n, s in (("w_mod", [2, D, 6 * D]), ("rw_wr", [D, D]), ("rw_wk", [D, D]), ("rw_wv", [D, D]), ("rw_wo", [D, D]),
                 ("rw_w1", [D, 96]), ("rw_w2", [96, D]), ("rw_a1", [D, 128]), ("rw_a2", [128, D]),
                 ("rw_g1", [D, 256]), ("rw_g2", [256, D]), ("lru_w_in", [D, 2 * D]), ("lru_wa", [8, 256, 256]),
                 ("lru_wx", [8, 256, 256]), ("lru_w_out", [D, D]), ("ffn_w_gate", [2, D, DFF]),
                 ("ffn_w_up", [2, D, DFF]), ("ffn_w_down", [2, DFF, D])):
        W[n] = IN(n, s)
    yp_d = OUT("y_p", [T, D])
    ys_d = OUT("y_s", [NS, D])
    pwkv_d = OUT("p_wkv", [NH, 64, 64])
    pshift_d = OUT("p_shift", [16, 128])
    ph_d = OUT("p_h", [16, 128])
    plc_d = OUT("p_lconv", [3, D])
    pfc_d = OUT("p_fconv", [2, 2, DFF])
    swkv_d = OUT("s_wkv", [NS * NH, 4096])
    sshift_d = OUT("s_shift", [NS, D])
    sh_d = OUT("s_h", [NS, D])
    slc_d = OUT("s_lconv", [NS, 3, D])
    sfc_d = OUT("s_fconv", [2, NS, 2, DFF])
    import os
    DBG = os.environ.get("KDBG") == "1"
    dbg_n = {"n": 0}

    def dump(name, v, shape, dt_is_bf16=False):
        if not DBG:
            return
        d = k.dram("dbg_" + name, list(shape), F32, kind="ExternalOutput")
        if dt_is_bf16:
            k.dma(d, v, eng="pool")
        else:
            k.dma(d, v)
    STOP = int(os.environ.get("KSTOP", "0"))

    class StopBuild(Exception):
        pass

    def ckpt(n):
        if STOP == n:
            raise StopBuild()
    xres_d = [k.dram(f"xres{c}", [128, TS]) for c in range(16)]
    scr6_d = k.dram("scr6", [6, NS, D])
    scro_d = k.dram("scro", [NS, D])

    ones_f = k.alloc([128, 128], F32, "ones_f")
    k.memset("pool", ones_f, 1.0)
    identf = k.alloc([128, 128], F32, "identf")
    k.op("pool", lambda e: e.affine_select(out=identf.ap, in_=ones_f.ap, pattern=[[-1, 128]], compare_op=ALU.is_equal,
                                           fill=0.0, base=0, channel_multiplier=1), [ones_f], [identf])
    identb = k.alloc([128, 128], BF16, "identb")
    k.copy("pool", identb, identf)
    onesb = k.alloc([128, 128], BF16, "onesb")
    k.copy("pool", onesb, ones_f)
    blkdiag = k.alloc([128, 128], BF16, "blkdiag")
    k.memset("pool", blkdiag, 0.0)
    k.memset("pool", blkdiag[0:64, 0:64], 1.0)
    k.memset("pool", blkdiag[64:128, 64:128], 1.0)
    blk2 = k.alloc([128, 2], BF16, "blk2")
    k.memset("pool", blk2, 0.0)
    k.memset("pool", blk2[0:64, 0:1], 1.0)
    k.memset("pool", blk2[64:128, 1:2], 1.0)
    mA = k.alloc([128, 512], F32, "mA")
    mB = k.alloc([128, 512], F32, "mB")
    mC = k.alloc([128, 256], F32, "mC")

    def sel(dst, kind):
        if kind == "Ls":
            pat, base, cm = [[-1, 128]], -1, 1
        elif kind == "Us":
            pat, base, cm = [[1, 128]], -1, -1
        else:
            pat, base, cm = [[1, 128]], 0, -1
        k.op("pool", lambda e: e.affine_select(out=dst.ap, in_=ones_f.ap, pattern=pat, compare_op=ALU.is_ge,
                                               fill=0.0, base=base, channel_multiplier=cm), [ones_f], [dst])
    for i, kd in enumerate(("Ls", "Ls", "Us", "Us")):
        sel(mA[:, i * 128:(i + 1) * 128], kd)
    for i, kd in enumerate(("Us", "Us", "Ui", "Ui")):
        sel(mB[:, i * 128:(i + 1) * 128], kd)
    for i, kd in enumerate(("Ui", "Ui")):
        sel(mC[:, i * 128:(i + 1) * 128], kd)

    ML = k.alloc([128, 7, 512], BF16, "ML")
    for l_ in range(7):
        k.dma(ML[:, l_, :], cmask_d[:, l_, :], eng="pool")
    I4 = k.alloc([128, 512], BF16, "I4")
    for i_ in range(4):
        k.copy("pool", I4[:, i_ * 128:(i_ + 1) * 128], identf)
    pv = k.alloc([128, PV_N], F32, "pv")
    k.dma(pv, pvec_d)

    def P(name, c):
        o, w = PV_OFF[name]
        return pv[:, o + c:o + c + 1]

    def PR(name):
        o, w = PV_OFF[name]
        return pv[:, o:o + w]
    k.ts("dve", PR("omka"), PR("ka"), -1.0, 1.0, ALU.mult, ALU.add)
    k.act(PR("cL"), PR("lam"), AF.Exp, scale=-1.0)
    k.act(PR("cL"), PR("cL"), AF.Ln, bias=1.0)
    k.ts("dve", PR("cL"), PR("cL"), -8.0, None, ALU.mult)
    k.ts("dve", PR("hcL"), PR("cL"), 0.5, None, ALU.mult)
    k.ts("dve", PR("hba"), PR("ba"), 0.5, None, ALU.mult)
    k.ts("dve", PR("hbx"), PR("bx"), 0.5, None, ALU.mult)

    modT = [k.alloc([128, 96, 17], F32, f"modT{i}") for i in range(2)]
    shift_car = k.alloc([128, 16], F32, "shift_car")
    k.memset("pool", shift_car, 0.0)
    fcar = [k.alloc([128, 2, NF], F32, f"fcar{i}") for i in range(2)]
    for i in range(2):
        k.memset("pool", fcar[i], 0.0)
    lcar = k.alloc([128, 3, 16], F32, "lcar")
    k.memset("pool", lcar, 0.0)
    hcar = k.alloc([128, 16], F32, "hcar")
    k.memset("pool", hcar, 0.0)
    T32 = k.alloc([128, 16, 64], F32, "T32")
    k.memset("pool", T32, 0.0)
    xsT = [k.alloc([128, NS], F32, f"xsT{c}") for c in range(16)]
    wbufs = [k.alloc([128, WB_ELEMS], BF16, f"wb{i}") for i in range(NWB)]
    wstate = {"n": 0}

    class WS:
        def __init__(self, reqs, extra=None):
            self.reqs = reqs
            self.emitted = 0
            self.views = {}
            self.bufs = wbufs + (extra or [])
            self.n = wstate["n"] % NWB

        def _emit(self, i):
            buf = self.bufs[self.n % len(self.bufs)]
            self.n += 1
            wstate["n"] += 1
            off = 0
            vs = []
            for (src, nk) in self.reqs[i]:
                rows, cols = src.ap.shape
                pr = rows // nk
                dst = buf[0:pr, off:off + nk * cols].re("p (a b) -> p a b", b=cols)
                k.dma(dst, src.re("(a p) c -> p a c", p=pr), eng="pool")
                vs.append(dst)
                off += nk * cols
            assert off <= WB_ELEMS
            self.views[i] = vs

        def get(self, i):
            lim = min(len(self.reqs), i + len(self.bufs) - 1)
            while self.emitted < max(lim, i + 1):
                self._emit(self.emitted)
                self.emitted += 1
            return self.views.pop(i)

    evt = {"n": 0}

    def ev(out, in_):
        evt["n"] += 1
        k.copy("act" if evt["n"] % 2 else "dve", out, in_)

    def tok2fm(dst, src_d, R):
        m = k.mark()
        t = k.alloc([R, 128], F32, "t2f")
        k.dma(t, src_d)
        b = k.bank()
        k.tr(b[:, 0:R], t, identf[0:R, 0:R])
        ev(dst, b[:, 0:R])
        k.release(m)

    def fm2tok(dst_d, src, R):
        m = k.mark()
        t = k.alloc([R, 128], F32, "f2t")
        b = k.bank()
        k.tr(b[0:R, 0:128], src, identf)
        ev(t, b[0:R, 0:128])
        k.dma(dst_d, t)
        k.release(m)

    def compute_mod():
        m = k.mark()
        ct = k.alloc([17, D], F32, "ct")
        k.dma(ct, call_d)
        k.act(ct, ct, AF.Silu)
        siluT = k.alloc([128, 16, 17], BF16, "siluT")
        for c in range(16):
            b = k.bank()
            k.tr(b[:, 0:17], ct[:, c * 128:(c + 1) * 128], identf[0:17, 0:17])
            ev(siluT[:, c, :], b[:, 0:17])
        for i in range(2):
            ws = WS([[(W["w_mod"][i][:, g * 256:(g + 1) * 256], 16)] for g in range(48)])
            for g in range(48):
                (wv,) = ws.get(g)
                b = k.bank()
                for j in range(2):
                    for kc in range(16):
                        k.mm(b[:, j * 17:(j + 1) * 17], wv[:, kc, j * 128:(j + 1) * 128], siluT[:, kc, :],
                             start=(kc == 0), stop=(kc == 15))
                for j in range(2):
                    jc = g * 2 + j
                    k.act(modT[i][:, jc, :], b[:, j * 17:(j + 1) * 17], AF.Identity, bias=P(f"bmod{i}", jc))
            for c in range(16):
                k.ts("dve", modT[i][:, 16 + c, :], modT[i][:, 16 + c, :], 1.0, P(f"ng{i}0", c), ALU.add, ALU.mult)
                k.ts("dve", modT[i][:, 64 + c, :], modT[i][:, 64 + c, :], 1.0, P(f"ng{i}2", c), ALU.add, ALU.mult)
                k.ts("dve", modT[i][:, 32 + c, :], modT[i][:, 32 + c, :], P(f"ng{i}1", c), None, ALU.mult)
                k.ts("dve", modT[i][:, 80 + c, :], modT[i][:, 80 + c, :], P(f"ng{i}3", c), None, ALU.mult)
        k.release(m)

    def modv(i, j, c, sample):
        return modT[i][:, j * 16 + c, 1:17] if sample else modT[i][:, j * 16 + c, 0:1]

    def rstd_of(chunks, N):
        rstd = k.alloc([128, N], F32, "rstd")
        m = k.mark()
        sq = [k.alloc([128, N], BF16, "sq") for _ in range(2)]
        bank = k.reserve()
        for c in range(len(chunks)):
            k.act(sq[c % 2], chunks[c], AF.Square)
            k.mm(bank[:, 0:N], onesb, sq[c % 2], start=(c == 0), stop=(c == len(chunks) - 1))
        k.act(rstd, bank[:, 0:N], AF.Ln, bias=EPS, scale=1.0 / D)
        k.act(rstd, rstd, AF.Exp, scale=-0.5)
        k.unreserve(bank)
        k.release(m)
        return rstd

    def adaln_chunk(out, tmp, x_c, rstd, gsc, sh, sample):
        k.tt("dve", tmp, x_c, rstd, ALU.mult)
        if not sample:
            k.act(out, tmp, AF.Identity, bias=sh, scale=gsc)
        else:
            k.tt("dve", tmp, tmp, gsc, ALU.mult)
            k.tt("dve", out, tmp, sh, ALU.add)

    def load_x():
        xs_ = [k.alloc([128, TS], F32, f"x{c}") for c in range(16)]
        for c in range(16):
            k.dma(xs_[c], xres_d[c])
        return xs_

    def proj_post(layer, wsrc, KC, in_p, in_s, jgate, ncol, extra=None):
        m = k.mark()
        pout = [k.alloc([128, TS], F32, f"po{c}") for c in range(16)]
        pouts = [k.alloc([128, NS], F32, f"pos{c}") for c in range(16)] if in_s else None
        sq = [k.alloc([128, TS], BF16, "psq") for _ in range(2)]
        sqs = k.alloc([128, NS], BF16, "psqs")
        ngrp = D // ncol
        ws = WS([[(wsrc[:, g * ncol:(g + 1) * ncol], KC)] for g in range(ngrp)], extra)
        bssq = k.reserve()
        bssqs = k.reserve() if in_s else None
        for g in range(ngrp):
            (wv,) = ws.get(g)
            for j in range(ncol // 128):
                c = g * (ncol // 128) + j
                b = k.bank()
                for kc in range(KC):
                    k.mm(b, wv[:, kc, j * 128:(j + 1) * 128], in_p[kc], start=(kc == 0), stop=(kc == KC - 1))
                k.copy("act", pout[c], b)
                k.act(sq[c % 2], pout[c], AF.Square)
                k.mm(bssq, onesb, sq[c % 2], start=(c == 0), stop=(c == 15))
                if in_s:
                    b2 = k.bank()
                    for kc in range(KC):
                        k.mm(b2[:, 0:NS], wv[:, kc, j * 128:(j + 1) * 128], in_s[kc], start=(kc == 0), stop=(kc == KC - 1))
                    k.copy("dve", pouts[c], b2[:, 0:NS])
                    k.act(sqs, pouts[c], AF.Square)
                    k.mm(bssqs[:, 0:NS], onesb, sqs, start=(c == 0), stop=(c == 15))
        rstd = k.alloc([128, TS], F32, "prstd")
        k.act(rstd, bssq, AF.Ln, bias=EPS, scale=1.0 / D)
        k.act(rstd, rstd, AF.Exp, scale=-0.5)
        k.unreserve(bssq)
        if in_s:
            rstds = k.alloc([128, NS], F32, "prstds")
            k.act(rstds, bssqs[:, 0:NS], AF.Ln, bias=EPS, scale=1.0 / D)
            k.act(rstds, rstds, AF.Exp, scale=-0.5)
            k.unreserve(bssqs)
        xt = [k.alloc([128, TS], F32, "pxt") for _ in range(2)]
        for c in range(16):
            x_ = xt[c % 2]
            k.dma(x_, xres_d[c])
            k.tt("dve", pout[c], pout[c], rstd, ALU.mult)
            k.stt(x_, pout[c], modv(layer, jgate, c, False), x_, ALU.mult, ALU.add)
            k.dma(xres_d[c], x_)
            if in_s:
                k.tt("dve", pouts[c], pouts[c], rstds, ALU.mult)
                k.tt("dve", pouts[c], pouts[c], modv(layer, jgate, c, True), ALU.mult)
                k.tt("dve", xsT[c], xsT[c], pouts[c], ALU.add)
        k.release(m)

    def rwkv_layer(seg, do_s, last):
        L = 0
        mtop = k.mark()
        mixes = {j: [k.alloc([128, TS], BF16, f"mx{j}_{c}") for c in range(16)] for j in (0, 2, 3)}
        yg = [k.alloc([128, TS], BF16, f"yg{c}") for c in range(16)]
        tw = k.alloc([128, TS], BF16, "tw")
        la = k.alloc([128, TS], BF16, "la")
        sg = [k.alloc([128, TS], BF16, f"sg{i}") for i in range(2)]
        if do_s:
            smix = {j: [k.alloc([128, NS], BF16, f"smx{j}_{c}") for c in range(16)] for j in range(6)}
            ygs = [k.alloc([128, NS], BF16, f"ygs{c}") for c in range(16)]
            tws = k.alloc([128, NS], BF16, "tws")
            las = k.alloc([128, NS], BF16, "las")
            sgs = [k.alloc([128, NS], BF16, f"sgs{i}") for i in range(2)]
            gs = k.alloc([128, 16, NS], F32, "gs")
        m1 = k.mark()
        x = [k.alloc([128, TS], F32, f"x{c}") for c in range(16)]
        mxin = k.mark()
        xin = [k.alloc([128, D], F32, "xin") for _ in range(2)]
        for tt_ in range(4):
            xi = xin[tt_ % 2]
            k.dma(xi, xp_d[seg * TS + tt_ * 128: seg * TS + (tt_ + 1) * 128, :])
            for c4 in range(4):
                if os.environ.get("KVAR") == "notr":
                    continue
                b = k.bank()
                for q in range(4):
                    c = c4 * 4 + q
                    k.tr(b[:, q * 128:(q + 1) * 128], xi[:, c * 128:(c + 1) * 128], identf)
                for q in range(4):
                    c = c4 * 4 + q
                    ev(x[c][:, tt_ * 128:(tt_ + 1) * 128], b[:, q * 128:(q + 1) * 128])
        if os.environ.get("KVAR") != "nostore":
            for c in range(16):
                k.dma(xres_d[c], x[c])
        k.release(mxin)
        ckpt(11)
        rstd = rstd_of(x, TS)
        ckpt(12)
        wsl = WS([[(W["rw_w1"], 16), (W["rw_a1"], 16)], [(W["rw_g1"], 16)]])
        w1v, a1v = wsl.get(0)
        (g1v,) = wsl.get(1)
        bw, ba, bg0, bg1 = k.reserve(), k.reserve(), k.reserve(), k.reserve()
        hc = [k.alloc([128, 1 + TS], F32, "hc") for _ in range(2)]
        xx = k.alloc([128, TS], F32, "xx")
        tmpn = k.alloc([128, TS], F32, "tmpn")
        tb = [k.alloc([128, TS], BF16, "tb") for _ in range(3)]
        for c in range(16):
            h_ = hc[c % 2]
            k.copy("dve", h_[:, 0:1], shift_car[:, c:c + 1])
            adaln_chunk(h_[:, 1:1 + TS], tmpn, x[c], rstd, modv(L, 1, c, False), modv(L, 0, c, False), False)
            k.copy("dve", shift_car[:, c:c + 1], h_[:, TS:TS + 1])
            k.tt("dve", xx, h_[:, 0:TS], h_[:, 1:1 + TS], ALU.subtract)
            ti = 0
            for j in range(6):
                if j in (0, 2, 3):
                    dst = mixes[j][c]
                else:
                    dst = tb[ti]
                    ti += 1
                k.stt(dst, xx, P(f"mix{j}", c), h_[:, 1:1 + TS], ALU.mult, ALU.add)
                if j == 1:
                    k.mm(bw[0:96, :], w1v[:, c, :], dst, start=(c == 0), stop=(c == 15))
                elif j == 4:
                    k.mm(ba, a1v[:, c, :], dst, start=(c == 0), stop=(c == 15))
                elif j == 5:
                    k.mm(bg0, g1v[:, c, 0:128], dst, start=(c == 0), stop=(c == 15))
                    k.mm(bg1, g1v[:, c, 128:256], dst, start=(c == 0), stop=(c == 15))
        k.act(tw[0:96, :], bw[0:96, :], AF.Tanh)
        k.copy("dve", la, ba)
        k.act(sg[0], bg0, AF.Sigmoid)
        k.act(sg[1], bg1, AF.Sigmoid)
        for b_ in (bw, ba, bg0, bg1):
            k.unreserve(b_)
        ckpt(13)
        if do_s:
            rstds = rstd_of(xsT, NS)
            hs = k.alloc([128, NS], F32, "hs")
            hp = k.alloc([128, NS], F32, "hp")
            xxs = k.alloc([128, NS], F32, "xxs")
            tmps = k.alloc([128, NS], F32, "tmps")
            for c in range(16):
                adaln_chunk(hs, tmps, xsT[c], rstds, modv(L, 1, c, True), modv(L, 0, c, True), True)
                tok2fm(hp, stsh_d[:, c * 128:(c + 1) * 128], NS)
                fm2tok(sshift_d[:, c * 128:(c + 1) * 128], hs, NS)
                k.tt("dve", xxs, hp, hs, ALU.subtract)
                for j in range(6):
                    k.stt(smix[j][c], xxs, P(f"mix{j}", c), hs, ALU.mult, ALU.add)
            b = k.bank()
            for c in range(16):
                k.mm(b[0:96, 0:16], w1v[:, c, :], smix[1][c], start=(c == 0), stop=(c == 15))
            for c in range(16):
                k.mm(b[:, 16:32], a1v[:, c, :], smix[4][c], start=(c == 0), stop=(c == 15))
            for c in range(16):
                k.mm(b[:, 32:48], g1v[:, c, 0:128], smix[5][c], start=(c == 0), stop=(c == 15))
            for c in range(16):
                k.mm(b[:, 48:64], g1v[:, c, 128:256], smix[5][c], start=(c == 0), stop=(c == 15))
            k.act(tws[0:96, :], b[0:96, 0:16], AF.Tanh)
            k.copy("dve", las, b[:, 16:32])
            k.act(sgs[0], b[:, 32:48], AF.Sigmoid)
            k.act(sgs[1], b[:, 48:64], AF.Sigmoid)
        k.release(m1)
        ckpt(2)

        m2 = k.mark()
        l2t = [k.alloc([128, 4, 128], BF16, f"lora2_{i}") for i in range(2)]
        A = lambda n, dt=F32, w=TS: k.alloc([128, w], dt, n)
        at_f, bt_f, kt_f, rt_f = [A(n, BF16) for n in ("at_f", "bt_f", "kt_f", "rt_f")]
        gTb = A("gTb", BF16)
        at_b, bt_b, rt_b = [k.alloc([128, 4, 2, 128], BF16, n) for n in ("at_b", "bt_b", "rt_b")]
        for t_ in (at_b, bt_b, rt_b):
            k.memset("pool", t_, 0.0)
        v32 = k.alloc([128, 4, 128], F32, "v32")
        vb = k.alloc([128, 4, 128], BF16, "vb")
        s_tok = A("s_tok", F32, 8)
        DL = A("DL", F32, 4)
        Tblk = k.alloc([128, 2, 64], BF16, "Tblk")
        k.memset("pool", Tblk, 0.0)
        Gbc = A("Gbc", F32, 128)
        Bbc = A("Bbc", F32, 128)
        if do_s:
            Q5 = k.alloc([128, 5, NS], F32, "Q5")
            sA = [A(f"sA{i}", F32, NS) for i in range(6)]
            sqs_ = A("sqs_", BF16, NS)
            q4t = k.alloc([NS, 4, 128], F32, "q4t")
            q2t = k.alloc([NS, 2, 128], F32, "q2t")
        ws = WS([[(W["rw_wr"][:, m * 128:(m + 1) * 128], 16), (W["rw_wk"][:, m * 128:(m + 1) * 128], 16),
                  (W["rw_wv"][:, m * 128:(m + 1) * 128], 16)] for m in range(16)])
        pre = {}

        def prefetch(m):
            mc = slice(m * 128, (m + 1) * 128)
            wr, wk, wv = ws.get(m)
            l2 = l2t[m % 2]
            k.dma(l2[0:96, 0, :], W["rw_w2"][:, mc], eng="pool")
            k.dma(l2[:, 1, :], W["rw_a2"][:, mc], eng="pool")
            k.dma(l2[:, 2, :], W["rw_g2"][0:128, mc], eng="pool")
            k.dma(l2[:, 3, :], W["rw_g2"][128:256, mc], eng="pool")
            br, bk, bv = k.reserve(), k.reserve(), k.reserve()
            for kc in range(16):
                k.mm(br, wr[:, kc, :], mixes[0][kc], start=(kc == 0), stop=(kc == 15))
            for kc in range(16):
                k.mm(bk, wk[:, kc, :], mixes[2][kc], start=(kc == 0), stop=(kc == 15))
            for t4 in range(4):
                for kc in range(16):
                    k.mm(bv[:, t4 * 128:(t4 + 1) * 128], mixes[3][kc][:, t4 * 128:(t4 + 1) * 128], wv[:, kc, :],
                         start=(kc == 0), stop=(kc == 15))
            pre[m] = (wr, wk, wv, l2, br, bk, bv)
        prefetch(0)
        for m in range(16):
            mc = slice(m * 128, (m + 1) * 128)
            wr, wk, wv, l2, br, bk, bv = pre.pop(m)
            k.dma(Gbc, V(rows_d.ap[0:1, mc].partition_broadcast(128), rows_d.b))
            k.dma(Bbc, V(rows_d.ap[1:2, mc].partition_broadcast(128), rows_d.b))
            mE = k.mark()
            r32, k32, ag, sgw, kk, cum, e_pos, e_neg, e_prev, tA, tB = [A(n) for n in
                ("r32", "k32", "ag", "sgw", "kk", "cum", "e_pos", "e_neg", "e_prev", "tA", "tB")]
            sqb, rkr = A("sqb", BF16), A("rkr", BF16)
            k.copy("act", r32, br)
            k.copy("act", k32, bk)
            k.copy("dve", v32.re("p a b -> p (a b)"), bv)
            k.copy("dve", vb.re("p a b -> p (a b)"), bv)
            for b_ in (br, bk, bv):
                k.unreserve(b_)
            b = k.bank()
            k.mm(b, l2[0:96, 0, :], tw[0:96, :])
            k.act(sgw, b, AF.Sigmoid, bias=P("w0", m))
            b = k.bank()
            k.mm(b, l2[:, 1, :], la)
            k.act(ag, b, AF.Sigmoid, bias=P("a0", m))
            b = k.bank()
            k.mm(b, l2[:, 2, :], sg[0], start=True, stop=False)
            k.mm(b, l2[:, 3, :], sg[1], start=False, stop=True)
            k.copy("act", gTb, b)
            k.ts("dve", kk, k32, P("kk", m), None, ALU.mult)
            k.act(sqb, kk, AF.Square)
            b = k.bank()
            k.mm(b, blkdiag, sqb)
            k.act(tA, b, AF.Ln)
            k.act(tA, tA, AF.Exp, scale=-0.5)
            k.tt("dve", kk, kk, tA, ALU.mult)
            k.ts("dve", tB, ag, P("ka", m), P("omka", m), ALU.mult, ALU.add)
            k.tt("dve", k32, k32, tB, ALU.mult)
            k.stt(rkr, r32, P("rk", m), k32, ALU.mult, ALU.mult)
            b = k.bank()
            for t4 in range(4):
                k.mm(b[:, t4 * 2:(t4 + 1) * 2], rkr[:, t4 * 128:(t4 + 1) * 128], blk2)
            k.copy("act", s_tok, b[:, 0:8])
            for cc in range(4):
                sl = slice(cc * 128, (cc + 1) * 128)
                k.op("dve", lambda e, sl=sl: e.tensor_tensor_scan(out=cum.ap[:, sl], data0=ones_f.ap, data1=sgw.ap[:, sl],
                                                                  initial=0.0, op0=ALU.mult, op1=ALU.add),
                     [ones_f, sgw], [cum])
            k.act(e_pos, cum, AF.Exp, scale=DEC)
            k.act(e_neg, cum, AF.Exp, scale=-DEC)
            k.tt("dve", tA, cum, sgw, ALU.subtract)
            k.act(e_prev, tA, AF.Exp, scale=DEC)
            k.copy("act", DL, e_pos.re("p (c t) -> p c t", t=128)[:, :, 127])
            k.tt("dve", rt_f, r32, e_pos, ALU.mult)
            k.tt("dve", kt_f, k32, e_neg, ALU.mult)
            k.tt("dve", tB, kk, ag, ALU.mult)
            k.tt("dve", bt_f, tB, e_neg, ALU.mult)
            k.stt(at_f, kk, -1.0, e_prev, ALU.mult, ALU.mult)
            for (Xb_, Xf_) in ((at_b, at_f), (bt_b, bt_f), (rt_b, rt_f)):
                for h in range(2):
                    k.copy("act" if h else "dve", Xb_[64 * h:64 * h + 64, :, h, :],
                           Xf_[64 * h:64 * h + 64, :].re("p (c t) -> p c t", t=128))
            if do_s:
                r_s, k_s, sgw_s, ag_s, kk_s, t_s = sA
                b = k.bank()
                for kc in range(16):
                    k.mm(b[:, 0:16], wr[:, kc, :], smix[0][kc], start=(kc == 0), stop=(kc == 15))
                for kc in range(16):
                    k.mm(b[:, 16:32], wk[:, kc, :], smix[2][kc], start=(kc == 0), stop=(kc == 15))
                k.mm(b[:, 32:48], l2[0:96, 0, :], tws[0:96, :])
                k.mm(b[:, 48:64], l2[:, 1, :], las)
                k.mm(b[:, 64:80], l2[:, 2, :], sgs[0], start=True, stop=False)
                k.mm(b[:, 64:80], l2[:, 3, :], sgs[1], start=False, stop=True)
                bv_ = k.bank()
                for kc in range(16):
                    k.mm(bv_[0:NS, 0:128], smix[3][kc], wv[:, kc, :], start=(kc == 0), stop=(kc == 15))
                k.copy("act", Q5[:, 0, :], b[:, 0:16])
                k.copy("act", k_s, b[:, 16:32])
                k.act(sgw_s, b[:, 32:48], AF.Sigmoid, bias=P("w0", m))
                k.act(ag_s, b[:, 48:64], AF.Sigmoid, bias=P("a0", m))
                k.copy("act", gs[:, m, :], b[:, 64:80])
                k.ts("dve", kk_s, k_s, P("kk", m), None, ALU.mult)
                k.act(sqs_, kk_s, AF.Square)
                b2 = k.bank()
                k.mm(b2[:, 0:NS], blkdiag, sqs_)
                k.act(t_s, b2[:, 0:NS], AF.Ln)
                k.act(t_s, t_s, AF.Exp, scale=-0.5)
                k.tt("dve", kk_s, kk_s, t_s, ALU.mult)
                k.act(Q5[:, 1, :], sgw_s, AF.Exp, scale=DEC)
                k.ts("dve", t_s, ag_s, P("ka", m), P("omka", m), ALU.mult, ALU.add)
                k.tt("dve", Q5[:, 2, :], k_s, t_s, ALU.mult)
                k.ts("dve", Q5[:, 3, :], kk_s, -1.0, None, ALU.mult)
                k.tt("dve", Q5[:, 4, :], kk_s, ag_s, ALU.mult)
                bq = k.bank()
                for q in range(4):
                    k.tr(bq[0:NS, q * 128:(q + 1) * 128], Q5[:, q, :], identf)
                k.tr(bv_[0:NS, 128:256], Q5[:, 4, :], identf)
                k.copy("act", q4t.re("p a b -> p (a b)"), bq[0:NS, 0:512])
                k.copy("act", q2t.re("p a b -> p (a b)"), bv_[0:NS, 0:256])
                k.dma(scr6_d[0:4, :, mc].re("q r c -> r q c"), q4t)
                k.dma(scr6_d[4:6, :, mc].re("q r c -> r q c"), q2t)
            k.release(mE)
            mCh = k.mark()
            C4 = range(4)
            tok3 = [A("tok3", BF16, 384) for _ in C4]
            XA = [[A("XA", BF16, 512) for _ in range(2)] for _ in C4]
            AA = [A("AA", BF16, 512) for _ in C4]
            RQ = [A("RQ", BF16, 512) for _ in C4]
            XB = [A("XB", BF16, 512) for _ in C4]
            XC = [A("XC", BF16, 256) for _ in C4]
            Z32 = [A("Z32", F32, 256) for _ in C4]
            Zb = [A("Zb", BF16, 256) for _ in C4]
            AwT = [A("AwT", BF16, 128) for _ in C4]
            Ub = A("Ub", BF16, 128)
            tmpT = A("tmpT", F32, 64)
            y32 = [A("y32", F32, 128) for _ in C4]
            o1 = [A("o1", F32, 128) for _ in C4]
            st6L = [A("st6", F32, 12) for _ in C4]
            mvL = [A("mv", F32, 4) for _ in C4]
            rs2L = [A("rs2", F32, 2) for _ in C4]
            pwk = A("pwk", F32, 128)
            SL = [slice(cc * 128, (cc + 1) * 128) for cc in C4]
            atb = [at_b[:, cc].re("p a b -> p (a b)") for cc in C4]
            btb = [bt_b[:, cc].re("p a b -> p (a b)") for cc in C4]
            rtb = [rt_b[:, cc].re("p a b -> p (a b)") for cc in C4]
            for cc in C4:
                PBk = k.bank()
                PB = V(PBk.ap.bitcast(BF16), PBk.b)
                for i_, X_ in enumerate((at_f, bt_f, kt_f)):
                    k.tr(PB[:, i_ * 128:(i_ + 1) * 128], X_[:, SL[cc]], identb)
                k.copy("act", tok3[cc], PB[:, 0:384])
            for cc in C4:
                bA = k.bank()
                k.mm(bA[:, 0:256], at_f[:, SL[cc]], btb[cc])
                k.mm(bA[:, 256:512], bt_f[:, SL[cc]], atb[cc])
                k.tt("dve", AA[cc], bA, mA, ALU.mult)
            for cc in C4:
                bB = k.bank()
                k.mm(bB[:, 0:256], kt_f[:, SL[cc]], atb[cc])
                k.mm(bB[:, 256:512], bt_f[:, SL[cc]], rtb[cc])
                k.tt("dve", XB[cc], bB, mB, ALU.mult)
            for cc in C4:
                bC = k.bank()
                k.mm(bC[:, 0:256], kt_f[:, SL[cc]], rtb[cc])
                k.tt("dve", XC[cc], bC[:, 0:256], mC, ALU.mult)
            for cc in C4:
                bZ = k.bank()
                for h in range(2):
                    k.mm(bZ[:, h * 64:(h + 1) * 64], XB[cc][:, h * 128:(h + 1) * 128], vb[:, cc, h * 64:(h + 1) * 64])
                Z32v = Z32[cc].re("p (h a v) -> p h a v", h=2, a=2)
                k.copy("act", Z32v[:, :, 0, :], tok3[cc][:, 0:128].re("p (h v) -> p h v", v=64))
                k.copy("act", Z32v[:, :, 1, :], bZ[:, 0:128].re("p (h v) -> p h v", v=64))
                k.copy("pool", Zb[cc], Z32[cc])
            cur = 0
            for cc in C4:
                k.tt("pool", RQ[cc], AA[cc], ML[:, 0, :], ALU.mult)
                k.tt("pool", XA[cc][0], RQ[cc], I4, ALU.add)
            for l_ in range(1, 7):
                b1 = []
                for cc in C4:
                    Wc = XA[cc][cur]
                    b1_ = k.bank()
                    b1.append(b1_)
                    for h in range(2):
                        k.mm(b1_[:, h * 128:(h + 1) * 128], AA[cc][:, 256 + h * 128:256 + (h + 1) * 128], Wc[:, h * 128:(h + 1) * 128])
                    for h in range(2):
                        k.mm(b1_[:, 256 + h * 128:256 + (h + 1) * 128], AA[cc][:, h * 128:(h + 1) * 128], Wc[:, 256 + h * 128:256 + (h + 1) * 128])
                for cc in C4:
                    k.tt("dve", RQ[cc], b1[cc], ML[:, l_, :], ALU.mult)
                b2 = []
                for cc in C4:
                    Wc = XA[cc][cur]
                    b2_ = k.bank()
                    b2.append(b2_)
                    for h in range(2):
                        o_ = b2_[:, h * 128:(h + 1) * 128]
                        k.mm(o_, Wc[:, 256 + h * 128:256 + (h + 1) * 128], RQ[cc][:, h * 128:(h + 1) * 128], start=True, stop=False)
                        k.mm(o_, identb, Wc[:, h * 128:(h + 1) * 128], start=False, stop=True)
                    for h in range(2):
                        o_ = b2_[:, 256 + h * 128:256 + (h + 1) * 128]
                        k.mm(o_, Wc[:, h * 128:(h + 1) * 128], RQ[cc][:, 256 + h * 128:256 + (h + 1) * 128], start=True, stop=False)
                        k.mm(o_, identb, Wc[:, 256 + h * 128:256 + (h + 1) * 128], start=False, stop=True)
                for cc in C4:
                    k.copy("act", XA[cc][1 - cur], b2[cc])
                cur = 1 - cur
            for cc in C4:
                Wc = XA[cc][cur]
                bz = k.bank()
                for h in range(2):
                    k.mm(bz[:, h * 128:(h + 1) * 128], Wc[:, 256 + h * 128:256 + (h + 1) * 128], Zb[cc][:, h * 128:(h + 1) * 128])
                k.copy("dve", Z32[cc], bz[:, 0:256])
                k.copy("act", Zb[cc], bz[:, 0:256])
            for cc in C4:
                PBk = k.bank()
                PB = V(PBk.ap.bitcast(BF16), PBk.b)
                for h in range(2):
                    k.tr(PB[64 * h:64 * h + 64, 0:128], Zb[cc][:, h * 128:h * 128 + 64], identb)
                k.copy("act", AwT[cc], PB[:, 0:128])
            if m + 1 < 16:
                prefetch(m + 1)
            for h in range(2):
                k.copy("dve", Tblk[64 * h:64 * h + 64, h, :], T32[64 * h:64 * h + 64, m, :])
            Tflat = Tblk.re("p a b -> p (a b)")
            for cc in C4:
                sl = SL[cc]
                b_tok, k_tok = tok3[cc][:, 128:256], tok3[cc][:, 256:384]
                Z32v = Z32[cc].re("p (h a v) -> p h a v", h=2, a=2)
                bU = k.bank()
                k.mm(bU[:, 0:128], AwT[cc], Tflat)
                k.tt("dve", Ub.re("p (h v) -> p h v", v=64), bU[:, 0:128].re("p (h v) -> p h v", v=64), Z32v[:, :, 1, :], ALU.add)
                bT = k.bank()
                for h in range(2):
                    hs_ = slice(h * 64, (h + 1) * 64)
                    k.mm(bT[64 * h:64 * h + 64, 0:64], b_tok[:, hs_], Ub[:, hs_], start=True, stop=False)
                    k.mm(bT[64 * h:64 * h + 64, 0:64], k_tok[:, hs_], vb[:, cc, hs_], start=False, stop=True)
                bY = k.bank()
                k.mm(bY[:, 0:128], rt_f[:, sl], Tflat, start=True, stop=False)
                for h in range(2):
                    hs_ = slice(h * 64, (h + 1) * 64)
                    k.mm(bY[:, hs_], XB[cc][:, 256 + h * 128:256 + (h + 1) * 128], Ub[:, hs_], start=False, stop=False)
                    k.mm(bY[:, hs_], XC[cc][:, h * 128:(h + 1) * 128], vb[:, cc, hs_], start=False, stop=(h == 1))
                k.tt("dve", tmpT, bT[:, 0:64], T32[:, m, :], ALU.add)
                for h in range(2):
                    k.ts("dve", Tblk[64 * h:64 * h + 64, h, :], tmpT[64 * h:64 * h + 64, :], DL[64 * h:64 * h + 64, cc:cc + 1], None, ALU.mult)
                k.act(T32[:, m, :], tmpT, AF.Identity, scale=DL[:, cc:cc + 1])
                k.copy("act", y32[cc], bY[:, 0:128])
            for cc in C4:
                y_, st6, mv = y32[cc], st6L[cc], mvL[cc]
                for h in range(2):
                    hs_ = slice(h * 64, (h + 1) * 64)
                    k.op("dve", lambda e, h=h, hs_=hs_, y_=y_, st6=st6: e.bn_stats(out=st6.ap[:, h * 6:(h + 1) * 6], in_=y_.ap[:, hs_]),
                         [y_], [st6])
                    k.op("dve", lambda e, h=h, st6=st6, mv=mv: e.bn_aggr(out=mv.ap[:, 2 * h:2 * h + 2], in_=st6.ap[:, h * 6:(h + 1) * 6]),
                         [st6], [mv])
            for cc in C4:
                mvv = mvL[cc].re("p (h t) -> p h t", t=2)
                k.act(rs2L[cc], mvv[:, :, 1], AF.Ln, bias=LNX_EPS)
            for cc in C4:
                k.act(rs2L[cc], rs2L[cc], AF.Exp, scale=-0.5)
            for cc in C4:
                for h in range(2):
                    hs_ = slice(h * 64, (h + 1) * 64)
                    k.ts("pool", o1[cc][:, hs_], y32[cc][:, hs_], mvL[cc][:, 2 * h:2 * h + 1], rs2L[cc][:, h:h + 1], ALU.subtract, ALU.mult)
                k.tt("pool", o1[cc], o1[cc], Gbc, ALU.mult)
                k.tt("pool", o1[cc], o1[cc], Bbc, ALU.add)
            for cc in C4:
                for h in range(2):
                    hs_ = slice(h * 64, (h + 1) * 64)
                    k.stt(o1[cc][:, hs_], v32[:, cc, hs_], s_tok[:, cc * 2 + h:cc * 2 + h + 1], o1[cc][:, hs_], ALU.mult, ALU.add)
            bOs = []
            for cc in C4:
                bO = k.bank()
                bOs.append(bO)
                k.tr(bO[:, 0:128], o1[cc], identf)
            for cc in C4:
                k.tt("dve", yg[m][:, SL[cc]], bOs[cc][:, 0:128], gTb[:, SL[cc]], ALU.mult)
            if last:
                b = k.bank()
                k.tr(b[0:64, 0:128], T32[:, m, :], identf)
                k.copy("act", pwk[0:64, :], b[0:64, 0:128])
                k.dma(pwkv_d[2 * m:2 * m + 2].re("h v k -> v h k"), pwk[0:64, :].re("p (h k) -> p h k", k=64))
            k.release(mCh)
        k.release(m2)
        ckpt(3)

        if do_s:
            m3 = k.mark()
            Grh = k.alloc([128, 64], F32, "Grh")
            Brh = k.alloc([128, 64], F32, "Brh")
            RKrh = k.alloc([128, 64], F32, "RKrh")
            for r4 in range(4):
                k.dma(Grh[32 * r4:32 * r4 + 32, :], rows_d[0].re("(h v) -> h v", v=64))
                k.dma(Brh[32 * r4:32 * r4 + 32, :], rows_d[1].re("(h v) -> h v", v=64))
                k.dma(RKrh[32 * r4:32 * r4 + 32, :], rows_d[2].re("(h v) -> h v", v=64))
            q6 = k.alloc([128, 6, 64], F32, "q6")
            S = k.alloc([128, 32, 64], F32, "S")
            tmp = k.alloc([128, 32, 64], F32, "Stmp")
            sa = k.alloc([128, 32], F32, "sa")
            yrh = k.alloc([128, 64], F32, "yrh")
            orh = k.alloc([128, 64], F32, "orh")
            t64 = k.alloc([128, 64], F32, "t64")
            s1 = k.alloc([128, 1], F32, "s1")
            st6b = k.alloc([128, 6], F32, "st6b")
            mvb = k.alloc([128, 2], F32, "mvb")
            rsb = k.alloc([128, 1], F32, "rsb")
            B3 = [128, 32, 64]
            for i in range(4):
                k.dma(q6, scr6_d[:, 4 * i:4 * i + 4, :].re("q r (h k) -> (r h) q k", k=64))
                r_, d_, k_, a_, v_, b_ = [q6[:, q, :] for q in range(6)]
                for vh in range(2):
                    vs = slice(vh * 32, (vh + 1) * 32)
                    k.dma(S, stw_d[i * 128:(i + 1) * 128, vh * 2048:(vh + 1) * 2048].re("p (v k) -> p v k", k=64))
                    k.tt("dve", tmp, S, a_.us(1).bc(B3), ALU.mult)
                    k.op("dve", lambda e: e.tensor_reduce(out=sa.ap, in_=tmp.ap, axis=AX.X, op=ALU.add), [tmp], [sa])
                    k.tt("dve", S, S, d_.us(1).bc(B3), ALU.mult)
                    k.tt("dve", tmp, sa.us(2).bc(B3), b_.us(1).bc(B3), ALU.mult)
                    k.tt("dve", S, S, tmp, ALU.add)
                    k.tt("dve", tmp, v_[:, vs].us(2).bc(B3), k_.us(1).bc(B3), ALU.mult)
                    k.tt("dve", S, S, tmp, ALU.add)
                    k.dma(swkv_d[i * 128:(i + 1) * 128, vh * 2048:(vh + 1) * 2048].re("p (v k) -> p v k", k=64), S)
                    k.tt("dve", tmp, S, r_.us(1).bc(B3), ALU.mult)
                    k.op("dve", lambda e, vs=vs: e.tensor_reduce(out=yrh.ap[:, vs], in_=tmp.ap, axis=AX.X, op=ALU.add),
                         [tmp], [yrh])
                k.op("dve", lambda e: e.bn_stats(out=st6b.ap, in_=yrh.ap), [yrh], [st6b])
                k.op("dve", lambda e: e.bn_aggr(out=mvb.ap, in_=st6b.ap), [st6b], [mvb])
                k.act(rsb, mvb[:, 1:2], AF.Ln, bias=LNX_EPS)
                k.act(rsb, rsb, AF.Exp, scale=-0.5)
                k.ts("dve", orh, yrh, mvb[:, 0:1], rsb, ALU.subtract, ALU.mult)
                k.tt("dve", orh, orh, Grh, ALU.mult)
                k.tt("dve", orh, orh, Brh, ALU.add)
                k.tt("dve", t64, r_, k_, ALU.mult)
                k.tt("dve", t64, t64, RKrh, ALU.mult)
                k.op("dve", lambda e: e.tensor_reduce(out=s1.ap, in_=t64.ap, axis=AX.X, op=ALU.add), [t64], [s1])
                k.stt(orh, v_, s1, orh, ALU.mult, ALU.add)
                k.dma(scro_d[4 * i:4 * i + 4, :].re("r (h v) -> (r h) v", v=64), orh)
            of = k.alloc([128, NS], F32, "of")
            for m in range(16):
                tok2fm(of, scro_d[:, m * 128:(m + 1) * 128], NS)
                k.tt("dve", ygs[m], of, gs[:, m, :], ALU.mult)
            k.release(m3)

        ckpt(4)
        proj_post(L, W["rw_wo"], 16, yg, ygs if do_s else None, 2, 256)
        ckpt(5)
        k.release(mtop)

    def ffn_layer(L, seg, do_s, last):
        mtop = k.mark()
        xw = [k.alloc([128, WB_ELEMS], BF16, f"xwb{i}") for i in range(2)]
        z = [k.alloc([128, TS], BF16, f"z{f}") for f in range(NF)]
        zs = [k.alloc([128, NS], BF16, f"zs{f}") for f in range(NF)] if do_s else None
        m1 = k.mark()
        h2 = [k.alloc([128, TS], BF16, f"h2_{c}") for c in range(16)]
        h2s = [k.alloc([128, NS], BF16, f"h2s_{c}") for c in range(16)] if do_s else None
        m2 = k.mark()
        x = load_x()
        rstd = rstd_of(x, TS)
        tmpn = k.alloc([128, TS], F32, "tmpn")
        for c in range(16):
            adaln_chunk(h2[c], tmpn, x[c], rstd, modv(L, 4, c, False), modv(L, 3, c, False), False)
        if do_s:
            rstds = rstd_of(xsT, NS)
            tmps = k.alloc([128, NS], F32, "tmps")
            for c in range(16):
                adaln_chunk(h2s[c], tmps, xsT[c], rstds, modv(L, 4, c, True), modv(L, 3, c, True), True)
        k.release(m2)
        gt = [k.alloc([128, 2 + TS], F32, "gt") for _ in range(2)]
        t1 = [k.alloc([128, TS], F32, "t1") for _ in range(2)]
        if do_s:
            cs = k.alloc([128, 32], F32, "cs")
            gts = k.alloc([128, NS], F32, "gts")
            t1s = k.alloc([128, NS], F32, "t1s")
            o2 = k.alloc([128, NS, 2], F32, "o2")
            tk = k.alloc([32, 128], F32, "tk")
            tk2 = k.alloc([32, 128], F32, "tk2")
        reqs = []
        for g in range(22):
            reqs.append([(W["ffn_w_gate"][L][:, g * 256:(g + 1) * 256], 16)])
            reqs.append([(W["ffn_w_up"][L][:, g * 256:(g + 1) * 256], 16)])
        ws = WS(reqs, xw)
        for g in range(22):
            (wg,) = ws.get(2 * g)
            (wu,) = ws.get(2 * g + 1)
            for j in range(2):
                f = 2 * g + j
                fc = slice(f * 128, (f + 1) * 128)
                js = slice(j * 128, (j + 1) * 128)
                bg = k.bank()
                for kc in range(16):
                    k.mm(bg, wg[:, kc, js], h2[kc], start=(kc == 0), stop=(kc == 15))
                bu = k.bank()
                for kc in range(16):
                    k.mm(bu, wu[:, kc, js], h2[kc], start=(kc == 0), stop=(kc == 15))
                g_ = gt[f % 2]
                t_ = t1[f % 2]
                k.copy("dve", g_[:, 0:2], fcar[L][:, :, f])
                k.copy("act", g_[:, 2:2 + TS], bg)
                k.copy("dve", fcar[L][:, :, f], g_[:, TS:TS + 2])
                k.ts("dve", t_, g_[:, 0:TS], P(f"fcw{L}0", f), P(f"fcb{L}", f), ALU.mult, ALU.add)
                k.stt(t_, g_[:, 1:1 + TS], P(f"fcw{L}1", f), t_, ALU.mult, ALU.add)
                k.stt(t_, g_[:, 2:2 + TS], P(f"fcw{L}2", f), t_, ALU.mult, ALU.add)
                k.act(t_, t_, AF.Gelu_apprx_tanh)
                k.tt("dve", z[f], bu, t_, ALU.mult)
                if do_s:
                    bs_ = k.bank()
                    for kc in range(16):
                        k.mm(bs_[:, 0:16], wg[:, kc, js], h2s[kc], start=(kc == 0), stop=(kc == 15))
                    for kc in range(16):
                        k.mm(bs_[:, 16:32], wu[:, kc, js], h2s[kc], start=(kc == 0), stop=(kc == 15))
                    k.dma(tk, stfc_d[L][:, :, fc].re("r j c -> (r j) c"))
                    b3 = k.bank()
                    k.tr(b3[:, 0:32], tk, identf[0:32, 0:32])
                    k.copy("act", cs, b3[:, 0:32])
                    csv = cs.re("p (r j) -> p r j", j=2)
                    k.copy("dve", gts, bs_[:, 0:16])
                    k.ts("dve", t1s, csv[:, :, 0], P(f"fcw{L}0", f), P(f"fcb{L}", f), ALU.mult, ALU.add)
                    k.stt(t1s, csv[:, :, 1], P(f"fcw{L}1", f), t1s, ALU.mult, ALU.add)
                    k.stt(t1s, gts, P(f"fcw{L}2", f), t1s, ALU.mult, ALU.add)
                    k.act(t1s, t1s, AF.Gelu_apprx_tanh)
                    k.tt("dve", zs[f], bs_[:, 16:32], t1s, ALU.mult)
                    k.copy("dve", o2[:, :, 0], csv[:, :, 1])
                    k.copy("dve", o2[:, :, 1], gts)
                    b4 = k.bank()
                    k.tr(b4[0:32, 0:128], o2.re("p r j -> p (r j)"), identf)
                    k.copy("act", tk2, b4[0:32, 0:128])
                    k.dma(sfc_d[L][:, :, fc].re("r j c -> (r j) c"), tk2)
        if last:
            t88 = k.alloc([88, 128], F32, "t88")
            b = k.bank()
            k.tr(b[0:88, 0:128], fcar[L].re("p j f -> p (j f)"), identf)
            k.copy("act", t88, b[0:88, 0:128])
            for j in range(2):
                k.dma(pfc_d[L][j].re("(f p) -> f p", p=128), t88[j * NF:(j + 1) * NF, :])
        k.release(m1)
        proj_post(L, W["ffn_w_down"][L], NF, z, zs, 5, 128, xw)
        k.release(mtop)

    def lru_layer(seg, do_s, last):
        L = 1
        mtop = k.mark()
        xw = [k.alloc([128, WB_ELEMS], BF16, f"xwb{i}") for i in range(3)]
        og = [k.alloc([128, TS], BF16, f"og{c}") for c in range(16)]
        ogs = [k.alloc([128, NS], BF16, f"ogs{c}") for c in range(16)] if do_s else None
        mL = k.mark()
        h = [k.alloc([128, TS], BF16, f"h_{c}") for c in range(16)]
        hs_l = [k.alloc([128, NS], BF16, f"hs_{c}") for c in range(16)] if do_s else None
        m2 = k.mark()
        x = load_x()
        rstd = rstd_of(x, TS)
        tmpn = k.alloc([128, TS], F32, "tmpn")
        for c in range(16):
            adaln_chunk(h[c], tmpn, x[c], rstd, modv(L, 1, c, False), modv(L, 0, c, False), False)
        if do_s:
            rstds = rstd_of(xsT, NS)
            tmps = k.alloc([128, NS], F32, "tmps")
            for c in range(16):
                adaln_chunk(hs_l[c], tmps, xsT[c], rstds, modv(L, 1, c, True), modv(L, 0, c, True), True)
        k.release(m2)
        wab = k.alloc([128, 8, 2, 256], BF16, "wab")
        wxb = k.alloc([128, 8, 2, 256], BF16, "wxb")
        k.dma(wab, W["lru_wa"].re("n (a p) c -> p n a c", p=128), eng="pool")
        k.dma(wxb, W["lru_wx"].re("n (a p) c -> p n a c", p=128), eng="pool")
        ybr = [k.alloc([128, TS], BF16, f"ybr{j}") for j in range(2)]
        xb32 = [k.alloc([128, 3 + TS], F32, f"xb32{j}") for j in range(2)]
        xc32 = [k.alloc([128, TS], F32, f"xc32{j}") for j in range(2)]
        xcb = [k.alloc([128, TS], BF16, f"xcb{j}") for j in range(2)]
        aa, mu, bt, hsn = [k.alloc([128, TS], F32, n) for n in ("aa", "mu", "bt", "hsn")]
        gA = [k.alloc([128, TS], F32, f"gA{j}") for j in range(2)]
        gX = [k.alloc([128, TS], F32, f"gX{j}") for j in range(2)]
        if do_s:
            ybrs = [k.alloc([128, NS], BF16, f"ybrs{j}") for j in range(2)]
            xbs = [k.alloc([128, NS], F32, f"xbs{j}") for j in range(2)]
            xcs = [k.alloc([128, NS], F32, f"xcs{j}") for j in range(2)]
            xcbs = [k.alloc([128, NS], BF16, f"xcbs{j}") for j in range(2)]
            csl = k.alloc([128, 48], F32, "csl")
            h0s = k.alloc([128, NS], F32, "h0s")
            o3 = k.alloc([128, NS, 3], F32, "o3")
            tk3 = k.alloc([48, 128], F32, "tk3")
            tk4 = k.alloc([48, 128], F32, "tk4")
            aas, mus, bts, hns = [k.alloc([128, NS], F32, n) for n in ("aas", "mus", "bts", "hns")]
            gAs = [k.alloc([128, NS], F32, f"gAs{j}") for j in range(2)]
            gXs = [k.alloc([128, NS], F32, f"gXs{j}") for j in range(2)]
        reqs = []
        for n in range(8):
            reqs.append([(W["lru_w_in"][:, n * 256:(n + 1) * 256], 16)])
            reqs.append([(W["lru_w_in"][:, D + n * 256:D + (n + 1) * 256], 16)])
        ws = WS(reqs, xw)
        for n in range(8):
            (wy,) = ws.get(2 * n)
            (wx_,) = ws.get(2 * n + 1)
            for j in range(2):
                c = 2 * n + j
                js = slice(j * 128, (j + 1) * 128)
                cc_ = slice(c * 128, (c + 1) * 128)
                b = k.bank()
                for kc in range(16):
                    k.mm(b, wy[:, kc, js], h[kc], start=(kc == 0), stop=(kc == 15))
                k.act(ybr[j], b, AF.Gelu_apprx_tanh, bias=P("bin", c))
                b = k.bank()
                for kc in range(16):
                    k.mm(b, wx_[:, kc, js], h[kc], start=(kc == 0), stop=(kc == 15))
                xb_ = xb32[j]
                k.copy("dve", xb_[:, 0:3], lcar[:, :, c])
                k.act(xb_[:, 3:3 + TS], b, AF.Identity, bias=P("bin", 16 + c))
                k.copy("dve", lcar[:, :, c], xb_[:, TS:TS + 3])
                k.ts("dve", xc32[j], xb_[:, 0:TS], P("lcw0", c), P("lcb", c), ALU.mult, ALU.add)
                for q in range(1, 4):
                    k.stt(xc32[j], xb_[:, q:q + TS], P(f"lcw{q}", c), xc32[j], ALU.mult, ALU.add)
                k.copy("act", xcb[j], xc32[j])
                if do_s:
                    b = k.bank()
                    for kc in range(16):
                        k.mm(b[:, 0:16], wy[:, kc, js], hs_l[kc], start=(kc == 0), stop=(kc == 15))
                    for kc in range(16):
                        k.mm(b[:, 16:32], wx_[:, kc, js], hs_l[kc], start=(kc == 0), stop=(kc == 15))
                    k.act(ybrs[j], b[:, 0:16], AF.Gelu_apprx_tanh, bias=P("bin", c))
                    k.act(xbs[j], b[:, 16:32], AF.Identity, bias=P("bin", 16 + c))
                    k.dma(tk3, stlc_d[:, :, cc_].re("r j c -> (r j) c"))
                    b3 = k.bank()
                    k.tr(b3[:, 0:48], tk3, identf[0:48, 0:48])
                    k.copy("act", csl, b3[:, 0:48])
                    cv = csl.re("p (r j) -> p r j", j=3)
                    k.ts("dve", xcs[j], cv[:, :, 0], P("lcw0", c), P("lcb", c), ALU.mult, ALU.add)
                    k.stt(xcs[j], cv[:, :, 1], P("lcw1", c), xcs[j], ALU.mult, ALU.add)
                    k.stt(xcs[j], cv[:, :, 2], P("lcw2", c), xcs[j], ALU.mult, ALU.add)
                    k.stt(xcs[j], xbs[j], P("lcw3", c), xcs[j], ALU.mult, ALU.add)
                    k.copy("act", xcbs[j], xcs[j])
                    k.copy("dve", o3[:, :, 0], cv[:, :, 1])
                    k.copy("dve", o3[:, :, 1], cv[:, :, 2])
                    k.copy("dve", o3[:, :, 2], xbs[j])
                    b4 = k.bank()
                    k.tr(b4[0:48, 0:128], o3.re("p r j -> p (r j)"), identf)
                    k.copy("act", tk4, b4[0:48, 0:128])
                    k.dma(slc_d[:, :, cc_].re("r j c -> (r j) c"), tk4)
            LN_HALF = -0.6931471805599453
            for jo in range(2):
                c = 2 * n + jo
                jos = slice(jo * 128, (jo + 1) * 128)
                ba_ = k.bank()
                for ji in range(2):
                    k.mm(ba_, wab[:, n, ji, jos], xcb[ji], start=(ji == 0), stop=(ji == 1))
                bx_ = k.bank()
                for ji in range(2):
                    k.mm(bx_, wxb[:, n, ji, jos], xcb[ji], start=(ji == 0), stop=(ji == 1))
                k.act(gA[jo], ba_, AF.Tanh, bias=P("hba", c), scale=0.5)
                k.act(gX[jo], bx_, AF.Tanh, bias=P("hbx", c), scale=0.5)
                if do_s:
                    bs_ = k.bank()
                    for ji in range(2):
                        k.mm(bs_[:, 0:16], wab[:, n, ji, jos], xcbs[ji], start=(ji == 0), stop=(ji == 1))
                    for ji in range(2):
                        k.mm(bs_[:, 16:32], wxb[:, n, ji, jos], xcbs[ji], start=(ji == 0), stop=(ji == 1))
                    k.act(gAs[jo], bs_[:, 0:16], AF.Tanh, bias=P("hba", c), scale=0.5)
                    k.act(gXs[jo], bs_[:, 16:32], AF.Tanh, bias=P("hbx", c), scale=0.5)
            for jo in range(2):
                c = 2 * n + jo
                cc_ = slice(c * 128, (c + 1) * 128)
                k.act(aa, gA[jo], AF.Exp, bias=P("hcL", c), scale=P("hcL", c))
                k.tt("dve", mu, aa, aa, ALU.mult)
                k.act(mu, mu, AF.Ln, bias=1.0, scale=-1.0)
                k.act(mu, mu, AF.Exp, bias=LN_HALF, scale=0.5)
                k.stt(bt, gX[jo], 1.0, xc32[jo], ALU.add, ALU.mult)
                k.tt("dve", mu, mu, bt, ALU.mult)
                if seg == 0:
                    k.ts("dve", mu[:, 0:1], bt[:, 0:1], 0.5, None, ALU.mult)
                k.op("dve", lambda e, c=c: e.tensor_tensor_scan(out=hsn.ap, data0=aa.ap, data1=mu.ap,
                                                                initial=hcar.ap[:, c:c + 1], op0=ALU.mult, op1=ALU.add),
                     [aa, mu, hcar], [hsn])
                k.copy("dve", hcar[:, c:c + 1], hsn[:, TS - 1:TS])
                k.tt("dve", og[c], hsn, ybr[jo], ALU.mult)
                if do_s:
                    k.act(aas, gAs[jo], AF.Exp, bias=P("hcL", c), scale=P("hcL", c))
                    k.tt("dve", mus, aas, aas, ALU.mult)
                    k.act(mus, mus, AF.Ln, bias=1.0, scale=-1.0)
                    k.act(mus, mus, AF.Exp, bias=LN_HALF, scale=0.5)
                    k.stt(bts, gXs[jo], 1.0, xcs[jo], ALU.add, ALU.mult)
                    k.tt("dve", mus, mus, bts, ALU.mult)
                    tok2fm(h0s, sth_d[:, cc_], NS)
                    k.tt("dve", hns, aas, h0s, ALU.mult)
                    k.tt("dve", hns, hns, mus, ALU.add)
                    fm2tok(sh_d[:, cc_], hns, NS)
                    k.tt("dve", ogs[c], hns, ybrs[jo], ALU.mult)
        if last:
            fm2tok(ph_d, hcar, 16)
            t48 = k.alloc([48, 128], F32, "t48")
            b = k.bank()
            k.tr(b[0:48, 0:128], lcar.re("p j c -> p (j c)"), identf)
            k.copy("act", t48, b[0:48, 0:128])
            for j in range(3):
                k.dma(plc_d[j].re("(c p) -> c p", p=128), t48[j * 16:(j + 1) * 16, :])
        k.release(mL)
        proj_post(L, W["lru_w_out"], 16, og, ogs, 2, 256, xw)
        k.release(mtop)

    try:
        compute_mod()
        ckpt(1)
        for c in range(16):
            tok2fm(xsT[c], xs_d[:, c * 128:(c + 1) * 128], NS)
        ckpt(10)
        for seg in range(NSEG):
            do_s = (seg == 0)
            last = (seg == NSEG - 1)
            rwkv_layer(seg, do_s, last)
            ffn_layer(0, seg, do_s, last)
            ckpt(6)
            lru_layer(seg, do_s, last)
            ckpt(7)
            ffn_layer(1, seg, do_s, last)
            m = k.mark()
            x = load_x()
            yt = [k.alloc([128, D], F32, "yt") for _ in range(2)]
            for t4 in range(4):
                y_ = yt[t4 % 2]
                for c4 in range(4):
                    b = k.bank()
                    for q in range(4):
                        c = c4 * 4 + q
                        k.tr(b[:, q * 128:(q + 1) * 128], x[c][:, t4 * 128:(t4 + 1) * 128], identf)
                    ev(y_[:, c4 * 512:(c4 + 1) * 512], b)
                k.dma(yp_d[seg * TS + t4 * 128:seg * TS + (t4 + 1) * 128, :], y_)
            k.release(m)
            if last:
                fm2tok(pshift_d, shift_car, 16)
        for c in range(16):
            fm2tok(ys_d[:, c * 128:(c + 1) * 128], xsT[c], NS)

    except StopBuild:
        pass
    nc = k.emit()
    return nc, k


def _fm(v):
    v = np.asarray(v, np.float32).reshape(-1, 128)
    return np.ascontiguousarray(v.T)


def make_pvec(I):
    pv = np.zeros((128, PV_N), np.float32)

    def put(name, v):
        o, w = PV_OFF[name]
        pv[:, o:o + w] = _fm(v)
    for i in range(2):
        for j in range(4):
            put(f"ng{i}{j}", I["norm_g"][i, j])
        put(f"bmod{i}", I["b_mod"][i])
    for j in range(6):
        put(f"mix{j}", I["rw_mix"][0, j])
    put("w0", I["rw_w0"][0]); put("a0", I["rw_a0"][0]); put("kk", I["rw_kk"][0]); put("ka", I["rw_ka"][0])
    put("rk", I["rw_rk"][0].reshape(-1)); put("lcb", I["lru_conv_b"][0]); put("ba", I["lru_ba"][0])
    put("bx", I["lru_bx"][0]); put("lam", I["lru_lambda"][0]); put("bin", I["lru_b_in"][0])
    for j in range(4):
        put(f"lcw{j}", I["lru_conv_w"][0, j])
    for i in range(2):
        for j in range(3):
            put(f"fcw{i}{j}", I["ffn_conv_w"][i, j])
        put(f"fcb{i}", I["ffn_conv_b"][i])
    return pv


def make_cmask():
    t = np.arange(128)[:, None]
    c = np.arange(128)[None, :]
    out = np.zeros((128, 7, 512), np.float32)
    for l in range(7):
        b = 1 << l
        M = ((t // (2 * b) == c // (2 * b)) & (t % (2 * b) >= b) & (c % (2 * b) < b)).astype(np.float32)
        out[:, l, 0:128] = M
        out[:, l, 128:256] = M
        out[:, l, 256:384] = M.T
        out[:, l, 384:512] = M.T
    return out


_CACHE = {}


def kernel(**I):
    I = {k_: np.asarray(v) for k_, v in I.items()}
    B, S, _ = I["x_prompt"].shape
    NSEG = S // TS
    if NSEG not in _CACHE:
        _CACHE[NSEG] = build(NSEG)[0]
    nc = _CACHE[NSEG]
    pv = make_pvec(I)
    rows = np.ascontiguousarray(np.stack([I["rw_lnx_g"][0], I["rw_lnx_b"][0], I["rw_rk"][0].reshape(-1)]).astype(np.float32))
    shared = {
        "pvec": pv, "rows": rows, "cmask": make_cmask(),
        "w_mod": I["w_mod"], "rw_wr": I["rw_wr"][0], "rw_wk": I["rw_wk"][0], "rw_wv": I["rw_wv"][0], "rw_wo": I["rw_wo"][0],
        "rw_w1": I["rw_w1"][0], "rw_w2": I["rw_w2"][0], "rw_a1": I["rw_a1"][0], "rw_a2": I["rw_a2"][0],
        "rw_g1": I["rw_g1"][0], "rw_g2": I["rw_g2"][0], "lru_w_in": I["lru_w_in"][0], "lru_wa": I["lru_wa"][0],
        "lru_wx": I["lru_wx"][0], "lru_w_out": I["lru_w_out"][0], "ffn_w_gate": I["ffn_w_gate"],
        "ffn_w_up": I["ffn_w_up"], "ffn_w_down": I["ffn_w_down"],
    }
    shared = {k_: np.ascontiguousarray(v, dtype=np.float32) for k_, v in shared.items()}
    in_maps = []
    for c in range(8):
        b = c % B
        r = slice(NS * c, NS * (c + 1))
        d = dict(shared)
        d["xp"] = np.ascontiguousarray(I["x_prompt"][b])
        d["xs"] = np.ascontiguousarray(I["x_sample"][r, 0])
        d["call"] = np.ascontiguousarray(np.concatenate([I["c_prompt"][b:b + 1], I["c_sample"][r]], 0))
        d["st_wkv"] = np.ascontiguousarray(I["state_rwkv_wkv"][0, r].reshape(NS * NH, 4096))
        d["st_shift"] = np.ascontiguousarray(I["state_rwkv_shift"][0, r])
        d["st_h"] = np.ascontiguousarray(I["state_lru_h"][0, r])
        d["st_lconv"] = np.ascontiguousarray(I["state_lru_conv"][0, r])
        d["st_fconv"] = np.ascontiguousarray(I["state_ffn_conv"][:, r])
        in_maps.append(d)
    res = run_bass_kernel_spmd(nc, in_maps, core_ids=list(range(8)))
    R = res.results
    f32 = np.float32
    y_prompt = np.stack([R[b]["y_p"] for b in range(B)]).astype(f32)
    y_sample = np.concatenate([R[c]["y_s"] for c in range(8)], 0).reshape(8 * NS, 1, D).astype(f32)
    p_wkv = np.stack([R[b]["p_wkv"] for b in range(B)])[None].astype(f32)
    p_shift = np.stack([R[b]["p_shift"].reshape(D) for b in range(B)])[None].astype(f32)
    p_h = np.stack([R[b]["p_h"].reshape(D) for b in range(B)])[None].astype(f32)
    p_lconv = np.stack([R[b]["p_lconv"] for b in range(B)])[None].astype(f32)
    p_fconv = np.stack([R[b]["p_fconv"] for b in range(B)], 1).astype(f32)
    s_wkv = np.concatenate([R[c]["s_wkv"].reshape(NS, NH, 64, 64) for c in range(8)], 0)[None].astype(f32)
    s_shift = np.concatenate([R[c]["s_shift"] for c in range(8)], 0)[None].astype(f32)
    s_h = np.concatenate([R[c]["s_h"] for c in range(8)], 0)[None].astype(f32)
    s_lconv = np.concatenate([R[c]["s_lconv"] for c in range(8)], 0)[None].astype(f32)
    s_fconv = np.concatenate([R[c]["s_fconv"] for c in range(8)], 1).astype(f32)
    return (y_prompt, y_sample, p_wkv, p_shift, p_h, p_lconv, p_fconv, s_wkv, s_shift, s_h, s_lconv, s_fconv)
```

```python
import numpy as np
import concourse.bass as bass
import concourse.mybir as mybir
from concourse.bass_utils import run_bass_kernel_spmd

F32 = mybir.dt.float32
BF16 = mybir.dt.bfloat16
AF = mybir.ActivationFunctionType
ALU = mybir.AluOpType
AX = mybir.AxisListType

D = 2048
NCH = 16
DFF = 5632
NF = 44
TS = 512
NS = 16
NH = 32
EPS = 1e-6
LNX_EPS = 64e-5
DEC = -0.6065306597126334
WB_ELEMS = 6144
NWB = 3


class Buf:
    __slots__ = ("name", "lw", "wd", "rd", "rdd", "alias", "psum", "notrack")

    def __init__(self, name):
        self.name = name
        self.lw = None
        self.wd = []
        self.rd = {}
        self.rdd = []
        self.alias = []
        self.psum = False
        self.notrack = False


class V:
    __slots__ = ("ap", "b")

    def __init__(self, ap, b):
        self.ap = ap
        self.b = b

    def __getitem__(self, k):
        return V(self.ap[k], self.b)

    def re(self, s, **kw):
        return V(self.ap.rearrange(s, **kw), self.b)

    def bc(self, shape):
        return V(self.ap.to_broadcast(list(shape)), self.b)

    def us(self, ax):
        return V(self.ap.unsqueeze(ax), self.b)


class Op:
    __slots__ = ("eng", "fn", "reads", "writes", "dma", "waits", "sig", "cnt", "slot", "slotcnt", "prewait")

    def __init__(self, eng, fn, reads, writes, dma):
        self.eng = eng
        self.fn = fn
        self.reads = reads
        self.writes = writes
        self.dma = dma
        self.waits = []
        self.sig = False
        self.cnt = 0


NSLOT = 16
ARENA_WORDS = 53150


class KB:
    ENGS = ("pe", "dve", "act", "pool", "sp")

    def __init__(self):
        self.nc = bass.Bass("TRN2", target_bir_lowering=False)
        self.ops = []
        self._ctx = []
        g = self.nc.sbuf_tensor("arena", [128, ARENA_WORDS], F32)
        self.arena = g.__enter__()[:]
        self._ctx.append(g)
        self.top = 0
        self.live = []
        self.dead = []
        self.banks = []
        for i in range(8):
            g = self.nc.psum_tensor(f"bank{i}", [128, 512], F32)
            self.banks.append(V(g.__enter__()[:], Buf(f"bank{i}")))
            self.banks[-1].b.psum = True
            self._ctx.append(g)
        self.reserved = set()
        self.bi = 0
        self.maxtop = 0

    def alloc(self, shape, dt=F32, name="t"):
        n = 1
        for s in shape[1:]:
            n *= s
        words = n if dt == F32 else (n + 1) // 2
        words = (words + 1) // 2 * 2
        st = self.top
        en = st + words
        assert en <= ARENA_WORDS, f"arena overflow {en}"
        self.top = en
        self.maxtop = max(self.maxtop, en)
        ap = self.arena[:, st:en]
        if dt != F32:
            ap = ap.bitcast(dt)
        ap = ap[0:shape[0], 0:n]
        if len(shape) == 3:
            ap = ap.rearrange("p (a b) -> p a b", b=shape[2])
        elif len(shape) == 4:
            ap = ap.rearrange("p (a b c) -> p a b c", b=shape[2], c=shape[3])
        b = Buf(name)
        keep = []
        for (s0, e0, ob) in self.dead:
            if s0 < en and st < e0:
                b.alias.append(ob)
                if s0 < st:
                    keep.append((s0, st, ob))
                if e0 > en:
                    keep.append((en, e0, ob))
            else:
                keep.append((s0, e0, ob))
        self.dead = keep
        self.live.append((st, en, b))
        return V(ap, b)

    def mark(self):
        return (self.top, len(self.live))

    def release(self, m):
        top, nl = m
        for ent in self.live[nl:]:
            self.dead.append(ent)
        del self.live[nl:]
        self.top = top

    def bank(self):
        for _ in range(8):
            i = self.bi % 8
            self.bi += 1
            if i not in self.reserved:
                return self.banks[i]
        raise RuntimeError("no bank")

    def reserve(self):
        for _ in range(8):
            i = self.bi % 8
            self.bi += 1
            if i not in self.reserved:
                self.reserved.add(i)
                return self.banks[i]
        raise RuntimeError("no bank")

    def unreserve(self, b):
        for i, x in enumerate(self.banks):
            if x.b is b.b:
                self.reserved.discard(i)

    def dram(self, name, shape, dt=F32, kind="Internal"):
        t = self.nc.dram_tensor(name, list(shape), dt, kind=kind)
        return V(t.ap(), Buf(name))

    def op(self, eng, fn, reads=(), writes=(), dma=False):
        o = Op(eng, fn, [r.b for r in reads], [w.b for w in writes], dma)
        self.ops.append(o)
        return o

    def dma(self, out, in_, eng="sp"):
        return self.op(eng, lambda e: e.dma_start(out=out.ap, in_=in_.ap), [in_], [out], dma=True)

    def mm(self, out, lhsT, rhs, start=True, stop=True):
        return self.op("pe", lambda e: e.matmul(out=out.ap, lhsT=lhsT.ap, rhs=rhs.ap, start=start, stop=stop),
                       [lhsT, rhs], [out])

    def tr(self, out, in_, ident):
        return self.op("pe", lambda e: e.transpose(out=out.ap, in_=in_.ap, identity=ident.ap), [in_, ident], [out])

    def act(self, out, in_, func, bias=None, scale=None):
        reads = [in_] + [x for x in (bias, scale) if isinstance(x, V)]
        kw = {}
        if bias is not None:
            kw["bias"] = bias.ap if isinstance(bias, V) else float(bias)
        if scale is not None:
            kw["scale"] = scale.ap if isinstance(scale, V) else float(scale)
        return self.op("act", lambda e: e.activation(out=out.ap, in_=in_.ap, func=func, **kw), reads, [out])

    def copy(self, eng, out, in_):
        if eng == "act":
            return self.op("act", lambda e: e.copy(out=out.ap, in_=in_.ap), [in_], [out])
        return self.op(eng, lambda e: e.tensor_copy(out=out.ap, in_=in_.ap), [in_], [out])

    def tt(self, eng, out, a, b, op):
        return self.op(eng, lambda e: e.tensor_tensor(out=out.ap, in0=a.ap, in1=b.ap, op=op), [a, b], [out])

    def ts(self, eng, out, a, s1, s2, op0, op1=None):
        reads = [a] + [x for x in (s1, s2) if isinstance(x, V)]
        v1 = s1.ap if isinstance(s1, V) else float(s1)
        v2 = None if s2 is None else (s2.ap if isinstance(s2, V) else float(s2))
        if op1 is None:
            return self.op(eng, lambda e: e.tensor_scalar(out=out.ap, in0=a.ap, scalar1=v1, scalar2=None, op0=op0),
                           reads, [out])
        return self.op(eng, lambda e: e.tensor_scalar(out=out.ap, in0=a.ap, scalar1=v1, scalar2=v2, op0=op0, op1=op1),
                       reads, [out])

    def stt(self, out, a, s, b, op0, op1):
        reads = [a, b] + ([s] if isinstance(s, V) else [])
        sv = s.ap if isinstance(s, V) else float(s)
        return self.op("dve", lambda e: e.scalar_tensor_tensor(out=out.ap, in0=a.ap, scalar=sv, in1=b.ap, op0=op0, op1=op1),
                       reads, [out])

    def memset(self, eng, out, val):
        return self.op(eng, lambda e: e.memset(out.ap, val), [], [out])

    def finalize(self):
        ops = self.ops
        for i, o in enumerate(ops):
            deps = set()
            for b in o.reads:
                if b.lw is not None:
                    deps.add(b.lw)
                deps.update(b.wd)
                if b.psum:
                    for e, j in b.rd.items():
                        if e != o.eng:
                            deps.add(j)
            for b in o.writes:
                if b.notrack:
                    continue
                for ob in b.alias:
                    if ob.lw is not None:
                        deps.add(ob.lw)
                    deps.update(ob.wd)
                    deps.update(ob.rd.values())
                    deps.update(ob.rdd)
                b.alias = []
                if b.lw is not None:
                    deps.add(b.lw)
                if not o.dma:
                    deps.update(b.wd)
                deps.update(b.rd.values())
                deps.update(b.rdd)
            for b in o.reads:
                if o.dma:
                    b.rdd.append(i)
                else:
                    b.rd[o.eng] = i
            for b in o.writes:
                if b.notrack:
                    continue
                had_readers = bool(b.rd or b.rdd)
                if o.dma:
                    if had_readers:
                        b.wd = [i]
                        b.lw = None
                    else:
                        b.wd.append(i)
                else:
                    b.lw = i
                    b.wd = []
                b.rd = {}
                b.rdd = []
            best = {}
            dl = []
            for j in deps:
                if j == i:
                    continue
                p = ops[j]
                e = p.eng
                if p.dma:
                    dl.append(j)
                else:
                    if e == "pe" and o.eng == "pe" and not o.dma:
                        continue
                    if e not in best or best[e] < j:
                        best[e] = j
            o.waits = sorted(set(list(best.values()) + dl))
            for j in o.waits:
                ops[j].sig = True
        cnt = {e: 0 for e in self.ENGS}
        dcount = {e: 0 for e in self.ENGS}
        slotcnt = {e: [0] * NSLOT for e in self.ENGS}
        for o in ops:
            if o.dma:
                kk = dcount[o.eng]
                dcount[o.eng] += 1
                s = kk % NSLOT
                o.slot = s
                o.prewait = slotcnt[o.eng][s]
                slotcnt[o.eng][s] += 16
                o.slotcnt = slotcnt[o.eng][s]
            elif o.sig:
                cnt[o.eng] += 1
                o.cnt = cnt[o.eng]
        self.final_dma = {e: list(slotcnt[e]) for e in self.ENGS}

    def emit(self):
        nc = self.nc
        ops = self.ops
        self.finalize()
        sems = {}
        for e in self.ENGS:
            g = nc.semaphore(f"s_{e}")
            sems[e] = g.__enter__()
            self._ctx.append(g)
        dsem = {}
        for e in self.ENGS:
            if any(o.dma and o.eng == e for o in ops):
                dsem[e] = []
                for s in range(NSLOT):
                    g = nc.semaphore(f"d_{e}_{s}")
                    dsem[e].append(g.__enter__())
                    self._ctx.append(g)
        per = {e: [o for o in ops if o.eng == e] for e in self.ENGS}
        final_dma = self.final_dma

        def run(eng_name, eng):
            waited = {}
            for o in per[eng_name]:
                if o.dma and o.prewait > 0:
                    key = ("d", eng_name, o.slot)
                    if waited.get(key, 0) < o.prewait:
                        eng.wait_ge(dsem[eng_name][o.slot], o.prewait)
                        waited[key] = o.prewait
                for j in o.waits:
                    p = ops[j]
                    if p.dma:
                        key = ("d", p.eng, p.slot)
                        val = p.slotcnt
                        sem = dsem[p.eng][p.slot]
                    else:
                        key = ("c", p.eng)
                        val = p.cnt
                        sem = sems[p.eng]
                    if waited.get(key, 0) < val:
                        eng.wait_ge(sem, val)
                        waited[key] = val
                ins = o.fn(eng)
                if o.dma:
                    ins.then_inc(dsem[eng_name][o.slot], 16)
                elif o.sig:
                    ins.then_inc(sems[eng_name], 1)
            if eng_name in dsem:
                for s in range(NSLOT):
                    v = final_dma[eng_name][s]
                    if v > 0 and waited.get(("d", eng_name, s), 0) < v:
                        eng.wait_ge(dsem[eng_name][s], v)

        with nc.Block() as block:
            @block.sync
            def _(e):
                run("sp", e)

            @block.tensor
            def _(e):
                run("pe", e)

            @block.vector
            def _(e):
                run("dve", e)

            @block.scalar
            def _(e):
                run("act", e)

            @block.gpsimd
            def _(e):
                run("pool", e)
        return nc


def pvec_layout():
    names = []
    for i in range(2):
        for j in range(4):
            names.append((f"ng{i}{j}", 16))
    for i in range(2):
        names.append((f"bmod{i}", 96))
    for j in range(6):
        names.append((f"mix{j}", 16))
    for n in ("w0", "a0", "kk", "ka", "rk", "lcb", "ba", "bx", "lam"):
        names.append((n, 16))
    names.append(("bin", 32))
    for j in range(4):
        names.append((f"lcw{j}", 16))
    for i in range(2):
        for j in range(3):
            names.append((f"fcw{i}{j}", 44))
        names.append((f"fcb{i}", 44))
    names.append(("omka", 16))
    names.append(("cL", 16))
    names.append(("hba", 16))
    names.append(("hbx", 16))
    names.append(("hcL", 16))
    off = {}
    o = 0
    for n, w in names:
        off[n] = (o, w)
        o += w
    return off, o


PV_OFF, PV_N = pvec_layout()


def build(NSEG):
    k = KB()
    T = NSEG * TS
    IN = lambda n, s: k.dram(n, s, F32, kind="ExternalInput")
    def OUT(n, s):
        v = k.dram(n, s, F32, kind="ExternalOutput")
        v.b.notrack = True
        return v
    xp_d = IN("xp", [T, D])
    xs_d = IN("xs", [NS, D])
    call_d = IN("call", [17, D])
    stw_d = IN("st_wkv", [NS * NH, 4096])
    stsh_d = IN("st_shift", [NS, D])
    sth_d = IN("st_h", [NS, D])
    stlc_d = IN("st_lconv", [NS, 3, D])
    stfc_d = IN("st_fconv", [2, NS, 2, DFF])
    pvec_d = IN("pvec", [128, PV_N])
    rows_d = IN("rows", [3, D])
    cmask_d = IN("cmask", [128, 7, 512])
    W = {}
    for n, s in (("w_mod", [2, D, 6 * D]), ("rw_wr", [D, D]), ("rw_wk", [D, D]), ("rw_wv", [D, D]), ("rw_wo", [D, D]),
                 ("rw_w1", [D, 96]), ("rw_w2", [96, D]), ("rw_a1", [D, 128]), ("rw_a2", [128, D]),
                 ("rw_g1", [D, 256]), ("rw_g2", [256, D]), ("lru_w_in", [D, 2 * D]), ("lru_wa", [8, 256, 256]),
                 ("lru_wx", [8, 256, 256]), ("lru_w_out", [D, D]), ("ffn_w_gate", [2, D, DFF]),
                 ("ffn_w_up", [2, D, DFF]), ("ffn_w_down", [2, DFF, D])):
        W[n] = IN(n, s)
    yp_d = OUT("y_p", [T, D])
    ys_d = OUT("y_s", [NS, D])
    pwkv_d = OUT("p_wkv", [NH, 64, 64])
    pshift_d = OUT("p_shift", [16, 128])
    ph_d = OUT("p_h", [16, 128])
    plc_d = OUT("p_lconv", [3, D])
    pfc_d = OUT("p_fconv", [2, 2, DFF])
    swkv_d = OUT("s_wkv", [NS * NH, 4096])
    sshift_d = OUT("s_shift", [NS, D])
    sh_d = OUT("s_h", [NS, D])
    slc_d = OUT("s_lconv", [NS, 3, D])
    sfc_d = OUT("s_fconv", [2, NS, 2, DFF])
    import os
    DBG = os.environ.get("KDBG") == "1"
    dbg_n = {"n": 0}

    def dump(name, v, shape, dt_is_bf16=False):
        if not DBG:
            return
        d = k.dram("dbg_" + name, list(shape), F32, kind="ExternalOutput")
        if dt_is_bf16:
            k.dma(d, v, eng="pool")
        else:
            k.dma(d, v)
    STOP = int(os.environ.get("KSTOP", "0"))

    class StopBuild(Exception):
        pass

    def ckpt(n):
        if STOP == n:
            raise StopBuild()
    xres_d = [k.dram(f"xres{c}", [128, TS]) for c in range(16)]
    scr6_d = k.dram("scr6", [6, NS, D])
    scro_d = k.dram("scro", [NS, D])

    ones_f = k.alloc([128, 128], F32, "ones_f")
    k.memset("pool", ones_f, 1.0)
    identf = k.alloc([128, 128], F32, "identf")
    k.op("pool", lambda e: e.affine_select(out=identf.ap, in_=ones_f.ap, pattern=[[-1, 128]], compare_op=ALU.is_equal,
                                           fill=0.0, base=0, channel_multiplier=1), [ones_f], [identf])
    identb = k.alloc([128, 128], BF16, "identb")
    k.copy("pool", identb, identf)
    onesb = k.alloc([128, 128], BF16, "onesb")
    k.copy("pool", onesb, ones_f)
    blkdiag = k.alloc([128, 128], BF16, "blkdiag")
    k.memset("pool", blkdiag, 0.0)
    k.memset("pool", blkdiag[0:64, 0:64], 1.0)
    k.memset("pool", blkdiag[64:128, 64:128], 1.0)
    blk2 = k.alloc([128, 2], BF16, "blk2")
    k.memset("pool", blk2, 0.0)
    k.memset("pool", blk2[0:64, 0:1], 1.0)
    k.memset("pool", blk2[64:128, 1:2], 1.0)
    mA = k.alloc([128, 512], F32, "mA")
    mB = k.alloc([128, 512], F32, "mB")
    mC = k.alloc([128, 256], F32, "mC")

    def sel(dst, kind):
        if kind == "Ls":
            pat, base, cm = [[-1, 128]], -1, 1
        elif kind == "Us":
            pat, base, cm = [[1, 128]], -1, -1
        else:
            pat, base, cm = [[1, 128]], 0, -1
        k.op("pool", lambda e: e.affine_select(out=dst.ap, in_=ones_f.ap, pattern=pat, compare_op=ALU.is_ge,
                                               fill=0.0, base=base, channel_multiplier=cm), [ones_f], [dst])
    for i, kd in enumerate(("Ls", "Ls", "Us", "Us")):
        sel(mA[:, i * 128:(i + 1) * 128], kd)
    for i, kd in enumerate(("Us", "Us", "Ui", "Ui")):
        sel(mB[:, i * 128:(i + 1) * 128], kd)
    for i, kd in enumerate(("Ui", "Ui")):
        sel(mC[:, i * 128:(i + 1) * 128], kd)

    ML = k.alloc([128, 7, 512], BF16, "ML")
    for l_ in range(7):
        k.dma(ML[:, l_, :], cmask_d[:, l_, :], eng="pool")
    I4 = k.alloc([128, 512], BF16, "I4")
    for i_ in range(4):
        k.copy("pool", I4[:, i_ * 128:(i_ + 1) * 128], identf)
    pv = k.alloc([128, PV_N], F32, "pv")
    k.dma(pv, pvec_d)

    def P(name, c):
        o, w = PV_OFF[name]
        return pv[:, o + c:o + c + 1]

    def PR(name):
        o, w = PV_OFF[name]
        return pv[:, o:o + w]
    k.ts("dve", PR("omka"), PR("ka"), -1.0, 1.0, ALU.mult, ALU.add)
    k.act(PR("cL"), PR("lam"), AF.Exp, scale=-1.0)
    k.act(PR("cL"), PR("cL"), AF.Ln, bias=1.0)
    k.ts("dve", PR("cL"), PR("cL"), -8.0, None, ALU.mult)
    k.ts("dve", PR("hcL"), PR("cL"), 0.5, None, ALU.mult)
    k.ts("dve", PR("hba"), PR("ba"), 0.5, None, ALU.mult)
    k.ts("dve", PR("hbx"), PR("bx"), 0.5, None, ALU.mult)

    modT = [k.alloc([128, 96, 17], F32, f"modT{i}") for i in range(2)]
    shift_car = k.alloc([128, 16], F32, "shift_car")
    k.memset("pool", shift_car, 0.0)
    fcar = [k.alloc([128, 2, NF], F32, f"fcar{i}") for i in range(2)]
    for i in range(2):
        k.memset("pool", fcar[i], 0.0)
    lcar = k.alloc([128, 3, 16], F32, "lcar")
    k.memset("pool", lcar, 0.0)
    hcar = k.alloc([128, 16], F32, "hcar")
    k.memset("pool", hcar, 0.0)
    T32 = k.alloc([128, 16, 64], F32, "T32")
    k.memset("pool", T32, 0.0)
    xsT = [k.alloc([128, NS], F32, f"xsT{c}") for c in range(16)]
    wbufs = [k.alloc([128, WB_ELEMS], BF16, f"wb{i}") for i in range(NWB)]
    wstate = {"n": 0}

    class WS:
        def __init__(self, reqs, extra=None):
            self.reqs = reqs
            self.emitted = 0
            self.views = {}
            self.bufs = wbufs + (extra or [])
            self.n = wstate["n"] % NWB

        def _emit(self, i):
            buf = self.bufs[self.n % len(self.bufs)]
            self.n += 1
            wstate["n"] += 1
            off = 0
            vs = []
            for (src, nk) in self.reqs[i]:
                rows, cols = src.ap.shape
                pr = rows // nk
                dst = buf[0:pr, off:off + nk * cols].re("p (a b) -> p a b", b=cols)
                k.dma(dst, src.re("(a p) c -> p a c", p=pr), eng="pool")
                vs.append(dst)
                off += nk * cols
            assert off <= WB_ELEMS
            self.views[i] = vs

        def get(self, i):
            lim = min(len(self.reqs), i + len(self.bufs) - 1)
            while self.emitted < max(lim, i + 1):
                self._emit(self.emitted)
                self.emitted += 1
            return self.views.pop(i)

    evt = {"n": 0}

    def ev(out, in_):
        evt["n"] += 1
        k.copy("act" if evt["n"] % 2 else "dve", out, in_)

    def tok2fm(dst, src_d, R):
        m = k.mark()
        t = k.alloc([R, 128], F32, "t2f")
        k.dma(t, src_d)
        b = k.bank()
        k.tr(b[:, 0:R], t, identf[0:R, 0:R])
        ev(dst, b[:, 0:R])
        k.release(m)

    def fm2tok(dst_d, src, R):
        m = k.mark()
        t = k.alloc([R, 128], F32, "f2t")
        b = k.bank()
        k.tr(b[0:R, 0:128], src, identf)
        ev(t, b[0:R, 0:128])
        k.dma(dst_d, t)
        k.release(m)

    def compute_mod():
        m = k.mark()
        ct = k.alloc([17, D], F32, "ct")
        k.dma(ct, call_d)
        k.act(ct, ct, AF.Silu)
        siluT = k.alloc([128, 16, 17], BF16, "siluT")
        for c in range(16):
            b = k.bank()
            k.tr(b[:, 0:17], ct[:, c * 128:(c + 1) * 128], identf[0:17, 0:17])
            ev(siluT[:, c, :], b[:, 0:17])
        for i in range(2):
            ws = WS([[(W["w_mod"][i][:, g * 256:(g + 1) * 256], 16)] for g in range(48)])
            for g in range(48):
                (wv,) = ws.get(g)
                b = k.bank()
                for j in range(2):
                    for kc in range(16):
                        k.mm(b[:, j * 17:(j + 1) * 17], wv[:, kc, j * 128:(j + 1) * 128], siluT[:, kc, :],
                             start=(kc == 0), stop=(kc == 15))
                for j in range(2):
                    jc = g * 2 + j
                    k.act(modT[i][:, jc, :], b[:, j * 17:(j + 1) * 17], AF.Identity, bias=P(f"bmod{i}", jc))
            for c in range(16):
                k.ts("dve", modT[i][:, 16 + c, :], modT[i][:, 16 + c, :], 1.0, P(f"ng{i}0", c), ALU.add, ALU.mult)
                k.ts("dve", modT[i][:, 64 + c, :], modT[i][:, 64 + c, :], 1.0, P(f"ng{i}2", c), ALU.add, ALU.mult)
                k.ts("dve", modT[i][:, 32 + c, :], modT[i][:, 32 + c, :], P(f"ng{i}1", c), None, ALU.mult)
                k.ts("dve", modT[i][:, 80 + c, :], modT[i][:, 80 + c, :], P(f"ng{i}3", c), None, ALU.mult)
        k.release(m)

    def modv(i, j, c, sample):
        return modT[i][:, j * 16 + c, 1:17] if sample else modT[i][:, j * 16 + c, 0:1]

    def rstd_of(chunks, N):
        rstd = k.alloc([128, N], F32, "rstd")
        m = k.mark()
        sq = [k.alloc([128, N], BF16, "sq") for _ in range(2)]
        bank = k.reserve()
        for c in range(len(chunks)):
            k.act(sq[c % 2], chunks[c], AF.Square)
            k.mm(bank[:, 0:N], onesb, sq[c % 2], start=(c == 0), stop=(c == len(chunks) - 1))
        k.act(rstd, bank[:, 0:N], AF.Ln, bias=EPS, scale=1.0 / D)
        k.act(rstd, rstd, AF.Exp, scale=-0.5)
        k.unreserve(bank)
        k.release(m)
        return rstd

    def adaln_chunk(out, tmp, x_c, rstd, gsc, sh, sample):
        k.tt("dve", tmp, x_c, rstd, ALU.mult)
        if not sample:
            k.act(out, tmp, AF.Identity, bias=sh, scale=gsc)
        else:
            k.tt("dve", tmp, tmp, gsc, ALU.mult)
            k.tt("dve", out, tmp, sh, ALU.add)

    def load_x():
        xs_ = [k.alloc([128, TS], F32, f"x{c}") for c in range(16)]
        for c in range(16):
            k.dma(xs_[c], xres_d[c])
        return xs_

    def proj_post(layer, wsrc, KC, in_p, in_s, jgate, ncol, extra=None):
        m = k.mark()
        pout = [k.alloc([128, TS], F32, f"po{c}") for c in range(16)]
        pouts = [k.alloc([128, NS], F32, f"pos{c}") for c in range(16)] if in_s else None
        sq = [k.alloc([128, TS], BF16, "psq") for _ in range(2)]
        sqs = k.alloc([128, NS], BF16, "psqs")
        ngrp = D // ncol
        ws = WS([[(wsrc[:, g * ncol:(g + 1) * ncol], KC)] for g in range(ngrp)], extra)
        bssq = k.reserve()
        bssqs = k.reserve() if in_s else None
        for g in range(ngrp):
            (wv,) = ws.get(g)
            for j in range(ncol // 128):
                c = g * (ncol // 128) + j
                b = k.bank()
                for kc in range(KC):
                    k.mm(b, wv[:, kc, j * 128:(j + 1) * 128], in_p[kc], start=(kc == 0), stop=(kc == KC - 1))
                k.copy("act", pout[c], b)
                k.act(sq[c % 2], pout[c], AF.Square)
                k.mm(bssq, onesb, sq[c % 2], start=(c == 0), stop=(c == 15))
                if in_s:
                    b2 = k.bank()
                    for kc in range(KC):
                        k.mm(b2[:, 0:NS], wv[:, kc, j * 128:(j + 1) * 128], in_s[kc], start=(kc == 0), stop=(kc == KC - 1))
                    k.copy("dve", pouts[c], b2[:, 0:NS])
                    k.act(sqs, pouts[c], AF.Square)
                    k.mm(bssqs[:, 0:NS], onesb, sqs, start=(c == 0), stop=(c == 15))
        rstd = k.alloc([128, TS], F32, "prstd")
        k.act(rstd, bssq, AF.Ln, bias=EPS, scale=1.0 / D)
        k.act(rstd, rstd, AF.Exp, scale=-0.5)
        k.unreserve(bssq)
        if in_s:
            rstds = k.alloc([128, NS], F32, "prstds")
            k.act(rstds, bssqs[:, 0:NS], AF.Ln, bias=EPS, scale=1.0 / D)
            k.act(rstds, rstds, AF.Exp, scale=-0.5)
            k.unreserve(bssqs)
        xt = [k.alloc([128, TS], F32, "pxt") for _ in range(4)]
        for c in range(16):
            x_ = xt[c % 4]
            k.dma(x_, xres_d[c])
            k.tt("dve", pout[c], pout[c], rstd, ALU.mult)
            k.stt(x_, pout[c], modv(layer, jgate, c, False), x_, ALU.mult, ALU.add)
            k.dma(xres_d[c], x_)
            if in_s:
                k.tt("dve", pouts[c], pouts[c], rstds, ALU.mult)
                k.tt("dve", pouts[c], pouts[c], modv(layer, jgate, c, True), ALU.mult)
                k.tt("dve", xsT[c], xsT[c], pouts[c], ALU.add)
        k.release(m)

    def rwkv_layer(seg, do_s, last):
        L = 0
        mtop = k.mark()
        mixes = {j: [k.alloc([128, TS], BF16, f"mx{j}_{c}") for c in range(16)] for j in (0, 2, 3)}
        yg = [k.alloc([128, TS], BF16, f"yg{c}") for c in range(16)]
        tw = k.alloc([128, TS], BF16, "tw")
        la = k.alloc([128, TS], BF16, "la")
        sg = [k.alloc([128, TS], BF16, f"sg{i}") for i in range(2)]
        if do_s:
            smix = {j: [k.alloc([128, NS], BF16, f"smx{j}_{c}") for c in range(16)] for j in range(6)}
            ygs = [k.alloc([128, NS], BF16, f"ygs{c}") for c in range(16)]
            tws = k.alloc([128, NS], BF16, "tws")
            las = k.alloc([128, NS], BF16, "las")
            sgs = [k.alloc([128, NS], BF16, f"sgs{i}") for i in range(2)]
            gs = k.alloc([128, 16, NS], F32, "gs")
        m1 = k.mark()
        x = [k.alloc([128, TS], F32, f"x{c}") for c in range(16)]
        mxin = k.mark()
        xin = [k.alloc([128, D], F32, "xin") for _ in range(2)]
        for tt_ in range(4):
            xi = xin[tt_ % 2]
            k.dma(xi, xp_d[seg * TS + tt_ * 128: seg * TS + (tt_ + 1) * 128, :])
            for c4 in range(4):
                if os.environ.get("KVAR") == "notr":
                    continue
                b = k.bank()
                for q in range(4):
                    c = c4 * 4 + q
                    k.tr(b[:, q * 128:(q + 1) * 128], xi[:, c * 128:(c + 1) * 128], identf)
                for q in range(4):
                    c = c4 * 4 + q
                    ev(x[c][:, tt_ * 128:(tt_ + 1) * 128], b[:, q * 128:(q + 1) * 128])
        if os.environ.get("KVAR") != "nostore":
            for c in range(16):
                k.dma(xres_d[c], x[c])
        k.release(mxin)
        ckpt(11)
        rstd = rstd_of(x, TS)
        ckpt(12)
        wsl = WS([[(W["rw_w1"], 16), (W["rw_a1"], 16)], [(W["rw_g1"], 16)]])
        w1v, a1v = wsl.get(0)
        (g1v,) = wsl.get(1)
        bw, ba, bg0, bg1 = k.reserve(), k.reserve(), k.reserve(), k.reserve()
        hc = [k.alloc([128, 1 + TS], F32, "hc") for _ in range(2)]
        xx = k.alloc([128, TS], F32, "xx")
        tmpn = k.alloc([128, TS], F32, "tmpn")
        tb = [k.alloc([128, TS], BF16, "tb") for _ in range(3)]
        for c in range(16):
            h_ = hc[c % 2]
            k.copy("dve", h_[:, 0:1], shift_car[:, c:c + 1])
            adaln_chunk(h_[:, 1:1 + TS], tmpn, x[c], rstd, modv(L, 1, c, False), modv(L, 0, c, False), False)
            k.copy("dve", shift_car[:, c:c + 1], h_[:, TS:TS + 1])
            k.tt("dve", xx, h_[:, 0:TS], h_[:, 1:1 + TS], ALU.subtract)
            ti = 0
            for j in range(6):
                if j in (0, 2, 3):
                    dst = mixes[j][c]
                else:
                    dst = tb[ti]
                    ti += 1
                k.stt(dst, xx, P(f"mix{j}", c), h_[:, 1:1 + TS], ALU.mult, ALU.add)
                if j == 1:
                    k.mm(bw[0:96, :], w1v[:, c, :], dst, start=(c == 0), stop=(c == 15))
                elif j == 4:
                    k.mm(ba, a1v[:, c, :], dst, start=(c == 0), stop=(c == 15))
                elif j == 5:
                    k.mm(bg0, g1v[:, c, 0:128], dst, start=(c == 0), stop=(c == 15))
                    k.mm(bg1, g1v[:, c, 128:256], dst, start=(c == 0), stop=(c == 15))
        k.act(tw[0:96, :], bw[0:96, :], AF.Tanh)
        k.copy("dve", la, ba)
        k.act(sg[0], bg0, AF.Sigmoid)
        k.act(sg[1], bg1, AF.Sigmoid)
        for b_ in (bw, ba, bg0, bg1):
            k.unreserve(b_)
        ckpt(13)
        if do_s:
            rstds = rstd_of(xsT, NS)
            hs = k.alloc([128, NS], F32, "hs")
            hp = k.alloc([128, NS], F32, "hp")
            xxs = k.alloc([128, NS], F32, "xxs")
            tmps = k.alloc([128, NS], F32, "tmps")
            for c in range(16):
                adaln_chunk(hs, tmps, xsT[c], rstds, modv(L, 1, c, True), modv(L, 0, c, True), True)
                tok2fm(hp, stsh_d[:, c * 128:(c + 1) * 128], NS)
                fm2tok(sshift_d[:, c * 128:(c + 1) * 128], hs, NS)
                k.tt("dve", xxs, hp, hs, ALU.subtract)
                for j in range(6):
                    k.stt(smix[j][c], xxs, P(f"mix{j}", c), hs, ALU.mult, ALU.add)
            b = k.bank()
            for c in range(16):
                k.mm(b[0:96, 0:16], w1v[:, c, :], smix[1][c], start=(c == 0), stop=(c == 15))
            for c in range(16):
                k.mm(b[:, 16:32], a1v[:, c, :], smix[4][c], start=(c == 0), stop=(c == 15))
            for c in range(16):
                k.mm(b[:, 32:48], g1v[:, c, 0:128], smix[5][c], start=(c == 0), stop=(c == 15))
            for c in range(16):
                k.mm(b[:, 48:64], g1v[:, c, 128:256], smix[5][c], start=(c == 0), stop=(c == 15))
            k.act(tws[0:96, :], b[0:96, 0:16], AF.Tanh)
            k.copy("dve", las, b[:, 16:32])
            k.act(sgs[0], b[:, 32:48], AF.Sigmoid)
            k.act(sgs[1], b[:, 48:64], AF.Sigmoid)
        k.release(m1)
        ckpt(2)

        m2 = k.mark()
        l2t = [k.alloc([128, 4, 128], BF16, f"lora2_{i}") for i in range(2)]
        A = lambda n, dt=F32, w=TS: k.alloc([128, w], dt, n)
        at_f, bt_f, kt_f, rt_f = [A(n, BF16) for n in ("at_f", "bt_f", "kt_f", "rt_f")]
        gTb = A("gTb", BF16)
        at_b, bt_b, rt_b = [k.alloc([128, 4, 2, 128], BF16, n) for n in ("at_b", "bt_b", "rt_b")]
        for t_ in (at_b, bt_b, rt_b):
            k.memset("pool", t_, 0.0)
        v32 = k.alloc([128, 4, 128], F32, "v32")
        vb = k.alloc([128, 4, 128], BF16, "vb")
        s_tok = A("s_tok", F32, 8)
        DL = A("DL", F32, 4)
        Tblk = k.alloc([128, 2, 64], BF16, "Tblk")
        k.memset("pool", Tblk, 0.0)
        Gbc = A("Gbc", F32, 128)
        Bbc = A("Bbc", F32, 128)
        if do_s:
            Q5 = k.alloc([128, 5, NS], F32, "Q5")
            sA = [A(f"sA{i}", F32, NS) for i in range(6)]
            sqs_ = A("sqs_", BF16, NS)
            q4t = k.alloc([NS, 4, 128], F32, "q4t")
            q2t = k.alloc([NS, 2, 128], F32, "q2t")
        ws = WS([[(W["rw_wr"][:, m * 128:(m + 1) * 128], 16), (W["rw_wk"][:, m * 128:(m + 1) * 128], 16),
                  (W["rw_wv"][:, m * 128:(m + 1) * 128], 16)] for m in range(16)])
        pre = {}

        def prefetch(m):
            mc = slice(m * 128, (m + 1) * 128)
            wr, wk, wv = ws.get(m)
            l2 = l2t[m % 2]
            k.dma(l2[0:96, 0, :], W["rw_w2"][:, mc], eng="pool")
            k.dma(l2[:, 1, :], W["rw_a2"][:, mc], eng="pool")
            k.dma(l2[:, 2, :], W["rw_g2"][0:128, mc], eng="pool")
            k.dma(l2[:, 3, :], W["rw_g2"][128:256, mc], eng="pool")
            br, bk, bv = k.reserve(), k.reserve(), k.reserve()
            for kc in range(16):
                k.mm(br, wr[:, kc, :], mixes[0][kc], start=(kc == 0), stop=(kc == 15))
            for kc in range(16):
                k.mm(bk, wk[:, kc, :], mixes[2][kc], start=(kc == 0), stop=(kc == 15))
            for t4 in range(4):
                for kc in range(16):
                    k.mm(bv[:, t4 * 128:(t4 + 1) * 128], mixes[3][kc][:, t4 * 128:(t4 + 1) * 128], wv[:, kc, :],
                         start=(kc == 0), stop=(kc == 15))
            pre[m] = (wr, wk, wv, l2, br, bk, bv)
        prefetch(0)
        for m in range(16):
            mc = slice(m * 128, (m + 1) * 128)
            wr, wk, wv, l2, br, bk, bv = pre.pop(m)
            k.dma(Gbc, V(rows_d.ap[0:1, mc].partition_broadcast(128), rows_d.b))
            k.dma(Bbc, V(rows_d.ap[1:2, mc].partition_broadcast(128), rows_d.b))
            mE = k.mark()
            r32, k32, ag, sgw, kk, cum, e_pos, e_neg, e_prev, tA, tB = [A(n) for n in
                ("r32", "k32", "ag", "sgw", "kk", "cum", "e_pos", "e_neg", "e_prev", "tA", "tB")]
            sqb, rkr = A("sqb", BF16), A("rkr", BF16)
            k.copy("act", r32, br)
            k.copy("act", k32, bk)
            k.copy("dve", v32.re("p a b -> p (a b)"), bv)
            k.copy("dve", vb.re("p a b -> p (a b)"), bv)
            for b_ in (br, bk, bv):
                k.unreserve(b_)
            b = k.bank()
            k.mm(b, l2[0:96, 0, :], tw[0:96, :])
            k.act(sgw, b, AF.Sigmoid, bias=P("w0", m))
            b = k.bank()
            k.mm(b, l2[:, 1, :], la)
            k.act(ag, b, AF.Sigmoid, bias=P("a0", m))
            b = k.bank()
            k.mm(b, l2[:, 2, :], sg[0], start=True, stop=False)
            k.mm(b, l2[:, 3, :], sg[1], start=False, stop=True)
            k.copy("act", gTb, b)
            k.ts("dve", kk, k32, P("kk", m), None, ALU.mult)
            k.act(sqb, kk, AF.Square)
            b = k.bank()
            k.mm(b, blkdiag, sqb)
            k.act(tA, b, AF.Ln)
            k.act(tA, tA, AF.Exp, scale=-0.5)
            k.tt("dve", kk, kk, tA, ALU.mult)
            k.ts("pool", tB, ag, P("ka", m), P("omka", m), ALU.mult, ALU.add)
            k.tt("pool", k32, k32, tB, ALU.mult)
            k.stt(rkr, r32, P("rk", m), k32, ALU.mult, ALU.mult)
            b = k.bank()
            for t4 in range(4):
                k.mm(b[:, t4 * 2:(t4 + 1) * 2], rkr[:, t4 * 128:(t4 + 1) * 128], blk2)
            k.copy("act", s_tok, b[:, 0:8])
            for cc in range(4):
                sl = slice(cc * 128, (cc + 1) * 128)
                k.op("dve", lambda e, sl=sl: e.tensor_tensor_scan(out=cum.ap[:, sl], data0=ones_f.ap, data1=sgw.ap[:, sl],
                                                                  initial=0.0, op0=ALU.mult, op1=ALU.add),
                     [ones_f, sgw], [cum])
            k.act(e_pos, cum, AF.Exp, scale=DEC)
            k.act(e_neg, cum, AF.Exp, scale=-DEC)
            k.tt("pool", tA, cum, sgw, ALU.subtract)
            k.act(e_prev, tA, AF.Exp, scale=DEC)
            k.copy("act", DL, e_pos.re("p (c t) -> p c t", t=128)[:, :, 127])
            k.tt("dve", rt_f, r32, e_pos, ALU.mult)
            k.tt("pool", kt_f, k32, e_neg, ALU.mult)
            k.tt("pool", tB, kk, ag, ALU.mult)
            k.tt("dve", bt_f, tB, e_neg, ALU.mult)
            k.stt(at_f, kk, -1.0, e_prev, ALU.mult, ALU.mult)
            for (Xb_, Xf_) in ((at_b, at_f), (bt_b, bt_f), (rt_b, rt_f)):
                for h in range(2):
                    k.copy("act" if h else "pool", Xb_[64 * h:64 * h + 64, :, h, :],
                           Xf_[64 * h:64 * h + 64, :].re("p (c t) -> p c t", t=128))
            if do_s:
                r_s, k_s, sgw_s, ag_s, kk_s, t_s = sA
                b = k.bank()
                for kc in range(16):
                    k.mm(b[:, 0:16], wr[:, kc, :], smix[0][kc], start=(kc == 0), stop=(kc == 15))
                for kc in range(16):
                    k.mm(b[:, 16:32], wk[:, kc, :], smix[2][kc], start=(kc == 0), stop=(kc == 15))
                k.mm(b[:, 32:48], l2[0:96, 0, :], tws[0:96, :])
                k.mm(b[:, 48:64], l2[:, 1, :], las)
                k.mm(b[:, 64:80], l2[:, 2, :], sgs[0], start=True, stop=False)
                k.mm(b[:, 64:80], l2[:, 3, :], sgs[1], start=False, stop=True)
                bv_ = k.bank()
                for kc in range(16):
                    k.mm(bv_[0:NS, 0:128], smix[3][kc], wv[:, kc, :], start=(kc == 0), stop=(kc == 15))
                k.copy("act", Q5[:, 0, :], b[:, 0:16])
                k.copy("act", k_s, b[:, 16:32])
                k.act(sgw_s, b[:, 32:48], AF.Sigmoid, bias=P("w0", m))
                k.act(ag_s, b[:, 48:64], AF.Sigmoid, bias=P("a0", m))
                k.copy("act", gs[:, m, :], b[:, 64:80])
                k.ts("dve", kk_s, k_s, P("kk", m), None, ALU.mult)
                k.act(sqs_, kk_s, AF.Square)
                b2 = k.bank()
                k.mm(b2[:, 0:NS], blkdiag, sqs_)
                k.act(t_s, b2[:, 0:NS], AF.Ln)
                k.act(t_s, t_s, AF.Exp, scale=-0.5)
                k.tt("dve", kk_s, kk_s, t_s, ALU.mult)
                k.act(Q5[:, 1, :], sgw_s, AF.Exp, scale=DEC)
                k.ts("dve", t_s, ag_s, P("ka", m), P("omka", m), ALU.mult, ALU.add)
                k.tt("dve", Q5[:, 2, :], k_s, t_s, ALU.mult)
                k.ts("dve", Q5[:, 3, :], kk_s, -1.0, None, ALU.mult)
                k.tt("dve", Q5[:, 4, :], kk_s, ag_s, ALU.mult)
                bq = k.bank()
                for q in range(4):
                    k.tr(bq[0:NS, q * 128:(q + 1) * 128], Q5[:, q, :], identf)
                k.tr(bv_[0:NS, 128:256], Q5[:, 4, :], identf)
                k.copy("act", q4t.re("p a b -> p (a b)"), bq[0:NS, 0:512])
                k.copy("act", q2t.re("p a b -> p (a b)"), bv_[0:NS, 0:256])
                k.dma(scr6_d[0:4, :, mc].re("q r c -> r q c"), q4t)
                k.dma(scr6_d[4:6, :, mc].re("q r c -> r q c"), q2t)
            k.release(mE)
            mCh = k.mark()
            C4 = range(4)
            tok3 = [A("tok3", BF16, 384) for _ in C4]
            XA = [[A("XA", BF16, 512) for _ in range(2)] for _ in C4]
            AA = [A("AA", BF16, 512) for _ in C4]
            RQ = [A("RQ", BF16, 512) for _ in C4]
            XB = [A("XB", BF16, 512) for _ in C4]
            XC = [A("XC", BF16, 256) for _ in C4]
            Z32 = [A("Z32", F32, 256) for _ in C4]
            Zb = [A("Zb", BF16, 256) for _ in C4]
            AwT = [A("AwT", BF16, 128) for _ in C4]
            Ub = A("Ub", BF16, 128)
            tmpT = A("tmpT", F32, 64)
            y32 = [A("y32", F32, 128) for _ in C4]
            o1 = [A("o1", F32, 128) for _ in C4]
            st6L = [A("st6", F32, 12) for _ in C4]
            mvL = [A("mv", F32, 4) for _ in C4]
            rs2L = [A("rs2", F32, 2) for _ in C4]
            pwk = A("pwk", F32, 128)
            SL = [slice(cc * 128, (cc + 1) * 128) for cc in C4]
            atb = [at_b[:, cc].re("p a b -> p (a b)") for cc in C4]
            btb = [bt_b[:, cc].re("p a b -> p (a b)") for cc in C4]
            rtb = [rt_b[:, cc].re("p a b -> p (a b)") for cc in C4]
            for cc in C4:
                PBk = k.bank()
                PB = V(PBk.ap.bitcast(BF16), PBk.b)
                for i_, X_ in enumerate((at_f, bt_f, kt_f)):
                    k.tr(PB[:, i_ * 128:(i_ + 1) * 128], X_[:, SL[cc]], identb)
                k.copy("act", tok3[cc], PB[:, 0:384])
            for cc in C4:
                bA = k.bank()
                k.mm(bA[:, 0:256], at_f[:, SL[cc]], btb[cc])
                k.mm(bA[:, 256:512], bt_f[:, SL[cc]], atb[cc])
                k.tt("dve", AA[cc], bA, mA, ALU.mult)
            for cc in C4:
                bB = k.bank()
                k.mm(bB[:, 0:256], kt_f[:, SL[cc]], atb[cc])
                k.mm(bB[:, 256:512], bt_f[:, SL[cc]], rtb[cc])
                k.tt("dve", XB[cc], bB, mB, ALU.mult)
            for cc in C4:
                bC = k.bank()
                k.mm(bC[:, 0:256], kt_f[:, SL[cc]], rtb[cc])
                k.tt("dve", XC[cc], bC[:, 0:256], mC, ALU.mult)
            for cc in C4:
                bZ = k.bank()
                for h in range(2):
                    k.mm(bZ[:, h * 64:(h + 1) * 64], XB[cc][:, h * 128:(h + 1) * 128], vb[:, cc, h * 64:(h + 1) * 64])
                Z32v = Z32[cc].re("p (h a v) -> p h a v", h=2, a=2)
                k.copy("act", Z32v[:, :, 0, :], tok3[cc][:, 0:128].re("p (h v) -> p h v", v=64))
                k.copy("act", Z32v[:, :, 1, :], bZ[:, 0:128].re("p (h v) -> p h v", v=64))
                k.copy("pool", Zb[cc], Z32[cc])
            cur = 0
            for cc in C4:
                k.tt("pool", RQ[cc], AA[cc], ML[:, 0, :], ALU.mult)
                k.tt("pool", XA[cc][0], RQ[cc], I4, ALU.add)
            for l_ in range(1, 7):
                b1 = []
                for cc in C4:
                    Wc = XA[cc][cur]
                    b1_ = k.bank()
                    b1.append(b1_)
                    for h in range(2):
                        k.mm(b1_[:, h * 128:(h + 1) * 128], AA[cc][:, 256 + h * 128:256 + (h + 1) * 128], Wc[:, h * 128:(h + 1) * 128])
                    for h in range(2):
                        k.mm(b1_[:, 256 + h * 128:256 + (h + 1) * 128], AA[cc][:, h * 128:(h + 1) * 128], Wc[:, 256 + h * 128:256 + (h + 1) * 128])
                for cc in C4:
                    k.tt("dve", RQ[cc], b1[cc], ML[:, l_, :], ALU.mult)
                b2 = []
                for cc in C4:
                    Wc = XA[cc][cur]
                    b2_ = k.bank()
                    b2.append(b2_)
                    for h in range(2):
                        o_ = b2_[:, h * 128:(h + 1) * 128]
                        k.mm(o_, Wc[:, 256 + h * 128:256 + (h + 1) * 128], RQ[cc][:, h * 128:(h + 1) * 128], start=True, stop=False)
                        k.mm(o_, identb, Wc[:, h * 128:(h + 1) * 128], start=False, stop=True)
                    for h in range(2):
                        o_ = b2_[:, 256 + h * 128:256 + (h + 1) * 128]
                        k.mm(o_, Wc[:, h * 128:(h + 1) * 128], RQ[cc][:, 256 + h * 128:256 + (h + 1) * 128], start=True, stop=False)
                        k.mm(o_, identb, Wc[:, 256 + h * 128:256 + (h + 1) * 128], start=False, stop=True)
                for cc in C4:
                    k.copy("act", XA[cc][1 - cur], b2[cc])
                cur = 1 - cur
            for cc in C4:
                Wc = XA[cc][cur]
                bz = k.bank()
                for h in range(2):
                    k.mm(bz[:, h * 128:(h + 1) * 128], Wc[:, 256 + h * 128:256 + (h + 1) * 128], Zb[cc][:, h * 128:(h + 1) * 128])
                k.copy("dve", Z32[cc], bz[:, 0:256])
                k.copy("act", Zb[cc], bz[:, 0:256])
            for cc in C4:
                PBk = k.bank()
                PB = V(PBk.ap.bitcast(BF16), PBk.b)
                for h in range(2):
                    k.tr(PB[64 * h:64 * h + 64, 0:128], Zb[cc][:, h * 128:h * 128 + 64], identb)
                k.copy("act", AwT[cc], PB[:, 0:128])
            if m + 1 < 16:
                prefetch(m + 1)
            for h in range(2):
                k.copy("dve", Tblk[64 * h:64 * h + 64, h, :], T32[64 * h:64 * h + 64, m, :])
            Tflat = Tblk.re("p a b -> p (a b)")
            for cc in C4:
                sl = SL[cc]
                b_tok, k_tok = tok3[cc][:, 128:256], tok3[cc][:, 256:384]
                Z32v = Z32[cc].re("p (h a v) -> p h a v", h=2, a=2)
                bU = k.bank()
                k.mm(bU[:, 0:128], AwT[cc], Tflat)
                k.tt("dve", Ub.re("p (h v) -> p h v", v=64), bU[:, 0:128].re("p (h v) -> p h v", v=64), Z32v[:, :, 1, :], ALU.add)
                bT = k.bank()
                for h in range(2):
                    hs_ = slice(h * 64, (h + 1) * 64)
                    k.mm(bT[64 * h:64 * h + 64, 0:64], b_tok[:, hs_], Ub[:, hs_], start=True, stop=False)
                    k.mm(bT[64 * h:64 * h + 64, 0:64], k_tok[:, hs_], vb[:, cc, hs_], start=False, stop=True)
                bY = k.bank()
                k.mm(bY[:, 0:128], rt_f[:, sl], Tflat, start=True, stop=False)
                for h in range(2):
                    hs_ = slice(h * 64, (h + 1) * 64)
                    k.mm(bY[:, hs_], XB[cc][:, 256 + h * 128:256 + (h + 1) * 128], Ub[:, hs_], start=False, stop=False)
                    k.mm(bY[:, hs_], XC[cc][:, h * 128:(h + 1) * 128], vb[:, cc, hs_], start=False, stop=(h == 1))
                k.tt("dve", tmpT, bT[:, 0:64], T32[:, m, :], ALU.add)
                for h in range(2):
                    k.ts("dve", Tblk[64 * h:64 * h + 64, h, :], tmpT[64 * h:64 * h + 64, :], DL[64 * h:64 * h + 64, cc:cc + 1], None, ALU.mult)
                k.act(T32[:, m, :], tmpT, AF.Identity, scale=DL[:, cc:cc + 1])
                k.copy("act", y32[cc], bY[:, 0:128])
            for cc in C4:
                y_, st6, mv = y32[cc], st6L[cc], mvL[cc]
                for h in range(2):
                    hs_ = slice(h * 64, (h + 1) * 64)
                    k.op("dve", lambda e, h=h, hs_=hs_, y_=y_, st6=st6: e.bn_stats(out=st6.ap[:, h * 6:(h + 1) * 6], in_=y_.ap[:, hs_]),
                         [y_], [st6])
                    k.op("dve", lambda e, h=h, st6=st6, mv=mv: e.bn_aggr(out=mv.ap[:, 2 * h:2 * h + 2], in_=st6.ap[:, h * 6:(h + 1) * 6]),
                         [st6], [mv])
            for cc in C4:
                mvv = mvL[cc].re("p (h t) -> p h t", t=2)
                k.act(rs2L[cc], mvv[:, :, 1], AF.Ln, bias=LNX_EPS)
            for cc in C4:
                k.act(rs2L[cc], rs2L[cc], AF.Exp, scale=-0.5)
            for cc in C4:
                for h in range(2):
                    hs_ = slice(h * 64, (h + 1) * 64)
                    k.ts("pool", o1[cc][:, hs_], y32[cc][:, hs_], mvL[cc][:, 2 * h:2 * h + 1], rs2L[cc][:, h:h + 1], ALU.subtract, ALU.mult)
                k.tt("pool", o1[cc], o1[cc], Gbc, ALU.mult)
                k.tt("pool", o1[cc], o1[cc], Bbc, ALU.add)
            for cc in C4:
                for h in range(2):
                    hs_ = slice(h * 64, (h + 1) * 64)
                    k.stt(o1[cc][:, hs_], v32[:, cc, hs_], s_tok[:, cc * 2 + h:cc * 2 + h + 1], o1[cc][:, hs_], ALU.mult, ALU.add)
            bOs = []
            for cc in C4:
                bO = k.bank()
                bOs.append(bO)
                k.tr(bO[:, 0:128], o1[cc], identf)
            for cc in C4:
                k.tt("dve", yg[m][:, SL[cc]], bOs[cc][:, 0:128], gTb[:, SL[cc]], ALU.mult)
            if last:
                b = k.bank()
                k.tr(b[0:64, 0:128], T32[:, m, :], identf)
                k.copy("act", pwk[0:64, :], b[0:64, 0:128])
                k.dma(pwkv_d[2 * m:2 * m + 2].re("h v k -> v h k"), pwk[0:64, :].re("p (h k) -> p h k", k=64))
            k.release(mCh)
        k.release(m2)
        ckpt(3)

        if do_s:
            m3 = k.mark()
            Grh = k.alloc([128, 64], F32, "Grh")
            Brh = k.alloc([128, 64], F32, "Brh")
            RKrh = k.alloc([128, 64], F32, "RKrh")
            for r4 in range(4):
                k.dma(Grh[32 * r4:32 * r4 + 32, :], rows_d[0].re("(h v) -> h v", v=64))
                k.dma(Brh[32 * r4:32 * r4 + 32, :], rows_d[1].re("(h v) -> h v", v=64))
                k.dma(RKrh[32 * r4:32 * r4 + 32, :], rows_d[2].re("(h v) -> h v", v=64))
            q6 = k.alloc([128, 6, 64], F32, "q6")
            S = k.alloc([128, 32, 64], F32, "S")
            tmp = k.alloc([128, 32, 64], F32, "Stmp")
            sa = k.alloc([128, 32], F32, "sa")
            yrh = k.alloc([128, 64], F32, "yrh")
            orh = k.alloc([128, 64], F32, "orh")
            t64 = k.alloc([128, 64], F32, "t64")
            s1 = k.alloc([128, 1], F32, "s1")
            st6b = k.alloc([128, 6], F32, "st6b")
            mvb = k.alloc([128, 2], F32, "mvb")
            rsb = k.alloc([128, 1], F32, "rsb")
            B3 = [128, 32, 64]
            for i in range(4):
                k.dma(q6, scr6_d[:, 4 * i:4 * i + 4, :].re("q r (h k) -> (r h) q k", k=64))
                r_, d_, k_, a_, v_, b_ = [q6[:, q, :] for q in range(6)]
                for vh in range(2):
                    vs = slice(vh * 32, (vh + 1) * 32)
                    k.dma(S, stw_d[i * 128:(i + 1) * 128, vh * 2048:(vh + 1) * 2048].re("p (v k) -> p v k", k=64))
                    k.tt("dve", tmp, S, a_.us(1).bc(B3), ALU.mult)
                    k.op("dve", lambda e: e.tensor_reduce(out=sa.ap, in_=tmp.ap, axis=AX.X, op=ALU.add), [tmp], [sa])
                    k.tt("dve", S, S, d_.us(1).bc(B3), ALU.mult)
                    k.tt("dve", tmp, sa.us(2).bc(B3), b_.us(1).bc(B3), ALU.mult)
                    k.tt("dve", S, S, tmp, ALU.add)
                    k.tt("dve", tmp, v_[:, vs].us(2).bc(B3), k_.us(1).bc(B3), ALU.mult)
                    k.tt("dve", S, S, tmp, ALU.add)
                    k.dma(swkv_d[i * 128:(i + 1) * 128, vh * 2048:(vh + 1) * 2048].re("p (v k) -> p v k", k=64), S)
                    k.tt("dve", tmp, S, r_.us(1).bc(B3), ALU.mult)
                    k.op("dve", lambda e, vs=vs: e.tensor_reduce(out=yrh.ap[:, vs], in_=tmp.ap, axis=AX.X, op=ALU.add),
                         [tmp], [yrh])
                k.op("dve", lambda e: e.bn_stats(out=st6b.ap, in_=yrh.ap), [yrh], [st6b])
                k.op("dve", lambda e: e.bn_aggr(out=mvb.ap, in_=st6b.ap), [st6b], [mvb])
                k.act(rsb, mvb[:, 1:2], AF.Ln, bias=LNX_EPS)
                k.act(rsb, rsb, AF.Exp, scale=-0.5)
                k.ts("dve", orh, yrh, mvb[:, 0:1], rsb, ALU.subtract, ALU.mult)
                k.tt("dve", orh, orh, Grh, ALU.mult)
                k.tt("dve", orh, orh, Brh, ALU.add)
                k.tt("dve", t64, r_, k_, ALU.mult)
                k.tt("dve", t64, t64, RKrh, ALU.mult)
                k.op("dve", lambda e: e.tensor_reduce(out=s1.ap, in_=t64.ap, axis=AX.X, op=ALU.add), [t64], [s1])
                k.stt(orh, v_, s1, orh, ALU.mult, ALU.add)
                k.dma(scro_d[4 * i:4 * i + 4, :].re("r (h v) -> (r h) v", v=64), orh)
            of = k.alloc([128, NS], F32, "of")
            for m in range(16):
                tok2fm(of, scro_d[:, m * 128:(m + 1) * 128], NS)
                k.tt("dve", ygs[m], of, gs[:, m, :], ALU.mult)
            k.release(m3)

        ckpt(4)
        proj_post(L, W["rw_wo"], 16, yg, ygs if do_s else None, 2, 256)
        ckpt(5)
        k.release(mtop)

    def ffn_layer(L, seg, do_s, last):
        mtop = k.mark()
        xw = [k.alloc([128, WB_ELEMS], BF16, f"xwb{i}") for i in range(2)]
        z = [k.alloc([128, TS], BF16, f"z{f}") for f in range(NF)]
        zs = [k.alloc([128, NS], BF16, f"zs{f}") for f in range(NF)] if do_s else None
        m1 = k.mark()
        h2 = [k.alloc([128, TS], BF16, f"h2_{c}") for c in range(16)]
        h2s = [k.alloc([128, NS], BF16, f"h2s_{c}") for c in range(16)] if do_s else None
        m2 = k.mark()
        x = load_x()
        rstd = rstd_of(x, TS)
        tmpn = k.alloc([128, TS], F32, "tmpn")
        for c in range(16):
            adaln_chunk(h2[c], tmpn, x[c], rstd, modv(L, 4, c, False), modv(L, 3, c, False), False)
        if do_s:
            rstds = rstd_of(xsT, NS)
            tmps = k.alloc([128, NS], F32, "tmps")
            for c in range(16):
                adaln_chunk(h2s[c], tmps, xsT[c], rstds, modv(L, 4, c, True), modv(L, 3, c, True), True)
        k.release(m2)
        gt = [k.alloc([128, 2 + TS], F32, "gt") for _ in range(2)]
        t1 = [k.alloc([128, TS], F32, "t1") for _ in range(2)]
        if do_s:
            cs = k.alloc([128, 32], F32, "cs")
            gts = k.alloc([128, NS], F32, "gts")
            t1s = k.alloc([128, NS], F32, "t1s")
            o2 = k.alloc([128, NS, 2], F32, "o2")
            tk = k.alloc([32, 128], F32, "tk")
            tk2 = k.alloc([32, 128], F32, "tk2")
        reqs = []
        for g in range(22):
            reqs.append([(W["ffn_w_gate"][L][:, g * 256:(g + 1) * 256], 16)])
            reqs.append([(W["ffn_w_up"][L][:, g * 256:(g + 1) * 256], 16)])
        ws = WS(reqs, xw)
        for g in range(22):
            (wg,) = ws.get(2 * g)
            (wu,) = ws.get(2 * g + 1)
            for j in range(2):
                f = 2 * g + j
                fc = slice(f * 128, (f + 1) * 128)
                js = slice(j * 128, (j + 1) * 128)
                bg = k.bank()
                for kc in range(16):
                    k.mm(bg, wg[:, kc, js], h2[kc], start=(kc == 0), stop=(kc == 15))
                bu = k.bank()
                for kc in range(16):
                    k.mm(bu, wu[:, kc, js], h2[kc], start=(kc == 0), stop=(kc == 15))
                g_ = gt[f % 2]
                t_ = t1[f % 2]
                k.copy("dve", g_[:, 0:2], fcar[L][:, :, f])
                k.copy("act", g_[:, 2:2 + TS], bg)
                k.copy("dve", fcar[L][:, :, f], g_[:, TS:TS + 2])
                k.ts("dve", t_, g_[:, 0:TS], P(f"fcw{L}0", f), P(f"fcb{L}", f), ALU.mult, ALU.add)
                k.stt(t_, g_[:, 1:1 + TS], P(f"fcw{L}1", f), t_, ALU.mult, ALU.add)
                k.stt(t_, g_[:, 2:2 + TS], P(f"fcw{L}2", f), t_, ALU.mult, ALU.add)
                k.act(t_, t_, AF.Gelu_apprx_tanh)
                k.tt("dve", z[f], bu, t_, ALU.mult)
                if do_s:
                    bs_ = k.bank()
                    for kc in range(16):
                        k.mm(bs_[:, 0:16], wg[:, kc, js], h2s[kc], start=(kc == 0), stop=(kc == 15))
                    for kc in range(16):
                        k.mm(bs_[:, 16:32], wu[:, kc, js], h2s[kc], start=(kc == 0), stop=(kc == 15))
                    k.dma(tk, stfc_d[L][:, :, fc].re("r j c -> (r j) c"))
                    b3 = k.bank()
                    k.tr(b3[:, 0:32], tk, identf[0:32, 0:32])
                    k.copy("act", cs, b3[:, 0:32])
                    csv = cs.re("p (r j) -> p r j", j=2)
                    k.copy("dve", gts, bs_[:, 0:16])
                    k.ts("dve", t1s, csv[:, :, 0], P(f"fcw{L}0", f), P(f"fcb{L}", f), ALU.mult, ALU.add)
                    k.stt(t1s, csv[:, :, 1], P(f"fcw{L}1", f), t1s, ALU.mult, ALU.add)
                    k.stt(t1s, gts, P(f"fcw{L}2", f), t1s, ALU.mult, ALU.add)
                    k.act(t1s, t1s, AF.Gelu_apprx_tanh)
                    k.tt("dve", zs[f], bs_[:, 16:32], t1s, ALU.mult)
                    k.copy("dve", o2[:, :, 0], csv[:, :, 1])
                    k.copy("dve", o2[:, :, 1], gts)
                    b4 = k.bank()
                    k.tr(b4[0:32, 0:128], o2.re("p r j -> p (r j)"), identf)
                    k.copy("act", tk2, b4[0:32, 0:128])
                    k.dma(sfc_d[L][:, :, fc].re("r j c -> (r j) c"), tk2)
        if last:
            t88 = k.alloc([88, 128], F32, "t88")
            b = k.bank()
            k.tr(b[0:88, 0:128], fcar[L].re("p j f -> p (j f)"), identf)
            k.copy("act", t88, b[0:88, 0:128])
            for j in range(2):
                k.dma(pfc_d[L][j].re("(f p) -> f p", p=128), t88[j * NF:(j + 1) * NF, :])
        k.release(m1)
        proj_post(L, W["ffn_w_down"][L], NF, z, zs, 5, 128, xw)
        k.release(mtop)

    def lru_layer(seg, do_s, last):
        L = 1
        mtop = k.mark()
        xw = [k.alloc([128, WB_ELEMS], BF16, f"xwb{i}") for i in range(3)]
        og = [k.alloc([128, TS], BF16, f"og{c}") for c in range(16)]
        ogs = [k.alloc([128, NS], BF16, f"ogs{c}") for c in range(16)] if do_s else None
        mL = k.mark()
        h = [k.alloc([128, TS], BF16, f"h_{c}") for c in range(16)]
        hs_l = [k.alloc([128, NS], BF16, f"hs_{c}") for c in range(16)] if do_s else None
        m2 = k.mark()
        x = load_x()
        rstd = rstd_of(x, TS)
        tmpn = k.alloc([128, TS], F32, "tmpn")
        for c in range(16):
            adaln_chunk(h[c], tmpn, x[c], rstd, modv(L, 1, c, False), modv(L, 0, c, False), False)
        if do_s:
            rstds = rstd_of(xsT, NS)
            tmps = k.alloc([128, NS], F32, "tmps")
            for c in range(16):
                adaln_chunk(hs_l[c], tmps, xsT[c], rstds, modv(L, 1, c, True), modv(L, 0, c, True), True)
        k.release(m2)
        wab = k.alloc([128, 8, 2, 256], BF16, "wab")
        wxb = k.alloc([128, 8, 2, 256], BF16, "wxb")
        k.dma(wab, W["lru_wa"].re("n (a p) c -> p n a c", p=128), eng="pool")
        k.dma(wxb, W["lru_wx"].re("n (a p) c -> p n a c", p=128), eng="pool")
        ybr = [k.alloc([128, TS], BF16, f"ybr{j}") for j in range(2)]
        xb32 = [k.alloc([128, 3 + TS], F32, f"xb32{j}") for j in range(2)]
        xc32 = [k.alloc([128, TS], F32, f"xc32{j}") for j in range(2)]
        xcb = [k.alloc([128, TS], BF16, f"xcb{j}") for j in range(2)]
        aa, mu, bt, hsn = [k.alloc([128, TS], F32, n) for n in ("aa", "mu", "bt", "hsn")]
        gA = [k.alloc([128, TS], F32, f"gA{j}") for j in range(2)]
        gX = [k.alloc([128, TS], F32, f"gX{j}") for j in range(2)]
        if do_s:
            ybrs = [k.alloc([128, NS], BF16, f"ybrs{j}") for j in range(2)]
            xbs = [k.alloc([128, NS], F32, f"xbs{j}") for j in range(2)]
            xcs = [k.alloc([128, NS], F32, f"xcs{j}") for j in range(2)]
            xcbs = [k.alloc([128, NS], BF16, f"xcbs{j}") for j in range(2)]
            csl = k.alloc([128, 48], F32, "csl")
            h0s = k.alloc([128, NS], F32, "h0s")
            o3 = k.alloc([128, NS, 3], F32, "o3")
            tk3 = k.alloc([48, 128], F32, "tk3")
            tk4 = k.alloc([48, 128], F32, "tk4")
            aas, mus, bts, hns = [k.alloc([128, NS], F32, n) for n in ("aas", "mus", "bts", "hns")]
            gAs = [k.alloc([128, NS], F32, f"gAs{j}") for j in range(2)]
            gXs = [k.alloc([128, NS], F32, f"gXs{j}") for j in range(2)]
        reqs = []
        for n in range(8):
            reqs.append([(W["lru_w_in"][:, n * 256:(n + 1) * 256], 16)])
            reqs.append([(W["lru_w_in"][:, D + n * 256:D + (n + 1) * 256], 16)])
        ws = WS(reqs, xw)
        for n in range(8):
            (wy,) = ws.get(2 * n)
            (wx_,) = ws.get(2 * n + 1)
            for j in range(2):
                c = 2 * n + j
                js = slice(j * 128, (j + 1) * 128)
                cc_ = slice(c * 128, (c + 1) * 128)
                b = k.bank()
                for kc in range(16):
                    k.mm(b, wy[:, kc, js], h[kc], start=(kc == 0), stop=(kc == 15))
                k.act(ybr[j], b, AF.Gelu_apprx_tanh, bias=P("bin", c))
                b = k.bank()
                for kc in range(16):
                    k.mm(b, wx_[:, kc, js], h[kc], start=(kc == 0), stop=(kc == 15))
                xb_ = xb32[j]
                k.copy("dve", xb_[:, 0:3], lcar[:, :, c])
                k.act(xb_[:, 3:3 + TS], b, AF.Identity, bias=P("bin", 16 + c))
                k.copy("dve", lcar[:, :, c], xb_[:, TS:TS + 3])
                k.ts("dve", xc32[j], xb_[:, 0:TS], P("lcw0", c), P("lcb", c), ALU.mult, ALU.add)
                for q in range(1, 4):
                    k.stt(xc32[j], xb_[:, q:q + TS], P(f"lcw{q}", c), xc32[j], ALU.mult, ALU.add)
                k.copy("act", xcb[j], xc32[j])
                if do_s:
                    b = k.bank()
                    for kc in range(16):
                        k.mm(b[:, 0:16], wy[:, kc, js], hs_l[kc], start=(kc == 0), stop=(kc == 15))
                    for kc in range(16):
                        k.mm(b[:, 16:32], wx_[:, kc, js], hs_l[kc], start=(kc == 0), stop=(kc == 15))
                    k.act(ybrs[j], b[:, 0:16], AF.Gelu_apprx_tanh, bias=P("bin", c))
                    k.act(xbs[j], b[:, 16:32], AF.Identity, bias=P("bin", 16 + c))
                    k.dma(tk3, stlc_d[:, :, cc_].re("r j c -> (r j) c"))
                    b3 = k.bank()
                    k.tr(b3[:, 0:48], tk3, identf[0:48, 0:48])
                    k.copy("act", csl, b3[:, 0:48])
                    cv = csl.re("p (r j) -> p r j", j=3)
                    k.ts("dve", xcs[j], cv[:, :, 0], P("lcw0", c), P("lcb", c), ALU.mult, ALU.add)
                    k.stt(xcs[j], cv[:, :, 1], P("lcw1", c), xcs[j], ALU.mult, ALU.add)
                    k.stt(xcs[j], cv[:, :, 2], P("lcw2", c), xcs[j], ALU.mult, ALU.add)
                    k.stt(xcs[j], xbs[j], P("lcw3", c), xcs[j], ALU.mult, ALU.add)
                    k.copy("act", xcbs[j], xcs[j])
                    k.copy("dve", o3[:, :, 0], cv[:, :, 1])
                    k.copy("dve", o3[:, :, 1], cv[:, :, 2])
                    k.copy("dve", o3[:, :, 2], xbs[j])
                    b4 = k.bank()
                    k.tr(b4[0:48, 0:128], o3.re("p r j -> p (r j)"), identf)
                    k.copy("act", tk4, b4[0:48, 0:128])
                    k.dma(slc_d[:, :, cc_].re("r j c -> (r j) c"), tk4)
            LN_HALF = -0.6931471805599453
            for jo in range(2):
                c = 2 * n + jo
                jos = slice(jo * 128, (jo + 1) * 128)
                ba_ = k.bank()
                for ji in range(2):
                    k.mm(ba_, wab[:, n, ji, jos], xcb[ji], start=(ji == 0), stop=(ji == 1))
                bx_ = k.bank()
                for ji in range(2):
                    k.mm(bx_, wxb[:, n, ji, jos], xcb[ji], start=(ji == 0), stop=(ji == 1))
                k.act(gA[jo], ba_, AF.Tanh, bias=P("hba", c), scale=0.5)
                k.act(gX[jo], bx_, AF.Tanh, bias=P("hbx", c), scale=0.5)
                if do_s:
                    bs_ = k.bank()
                    for ji in range(2):
                        k.mm(bs_[:, 0:16], wab[:, n, ji, jos], xcbs[ji], start=(ji == 0), stop=(ji == 1))
                    for ji in range(2):
                        k.mm(bs_[:, 16:32], wxb[:, n, ji, jos], xcbs[ji], start=(ji == 0), stop=(ji == 1))
                    k.act(gAs[jo], bs_[:, 0:16], AF.Tanh, bias=P("hba", c), scale=0.5)
                    k.act(gXs[jo], bs_[:, 16:32], AF.Tanh, bias=P("hbx", c), scale=0.5)
            for jo in range(2):
                c = 2 * n + jo
                cc_ = slice(c * 128, (c + 1) * 128)
                k.act(aa, gA[jo], AF.Exp, bias=P("hcL", c), scale=P("hcL", c))
                k.tt("dve", mu, aa, aa, ALU.mult)
                k.act(mu, mu, AF.Ln, bias=1.0, scale=-1.0)
                k.act(mu, mu, AF.Exp, bias=LN_HALF, scale=0.5)
                k.stt(bt, gX[jo], 1.0, xc32[jo], ALU.add, ALU.mult)
                k.tt("dve", mu, mu, bt, ALU.mult)
                if seg == 0:
                    k.ts("dve", mu[:, 0:1], bt[:, 0:1], 0.5, None, ALU.mult)
                k.op("dve", lambda e, c=c: e.tensor_tensor_scan(out=hsn.ap, data0=aa.ap, data1=mu.ap,
                                                                initial=hcar.ap[:, c:c + 1], op0=ALU.mult, op1=ALU.add),
                     [aa, mu, hcar], [hsn])
                k.copy("dve", hcar[:, c:c + 1], hsn[:, TS - 1:TS])
                k.tt("dve", og[c], hsn, ybr[jo], ALU.mult)
                if do_s:
                    k.act(aas, gAs[jo], AF.Exp, bias=P("hcL", c), scale=P("hcL", c))
                    k.tt("dve", mus, aas, aas, ALU.mult)
                    k.act(mus, mus, AF.Ln, bias=1.0, scale=-1.0)
                    k.act(mus, mus, AF.Exp, bias=LN_HALF, scale=0.5)
                    k.stt(bts, gXs[jo], 1.0, xcs[jo], ALU.add, ALU.mult)
                    k.tt("dve", mus, mus, bts, ALU.mult)
                    tok2fm(h0s, sth_d[:, cc_], NS)
                    k.tt("dve", hns, aas, h0s, ALU.mult)
                    k.tt("dve", hns, hns, mus, ALU.add)
                    fm2tok(sh_d[:, cc_], hns, NS)
                    k.tt("dve", ogs[c], hns, ybrs[jo], ALU.mult)
        if last:
            fm2tok(ph_d, hcar, 16)
            t48 = k.alloc([48, 128], F32, "t48")
            b = k.bank()
            k.tr(b[0:48, 0:128], lcar.re("p j c -> p (j c)"), identf)
            k.copy("act", t48, b[0:48, 0:128])
            for j in range(3):
                k.dma(plc_d[j].re("(c p) -> c p", p=128), t48[j * 16:(j + 1) * 16, :])
        k.release(mL)
        proj_post(L, W["lru_w_out"], 16, og, ogs, 2, 256, xw)
        k.release(mtop)

    try:
        compute_mod()
        ckpt(1)
        for c in range(16):
            tok2fm(xsT[c], xs_d[:, c * 128:(c + 1) * 128], NS)
        ckpt(10)
        for seg in range(NSEG):
            do_s = (seg == 0)
            last = (seg == NSEG - 1)
            rwkv_layer(seg, do_s, last)
            ffn_layer(0, seg, do_s, last)
            ckpt(6)
            lru_layer(seg, do_s, last)
            ckpt(7)
            ffn_layer(1, seg, do_s, last)
            m = k.mark()
            x = load_x()
            yt = [k.alloc([128, D], F32, "yt") for _ in range(2)]
            for t4 in range(4):
                y_ = yt[t4 % 2]
                for c4 in range(4):
                    b = k.bank()
                    for q in range(4):
                        c = c4 * 4 + q
                        k.tr(b[:, q * 128:(q + 1) * 128], x[c][:, t4 * 128:(t4 + 1) * 128], identf)
                    ev(y_[:, c4 * 512:(c4 + 1) * 512], b)
                k.dma(yp_d[seg * TS + t4 * 128:seg * TS + (t4 + 1) * 128, :], y_)
            k.release(m)
            if last:
                fm2tok(pshift_d, shift_car, 16)
        for c in range(16):
            fm2tok(ys_d[:, c * 128:(c + 1) * 128], xsT[c], NS)

    except StopBuild:
        pass
    nc = k.emit()
    return nc, k


def _fm(v):
    v = np.asarray(v, np.float32).reshape(-1, 128)
    return np.ascontiguousarray(v.T)


def make_pvec(I):
    pv = np.zeros((128, PV_N), np.float32)

    def put(name, v):
        o, w = PV_OFF[name]
        pv[:, o:o + w] = _fm(v)
    for i in range(2):
        for j in range(4):
            put(f"ng{i}{j}", I["norm_g"][i, j])
        put(f"bmod{i}", I["b_mod"][i])
    for j in range(6):
        put(f"mix{j}", I["rw_mix"][0, j])
    put("w0", I["rw_w0"][0]); put("a0", I["rw_a0"][0]); put("kk", I["rw_kk"][0]); put("ka", I["rw_ka"][0])
    put("rk", I["rw_rk"][0].reshape(-1)); put("lcb", I["lru_conv_b"][0]); put("ba", I["lru_ba"][0])
    put("bx", I["lru_bx"][0]); put("lam", I["lru_lambda"][0]); put("bin", I["lru_b_in"][0])
    for j in range(4):
        put(f"lcw{j}", I["lru_conv_w"][0, j])
    for i in range(2):
        for j in range(3):
            put(f"fcw{i}{j}", I["ffn_conv_w"][i, j])
        put(f"fcb{i}", I["ffn_conv_b"][i])
    return pv


def make_cmask():
    t = np.arange(128)[:, None]
    c = np.arange(128)[None, :]
    out = np.zeros((128, 7, 512), np.float32)
    for l in range(7):
        b = 1 << l
        M = ((t // (2 * b) == c // (2 * b)) & (t % (2 * b) >= b) & (c % (2 * b) < b)).astype(np.float32)
        out[:, l, 0:128] = M
        out[:, l, 128:256] = M
        out[:, l, 256:384] = M.T
        out[:, l, 384:512] = M.T
    return out


_CACHE = {}


def kernel(**I):
    I = {k_: np.asarray(v) for k_, v in I.items()}
    B, S, _ = I["x_prompt"].shape
    NSEG = S // TS
    if NSEG not in _CACHE:
        _CACHE[NSEG] = build(NSEG)[0]
    nc = _CACHE[NSEG]
    pv = make_pvec(I)
    rows = np.ascontiguousarray(np.stack([I["rw_lnx_g"][0], I["rw_lnx_b"][0], I["rw_rk"][0].reshape(-1)]).astype(np.float32))
    shared = {
        "pvec": pv, "rows": rows, "cmask": make_cmask(),
        "w_mod": I["w_mod"], "rw_wr": I["rw_wr"][0], "rw_wk": I["rw_wk"][0], "rw_wv": I["rw_wv"][0], "rw_wo": I["rw_wo"][0],
        "rw_w1": I["rw_w1"][0], "rw_w2": I["rw_w2"][0], "rw_a1": I["rw_a1"][0], "rw_a2": I["rw_a2"][0],
        "rw_g1": I["rw_g1"][0], "rw_g2": I["rw_g2"][0], "lru_w_in": I["lru_w_in"][0], "lru_wa": I["lru_wa"][0],
        "lru_wx": I["lru_wx"][0], "lru_w_out": I["lru_w_out"][0], "ffn_w_gate": I["ffn_w_gate"],
        "ffn_w_up": I["ffn_w_up"], "ffn_w_down": I["ffn_w_down"],
    }
    shared = {k_: np.ascontiguousarray(v, dtype=np.float32) for k_, v in shared.items()}
    in_maps = []
    for c in range(8):
        b = c % B
        r = slice(NS * c, NS * (c + 1))
        d = dict(shared)
        d["xp"] = np.ascontiguousarray(I["x_prompt"][b])
        d["xs"] = np.ascontiguousarray(I["x_sample"][r, 0])
        d["call"] = np.ascontiguousarray(np.concatenate([I["c_prompt"][b:b + 1], I["c_sample"][r]], 0))
        d["st_wkv"] = np.ascontiguousarray(I["state_rwkv_wkv"][0, r].reshape(NS * NH, 4096))
        d["st_shift"] = np.ascontiguousarray(I["state_rwkv_shift"][0, r])
        d["st_h"] = np.ascontiguousarray(I["state_lru_h"][0, r])
        d["st_lconv"] = np.ascontiguousarray(I["state_lru_conv"][0, r])
        d["st_fconv"] = np.ascontiguousarray(I["state_ffn_conv"][:, r])
        in_maps.append(d)
    res = run_bass_kernel_spmd(nc, in_maps, core_ids=list(range(8)))
    R = res.results
    f32 = np.float32
    y_prompt = np.stack([R[b]["y_p"] for b in range(B)]).astype(f32)
    y_sample = np.concatenate([R[c]["y_s"] for c in range(8)], 0).reshape(8 * NS, 1, D).astype(f32)
    p_wkv = np.stack([R[b]["p_wkv"] for b in range(B)])[None].astype(f32)
    p_shift = np.stack([R[b]["p_shift"].reshape(D) for b in range(B)])[None].astype(f32)
    p_h = np.stack([R[b]["p_h"].reshape(D) for b in range(B)])[None].astype(f32)
    p_lconv = np.stack([R[b]["p_lconv"] for b in range(B)])[None].astype(f32)
    p_fconv = np.stack([R[b]["p_fconv"] for b in range(B)], 1).astype(f32)
    s_wkv = np.concatenate([R[c]["s_wkv"].reshape(NS, NH, 64, 64) for c in range(8)], 0)[None].astype(f32)
    s_shift = np.concatenate([R[c]["s_shift"] for c in range(8)], 0)[None].astype(f32)
    s_h = np.concatenate([R[c]["s_h"] for c in range(8)], 0)[None].astype(f32)
    s_lconv = np.concatenate([R[c]["s_lconv"] for c in range(8)], 0)[None].astype(f32)
    s_fconv = np.concatenate([R[c]["s_fconv"] for c in range(8)], 1).astype(f32)
    return (y_prompt, y_sample, p_wkv, p_shift, p_h, p_lconv, p_fconv, s_wkv, s_shift, s_h, s_lconv, s_fconv)
```
